# Optimizing a Trainium2 kernel written in Bass

```python
import jax
import jax.numpy as jnp
from jax import lax
import numpy as np

D_MODEL = 1024
BATCH = 8
SEQ = 2048
DEPTH = 4

N_MIXERS = 3
HEAD_DIM = 64
N_HEADS = D_MODEL // HEAD_DIM
D_FF = 4 * D_MODEL
N_SHIFT_MIX = 6
DECAY_LORA = 64
ICLR_LORA = 64
VRES_LORA = 32
GATE_LORA = 160
CONV_WIDTH = 3
SB_BLOCK = 128
RMS_EPS = 1e-6
GN_EPS = 64e-5
N_RWKV = max((DEPTH + 2) // N_MIXERS, 1)
N_CONV = max((DEPTH + 1) // N_MIXERS, 1)
N_SB = max(DEPTH // N_MIXERS, 1)
N_VRES = max(N_RWKV - 1, 1)
N_KEYS = 40

kernel_name = 'hybrid_rwkv7_shortconv_stickbreaking'


def rms_norm(x, g):
    xf = x.astype(jnp.float32)
    y = xf * lax.rsqrt(jnp.mean(xf * xf, axis=-1, keepdims=True) + RMS_EPS)
    return (y * g.astype(jnp.float32)).astype(x.dtype)


def token_shift(x):
    return jnp.pad(x, ((0, 0), (1, 0), (0, 0)))[:, :-1]


def split_heads(t):
    b, s, _ = t.shape
    return t.reshape(b, s, N_HEADS, HEAD_DIM)


def wkv7_scan(r, w, k, v, a, b):
    bsz, _, nh, n = r.shape

    def step(state, inp):
        r_t, w_t, k_t, v_t, a_t, b_t = inp
        sa = jnp.einsum('bhvk,bhk->bhv', state, a_t)
        state = (state * w_t[:, :, None, :]
                 + sa[..., None] * b_t[:, :, None, :]
                 + v_t[..., None] * k_t[:, :, None, :])
        y_t = jnp.einsum('bhvk,bhk->bhv', state, r_t)
        return state, y_t

    xs = tuple(jnp.moveaxis(t, 1, 0) for t in (r, w, k, v, a, b))
    state0 = jnp.zeros((bsz, nh, n, n), jnp.float32)
    _, ys = lax.scan(step, state0, xs)
    return jnp.moveaxis(ys, 0, 1)


def rwkv7_time_mix(h, mu, w_r, w_k, w_v, w_o, w0, w1, w2, a0, a1, a2,
                   g1, g2, k_k, k_a, r_k, lnx_w, lnx_b, v_first, vres):
    f32 = jnp.float32
    bsz, s, d = h.shape
    xx = token_shift(h) - h
    xr = h + xx * mu[0]
    xw = h + xx * mu[1]
    xk = h + xx * mu[2]
    xv = h + xx * mu[3]
    xa = h + xx * mu[4]
    xg = h + xx * mu[5]
    r = xr @ w_r
    k = xk @ w_k
    v = xv @ w_v
    log_w = -jax.nn.softplus(-(w0 + jnp.tanh(xw @ w1) @ w2)) - 0.5
    decay = jnp.exp(-jnp.exp(log_w.astype(f32)))
    a = jax.nn.sigmoid(a0 + (xa @ a1) @ a2)
    g = jax.nn.sigmoid(xg @ g1) @ g2
    if vres is None:
        v_first = v
    else:
        v0, v1, v2 = vres
        v = v + (v_first - v) * jax.nn.sigmoid(v0 + (xv @ v1) @ v2)
    kk = split_heads((k * k_k).astype(f32))
    kk = kk * lax.rsqrt(jnp.maximum(jnp.sum(kk * kk, axis=-1, keepdims=True), 1e-24))
    k = k * (1.0 + (a - 1.0) * k_a)
    rh = split_heads(r.astype(f32))
    kh = split_heads(k.astype(f32))
    vh = split_heads(v.astype(f32))
    ah = split_heads(a.astype(f32))
    y = wkv7_scan(rh, split_heads(decay), kh, vh, -kk, kk * ah)
    mean = jnp.mean(y, axis=-1, keepdims=True)
    var = jnp.mean(jnp.square(y - mean), axis=-1, keepdims=True)
    y = ((y - mean) * lax.rsqrt(var + GN_EPS)).reshape(bsz, s, d)
    y = y * lnx_w.astype(f32) + lnx_b.astype(f32)
    bonus = jnp.sum(rh * kh * r_k.astype(f32), axis=-1, keepdims=True) * vh
    y = y + bonus.reshape(bsz, s, d)
    out = (y.astype(h.dtype) * g) @ w_o
    return out, v_first


def short_conv_mix(h, w_in, conv_w, w_out):
    s = h.shape[1]
    proj = h @ w_in
    d = proj.shape[-1] // 3
    b_gate = proj[..., :d]
    c_gate = proj[..., d:2 * d]
    u = proj[..., 2 * d:]
    z = c_gate * u
    zp = jnp.pad(z, ((0, 0), (CONV_WIDTH - 1, 0), (0, 0)))
    zc = (zp[:, 0:s] * conv_w[0]
          + zp[:, 1:s + 1] * conv_w[1]
          + zp[:, 2:s + 2] * conv_w[2])
    return (b_gate * zc) @ w_out


def stick_breaking_mix(h, w_qkv, q_gain, k_gain, w_o):
    f32 = jnp.float32
    bsz, s, d = h.shape
    proj = h @ w_qkv
    q = split_heads(proj[..., :d])
    k = split_heads(proj[..., d:2 * d])
    v = split_heads(proj[..., 2 * d:])
    q = rms_norm(q, q_gain)
    k = rms_norm(k, k_gain)
    scale = HEAD_DIM ** -0.5
    outs = []
    for blk in range(s // SB_BLOCK):
        q0 = blk * SB_BLOCK
        q1 = q0 + SB_BLOCK
        z = jnp.einsum('bthd,bshd->bhts', q[:, q0:q1], k[:, :q1]).astype(f32) * scale
        t_idx = q0 + jnp.arange(SB_BLOCK)[:, None]
        s_idx = jnp.arange(q1)[None, :]
        causal = s_idx < t_idx
        log_keep = jnp.where(causal, jax.nn.log_sigmoid(-z), 0.0)
        rev_cum = jnp.flip(jnp.cumsum(jnp.flip(log_keep, axis=-1), axis=-1), axis=-1)
        tail = rev_cum - log_keep
        att = jnp.where(causal, jnp.exp(jax.nn.log_sigmoid(z) + tail), 0.0)
        outs.append(jnp.einsum('bhts,bshd->bthd', att.astype(v.dtype), v[:, :q1]))
    o = jnp.concatenate(outs, axis=1).reshape(bsz, s, d)
    return o @ w_o


def sq_relu_mlp(h, w_up, w_down):
    return jnp.square(jax.nn.relu(h @ w_up)) @ w_down


def setup_inputs(seed: int = 0) -> dict:
    key = jax.random.key(seed)
    ks = jax.random.split(key, N_KEYS)
    D, R, V = D_MODEL, N_RWKV, N_VRES

    def nrm(i, shape, scale):
        return jax.random.normal(ks[i], shape, jnp.float32) * scale

    def unif(i, shape, lo, hi):
        return jax.random.uniform(ks[i], shape, jnp.float32, lo, hi)

    return {
        'x': nrm(0, (BATCH, SEQ, D), 1.0),
        'mix_norm': 1.0 + nrm(1, (DEPTH, D), 0.02),
        'mlp_norm': 1.0 + nrm(2, (DEPTH, D), 0.02),
        'mlp_up': nrm(3, (DEPTH, D, D_FF), D ** -0.5),
        'mlp_down': nrm(4, (DEPTH, D_FF, D), D_FF ** -0.5),
        'rwkv_mu': unif(5, (R, N_SHIFT_MIX, D), 0.0, 1.0),
        'rwkv_w_r': nrm(6, (R, D, D), D ** -0.5),
        'rwkv_w_k': nrm(7, (R, D, D), D ** -0.5),
        'rwkv_w_v': nrm(8, (R, D, D), D ** -0.5),
        'rwkv_w_o': nrm(9, (R, D, D), D ** -0.5),
        'rwkv_decay_w0': unif(10, (R, D), -5.5, 0.5),
        'rwkv_decay_w1': nrm(11, (R, D, DECAY_LORA), D ** -0.5),
        'rwkv_decay_w2': nrm(12, (R, DECAY_LORA, D), 0.5 * DECAY_LORA ** -0.5),
        'rwkv_iclr_a0': nrm(13, (R, D), 0.1),
        'rwkv_iclr_a1': nrm(14, (R, D, ICLR_LORA), D ** -0.5),
        'rwkv_iclr_a2': nrm(15, (R, ICLR_LORA, D), 0.5 * ICLR_LORA ** -0.5),
        'rwkv_gate_g1': nrm(16, (R, D, GATE_LORA), D ** -0.5),
        'rwkv_gate_g2': nrm(17, (R, GATE_LORA, D), GATE_LORA ** -0.5),
        'rwkv_k_k': 0.85 + nrm(18, (R, D), 0.05),
        'rwkv_k_a': 1.0 + nrm(19, (R, D), 0.05),
        'rwkv_r_k': nrm(20, (R, N_HEADS, HEAD_DIM), 0.1),
        'rwkv_lnx_w': 1.0 + nrm(21, (R, D), 0.02),
        'rwkv_lnx_b': nrm(22, (R, D), 0.02),
        'rwkv_vres_v0': nrm(23, (V, D), 0.1),
        'rwkv_vres_v1': nrm(24, (V, D, VRES_LORA), D ** -0.5),
        'rwkv_vres_v2': nrm(25, (V, VRES_LORA, D), 0.5 * VRES_LORA ** -0.5),
        'conv_w_in': nrm(26, (N_CONV, D, 3 * D), D ** -0.5),
        'conv_w': nrm(27, (N_CONV, CONV_WIDTH, D), CONV_WIDTH ** -0.5),
        'conv_w_out': nrm(28, (N_CONV, D, D), D ** -0.5),
        'sb_w_qkv': nrm(29, (N_SB, D, 3 * D), D ** -0.5),
        'sb_q_norm': 1.0 + nrm(30, (N_SB, HEAD_DIM), 0.02),
        'sb_k_norm': 1.0 + nrm(31, (N_SB, HEAD_DIM), 0.02),
        'sb_w_o': nrm(32, (N_SB, D, D), D ** -0.5),
    }


def reference(x, mix_norm, mlp_norm, mlp_up, mlp_down,
              rwkv_mu, rwkv_w_r, rwkv_w_k, rwkv_w_v, rwkv_w_o,
              rwkv_decay_w0, rwkv_decay_w1, rwkv_decay_w2,
              rwkv_iclr_a0, rwkv_iclr_a1, rwkv_iclr_a2,
              rwkv_gate_g1, rwkv_gate_g2, rwkv_k_k, rwkv_k_a, rwkv_r_k,
              rwkv_lnx_w, rwkv_lnx_b, rwkv_vres_v0, rwkv_vres_v1, rwkv_vres_v2,
              conv_w_in, conv_w, conv_w_out,
              sb_w_qkv, sb_q_norm, sb_k_norm, sb_w_o):
    h = x
    v_first = None
    for i in range(DEPTH):
        kind = i % N_MIXERS
        j = i // N_MIXERS
        u = rms_norm(h, mix_norm[i])
        if kind == 0:
            if j == 0:
                vres = None
            else:
                vres = (rwkv_vres_v0[j - 1], rwkv_vres_v1[j - 1], rwkv_vres_v2[j - 1])
            mixed, v_first = rwkv7_time_mix(
                u, rwkv_mu[j], rwkv_w_r[j], rwkv_w_k[j], rwkv_w_v[j], rwkv_w_o[j],
                rwkv_decay_w0[j], rwkv_decay_w1[j], rwkv_decay_w2[j],
                rwkv_iclr_a0[j], rwkv_iclr_a1[j], rwkv_iclr_a2[j],
                rwkv_gate_g1[j], rwkv_gate_g2[j], rwkv_k_k[j], rwkv_k_a[j], rwkv_r_k[j],
                rwkv_lnx_w[j], rwkv_lnx_b[j], v_first, vres)
        elif kind == 1:
            mixed = short_conv_mix(u, conv_w_in[j], conv_w[j], conv_w_out[j])
        else:
            mixed = stick_breaking_mix(u, sb_w_qkv[j], sb_q_norm[j], sb_k_norm[j], sb_w_o[j])
        h = h + mixed
        h = h + sq_relu_mlp(rms_norm(h, mlp_norm[i]), mlp_up[i], mlp_down[i])
    return h
```

```python
import numpy as np
import concourse.bass as bass
import concourse.mybir as mybir
from concourse.bass_utils import run_bass_kernel_spmd

F32 = mybir.dt.float32
F32R = mybir.dt.float32r
BF16 = mybir.dt.bfloat16
AF = mybir.ActivationFunctionType
ALU = mybir.AluOpType

D = 1024
S = 2048
NK = 8
TT = 512
NT = S // TT
DFF = 4096
DEPTH = 4
RMS_EPS = 1e-6
GN_EPS = 64e-5


class Sy:
    def __init__(self, nc):
        self.nc = nc
        self.eng = {"pe": nc.tensor, "dve": nc.vector, "act": nc.scalar,
                    "pool": nc.gpsimd, "sp": nc.sync}
        self.sem = {e: nc.alloc_semaphore("s_" + e) for e in self.eng}
        self.cnt = {e: 0 for e in self.eng}
        self.pend = {e: False for e in self.eng}
        self.waited = {e: {} for e in self.eng}
        self.last_w = {}
        self.readers = {}
        self.dsem = {}
        self.dcnt = {}
        self.n_wait = 0
        self.n_inst = 0

    def _wait(self, e, deps):
        need = {}
        for (sk, v) in deps:
            if need.get(sk, 0) < v:
                need[sk] = v
        for sk, v in need.items():
            if sk == e and e == "pe":
                continue
            if self.waited[e].get(sk, 0) >= v:
                continue
            sem = self.sem[sk] if sk in self.sem else self.dsem[sk]
            self.eng[e].wait_ge(sem, v)
            self.waited[e][sk] = v
            self.n_wait += 1

    def _deps(self, reads, writes):
        deps = []
        for k in reads:
            if k in self.last_w:
                deps.append(self.last_w[k])
        for k in writes:
            if k in self.last_w:
                deps.append(self.last_w[k])
            deps.extend(self.readers.get(k, {}).items())
        return deps

    def _record(self, tok, reads, writes):
        for k in reads:
            r = self.readers.setdefault(k, {})
            if r.get(tok[0], 0) < tok[1]:
                r[tok[0]] = tok[1]
        for k in writes:
            self.last_w[k] = tok
            self.readers[k] = {}

    def op(self, e, reads, writes, emit, inc=True):
        psr = [k for k in reads if isinstance(k, tuple) and k[0] == "ps" and k not in writes]
        if psr:
            writes = list(writes) + psr
        self._wait(e, self._deps(reads, writes))
        inst = emit(self.eng[e])
        self.n_inst += 1
        inc = True
        if inc:
            self.cnt[e] += 1
            inst.then_inc(self.sem[e], 1)
            self.pend[e] = False
            tok = (e, self.cnt[e])
        else:
            self.pend[e] = True
            tok = (e, self.cnt[e] + 1)
        self._record(tok, reads, writes)
        return inst

    def dma(self, e, out, in_, reads, writes, sk, **kw):
        if sk not in self.dsem:
            self.dsem[sk] = self.nc.alloc_semaphore("d_" + sk)
            self.dcnt[sk] = 0
        self._wait(e, self._deps(reads, writes))
        inst = self.eng[e].dma_start(out=out, in_=in_, **kw)
        self.dcnt[sk] += 16
        inst.then_inc(self.dsem[sk], 16)
        self.n_inst += 1
        self._record((sk, self.dcnt[sk]), reads, writes)
        return inst

    def barrier(self):
        for e in self.eng:
            deps = [(e2, self.cnt[e2]) for e2 in self.eng if e2 != e and self.cnt[e2] > 0]
            deps += [(sk, self.dcnt[sk]) for sk in self.dsem if self.dcnt[sk] > 0]
            self._wait(e, deps)

    def wait_all(self, e, keys):
        deps = []
        for k in keys:
            if k in self.last_w:
                deps.append(self.last_w[k])
            deps.extend(self.readers.get(k, {}).items())
        self._wait(e, deps)


VEC_COLS = {}


def _vec_layout():
    cols = {}
    off = 0

    def add(name, n=NK):
        nonlocal off
        cols[name] = off
        off += n
    for l in range(DEPTH):
        add(f"mix_norm{l}")
        add(f"mlp_norm{l}")
    for j in range(2):
        for m in range(6):
            add(f"mu{j}_{m}")
        for nm in ("w0", "a0", "k_k", "k_a", "r_k", "lnx_w", "lnx_b"):
            add(f"{nm}{j}")
    add("v0")
    for c in range(3):
        add(f"conv_w{c}")
    add("q_gain", 1)
    add("k_gain", 1)
    return cols, off


VEC_COLS, NVEC = _vec_layout()


def pack_vecs(inp):
    tab = np.zeros((128, NVEC), np.float32)

    def put(name, v):
        v = np.asarray(v, np.float32).reshape(-1)
        c = VEC_COLS[name]
        if v.size == D:
            tab[:, c:c + NK] = v.reshape(NK, 128).T
        else:
            tab[:, c] = np.concatenate([v, v])
    for l in range(DEPTH):
        put(f"mix_norm{l}", inp["mix_norm"][l])
        put(f"mlp_norm{l}", inp["mlp_norm"][l])
    for j in range(2):
        for m in range(6):
            put(f"mu{j}_{m}", inp["rwkv_mu"][j, m])
        put(f"w0{j}", inp["rwkv_decay_w0"][j])
        put(f"a0{j}", inp["rwkv_iclr_a0"][j])
        put(f"k_k{j}", inp["rwkv_k_k"][j])
        put(f"k_a{j}", inp["rwkv_k_a"][j])
        put(f"r_k{j}", inp["rwkv_r_k"][j])
        put(f"lnx_w{j}", inp["rwkv_lnx_w"][j])
        put(f"lnx_b{j}", inp["rwkv_lnx_b"][j])
    put("v0", inp["rwkv_vres_v0"][0])
    for c in range(3):
        put(f"conv_w{c}", inp["conv_w"][0, c])
    put("q_gain", inp["sb_q_norm"][0])
    put("k_gain", inp["sb_k_norm"][0])
    return tab


class Prog:
    def __init__(self, layers, n_layers_mlp=None):
        self.layers = layers
        nc = bass.Bass("TRN2", target_bir_lowering=False)
        self.nc = nc
        self.sy = Sy(nc)
        self._n = 0
        dt = nc.dram_tensor
        self.xT = dt("xT", [D, S], F32, kind="ExternalInput").ap()
        self.vecs_d = dt("vecs", [128, NVEC], F32, kind="ExternalInput").ap()
        self.outT = dt("outT", [D, S], F32, kind="ExternalOutput").ap()
        self.w = {}
        for name, shape in (
            ("mlp_up", [DEPTH, D, DFF]), ("mlp_down", [DEPTH, DFF, D]),
            ("rwkv_w_r", [2, D, D]), ("rwkv_w_k", [2, D, D]), ("rwkv_w_v", [2, D, D]),
            ("rwkv_w_o", [2, D, D]),
            ("rwkv_decay_w1", [2, D, 64]), ("rwkv_decay_w2", [2, 64, D]),
            ("rwkv_iclr_a1", [2, D, 64]), ("rwkv_iclr_a2", [2, 64, D]),
            ("rwkv_gate_g1", [2, D, 160]), ("rwkv_gate_g2", [2, 160, D]),
            ("rwkv_vres_v1", [1, D, 32]), ("rwkv_vres_v2", [1, 32, D]),
            ("conv_w_in", [1, D, 3 * D]), ("conv_w_out", [1, D, D]),
            ("sb_w_qkv", [1, D, 3 * D]), ("sb_w_o", [1, D, D]),
        ):
            self.w[name] = dt(name, shape, F32, kind="ExternalInput").ap()
        self.vfirst = dt("vfirst_scratch", [D, S], F32, kind="Internal").ap()
        self.psum = [nc.alloc_psum_tensor(f"ps{i}", [128, 512], F32).ap() for i in range(8)]
        self.ps_i = 0

    def sb(self, name, shape, dtype):
        return self.nc.alloc_sbuf_tensor(name, shape, dtype).ap()

    def ps(self):
        i = self.ps_i
        self.ps_i = (i + 1) % 6
        return self.psum[i], ("ps", i)

    def phase_begin(self):
        self.sy.barrier()
        self.arena_off = 0

    def carve(self, shape, dtype):
        n = 1
        for d in shape:
            n *= d
        nb = n * (4 if dtype in (F32, F32R) else 2)
        nb = (nb + 31) // 32 * 32
        off = self.arena_off
        assert off + nb <= self.ARENA * 2, (off, nb, self.ARENA * 2)
        self.arena_off = off + nb
        self._n += 1
        return self.nc.alloc_sbuf_tensor_at(f"cv{self._n}", [128] + list(shape), dtype,
                                            offset=self.arena_base + off).ap()

    def vcol(self, name, k=0):
        c = VEC_COLS[name] + k
        return self.vecs[:, c:c + 1]

    def build(self):
        nc, sy = self.nc, self.sy
        self.h = self.sb("h", [128, NK, S], F32)
        self.xn = self.sb("xn", [128, NK, S + 2], BF16)
        self.vecs = self.sb("vecs_sb", [128, NVEC], F32)
        self.ones_bf = self.sb("ones_bf", [128, 128], BF16)
        self.eps_t = self.sb("eps_t", [128, 1], F32)
        self.ARENA = 53 * 1024 + 512
        self.arena = self.sb("arena", [128, self.ARENA], BF16)
        self.arena_base = self.nc.sbuf_base - self.ARENA * 2
        self.arena_off = 0
        self.make_consts()
        sy.dma("sp", self.vecs, self.vecs_d, [], ["vecs"], "misc")
        for k in range(NK):
            sy.dma("sp", self.h[:, k, :], self.xT[k * 128:(k + 1) * 128, :], [], [("h", k, t) for t in range(NT)], f"xin{k}")
        sy.op("dve", [], ["ones"], lambda e: e.memset(self.ones_bf, 1.0))
        sy.op("dve", [], ["eps"], lambda e: e.memset(self.eps_t, RMS_EPS))
        sy.op("dve", [], [("xnpad",)], lambda e: e.memset(self.xn[:, :, 0:2], 0.0))
        for (kind, l) in self.layers:
            if kind == "mlp":
                self.rmsnorm(f"mlp_norm{l}")
                self.mlp(l)
            elif kind == "mix":
                self.rmsnorm(f"mix_norm{l}")
                if l % 3 == 1:
                    self.conv(l // 3)
                elif l % 3 == 2:
                    self.sbatt(l // 3)
                else:
                    self.rwkv(l // 3)
        for k in range(NK):
            sy.dma("sp", self.outT[k * 128:(k + 1) * 128, :], self.h[:, k, :],
                   [("h", k, t) for t in range(NT)], [("out", k)], "out")
        sy.wait_all("sp", [("out", k) for k in range(NK)])
        return nc

    def make_consts(self):
        sy = self.sy
        self.one_col = self.sb("one_col", [128, 1], F32)
        self.bones = self.sb("bones", [128, 128], F32)
        self.tri = self.sb("tri", [128, 128], F32R)
        self.onesr = self.sb("onesr", [128, 128], F32R)
        self.onesw = self.carve([128], F32)
        sy.op("dve", [], ["consts"], lambda e: e.memset(self.one_col, 1.0))
        sy.op("dve", [], ["consts"], lambda e: e.memset(self.bones, 0.0))
        sy.op("dve", [], ["consts"], lambda e: e.memset(self.bones[0:64, 0:64], 1.0))
        sy.op("dve", [], ["consts"], lambda e: e.memset(self.bones[64:128, 64:128], 1.0))
        sy.op("dve", [], ["consts"], lambda e: e.memset(self.onesw, 1.0))
        sy.op("pool", ["consts"], ["consts2"], lambda e: e.affine_select(
            out=self.tri, in_=self.onesw[:, 0:128], pattern=[[-1, 128]], compare_op=ALU.is_ge, fill=0.0,
            base=0, channel_multiplier=1))
        sy.op("pool", ["consts"], ["consts2"], lambda e: e.tensor_copy(out=self.onesr, in_=self.onesw[:, 0:128]))

    def rmsnorm(self, gname):
        sy = self.sy
        self.phase_begin()
        self.sq = [self.carve([TT], BF16) for i in range(2)]
        self.rstd = [self.carve([TT], F32) for i in range(2)]
        for t in range(NT):
            ts = slice(t * TT, (t + 1) * TT)
            pst, psk = self.ps()
            for k in range(NK):
                sq = self.sq[k % 2]
                sqk = ("sq", k % 2)
                sy.op("act", [("h", k, t)], [sqk],
                      lambda e, sq=sq, k=k: e.activation(out=sq, in_=self.h[:, k, ts], func=AF.Square))
                sy.op("pe", [sqk, "ones"], [psk],
                      lambda e, sq=sq, k=k: e.matmul(pst, self.ones_bf, sq, start=(k == 0), stop=(k == NK - 1)),
                      inc=(k == NK - 1))
            rs = self.rstd[t % 2]
            rsk = ("rstd", t % 2)
            sy.op("act", [psk, "eps"], [rsk],
                  lambda e: e.activation(out=rs, in_=pst, func=AF.Ln, bias=self.eps_t, scale=1.0 / D))
            sy.op("act", [rsk], [rsk], lambda e: e.activation(out=rs, in_=rs, func=AF.Exp, scale=-0.5))
            for k in range(NK):
                sy.op("dve", [("h", k, t), rsk, "vecs"], [("xn", k, t)],
                      lambda e, k=k: e.scalar_tensor_tensor(
                          out=self.xn[:, k, 2 + t * TT:2 + (t + 1) * TT], in0=self.h[:, k, ts],
                          scalar=self.vcol(gname, k), in1=rs, op0=ALU.mult, op1=ALU.mult))

    def alloc_mlp(self):
        self.phase_begin()
        self.GF = 512
        self.wup = [self.carve([NK, self.GF], BF16) for i in range(2)]
        self.wdn = [self.carve([self.GF // 128, D], BF16) for i in range(2)]
        self.hT = self.carve([self.GF // 128, S], BF16)
        self.relu_t = [self.carve([TT], F32) for i in range(2)]
        self.mlp_gi = 0

    def mlp_load(self, l, g):
        sy = self.sy
        GF = self.GF
        s = self.mlp_gi % 2
        self.mlp_gi += 1
        src_up = self.w["mlp_up"][l, :, g * GF:(g + 1) * GF].rearrange("(k p) f -> p k f", p=128)
        sy.dma("pool", self.wup[s], src_up, [], [("wup", s)], f"wup{s}")
        src_dn = self.w["mlp_down"][l, g * GF:(g + 1) * GF, :].rearrange("(c p) d -> p c d", p=128)
        sy.dma("pool", self.wdn[s], src_dn, [], [("wdn", s)], f"wdn{s}")
        return s

    def mlp(self, l):
        sy = self.sy
        self.alloc_mlp()
        GF = self.GF
        NG = DFF // GF
        NC = GF // 128
        slots = [None] * NG
        slots[0] = self.mlp_load(l, 0)
        ri = 0
        for g in range(NG):
            if g + 1 < NG:
                slots[g + 1] = self.mlp_load(l, g + 1)
            s = slots[g]
            for t in range(NT):
                for c in range(NC):
                    pst, psk = self.ps()
                    for k in range(NK):
                        sy.op("pe", [("wup", s), ("xn", k, t)], [psk],
                              lambda e, k=k, c=c, t=t: e.matmul(
                                  pst, self.wup[s][:, k, c * 128:(c + 1) * 128],
                                  self.xn[:, k, 2 + t * TT:2 + (t + 1) * TT],
                                  start=(k == 0), stop=(k == NK - 1)),
                              inc=(k == NK - 1))
                    rt = self.relu_t[ri % 2]
                    rk = ("relu", ri % 2)
                    ri += 1
                    sy.op("act", [psk], [rk], lambda e, rt=rt: e.activation(out=rt, in_=pst, func=AF.Relu))
                    sy.op("dve", [rk, psk], [("hT", c, t)],
                          lambda e, rt=rt, c=c, t=t: e.tensor_tensor(
                              out=self.hT[:, c, t * TT:(t + 1) * TT], in0=rt, in1=pst, op=ALU.mult))
            for t in range(NT):
                for m in range(NK):
                    pst, psk = self.ps()
                    for c in range(NC):
                        sy.op("pe", [("wdn", s), ("hT", c, t)], [psk],
                              lambda e, c=c, m=m, t=t: e.matmul(
                                  pst, self.wdn[s][:, c, m * 128:(m + 1) * 128],
                                  self.hT[:, c, t * TT:(t + 1) * TT],
                                  start=(c == 0), stop=(c == NC - 1)),
                              inc=(c == NC - 1))
                    sy.op("dve", [psk, ("h", m, t)], [("h", m, t)],
                          lambda e, m=m, t=t: e.tensor_tensor(
                              out=self.h[:, m, t * TT:(t + 1) * TT], in0=pst,
                              in1=self.h[:, m, t * TT:(t + 1) * TT], op=ALU.add))


    def proj_fm(self, wt, wkey, cols, t, shift=0, pst=None, psk=None, first=True, last=True):
        sy = self.sy
        if pst is None:
            pst, psk = self.ps()
        for k in range(NK):
            sy.op("pe", [wkey, ("xn", k, t)] + ([("xn", k, t - 1)] if (shift and t > 0) else []), [psk],
                  lambda e, k=k: e.matmul(pst, wt[:, k, cols],
                                          self.xn[:, k, 2 - shift + t * TT:2 - shift + (t + 1) * TT],
                                          start=(first and k == 0), stop=(last and k == NK - 1)))
        return pst, psk

    def outproj_acc(self, wo, wokey, y, ykey, t):
        sy = self.sy
        for m in range(NK):
            pst, psk = self.ps()
            sy.op("pe", [wokey, ykey], [psk],
                  lambda e, m=m: e.matmul(pst, wo[:, m * 128:(m + 1) * 128], y, start=True, stop=True))
            sy.op("dve", [psk, ("h", m, t)], [("h", m, t)],
                  lambda e, m=m: e.tensor_tensor(out=self.h[:, m, t * TT:(t + 1) * TT], in0=pst,
                                                 in1=self.h[:, m, t * TT:(t + 1) * TT], op=ALU.add))

    def alloc_mix(self):
        self.phase_begin()
        self.w3 = [self.carve([NK, 3, 128], BF16) for i in range(2)]
        self.wo = [self.carve([D], BF16) for i in range(2)]
        self.ybf = [self.carve([TT], BF16) for i in range(2)]
        self.mix_i = 0

    def load_w3(self, wname, j, dc, wo_name):
        sy = self.sy
        s = self.mix_i % 2
        self.mix_i += 1
        src = self.w[wname][j].rearrange("(k p) f -> p k f", p=128)
        for jj in range(3):
            sy.dma("pool", self.w3[s][:, :, jj, :], src[:, :, jj * D + dc * 128:jj * D + (dc + 1) * 128],
                   [], [("w3", s)], f"w3_{s}")
        sy.dma("pool", self.wo[s], self.w[wo_name][j, dc * 128:(dc + 1) * 128, :], [], [("wo", s)], f"wo_{s}")
        return s

    def conv(self, j):
        sy = self.sy
        self.alloc_mix()
        csb = [self.carve([TT], F32) for i in range(2)]
        bsb = [self.carve([TT], F32) for i in range(2)]
        acc = self.carve([TT], F32)
        zbufs = [self.carve([2 + S], F32) for i in range(2)]
        slots = [None] * NK
        slots[0] = self.load_w3("conv_w_in", j, 0, "conv_w_out")
        slots[1] = self.load_w3("conv_w_in", j, 1, "conv_w_out")
        items = [(dc, t) for dc in range(NK) for t in range(NT)]
        st = {"yi": 0}

        def stage1(n):
            dc, t = items[n]
            i2 = n % 2
            if t == 0:
                sy.op("dve", [], [("z", dc % 2, -1)], lambda e: e.memset(zbufs[dc % 2][:, 0:2], 0.0))
            s = slots[dc]
            w3 = self.w3[s]
            zb = zbufs[dc % 2]
            zs = slice(2 + t * TT, 2 + (t + 1) * TT)
            pb, pbk = self.proj_fm(w3[:, :, 0, :], ("w3", s), slice(0, 128), t)
            sy.op("act", [pbk], [("bsb", i2)], lambda e: e.activation(out=bsb[i2], in_=pb, func=AF.Copy))
            pc, pck = self.proj_fm(w3[:, :, 1, :], ("w3", s), slice(0, 128), t)
            sy.op("act", [pck], [("csb", i2)], lambda e: e.activation(out=csb[i2], in_=pc, func=AF.Copy))
            pu, puk = self.proj_fm(w3[:, :, 2, :], ("w3", s), slice(0, 128), t)
            sy.op("dve", [("csb", i2), puk], [("z", dc % 2, t)],
                  lambda e: e.tensor_tensor(out=zb[:, zs], in0=csb[i2], in1=pu, op=ALU.mult))

        def stage2(n):
            dc, t = items[n]
            i2 = n % 2
            s = slots[dc]
            wo = self.wo[s]
            zb = zbufs[dc % 2]
            zs = slice(2 + t * TT, 2 + (t + 1) * TT)
            zk = [("z", dc % 2, t), ("z", dc % 2, t - 1)]
            sy.op("dve", zk + ["vecs"], ["acc"],
                  lambda e: e.tensor_scalar(out=acc, in0=zb[:, t * TT:(t + 1) * TT],
                                            scalar1=self.vcol("conv_w0", dc), scalar2=None, op0=ALU.mult))
            sy.op("dve", zk + ["acc", "vecs"], ["acc"],
                  lambda e: e.scalar_tensor_tensor(out=acc, in0=zb[:, 1 + t * TT:1 + (t + 1) * TT],
                                                   scalar=self.vcol("conv_w1", dc), in1=acc,
                                                   op0=ALU.mult, op1=ALU.add))
            sy.op("dve", zk + ["acc", "vecs"], ["acc"],
                  lambda e: e.scalar_tensor_tensor(out=acc, in0=zb[:, zs],
                                                   scalar=self.vcol("conv_w2", dc), in1=acc,
                                                   op0=ALU.mult, op1=ALU.add))
            y = self.ybf[st["yi"] % 2]
            yk = ("ybf", st["yi"] % 2)
            st["yi"] += 1
            sy.op("dve", [("bsb", i2), "acc"], [yk], lambda e: e.tensor_tensor(out=y, in0=bsb[i2], in1=acc, op=ALU.mult))
            self.outproj_acc(wo, ("wo", s), y, yk, t)

        for n in range(len(items) + 1):
            if n < len(items):
                stage1(n)
            if n >= 1:
                stage2(n - 1)
                dcp, tp = items[n - 1]
                if tp == NT - 1 and dcp + 2 < NK:
                    slots[dcp + 2] = self.load_w3("conv_w_in", j, dcp + 2, "conv_w_out")

    def sbatt(self, j):
        sy = self.sy
        self.alloc_mix()
        qn = self.carve([S], F32R)
        kn = self.carve([S], F32R)
        vpA = self.carve([16, 128], BF16)
        vpB = self.carve([16, 128], BF16)
        raw_t = [self.carve([TT], F32) for i in range(2)]
        sq_t = [self.carve([TT], F32) for i in range(2)]
        rs_t = [self.carve([TT], F32) for i in range(2)]
        e_t = [self.carve([TT], F32) for i in range(3)]
        sp_t = [self.carve([TT], F32) for i in range(3)]
        lk_t = [self.carve([TT], F32R) for i in range(3)]
        u_t = [self.carve([TT], F32) for i in range(3)]
        arg_t = [self.carve([TT], F32) for i in range(3)]
        att_t = [self.carve([TT], BF16) for i in range(3)]
        R_t = [self.carve([TT], F32R) for i in range(2)]
        qgs = self.carve([1], F32)
        self.m01 = self.carve([896], BF16)
        self.mneg = self.carve([896], F32)
        onesw = self.carve([896], BF16)
        sy.op("dve", [], ["sbc"], lambda e: e.memset(onesw, 1.0))
        sy.op("pool", ["sbc"], ["consts2"], lambda e: e.affine_select(
            out=self.m01, in_=onesw, pattern=[[1, 896]], compare_op=ALU.is_gt, fill=0.0,
            base=-384, channel_multiplier=-1))
        sy.op("pool", ["consts2"], ["consts2"], lambda e: e.tensor_scalar(
            out=self.mneg, in0=self.m01, scalar1=-1.0, scalar2=None, op0=ALU.mult))
        sy.op("dve", ["vecs"], ["qgs"], lambda e: e.tensor_scalar(
            out=qgs, in0=self.vcol("q_gain"), scalar1=0.125, scalar2=None, op0=ALU.mult))
        sy.op("dve", [], ["vpA"], lambda e: e.memset(vpA, 0.0))
        sy.op("dve", [], ["vpB"], lambda e: e.memset(vpB, 0.0))
        slots = [None] * NK
        slots[0] = self.load_w3("sb_w_qkv", j, 0, "sb_w_o")
        ni = 0
        pi = 0
        oi = 0
        yi = 0
        for dc in range(NK):
            if dc + 1 < NK:
                slots[dc + 1] = self.load_w3("sb_w_qkv", j, dc + 1, "sb_w_o")
            s = slots[dc]
            w3, wo = self.w3[s], self.wo[s]
            for t in range(NT):
                ts = slice(t * TT, (t + 1) * TT)
                for (jj, dst, dkey, gcol) in ((0, qn, "qn", qgs), (1, kn, "kn", self.vcol("k_gain"))):
                    pp, ppk = self.proj_fm(w3[:, :, jj, :], ("w3", s), slice(0, 128), t)
                    raw, sq, rs = raw_t[ni % 2], sq_t[ni % 2], rs_t[ni % 2]
                    rk, sk_, rsk = ("raw", ni % 2), ("sqq", ni % 2), ("rsq", ni % 2)
                    ni += 1
                    sy.op("act", [ppk], [rk], lambda e, raw=raw, pp=pp: e.activation(out=raw, in_=pp, func=AF.Copy))
                    sy.op("act", [ppk], [sk_], lambda e, sq=sq, pp=pp: e.activation(out=sq, in_=pp, func=AF.Square))
                    p2, p2k = self.ps()
                    sy.op("pe", [sk_, "consts"], [p2k],
                          lambda e, sq=sq, p2=p2: e.matmul(p2, self.bones, sq, start=True, stop=True))
                    sy.op("act", [p2k, "eps"], [rsk],
                          lambda e, rs=rs, p2=p2: e.activation(out=rs, in_=p2, func=AF.Ln, bias=self.eps_t, scale=1.0 / 64))
                    sy.op("act", [rsk], [rsk], lambda e, rs=rs: e.activation(out=rs, in_=rs, func=AF.Exp, scale=-0.5))
                    sy.op("dve", [rk, rsk, "vecs", "qgs"], [(dkey, t)],
                          lambda e, raw=raw, rs=rs, dst=dst, gcol=gcol: e.scalar_tensor_tensor(
                              out=dst[:, ts], in0=raw, scalar=gcol, in1=rs, op0=ALU.mult, op1=ALU.mult))
                pv, pvk = self.ps()
                for q4 in range(4):
                    for k in range(NK):
                        sy.op("pe", [("w3", s), ("xn", k, t)], [pvk],
                              lambda e, k=k, q4=q4: e.matmul(
                                  pv[:, q4 * 128:(q4 + 1) * 128],
                                  self.xn[:, k, 2 + t * TT + q4 * 128:2 + t * TT + (q4 + 1) * 128],
                                  w3[:, k, 2, :], start=(k == 0), stop=(k == NK - 1)))
                pv3 = pv.rearrange("p (a b) -> p a b", a=4)
                sy.op("act", [pvk], ["vpA"], lambda e, pv3=pv3: e.activation(
                    out=vpA[:, 4 * t:4 * t + 4, 0:64], in_=pv3[:, :, 0:64], func=AF.Copy))
                sy.op("act", [pvk], ["vpB"], lambda e, pv3=pv3: e.activation(
                    out=vpB[:, 4 * t:4 * t + 4, 64:128], in_=pv3[:, :, 64:128], func=AF.Copy))
            pairs = []
            for T in range(NT):
                for hd in range(2):
                    cmax = 4 * T + 3
                    for c in range(cmax, -1, -1):
                        pairs.append(dict(T=T, hd=hd, c=c, cmax=cmax, first=(hd == 0 and c == cmax),
                                          last=(hd == 1 and c == 0)))
            NB = 3
            o_banks = {}
            for T in range(NT):
                o_banks[T] = 6 + oi % 2
                oi += 1
            rstate = {"cur": 0}

            def stage1(n, p):
                i2 = n % NB
                hp = slice(p["hd"] * 64, p["hd"] * 64 + 64)
                T, c = p["T"], p["c"]
                Ts = slice(T * TT, (T + 1) * TT)
                jd = c - 4 * T
                pz, pzk = self.ps()
                sy.op("pe", [("kn", c // 4), ("qn", T)], [pzk],
                      lambda e: e.matmul(pz, kn[hp, c * 128:(c + 1) * 128], qn[hp, Ts], start=True, stop=True))
                sy.op("act", [pzk], [("e", i2)], lambda e: e.activation(out=e_t[i2], in_=pz, func=AF.Exp))
                sy.op("act", [("e", i2), "consts"], [("sp", i2)],
                      lambda e: e.activation(out=sp_t[i2], in_=e_t[i2], func=AF.Ln, bias=self.one_col, scale=1.0))
                if jd >= 0:
                    sy.op("dve", [("sp", i2), "consts2"], [("lk", i2)],
                          lambda e: e.tensor_tensor(out=lk_t[i2], in0=sp_t[i2],
                                                    in1=self.mneg[:, 384 - 128 * jd:896 - 128 * jd], op=ALU.mult))
                else:
                    sy.op("dve", [("sp", i2)], [("lk", i2)],
                          lambda e: e.tensor_scalar(out=lk_t[i2], in0=sp_t[i2], scalar1=-1.0, scalar2=None, op0=ALU.mult))

            def stage2(n, p):
                i2 = n % NB
                T, c, cmax = p["T"], p["c"], p["cmax"]
                jd = c - 4 * T
                if c == cmax:
                    rstate["cur"] = 0
                rcur = rstate["cur"]
                hp = slice(p["hd"] * 64, p["hd"] * 64 + 64)
                Ts = slice(T * TT, (T + 1) * TT)
                pt, ptk = self.ps()
                sy.op("pe", [("lk", i2), "consts2"], [ptk],
                      lambda e: e.matmul(pt, self.tri, lk_t[i2], start=True, stop=False))
                if c < cmax:
                    sy.op("pe", [("R", rcur), "consts2"], [ptk],
                          lambda e: e.matmul(pt, self.onesr, R_t[rcur], start=False, stop=False))
                sy.op("pe", [("kn", c // 4), ("qn", T)], [ptk],
                      lambda e: e.matmul(pt, kn[hp, c * 128:(c + 1) * 128], qn[hp, Ts], start=False, stop=True))
                if c > 0:
                    rn = 1 - rcur
                    if c == cmax:
                        sy.op("pool", [("lk", i2)], [("R", rn)], lambda e: e.tensor_copy(out=R_t[rn], in_=lk_t[i2]))
                    else:
                        sy.op("pool", [("lk", i2), ("R", rcur)], [("R", rn)],
                              lambda e: e.tensor_tensor(out=R_t[rn], in0=R_t[rcur], in1=lk_t[i2], op=ALU.add))
                    rstate["cur"] = rn
                sy.op("act", [ptk], [("att", i2)],
                      lambda e: e.activation(out=att_t[i2], in_=pt, func=AF.Exp))
                if jd >= 0:
                    sy.op("dve", [("att", i2), "consts2"], [("att", i2)],
                          lambda e: e.tensor_tensor(out=att_t[i2], in0=att_t[i2],
                                                    in1=self.m01[:, 384 - 128 * jd:896 - 128 * jd], op=ALU.mult))

            def stage3(n, p):
                nonlocal yi
                i2 = n % NB
                T, c = p["T"], p["c"]
                ob = o_banks[T]
                o_ps, ok = self.psum[ob], ("ps", ob)
                vp, vpk = (vpA, "vpA") if p["hd"] == 0 else (vpB, "vpB")
                sy.op("pe", [vpk, ("att", i2)], [ok],
                      lambda e: e.matmul(o_ps, vp[:, c, :], att_t[i2], start=p["first"], stop=p["last"]))
                if p["last"]:
                    y = self.ybf[yi % 2]
                    yk = ("ybf", yi % 2)
                    yi += 1
                    sy.op("act", [ok], [yk], lambda e: e.activation(out=y, in_=o_ps, func=AF.Copy))
                    self.outproj_acc(wo, ("wo", s), y, yk, T)

            npairs = len(pairs)
            for n in range(npairs + 2):
                if n < npairs:
                    stage1(n, pairs[n])
                if 1 <= n and n - 1 < npairs:
                    stage2(n - 1, pairs[n - 1])
                if 2 <= n:
                    stage3(n - 2, pairs[n - 2])

    def rwkv(self, j):
        sy = self.sy
        self.phase_begin()
        RT = 256
        NRT = S // RT
        CD = 0.6065306597126334
        cv = self.carve
        P1, GA, GB = cv([S], BF16), cv([S], BF16), cv([S], BF16)
        WA2, GA2, GB2 = cv([D], BF16), cv([D], BF16), cv([D], BF16)
        omm, okka = cv([48], F32), cv([8], F32)
        ident, mk4, mkL, cmask = cv([128], F32), cv([512], F32), cv([128], F32), cv([RT], F32)
        gneps, tiny = cv([1], F32), cv([1], F32)
        mark = self.arena_off
        onesw = cv([512], F32)
        LWs, LW = cv([NK, 320], F32), cv([NK, 2, 320], BF16)
        mu0 = VEC_COLS[f"mu{j}_0"]
        mucols = self.vecs[:, mu0:mu0 + 48]
        sy.op("dve", [], ["rc"], lambda e: e.memset(onesw, 1.0))
        sy.op("dve", [], ["rc"], lambda e: e.memset(gneps, GN_EPS))
        sy.op("dve", [], ["rc"], lambda e: e.memset(tiny, 1e-24))
        sy.op("dve", [], ["cmask"], lambda e: e.memset(cmask, 1.0))
        sy.op("dve", [], ["cmask"], lambda e: e.memset(cmask[:, 0:1], 0.0))
        sy.op("dve", [], ["cmask"], lambda e: e.memset(cmask[:, 128:129], 0.0))
        sy.op("dve", ["vecs"], ["omm"], lambda e: e.tensor_scalar(
            out=omm, in0=mucols, scalar1=-1.0, scalar2=1.0, op0=ALU.mult, op1=ALU.add))
        ka0 = VEC_COLS[f"k_a{j}"]
        sy.op("dve", ["vecs"], ["omm"], lambda e: e.tensor_scalar(
            out=okka, in0=self.vecs[:, ka0:ka0 + 8], scalar1=-1.0, scalar2=1.0, op0=ALU.mult, op1=ALU.add))
        sy.op("pool", ["rc"], ["rc2"], lambda e: e.affine_select(
            out=ident, in_=onesw[:, 0:128], pattern=[[-1, 128]], compare_op=ALU.is_equal, fill=0.0,
            base=0, channel_multiplier=1))
        sy.op("pool", ["rc"], ["rc2"], lambda e: e.affine_select(
            out=mkL, in_=onesw[:, 0:128], pattern=[[-1, 128]], compare_op=ALU.is_gt, fill=0.0,
            base=0, channel_multiplier=1))
        sy.op("pool", ["rc"], ["rc2"], lambda e: e.affine_select(
            out=mk4, in_=onesw, pattern=[[0, 2], [1, 2], [1, 128]], compare_op=ALU.is_gt, fill=0.0,
            base=0, channel_multiplier=-1))
        wr = lambda n, jj=j: self.w[n][jj]
        sy.dma("sp", LWs[:, :, 0:64], wr("rwkv_decay_w1").rearrange("(k p) c -> p k c", p=128), [], ["LWs"], "lws")
        sy.dma("sp", LWs[:, :, 64:128], wr("rwkv_iclr_a1").rearrange("(k p) c -> p k c", p=128), [], ["LWs"], "lws")
        sy.dma("sp", LWs[:, :, 128:288], wr("rwkv_gate_g1").rearrange("(k p) c -> p k c", p=128), [], ["LWs"], "lws")
        if j == 1:
            sy.dma("sp", LWs[:, :, 288:320], self.w["rwkv_vres_v1"][0].rearrange("(k p) c -> p k c", p=128), [], ["LWs"], "lws")
        sy.dma("pool", WA2[0:64, :], wr("rwkv_decay_w2"), [], ["W2"], "w2s")
        sy.dma("pool", WA2[64:128, :], wr("rwkv_iclr_a2"), [], ["W2"], "w2s")
        sy.dma("pool", GA2, wr("rwkv_gate_g2")[0:128, :], [], ["W2"], "w2s")
        sy.dma("pool", GB2[0:32, :], wr("rwkv_gate_g2")[128:160, :], [], ["W2"], "w2s")
        if j == 1:
            sy.dma("pool", GB2[32:64, :], self.w["rwkv_vres_v2"][0], [], ["W2"], "w2s")
        blocks = [(0, 64, 1), (64, 128, 4), (128, 288, 5)] + ([(288, 320, 3)] if j == 1 else [])
        for k in range(NK):
            for (c0, c1, m) in blocks:
                sy.op("pool", ["LWs", "omm"], ["LW"], lambda e, k=k, c0=c0, c1=c1, m=m: e.tensor_scalar(
                    out=LW[:, k, 0, c0:c1], in0=LWs[:, k, c0:c1], scalar1=omm[:, m * 8 + k:m * 8 + k + 1],
                    scalar2=None, op0=ALU.mult))
                sy.op("pool", ["LWs", "vecs"], ["LW"], lambda e, k=k, c0=c0, c1=c1, m=m: e.tensor_scalar(
                    out=LW[:, k, 1, c0:c1], in0=LWs[:, k, c0:c1], scalar1=self.vecs[:, mu0 + m * 8 + k:mu0 + m * 8 + k + 1],
                    scalar2=None, op0=ALU.mult))
        NL3 = 64 if j == 1 else 32
        for t in range(NT):
            ts = slice(t * TT, (t + 1) * TT)
            for (c0, M, which) in ((0, 128, 0), (128, 128, 1), (256, NL3, 2)):
                pst, psk = self.ps()
                for k in range(NK):
                    for sh in range(2):
                        rd = [("xn", k, t), "LW"] + ([("xn", k, t - 1)] if (sh and t > 0) else [])
                        sy.op("pe", rd, [psk], lambda e, k=k, sh=sh, c0=c0, M=M, pst=pst: e.matmul(
                            pst[0:M, :], LW[:, k, sh, c0:c0 + M], self.xn[:, k, 2 - sh + t * TT:2 - sh + (t + 1) * TT],
                            start=(k == 0 and sh == 0), stop=(k == NK - 1 and sh == 1)))
                if which == 0:
                    sy.op("act", [psk], [("P1", t)], lambda e, pst=pst: e.activation(out=P1[0:64, ts], in_=pst[0:64, :], func=AF.Tanh))
                    sy.op("act", [psk], [("P1", t)], lambda e, pst=pst: e.activation(out=P1[64:128, ts], in_=pst[64:128, :], func=AF.Copy))
                elif which == 1:
                    sy.op("act", [psk], [("GA", t)], lambda e, pst=pst: e.activation(out=GA[:, ts], in_=pst, func=AF.Sigmoid))
                else:
                    sy.op("act", [psk], [("GB", t)], lambda e, pst=pst: e.activation(out=GB[0:32, ts], in_=pst[0:32, :], func=AF.Sigmoid))
                    if j == 1:
                        sy.op("act", [psk], [("GB", t)], lambda e, pst=pst: e.activation(out=GB[32:64, ts], in_=pst[32:64, :], func=AF.Copy))
        sy.barrier()
        self.arena_off = mark
        Wst, Wf = cv([NK, 3, 128], F32), cv([NK, 3, 2, 128], BF16)
        wo = [cv([D], BF16) for i in range(2)]
        f1 = lambda: cv([RT], F32)
        r_sb, k_sb, sg, csp, pinv, asig, kk, tA, tB, tC, YC, vf = [f1() for _ in range(12)]
        Yb = [f1() for _ in range(2)]
        BN3 = [f1() for _ in range(3)]
        Bset = [(f1(), f1(), cv([2, 2, 128], BF16), cv([RT], BF16), cv([RT], BF16), None, cv([RT], BF16)) for _ in range(2)]
        yout = [cv([RT], BF16) for i in range(2)]
        BH, KH = cv([128], BF16), cv([128], BF16)
        TMp = cv([4, 256], BF16)
        AM32 = [cv([128], F32) for i in range(2)]
        AMb = [cv([3, 128], BF16) for i in range(2)]
        Np = [[cv([128], F32) for i in range(2)] for h in range(2)]
        NpT = [[cv([128], F32) for i in range(2)] for h in range(2)]
        Wt = [[cv([2, 64], F32) for i in range(2)] for h in range(2)]
        Wfin = cv([2, 256], BF16)
        TMb = cv([4, 128], BF16)
        WB = cv([2, 128], BF16)
        M1, NCt = cv([128], F32), cv([128], F32)
        MBD, G = cv([128], BF16), cv([128], BF16)
        ST = [cv([128], BF16) for i in range(2)]
        identb = cv([128], BF16)
        sy.op("dve", ["rc2"], ["rc2"], lambda e: e.tensor_copy(out=identb, in_=ident))
        sy.op("dve", [], ["TMp"], lambda e: e.memset(TMp, 0.0))
        sy.op("dve", [], ["Wfin"], lambda e: e.memset(Wfin, 0.0))
        both = lambda ap2: ap2.rearrange("p (a b) -> p a b", a=4)[:, 0:4:3, :]
        wnames = ("rwkv_w_r", "rwkv_w_k", "rwkv_w_v")
        muidx = (0, 2, 3)

        def load_stage(dc):
            for jj in range(3):
                src = self.w[wnames[jj]][j].rearrange("(k p) c -> p k c", p=128)[:, :, dc * 128:(dc + 1) * 128]
                sy.dma("sp", Wst[:, :, jj, :], src, [], ["Wst"], "wst")
            sy.dma("pool", wo[dc % 2], self.w["rwkv_w_o"][j, dc * 128:(dc + 1) * 128, :], [], [("wo", dc % 2)], f"rwo{dc % 2}")

        load_stage(0)
        stt = {"sti": 0, "yi": 0}
        for dc in range(NK):
            dcs = slice(dc * 128, (dc + 1) * 128)
            for k in range(NK):
                for jj in range(3):
                    m = muidx[jj]
                    sy.op("act", ["Wst", "omm"], ["Wf"], lambda e, k=k, jj=jj, m=m: e.activation(
                        out=Wf[:, k, jj, 0, :], in_=Wst[:, k, jj, :], func=AF.Copy, scale=omm[:, m * 8 + k:m * 8 + k + 1]))
                    sy.op("dve", ["Wst", "vecs"], ["Wf"], lambda e, k=k, jj=jj, m=m: e.tensor_scalar(
                        out=Wf[:, k, jj, 1, :], in0=Wst[:, k, jj, :],
                        scalar1=self.vecs[:, mu0 + m * 8 + k:mu0 + m * 8 + k + 1], scalar2=None, op0=ALU.mult))
            if dc + 1 < NK:
                load_stage(dc + 1)
            wod, wok = wo[dc % 2], ("wo", dc % 2)
            stt["sti"] = 0
            sy.op("dve", [], [("ST", 0)], lambda e: e.memset(ST[0], 0.0))
            def make_ctx(rt, b):
                tok0 = rt * RT
                tk = slice(tok0, tok0 + RT)
                t5 = tok0 // TT
                xr = lambda k: [("xn", k, t5)] + ([("xn", k, t5 - 1)] if (tok0 % TT == 0 and t5 > 0) else [])

                def proj(jj):
                    pst, psk = self.ps()
                    for k in range(NK):
                        for sh in range(2):
                            sy.op("pe", xr(k) + ["Wf"], [psk], lambda e, k=k, sh=sh, pst=pst: e.matmul(
                                pst[:, 0:RT], Wf[:, k, jj, sh, :], self.xn[:, k, 2 - sh + tok0:2 - sh + tok0 + RT],
                                start=(k == 0 and sh == 0), stop=(k == NK - 1 and sh == 1)))
                    return pst[:, 0:RT], psk

                def small(lhsT, rhs, reads, M=128, pst=None, psk=None, start=True, stop=True, n=RT, c0=0):
                    if pst is None:
                        pst, psk = self.ps()
                    sy.op("pe", reads, [psk], lambda e: e.matmul(pst[0:M, c0:c0 + n], lhsT, rhs, start=start, stop=stop))
                    return pst, psk

                V = lambda nm, dc=dc: self.vcol(nm, dc)
                v_sb, cs, AR, BT, KT, BN, vbf = Bset[b]
                BN = BN3[rt % 3]
                Y = Yb[rt % 2]
                yk_ = ("Y", rt % 2)
                bnk = ("BN", rt % 3)
                kq = lambda nm: (nm, b)
                return dict(locals())

            def prologue(rt, b):
                c_ = make_ctx(rt, b)
                tok0, tk, t5, xr, proj, small, V = (c_[n] for n in ('tok0', 'tk', 't5', 'xr', 'proj', 'small', 'V'))
                v_sb, cs, AR, BT, KT, BN, vbf, kq, Y, yk_, bnk = (c_[n] for n in ('v_sb', 'cs', 'AR', 'BT', 'KT', 'BN', 'vbf', 'kq', 'Y', 'yk_', 'bnk'))
                pr, prk = proj(0)
                sy.op("act", [prk], ["r_sb"], lambda e: e.activation(out=r_sb, in_=pr, func=AF.Copy))
                pk, pkk = proj(1)
                sy.op("act", [pkk], ["k_sb"], lambda e: e.activation(out=k_sb, in_=pk, func=AF.Copy))
                pv, pvk = proj(2)
                sy.op("act", [pvk], [kq("v_sb")], lambda e: e.activation(out=v_sb, in_=pv, func=AF.Copy))
                yield
                plw, plwk = small(WA2[0:64, dcs], P1[0:64, tk], ["W2", ("P1", t5)])
                sy.op("act", [plwk, "vecs"], ["sg"], lambda e: e.activation(
                    out=sg, in_=plw[:, 0:RT], func=AF.Sigmoid, bias=V(f"w0{j}"), scale=1.0))
                pa, pak = small(WA2[64:128, dcs], P1[64:128, tk], ["W2", ("P1", t5)])
                sy.op("act", [pak, "vecs"], ["asig"], lambda e: e.activation(
                    out=asig, in_=pa[:, 0:RT], func=AF.Sigmoid, bias=V(f"a0{j}"), scale=1.0))
                if j == 1:
                    pg_, pgk_ = small(GB2[32:64, dcs], GB[32:64, tk], ["W2", ("GB", t5)])
                    sy.op("act", [pgk_, "vecs"], ["tB"], lambda e: e.activation(
                        out=tB, in_=pg_[:, 0:RT], func=AF.Sigmoid, bias=V("v0"), scale=1.0))
                    sy.dma("sp", vf, self.vfirst[dcs, tk], [("vfd", dc, rt)], ["vf"], "vfl")
                    sy.op("dve", ["vf", kq("v_sb")], ["vf"], lambda e: e.tensor_tensor(out=vf, in0=vf, in1=v_sb, op=ALU.subtract))
                    sy.op("dve", ["vf", "tB"], ["vf"], lambda e: e.tensor_tensor(out=vf, in0=vf, in1=tB, op=ALU.mult))
                    sy.op("dve", ["vf", kq("v_sb")], [kq("v_sb")], lambda e: e.tensor_tensor(out=v_sb, in0=v_sb, in1=vf, op=ALU.add))
                else:
                    sy.dma("sp", self.vfirst[dcs, tk], v_sb, [kq("v_sb")], [("vfd", dc, rt)], f"vfs{b}")
                sy.op("act", [kq("v_sb")], [kq("vbf")], lambda e: e.activation(out=vbf, in_=v_sb, func=AF.Copy))
                sy.op("dve", ["sg", "cmask"], [kq("cs")], lambda e: e.tensor_tensor_scan(
                    out=cs, data0=cmask, data1=sg, initial=0.0, op0=ALU.mult, op1=ALU.add))
                sy.op("dve", [kq("cs"), "sg"], ["csp"], lambda e: e.tensor_tensor(out=csp, in0=cs, in1=sg, op=ALU.subtract))
                sy.op("act", [kq("cs")], ["pinv"], lambda e: e.activation(out=pinv, in_=cs, func=AF.Exp, scale=CD))
                sy.op("act", [kq("cs")], [kq("cs")], lambda e: e.activation(out=cs, in_=cs, func=AF.Exp, scale=-CD))
                yield
                sy.op("act", ["csp"], ["csp"], lambda e: e.activation(out=csp, in_=csp, func=AF.Exp, scale=-CD))
                sy.op("dve", ["k_sb", "vecs"], ["kk"], lambda e: e.tensor_scalar(
                    out=kk, in0=k_sb, scalar1=V(f"k_k{j}"), scalar2=None, op0=ALU.mult))
                sy.op("act", ["kk"], ["tA"], lambda e: e.activation(out=tA, in_=kk, func=AF.Square))
                pss, pssk = small(self.bones, tA, ["tA", "consts"])
                sy.op("act", [pssk, "rc"], ["tA"], lambda e: e.activation(out=tA, in_=pss[:, 0:RT], func=AF.Ln, bias=tiny, scale=1.0))
                yield
                sy.op("act", ["tA"], ["tA"], lambda e: e.activation(out=tA, in_=tA, func=AF.Exp, scale=-0.5))
                sy.op("dve", ["kk", "tA"], ["kk"], lambda e: e.tensor_tensor(out=kk, in0=kk, in1=tA, op=ALU.mult))
                sy.op("dve", ["asig", "vecs", "omm"], ["tB"], lambda e: e.tensor_scalar(
                    out=tB, in0=asig, scalar1=V(f"k_a{j}"), scalar2=okka[:, dc:dc + 1], op0=ALU.mult, op1=ALU.add))
                sy.op("dve", ["k_sb", "tB"], ["k_sb"], lambda e: e.tensor_tensor(out=k_sb, in0=k_sb, in1=tB, op=ALU.mult))
                yield
                c3 = lambda ap: ap.rearrange("p (a b) -> p a b", a=2)
                sy.op("dve", ["kk", "csp"], [kq("AR0")], lambda e: e.scalar_tensor_tensor(
                    out=AR[:, :, 0, :], in0=c3(kk), scalar=-1.0, in1=c3(csp), op0=ALU.mult, op1=ALU.mult))
                sy.op("dve", ["r_sb", kq("cs")], [kq("AR1")], lambda e: e.tensor_tensor(
                    out=AR[:, :, 1, :], in0=c3(r_sb), in1=c3(cs), op=ALU.mult))
                sy.op("dve", ["kk", "asig"], ["tB"], lambda e: e.tensor_tensor(out=tB, in0=kk, in1=asig, op=ALU.mult))
                sy.op("dve", ["tB", "pinv"], [kq("BT")], lambda e: e.tensor_tensor(out=BT, in0=tB, in1=pinv, op=ALU.mult))
                sy.op("dve", ["k_sb", "pinv"], [kq("KT")], lambda e: e.tensor_tensor(out=KT, in0=k_sb, in1=pinv, op=ALU.mult))
                yield
                sy.op("dve", ["r_sb", "k_sb", "vecs"], ["tA"], lambda e: e.scalar_tensor_tensor(
                    out=tA, in0=r_sb, scalar=V(f"r_k{j}"), in1=k_sb, op0=ALU.mult, op1=ALU.mult))
                pbn, pbnk = small(self.bones, tA, ["tA", "consts"])
                sy.op("dve", [pbnk, kq("v_sb")], [bnk], lambda e: e.tensor_tensor(out=BN, in0=pbn[:, 0:RT], in1=v_sb, op=ALU.mult))
                yield

            def scanepi(rt, b):
                c_ = make_ctx(rt, b)
                tok0, tk, t5, xr, proj, small, V = (c_[n] for n in ('tok0', 'tk', 't5', 'xr', 'proj', 'small', 'V'))
                v_sb, cs, AR, BT, KT, BN, vbf, kq, Y, yk_, bnk = (c_[n] for n in ('v_sb', 'cs', 'AR', 'BT', 'KT', 'BN', 'vbf', 'kq', 'Y', 'yk_', 'bnk'))
                for ci in range(2):
                    cc = slice(ci * 128, (ci + 1) * 128)
                    pcol = cs[:, ci * 128 + 127:ci * 128 + 128]
                    sy.op("dve", [kq("BT"), kq("cs")], ["BH"], lambda e: e.tensor_scalar(out=BH, in0=BT[:, cc], scalar1=pcol, scalar2=None, op0=ALU.mult))
                    sy.op("dve", [kq("KT"), kq("cs")], ["KH"], lambda e: e.tensor_scalar(out=KH, in0=KT[:, cc], scalar1=pcol, scalar2=None, op0=ALU.mult))
                    ptm, ptmk = self.ps()
                    ptmb = ptm.bitcast(BF16)
                    for q, (src, skey) in enumerate(((AR[:, ci, 0, :], kq("AR0")), (BH, "BH"), (KH, "KH"), (vbf[:, cc], kq("vbf")))):
                        sy.op("pe", [skey, "rc2"], [ptmk], lambda e, q=q, src=src: e.transpose(
                            out=ptmb[:, q * 128:(q + 1) * 128], in_=src, identity=identb))
                    ptm3 = ptmb[:, 0:512].rearrange("p (a b) -> p a b", a=4)
                    sy.op("act", [ptmk], ["TMp"], lambda e: e.activation(out=TMp[:, :, 0:64], in_=ptm3[:, :, 0:64], func=AF.Copy))
                    sy.op("act", [ptmk], ["TMp"], lambda e: e.activation(out=TMp[:, :, 192:256], in_=ptm3[:, :, 64:128], func=AF.Copy))
                    sy.op("act", [ptmk], ["TMb"], lambda e: e.activation(out=TMb, in_=ptm3, func=AF.Copy))
                    hcs = (slice(0, 64), slice(192, 256))
                    for hd in range(2):
                        yield
                        hp = slice(hd * 64, hd * 64 + 64)
                        arh = AR[hp, ci, :, :]
                        pam, pamk = self.ps()
                        sy.op("pe", [kq("BT"), kq("AR0"), kq("AR1")], [pamk], lambda e, pam=pam, arh=arh, hp=hp: e.matmul(
                            pam[:, 0:256], BT[hp, cc], arh, start=True, stop=True))
                        sy.op("pe", [kq("KT"), kq("AR0"), kq("AR1")], [pamk], lambda e, pam=pam, arh=arh, hp=hp: e.matmul(
                            pam[:, 256:512], KT[hp, cc], arh, start=True, stop=True))
                        sy.op("dve", [pamk, "rc2"], [("AM", hd)], lambda e, pam=pam, hd=hd: e.tensor_tensor(
                            out=AM32[hd], in0=pam[:, 0:128], in1=mk4[:, 0:128], op=ALU.mult))
                        sy.op("dve", [pamk, "rc2"], [("AM", hd)], lambda e, pam=pam, hd=hd: e.tensor_tensor(
                            out=AMb[hd], in0=pam[:, 128:512].rearrange("p (a b) -> p a b", a=3),
                            in1=mk4[:, 128:512].rearrange("p (a b) -> p a b", a=3), op=ALU.mult))
                        pnt, pntk = self.ps()
                        sy.op("pe", [kq("BT"), kq("AR0")], [pntk], lambda e, pnt=pnt, hp=hp: e.matmul(
                            pnt[:, 0:128], AR[hp, ci, 0, :], BT[hp, cc], start=True, stop=True))
                        sy.op("dve", [pntk, "rc2"], [("NpT", hd, 0)], lambda e, pnt=pnt, hd=hd: e.tensor_tensor(
                            out=NpT[hd][0], in0=pnt[:, 0:128], in1=mkL, op=ALU.mult))
                        sy.op("pool", ["TMp"], [("Wt", hd, 0)], lambda e, hd=hd: e.tensor_copy(out=Wt[hd][0][:, 0, :], in_=TMp[:, 0, hcs[hd]]))
                        pxv, pxvk = self.ps()
                        sy.op("pe", [("AM", hd), "TMp"], [pxvk], lambda e, pxv=pxv, hd=hd: e.matmul(
                            pxv[:, 0:64], AMb[hd][:, 1, :], TMp[:, 3, hcs[hd]], start=True, stop=True))
                        sy.op("act", [pxvk], [("Wt", hd, 0)], lambda e, pxv=pxv, hd=hd: e.activation(
                            out=Wt[hd][0][:, 1, :], in_=pxv[:, 0:64], func=AF.Copy))
                    yield
                    for lvl in range(7):
                        yield
                        cur, nxt = lvl % 2, (lvl + 1) % 2
                        for hd in range(2):
                            npc = AM32[hd] if lvl == 0 else Np[hd][cur]
                            npk = ("AM", hd) if lvl == 0 else ("Np", hd, cur)
                            wcur = Wt[hd][cur]
                            pw, pwk = self.ps()
                            sy.op("pe", [npk, ("Wt", hd, cur)], [pwk], lambda e, pw=pw, npc=npc, wcur=wcur: e.matmul(
                                pw[:, 0:128], npc, wcur.rearrange("p a b -> p (a b)"), start=True, stop=True))
                            pw3 = pw[:, 0:128].rearrange("p (a b) -> p a b", a=2)
                            if lvl < 6:
                                sy.op("dve", [pwk, ("Wt", hd, cur)], [("Wt", hd, nxt)], lambda e, pw3=pw3, wcur=wcur, hd=hd, nxt=nxt: e.tensor_tensor(
                                    out=Wt[hd][nxt], in0=pw3, in1=wcur, op=ALU.add))
                                pn, pnk = self.ps()
                                sy.op("pe", [npk, ("NpT", hd, cur)], [pnk], lambda e, pn=pn, npc=npc, hd=hd, cur=cur: e.matmul(
                                    pn[:, 0:128], NpT[hd][cur], npc, start=True, stop=True))
                                sy.op("act", [pnk], [("Np", hd, nxt)], lambda e, pn=pn, hd=hd, nxt=nxt: e.activation(
                                    out=Np[hd][nxt], in_=pn[:, 0:128], func=AF.Copy))
                                pn2, pn2k = self.ps()
                                sy.op("pe", [npk, ("NpT", hd, cur)], [pn2k], lambda e, pn2=pn2, npc=npc, hd=hd, cur=cur: e.matmul(
                                    pn2[:, 0:128], npc, NpT[hd][cur], start=True, stop=True))
                                sy.op("act", [pn2k], [("NpT", hd, nxt)], lambda e, pn2=pn2, hd=hd, nxt=nxt: e.activation(
                                    out=NpT[hd][nxt], in_=pn2[:, 0:128], func=AF.Copy))
                            else:
                                sy.op("dve", [pwk, ("Wt", hd, cur)], ["Wfin"], lambda e, pw3=pw3, wcur=wcur, hd=hd: e.tensor_tensor(
                                    out=Wfin[:, :, hcs[hd]], in0=pw3, in1=wcur, op=ALU.add))
                                sy.op("dve", [pwk, ("Wt", hd, cur)], ["WB"], lambda e, pw3=pw3, wcur=wcur, hd=hd: e.tensor_tensor(
                                    out=WB[:, :, hd * 64:(hd + 1) * 64], in0=pw3, in1=wcur, op=ALU.add))
                    yield
                    Ah_b, Uh_b = WB[:, 0, :], WB[:, 1, :]
                    Bh_b, Kh_b, VT_b = TMb[:, 1, :], TMb[:, 2, :], TMb[:, 3, :]
                    stc, stn = ST[stt['sti'] % 2], ST[(stt['sti'] + 1) % 2]
                    stck, stnk = ("ST", stt['sti'] % 2), ("ST", (stt['sti'] + 1) % 2)
                    stt['sti'] += 1
                    pm, pmk = self.ps()
                    sy.op("pe", ["WB", "TMb"], [pmk], lambda e, pm=pm: e.matmul(pm[:, 0:128], Ah_b, Bh_b, start=True, stop=True))
                    sy.op("dve", [pmk, "consts"], ["M1"], lambda e, pm=pm: e.tensor_tensor(out=M1, in0=pm[:, 0:128], in1=self.bones, op=ALU.mult))
                    sy.op("dve", ["M1", "rc2", kq("cs")], ["MBD"], lambda e: e.scalar_tensor_tensor(
                        out=MBD, in0=ident, scalar=pcol, in1=M1, op0=ALU.mult, op1=ALU.add))
                    pn_, pnk_ = self.ps()
                    sy.op("pe", ["WB", "TMb"], [pnk_], lambda e, pn_=pn_: e.matmul(pn_[:, 0:128], Bh_b, Uh_b, start=True, stop=False))
                    sy.op("pe", ["TMb"], [pnk_], lambda e, pn_=pn_: e.matmul(pn_[:, 0:128], Kh_b, VT_b, start=False, stop=True))
                    sy.op("dve", [pnk_, "consts"], ["NCt"], lambda e, pn_=pn_: e.tensor_tensor(out=NCt, in0=pn_[:, 0:128], in1=self.bones, op=ALU.mult))
                    yield
                    pg, pgk = self.ps()
                    sy.op("pe", ["Wfin", ("AM", 0)], [pgk], lambda e, pg=pg: e.matmul(pg[:, 0:128], Wfin[:, 0, 0:128], AMb[0][:, 0, :], start=True, stop=False))
                    sy.op("pe", ["Wfin", ("AM", 1)], [pgk], lambda e, pg=pg: e.matmul(pg[:, 0:128], Wfin[:, 0, 128:256], AMb[1][:, 0, :], start=False, stop=True))
                    sy.op("dve", [pgk, kq("AR1")], ["G"], lambda e, pg=pg: e.tensor_tensor(out=G, in0=pg[:, 0:128], in1=AR[:, ci, 1, :], op=ALU.add))
                    yield
                    py, pyk = self.ps()
                    sy.op("pe", [stck, "G"], [pyk], lambda e, py=py, stc=stc: e.matmul(py[:, 0:128], stc, G, start=True, stop=False))
                    sy.op("pe", ["Wfin", ("AM", 0)], [pyk], lambda e, py=py: e.matmul(py[:, 0:128], Wfin[:, 1, 0:128], AMb[0][:, 0, :], start=False, stop=False))
                    sy.op("pe", ["Wfin", ("AM", 1)], [pyk], lambda e, py=py: e.matmul(py[:, 0:128], Wfin[:, 1, 128:256], AMb[1][:, 0, :], start=False, stop=False))
                    sy.op("pe", ["TMp", ("AM", 0)], [pyk], lambda e, py=py: e.matmul(py[:, 0:128], TMp[:, 3, 0:128], AMb[0][:, 2, :], start=False, stop=False))
                    sy.op("pe", ["TMp", ("AM", 1)], [pyk], lambda e, py=py: e.matmul(py[:, 0:128], TMp[:, 3, 128:256], AMb[1][:, 2, :], start=False, stop=True))
                    sy.op("act", [pyk], [yk_], lambda e, py=py: e.activation(out=Y[:, cc], in_=py[:, 0:128], func=AF.Copy))
                    yield
                    pst_, pstk_ = self.ps()
                    sy.op("pe", ["MBD", stck], [pstk_], lambda e, pst_=pst_, stc=stc: e.matmul(pst_[:, 0:128], MBD, stc, start=True, stop=True))
                    sy.op("dve", [pstk_, "NCt"], [stnk], lambda e, pst_=pst_, stn=stn: e.tensor_tensor(out=stn, in0=pst_[:, 0:128], in1=NCt, op=ALU.add))
                yield

            def epilogue(rt, b):
                c_ = make_ctx(rt, b)
                tok0, tk, t5, xr, proj, small, V = (c_[n] for n in ('tok0', 'tk', 't5', 'xr', 'proj', 'small', 'V'))
                v_sb, cs, AR, BT, KT, BN, vbf, kq, Y, yk_, bnk = (c_[n] for n in ('v_sb', 'cs', 'AR', 'BT', 'KT', 'BN', 'vbf', 'kq', 'Y', 'yk_', 'bnk'))
                pmn, pmnk = small(self.bones, Y, [yk_, "consts"])
                sy.op("dve", [pmnk, yk_], ["YC"], lambda e: e.scalar_tensor_tensor(
                    out=YC, in0=pmn[:, 0:RT], scalar=-1.0 / 64, in1=Y, op0=ALU.mult, op1=ALU.add))
                sy.op("act", ["YC"], ["tC"], lambda e: e.activation(out=tC, in_=YC, func=AF.Square))
                pvr, pvrk = small(self.bones, tC, ["tC", "consts"])
                sy.op("act", [pvrk, "rc"], ["tC"], lambda e: e.activation(out=tC, in_=pvr[:, 0:RT], func=AF.Ln, bias=gneps, scale=1.0 / 64))
                sy.op("act", ["tC"], ["tC"], lambda e: e.activation(out=tC, in_=tC, func=AF.Exp, scale=-0.5))
                sy.op("dve", ["YC", "tC"], ["YC"], lambda e: e.tensor_tensor(out=YC, in0=YC, in1=tC, op=ALU.mult))
                sy.op("dve", ["YC", "vecs"], ["YC"], lambda e: e.tensor_scalar(
                    out=YC, in0=YC, scalar1=V(f"lnx_w{j}"), scalar2=V(f"lnx_b{j}"), op0=ALU.mult, op1=ALU.add))
                sy.op("dve", ["YC", bnk], ["YC"], lambda e: e.tensor_tensor(out=YC, in0=YC, in1=BN, op=ALU.add))
                yield
                pgt, pgtk = small(GA2[:, dcs], GA[:, tk], ["W2", ("GA", t5)], stop=False)
                small(GB2[0:32, dcs], GB[0:32, tk], ["W2", ("GB", t5)], pst=pgt, psk=pgtk, start=False, stop=True)
                yo = yout[stt['yi'] % 2]
                yok = ("yout", stt['yi'] % 2)
                stt['yi'] += 1
                sy.op("dve", ["YC", pgtk], [yok], lambda e, yo=yo: e.tensor_tensor(out=yo, in0=YC, in1=pgt[:, 0:RT], op=ALU.mult))
                for m in range(NK):
                    po, pok = self.ps()
                    sy.op("pe", [wok, yok], [pok], lambda e, m=m, po=po, yo=yo: e.matmul(
                        po[:, 0:RT], wod[:, m * 128:(m + 1) * 128], yo, start=True, stop=True))
                    sy.op("dve", [pok, ("h", m, t5)], [("h", m, t5)], lambda e, m=m, po=po: e.tensor_tensor(
                        out=self.h[:, m, tk], in0=po[:, 0:RT], in1=self.h[:, m, tk], op=ALU.add))
                yield

            for _ in prologue(0, 0):
                pass
            for rt in range(NRT + 1):
                gens = []
                if rt < NRT:
                    gens.append(scanepi(rt, rt % 2))
                if rt >= 1:
                    gens.append(epilogue(rt - 1, (rt - 1) % 2))
                if rt + 1 < NRT:
                    gens.append(prologue(rt + 1, (rt + 1) % 2))
                while gens:
                    for g_ in list(gens):
                        try:
                            next(g_)
                        except StopIteration:
                            gens.remove(g_)


ALL_LAYERS = []
for _l in range(DEPTH):
    ALL_LAYERS += [("mix", _l), ("mlp", _l)]

WEIGHT_NAMES = ["mlp_up", "mlp_down", "rwkv_w_r", "rwkv_w_k", "rwkv_w_v", "rwkv_w_o",
                "rwkv_decay_w1", "rwkv_decay_w2", "rwkv_iclr_a1", "rwkv_iclr_a2",
                "rwkv_gate_g1", "rwkv_gate_g2", "rwkv_vres_v1", "rwkv_vres_v2",
                "conv_w_in", "conv_w_out", "sb_w_qkv", "sb_w_o"]


def run(inputs, layers, n_cores=8, trace=False):
    inp = {k: np.asarray(v) for k, v in inputs.items()}
    prog = Prog(layers)
    nc = prog.build()
    vecs = pack_vecs(inp)
    wts = {n: np.ascontiguousarray(inp[n], dtype=np.float32) for n in WEIGHT_NAMES}
    in_maps = []
    for b in range(n_cores):
        m = {"xT": np.ascontiguousarray(inp["x"][b].T), "vecs": vecs}
        m.update(wts)
        in_maps.append(m)
    res = run_bass_kernel_spmd(nc, in_maps, core_ids=list(range(n_cores)), trace=trace)
    out = np.stack([np.ascontiguousarray(r["outT"].T) for r in res.results], axis=0)
    return out, res, prog


def kernel(**inputs):
    out, _, _ = run(inputs, ALL_LAYERS)
    return out.astype(np.float32)
```

```python
import numpy as np
import concourse.bass as bass
import concourse.mybir as mybir
from concourse.bass_utils import run_bass_kernel_spmd

F32 = mybir.dt.float32
F32R = mybir.dt.float32r
BF16 = mybir.dt.bfloat16
AF = mybir.ActivationFunctionType
ALU = mybir.AluOpType

D = 1024
S = 2048
NK = 8
TT = 512
NT = S // TT
DFF = 4096
DEPTH = 4
RMS_EPS = 1e-6
GN_EPS = 64e-5


class Sy:
    def __init__(self, nc):
        self.nc = nc
        self.eng = {"pe": nc.tensor, "dve": nc.vector, "act": nc.scalar,
                    "pool": nc.gpsimd, "sp": nc.sync}
        self.sem = {e: nc.alloc_semaphore("s_" + e) for e in self.eng}
        self.cnt = {e: 0 for e in self.eng}
        self.pend = {e: False for e in self.eng}
        self.waited = {e: {} for e in self.eng}
        self.last_w = {}
        self.readers = {}
        self.dsem = {}
        self.dcnt = {}
        self.n_wait = 0
        self.n_inst = 0

    def _wait(self, e, deps):
        need = {}
        for (sk, v) in deps:
            if need.get(sk, 0) < v:
                need[sk] = v
        for sk, v in need.items():
            if sk == e and e == "pe":
                continue
            if self.waited[e].get(sk, 0) >= v:
                continue
            sem = self.sem[sk] if sk in self.sem else self.dsem[sk]
            self.eng[e].wait_ge(sem, v)
            self.waited[e][sk] = v
            self.n_wait += 1

    def _deps(self, reads, writes):
        deps = []
        for k in reads:
            if k in self.last_w:
                deps.append(self.last_w[k])
        for k in writes:
            if k in self.last_w:
                deps.append(self.last_w[k])
            deps.extend(self.readers.get(k, {}).items())
        return deps

    def _record(self, tok, reads, writes):
        for k in reads:
            r = self.readers.setdefault(k, {})
            if r.get(tok[0], 0) < tok[1]:
                r[tok[0]] = tok[1]
        for k in writes:
            self.last_w[k] = tok
            self.readers[k] = {}

    def op(self, e, reads, writes, emit, inc=True):
        psr = [k for k in reads if isinstance(k, tuple) and k[0] == "ps" and k not in writes]
        if psr:
            writes = list(writes) + psr
        self._wait(e, self._deps(reads, writes))
        inst = emit(self.eng[e])
        self.n_inst += 1
        inc = True
        if inc:
            self.cnt[e] += 1
            inst.then_inc(self.sem[e], 1)
            self.pend[e] = False
            tok = (e, self.cnt[e])
        else:
            self.pend[e] = True
            tok = (e, self.cnt[e] + 1)
        self._record(tok, reads, writes)
        return inst

    def dma(self, e, out, in_, reads, writes, sk, **kw):
        if sk not in self.dsem:
            self.dsem[sk] = self.nc.alloc_semaphore("d_" + sk)
            self.dcnt[sk] = 0
        self._wait(e, self._deps(reads, writes))
        inst = self.eng[e].dma_start(out=out, in_=in_, **kw)
        self.dcnt[sk] += 16
        inst.then_inc(self.dsem[sk], 16)
        self.n_inst += 1
        self._record((sk, self.dcnt[sk]), reads, writes)
        return inst

    def barrier(self):
        for e in self.eng:
            deps = [(e2, self.cnt[e2]) for e2 in self.eng if e2 != e and self.cnt[e2] > 0]
            deps += [(sk, self.dcnt[sk]) for sk in self.dsem if self.dcnt[sk] > 0]
            self._wait(e, deps)

    def wait_all(self, e, keys):
        deps = []
        for k in keys:
            if k in self.last_w:
                deps.append(self.last_w[k])
            deps.extend(self.readers.get(k, {}).items())
        self._wait(e, deps)


VEC_COLS = {}


def _vec_layout():
    cols = {}
    off = 0

    def add(name, n=NK):
        nonlocal off
        cols[name] = off
        off += n
    for l in range(DEPTH):
        add(f"mix_norm{l}")
        add(f"mlp_norm{l}")
    for j in range(2):
        for m in range(6):
            add(f"mu{j}_{m}")
        for nm in ("w0", "a0", "k_k", "k_a", "r_k", "lnx_w", "lnx_b"):
            add(f"{nm}{j}")
    add("v0")
    for c in range(3):
        add(f"conv_w{c}")
    add("q_gain", 1)
    add("k_gain", 1)
    return cols, off


VEC_COLS, NVEC = _vec_layout()


def pack_vecs(inp):
    tab = np.zeros((128, NVEC), np.float32)

    def put(name, v):
        v = np.asarray(v, np.float32).reshape(-1)
        c = VEC_COLS[name]
        if v.size == D:
            tab[:, c:c + NK] = v.reshape(NK, 128).T
        else:
            tab[:, c] = np.concatenate([v, v])
    for l in range(DEPTH):
        put(f"mix_norm{l}", inp["mix_norm"][l])
        put(f"mlp_norm{l}", inp["mlp_norm"][l])
    for j in range(2):
        for m in range(6):
            put(f"mu{j}_{m}", inp["rwkv_mu"][j, m])
        put(f"w0{j}", inp["rwkv_decay_w0"][j])
        put(f"a0{j}", inp["rwkv_iclr_a0"][j])
        put(f"k_k{j}", inp["rwkv_k_k"][j])
        put(f"k_a{j}", inp["rwkv_k_a"][j])
        put(f"r_k{j}", inp["rwkv_r_k"][j])
        put(f"lnx_w{j}", inp["rwkv_lnx_w"][j])
        put(f"lnx_b{j}", inp["rwkv_lnx_b"][j])
    put("v0", inp["rwkv_vres_v0"][0])
    for c in range(3):
        put(f"conv_w{c}", inp["conv_w"][0, c])
    put("q_gain", inp["sb_q_norm"][0])
    put("k_gain", inp["sb_k_norm"][0])
    return tab


class Prog:
    def __init__(self, layers, n_layers_mlp=None):
        self.layers = layers
        nc = bass.Bass("TRN2", target_bir_lowering=False)
        self.nc = nc
        self.sy = Sy(nc)
        self._n = 0
        dt = nc.dram_tensor
        self.xT = dt("xT", [D, S], F32, kind="ExternalInput").ap()
        self.vecs_d = dt("vecs", [128, NVEC], F32, kind="ExternalInput").ap()
        self.outT = dt("outT", [D, S], F32, kind="ExternalOutput").ap()
        self.w = {}
        for name, shape in (
            ("mlp_up", [DEPTH, D, DFF]), ("mlp_down", [DEPTH, DFF, D]),
            ("rwkv_w_r", [2, D, D]), ("rwkv_w_k", [2, D, D]), ("rwkv_w_v", [2, D, D]),
            ("rwkv_w_o", [2, D, D]),
            ("rwkv_decay_w1", [2, D, 64]), ("rwkv_decay_w2", [2, 64, D]),
            ("rwkv_iclr_a1", [2, D, 64]), ("rwkv_iclr_a2", [2, 64, D]),
            ("rwkv_gate_g1", [2, D, 160]), ("rwkv_gate_g2", [2, 160, D]),
            ("rwkv_vres_v1", [1, D, 32]), ("rwkv_vres_v2", [1, 32, D]),
            ("conv_w_in", [1, D, 3 * D]), ("conv_w_out", [1, D, D]),
            ("sb_w_qkv", [1, D, 3 * D]), ("sb_w_o", [1, D, D]),
        ):
            self.w[name] = dt(name, shape, F32, kind="ExternalInput").ap()
        self.vfirst = dt("vfirst_scratch", [D, S], F32, kind="Internal").ap()
        self.psum = [nc.alloc_psum_tensor(f"ps{i}", [128, 512], F32).ap() for i in range(8)]
        self.ps_i = 0

    def sb(self, name, shape, dtype):
        return self.nc.alloc_sbuf_tensor(name, shape, dtype).ap()

    def ps(self):
        i = self.ps_i
        self.ps_i = (i + 1) % 6
        return self.psum[i], ("ps", i)

    def phase_begin(self):
        self.sy.barrier()
        self.arena_off = 0

    def carve(self, shape, dtype):
        n = 1
        for d in shape:
            n *= d
        nb = n * (4 if dtype in (F32, F32R) else 2)
        nb = (nb + 31) // 32 * 32
        off = self.arena_off
        assert off + nb <= self.ARENA * 2, (off, nb, self.ARENA * 2)
        self.arena_off = off + nb
        self._n += 1
        return self.nc.alloc_sbuf_tensor_at(f"cv{self._n}", [128] + list(shape), dtype,
                                            offset=self.arena_base + off).ap()

    def vcol(self, name, k=0):
        c = VEC_COLS[name] + k
        return self.vecs[:, c:c + 1]

    def build(self):
        nc, sy = self.nc, self.sy
        self.h = self.sb("h", [128, NK, S], F32)
        self.xn = self.sb("xn", [128, NK, S + 2], BF16)
        self.vecs = self.sb("vecs_sb", [128, NVEC], F32)
        self.ones_bf = self.sb("ones_bf", [128, 128], BF16)
        self.eps_t = self.sb("eps_t", [128, 1], F32)
        self.ARENA = 53 * 1024 + 512
        self.arena = self.sb("arena", [128, self.ARENA], BF16)
        self.arena_base = self.nc.sbuf_base - self.ARENA * 2
        self.arena_off = 0
        self.make_consts()
        sy.dma("sp", self.vecs, self.vecs_d, [], ["vecs"], "misc")
        for k in range(NK):
            sy.dma("sp", self.h[:, k, :], self.xT[k * 128:(k + 1) * 128, :], [], [("h", k, t) for t in range(NT)], f"xin{k}")
        sy.op("dve", [], ["ones"], lambda e: e.memset(self.ones_bf, 1.0))
        sy.op("dve", [], ["eps"], lambda e: e.memset(self.eps_t, RMS_EPS))
        sy.op("dve", [], [("xnpad",)], lambda e: e.memset(self.xn[:, :, 0:2], 0.0))
        for (kind, l) in self.layers:
            if kind == "mlp":
                self.rmsnorm(f"mlp_norm{l}")
                self.mlp(l)
            elif kind == "mix":
                self.rmsnorm(f"mix_norm{l}")
                if l % 3 == 1:
                    self.conv(l // 3)
                elif l % 3 == 2:
                    self.sbatt(l // 3)
                else:
                    self.rwkv(l // 3)
        for k in range(NK):
            sy.dma("sp", self.outT[k * 128:(k + 1) * 128, :], self.h[:, k, :],
                   [("h", k, t) for t in range(NT)], [("out", k)], "out")
        sy.wait_all("sp", [("out", k) for k in range(NK)])
        return nc

    def make_consts(self):
        sy = self.sy
        self.one_col = self.sb("one_col", [128, 1], F32)
        self.bones = self.sb("bones", [128, 128], F32)
        self.tri = self.sb("tri", [128, 128], F32R)
        self.onesr = self.sb("onesr", [128, 128], F32R)
        self.onesw = self.carve([128], F32)
        sy.op("dve", [], ["consts"], lambda e: e.memset(self.one_col, 1.0))
        sy.op("dve", [], ["consts"], lambda e: e.memset(self.bones, 0.0))
        sy.op("dve", [], ["consts"], lambda e: e.memset(self.bones[0:64, 0:64], 1.0))
        sy.op("dve", [], ["consts"], lambda e: e.memset(self.bones[64:128, 64:128], 1.0))
        sy.op("dve", [], ["consts"], lambda e: e.memset(self.onesw, 1.0))
        sy.op("pool", ["consts"], ["consts2"], lambda e: e.affine_select(
            out=self.tri, in_=self.onesw[:, 0:128], pattern=[[-1, 128]], compare_op=ALU.is_ge, fill=0.0,
            base=0, channel_multiplier=1))
        sy.op("pool", ["consts"], ["consts2"], lambda e: e.tensor_copy(out=self.onesr, in_=self.onesw[:, 0:128]))

    def rmsnorm(self, gname):
        sy = self.sy
        self.phase_begin()
        self.sq = [self.carve([TT], BF16) for i in range(2)]
        self.rstd = [self.carve([TT], F32) for i in range(2)]
        for t in range(NT):
            ts = slice(t * TT, (t + 1) * TT)
            pst, psk = self.ps()
            for k in range(NK):
                sq = self.sq[k % 2]
                sqk = ("sq", k % 2)
                sy.op("act", [("h", k, t)], [sqk],
                      lambda e, sq=sq, k=k: e.activation(out=sq, in_=self.h[:, k, ts], func=AF.Square))
                sy.op("pe", [sqk, "ones"], [psk],
                      lambda e, sq=sq, k=k: e.matmul(pst, self.ones_bf, sq, start=(k == 0), stop=(k == NK - 1)),
                      inc=(k == NK - 1))
            rs = self.rstd[t % 2]
            rsk = ("rstd", t % 2)
            sy.op("act", [psk, "eps"], [rsk],
                  lambda e: e.activation(out=rs, in_=pst, func=AF.Ln, bias=self.eps_t, scale=1.0 / D))
            sy.op("act", [rsk], [rsk], lambda e: e.activation(out=rs, in_=rs, func=AF.Exp, scale=-0.5))
            for k in range(NK):
                sy.op("dve", [("h", k, t), rsk, "vecs"], [("xn", k, t)],
                      lambda e, k=k: e.scalar_tensor_tensor(
                          out=self.xn[:, k, 2 + t * TT:2 + (t + 1) * TT], in0=self.h[:, k, ts],
                          scalar=self.vcol(gname, k), in1=rs, op0=ALU.mult, op1=ALU.mult))

    def alloc_mlp(self):
        self.phase_begin()
        self.GF = 512
        self.wup = [self.carve([NK, self.GF], BF16) for i in range(2)]
        self.wdn = [self.carve([self.GF // 128, D], BF16) for i in range(2)]
        self.hT = self.carve([self.GF // 128, S], BF16)
        self.relu_t = [self.carve([TT], F32) for i in range(2)]
        self.mlp_gi = 0

    def mlp_load(self, l, g):
        sy = self.sy
        GF = self.GF
        s = self.mlp_gi % 2
        self.mlp_gi += 1
        src_up = self.w["mlp_up"][l, :, g * GF:(g + 1) * GF].rearrange("(k p) f -> p k f", p=128)
        sy.dma("pool", self.wup[s], src_up, [], [("wup", s)], f"wup{s}")
        src_dn = self.w["mlp_down"][l, g * GF:(g + 1) * GF, :].rearrange("(c p) d -> p c d", p=128)
        sy.dma("pool", self.wdn[s], src_dn, [], [("wdn", s)], f"wdn{s}")
        return s

    def mlp(self, l):
        sy = self.sy
        self.alloc_mlp()
        GF = self.GF
        NG = DFF // GF
        NC = GF // 128
        slots = [None] * NG
        slots[0] = self.mlp_load(l, 0)
        ri = 0
        for g in range(NG):
            if g + 1 < NG:
                slots[g + 1] = self.mlp_load(l, g + 1)
            s = slots[g]
            for t in range(NT):
                for c in range(NC):
                    pst, psk = self.ps()
                    for k in range(NK):
                        sy.op("pe", [("wup", s), ("xn", k, t)], [psk],
                              lambda e, k=k, c=c, t=t: e.matmul(
                                  pst, self.wup[s][:, k, c * 128:(c + 1) * 128],
                                  self.xn[:, k, 2 + t * TT:2 + (t + 1) * TT],
                                  start=(k == 0), stop=(k == NK - 1)),
                              inc=(k == NK - 1))
                    rt = self.relu_t[ri % 2]
                    rk = ("relu", ri % 2)
                    ri += 1
                    sy.op("act", [psk], [rk], lambda e, rt=rt: e.activation(out=rt, in_=pst, func=AF.Relu))
                    sy.op("dve", [rk, psk], [("hT", c, t)],
                          lambda e, rt=rt, c=c, t=t: e.tensor_tensor(
                              out=self.hT[:, c, t * TT:(t + 1) * TT], in0=rt, in1=pst, op=ALU.mult))
            for t in range(NT):
                for m in range(NK):
                    pst, psk = self.ps()
                    for c in range(NC):
                        sy.op("pe", [("wdn", s), ("hT", c, t)], [psk],
                              lambda e, c=c, m=m, t=t: e.matmul(
                                  pst, self.wdn[s][:, c, m * 128:(m + 1) * 128],
                                  self.hT[:, c, t * TT:(t + 1) * TT],
                                  start=(c == 0), stop=(c == NC - 1)),
                              inc=(c == NC - 1))
                    sy.op("dve", [psk, ("h", m, t)], [("h", m, t)],
                          lambda e, m=m, t=t: e.tensor_tensor(
                              out=self.h[:, m, t * TT:(t + 1) * TT], in0=pst,
                              in1=self.h[:, m, t * TT:(t + 1) * TT], op=ALU.add))


    def proj_fm(self, wt, wkey, cols, t, shift=0, pst=None, psk=None, first=True, last=True):
        sy = self.sy
        if pst is None:
            pst, psk = self.ps()
        for k in range(NK):
            sy.op("pe", [wkey, ("xn", k, t)] + ([("xn", k, t - 1)] if (shift and t > 0) else []), [psk],
                  lambda e, k=k: e.matmul(pst, wt[:, k, cols],
                                          self.xn[:, k, 2 - shift + t * TT:2 - shift + (t + 1) * TT],
                                          start=(first and k == 0), stop=(last and k == NK - 1)))
        return pst, psk

    def outproj_acc(self, wo, wokey, y, ykey, t):
        sy = self.sy
        for m in range(NK):
            pst, psk = self.ps()
            sy.op("pe", [wokey, ykey], [psk],
                  lambda e, m=m: e.matmul(pst, wo[:, m * 128:(m + 1) * 128], y, start=True, stop=True))
            sy.op("dve", [psk, ("h", m, t)], [("h", m, t)],
                  lambda e, m=m: e.tensor_tensor(out=self.h[:, m, t * TT:(t + 1) * TT], in0=pst,
                                                 in1=self.h[:, m, t * TT:(t + 1) * TT], op=ALU.add))

    def alloc_mix(self):
        self.phase_begin()
        self.w3 = [self.carve([NK, 3, 128], BF16) for i in range(2)]
        self.wo = [self.carve([D], BF16) for i in range(2)]
        self.ybf = [self.carve([TT], BF16) for i in range(2)]
        self.mix_i = 0

    def load_w3(self, wname, j, dc, wo_name):
        sy = self.sy
        s = self.mix_i % 2
        self.mix_i += 1
        src = self.w[wname][j].rearrange("(k p) f -> p k f", p=128)
        for jj in range(3):
            sy.dma("pool", self.w3[s][:, :, jj, :], src[:, :, jj * D + dc * 128:jj * D + (dc + 1) * 128],
                   [], [("w3", s)], f"w3_{s}")
        sy.dma("pool", self.wo[s], self.w[wo_name][j, dc * 128:(dc + 1) * 128, :], [], [("wo", s)], f"wo_{s}")
        return s

    def conv(self, j):
        sy = self.sy
        self.alloc_mix()
        csb = [self.carve([TT], F32) for i in range(2)]
        bsb = [self.carve([TT], F32) for i in range(2)]
        acc = self.carve([TT], F32)
        zbufs = [self.carve([2 + S], F32) for i in range(2)]
        slots = [None] * NK
        slots[0] = self.load_w3("conv_w_in", j, 0, "conv_w_out")
        slots[1] = self.load_w3("conv_w_in", j, 1, "conv_w_out")
        items = [(dc, t) for dc in range(NK) for t in range(NT)]
        st = {"yi": 0}

        def stage1(n):
            dc, t = items[n]
            i2 = n % 2
            if t == 0:
                sy.op("dve", [], [("z", dc % 2, -1)], lambda e: e.memset(zbufs[dc % 2][:, 0:2], 0.0))
            s = slots[dc]
            w3 = self.w3[s]
            zb = zbufs[dc % 2]
            zs = slice(2 + t * TT, 2 + (t + 1) * TT)
            pb, pbk = self.proj_fm(w3[:, :, 0, :], ("w3", s), slice(0, 128), t)
            sy.op("act", [pbk], [("bsb", i2)], lambda e: e.activation(out=bsb[i2], in_=pb, func=AF.Copy))
            pc, pck = self.proj_fm(w3[:, :, 1, :], ("w3", s), slice(0, 128), t)
            sy.op("act", [pck], [("csb", i2)], lambda e: e.activation(out=csb[i2], in_=pc, func=AF.Copy))
            pu, puk = self.proj_fm(w3[:, :, 2, :], ("w3", s), slice(0, 128), t)
            sy.op("dve", [("csb", i2), puk], [("z", dc % 2, t)],
                  lambda e: e.tensor_tensor(out=zb[:, zs], in0=csb[i2], in1=pu, op=ALU.mult))

        def stage2(n):
            dc, t = items[n]
            i2 = n % 2
            s = slots[dc]
            wo = self.wo[s]
            zb = zbufs[dc % 2]
            zs = slice(2 + t * TT, 2 + (t + 1) * TT)
            zk = [("z", dc % 2, t), ("z", dc % 2, t - 1)]
            sy.op("dve", zk + ["vecs"], ["acc"],
                  lambda e: e.tensor_scalar(out=acc, in0=zb[:, t * TT:(t + 1) * TT],
                                            scalar1=self.vcol("conv_w0", dc), scalar2=None, op0=ALU.mult))
            sy.op("dve", zk + ["acc", "vecs"], ["acc"],
                  lambda e: e.scalar_tensor_tensor(out=acc, in0=zb[:, 1 + t * TT:1 + (t + 1) * TT],
                                                   scalar=self.vcol("conv_w1", dc), in1=acc,
                                                   op0=ALU.mult, op1=ALU.add))
            sy.op("dve", zk + ["acc", "vecs"], ["acc"],
                  lambda e: e.scalar_tensor_tensor(out=acc, in0=zb[:, zs],
                                                   scalar=self.vcol("conv_w2", dc), in1=acc,
                                                   op0=ALU.mult, op1=ALU.add))
            y = self.ybf[st["yi"] % 2]
            yk = ("ybf", st["yi"] % 2)
            st["yi"] += 1
            sy.op("dve", [("bsb", i2), "acc"], [yk], lambda e: e.tensor_tensor(out=y, in0=bsb[i2], in1=acc, op=ALU.mult))
            self.outproj_acc(wo, ("wo", s), y, yk, t)

        for n in range(len(items) + 1):
            if n < len(items):
                stage1(n)
            if n >= 1:
                stage2(n - 1)
                dcp, tp = items[n - 1]
                if tp == NT - 1 and dcp + 2 < NK:
                    slots[dcp + 2] = self.load_w3("conv_w_in", j, dcp + 2, "conv_w_out")

    def sbatt(self, j):
        sy = self.sy
        self.alloc_mix()
        qn = self.carve([S], F32R)
        kn = self.carve([S], F32R)
        vpA = self.carve([16, 128], BF16)
        vpB = self.carve([16, 128], BF16)
        raw_t = [self.carve([TT], F32) for i in range(2)]
        sq_t = [self.carve([TT], F32) for i in range(2)]
        rs_t = [self.carve([TT], F32) for i in range(2)]
        e_t = [self.carve([TT], F32) for i in range(3)]
        sp_t = [self.carve([TT], F32) for i in range(3)]
        lk_t = [self.carve([TT], F32R) for i in range(3)]
        u_t = [self.carve([TT], F32) for i in range(3)]
        arg_t = [self.carve([TT], F32) for i in range(3)]
        att_t = [self.carve([TT], BF16) for i in range(3)]
        R_t = [self.carve([TT], F32R) for i in range(2)]
        qgs = self.carve([1], F32)
        self.m01 = self.carve([896], BF16)
        self.mneg = self.carve([896], F32)
        onesw = self.carve([896], BF16)
        sy.op("dve", [], ["sbc"], lambda e: e.memset(onesw, 1.0))
        sy.op("pool", ["sbc"], ["consts2"], lambda e: e.affine_select(
            out=self.m01, in_=onesw, pattern=[[1, 896]], compare_op=ALU.is_gt, fill=0.0,
            base=-384, channel_multiplier=-1))
        sy.op("pool", ["consts2"], ["consts2"], lambda e: e.tensor_scalar(
            out=self.mneg, in0=self.m01, scalar1=-1.0, scalar2=None, op0=ALU.mult))
        sy.op("dve", ["vecs"], ["qgs"], lambda e: e.tensor_scalar(
            out=qgs, in0=self.vcol("q_gain"), scalar1=0.125, scalar2=None, op0=ALU.mult))
        sy.op("dve", [], ["vpA"], lambda e: e.memset(vpA, 0.0))
        sy.op("dve", [], ["vpB"], lambda e: e.memset(vpB, 0.0))
        slots = [None] * NK
        slots[0] = self.load_w3("sb_w_qkv", j, 0, "sb_w_o")
        ni = 0
        pi = 0
        oi = 0
        yi = 0
        for dc in range(NK):
            if dc + 1 < NK:
                slots[dc + 1] = self.load_w3("sb_w_qkv", j, dc + 1, "sb_w_o")
            s = slots[dc]
            w3, wo = self.w3[s], self.wo[s]
            for t in range(NT):
                ts = slice(t * TT, (t + 1) * TT)
                for (jj, dst, dkey, gcol) in ((0, qn, "qn", qgs), (1, kn, "kn", self.vcol("k_gain"))):
                    pp, ppk = self.proj_fm(w3[:, :, jj, :], ("w3", s), slice(0, 128), t)
                    raw, sq, rs = raw_t[ni % 2], sq_t[ni % 2], rs_t[ni % 2]
                    rk, sk_, rsk = ("raw", ni % 2), ("sqq", ni % 2), ("rsq", ni % 2)
                    ni += 1
                    sy.op("act", [ppk], [rk], lambda e, raw=raw, pp=pp: e.activation(out=raw, in_=pp, func=AF.Copy))
                    sy.op("act", [ppk], [sk_], lambda e, sq=sq, pp=pp: e.activation(out=sq, in_=pp, func=AF.Square))
                    p2, p2k = self.ps()
                    sy.op("pe", [sk_, "consts"], [p2k],
                          lambda e, sq=sq, p2=p2: e.matmul(p2, self.bones, sq, start=True, stop=True))
                    sy.op("act", [p2k, "eps"], [rsk],
                          lambda e, rs=rs, p2=p2: e.activation(out=rs, in_=p2, func=AF.Ln, bias=self.eps_t, scale=1.0 / 64))
                    sy.op("act", [rsk], [rsk], lambda e, rs=rs: e.activation(out=rs, in_=rs, func=AF.Exp, scale=-0.5))
                    sy.op("dve", [rk, rsk, "vecs", "qgs"], [(dkey, t)],
                          lambda e, raw=raw, rs=rs, dst=dst, gcol=gcol: e.scalar_tensor_tensor(
                              out=dst[:, ts], in0=raw, scalar=gcol, in1=rs, op0=ALU.mult, op1=ALU.mult))
                pv, pvk = self.ps()
                for q4 in range(4):
                    for k in range(NK):
                        sy.op("pe", [("w3", s), ("xn", k, t)], [pvk],
                              lambda e, k=k, q4=q4: e.matmul(
                                  pv[:, q4 * 128:(q4 + 1) * 128],
                                  self.xn[:, k, 2 + t * TT + q4 * 128:2 + t * TT + (q4 + 1) * 128],
                                  w3[:, k, 2, :], start=(k == 0), stop=(k == NK - 1)))
                pv3 = pv.rearrange("p (a b) -> p a b", a=4)
                sy.op("act", [pvk], ["vpA"], lambda e, pv3=pv3: e.activation(
                    out=vpA[:, 4 * t:4 * t + 4, 0:64], in_=pv3[:, :, 0:64], func=AF.Copy))
                sy.op("act", [pvk], ["vpB"], lambda e, pv3=pv3: e.activation(
                    out=vpB[:, 4 * t:4 * t + 4, 64:128], in_=pv3[:, :, 64:128], func=AF.Copy))
            pairs = []
            for T in range(NT):
                for hd in range(2):
                    cmax = 4 * T + 3
                    for c in range(cmax, -1, -1):
                        pairs.append(dict(T=T, hd=hd, c=c, cmax=cmax, first=(hd == 0 and c == cmax),
                                          last=(hd == 1 and c == 0)))
            NB = 3
            o_banks = {}
            for T in range(NT):
                o_banks[T] = 6 + oi % 2
                oi += 1
            rstate = {"cur": 0}

            def stage1(n, p):
                i2 = n % NB
                hp = slice(p["hd"] * 64, p["hd"] * 64 + 64)
                T, c = p["T"], p["c"]
                Ts = slice(T * TT, (T + 1) * TT)
                jd = c - 4 * T
                pz, pzk = self.ps()
                sy.op("pe", [("kn", c // 4), ("qn", T)], [pzk],
                      lambda e: e.matmul(pz, kn[hp, c * 128:(c + 1) * 128], qn[hp, Ts], start=True, stop=True))
                sy.op("act", [pzk], [("e", i2)], lambda e: e.activation(out=e_t[i2], in_=pz, func=AF.Exp))
                sy.op("act", [("e", i2), "consts"], [("sp", i2)],
                      lambda e: e.activation(out=sp_t[i2], in_=e_t[i2], func=AF.Ln, bias=self.one_col, scale=1.0))
                if jd >= 0:
                    sy.op("dve", [("sp", i2), "consts2"], [("lk", i2)],
                          lambda e: e.tensor_tensor(out=lk_t[i2], in0=sp_t[i2],
                                                    in1=self.mneg[:, 384 - 128 * jd:896 - 128 * jd], op=ALU.mult))
                else:
                    sy.op("dve", [("sp", i2)], [("lk", i2)],
                          lambda e: e.tensor_scalar(out=lk_t[i2], in0=sp_t[i2], scalar1=-1.0, scalar2=None, op0=ALU.mult))

            def stage2(n, p):
                i2 = n % NB
                T, c, cmax = p["T"], p["c"], p["cmax"]
                jd = c - 4 * T
                if c == cmax:
                    rstate["cur"] = 0
                rcur = rstate["cur"]
                hp = slice(p["hd"] * 64, p["hd"] * 64 + 64)
                Ts = slice(T * TT, (T + 1) * TT)
                pt, ptk = self.ps()
                sy.op("pe", [("lk", i2), "consts2"], [ptk],
                      lambda e: e.matmul(pt, self.tri, lk_t[i2], start=True, stop=False))
                if c < cmax:
                    sy.op("pe", [("R", rcur), "consts2"], [ptk],
                          lambda e: e.matmul(pt, self.onesr, R_t[rcur], start=False, stop=False))
                sy.op("pe", [("kn", c // 4), ("qn", T)], [ptk],
                      lambda e: e.matmul(pt, kn[hp, c * 128:(c + 1) * 128], qn[hp, Ts], start=False, stop=True))
                if c > 0:
                    rn = 1 - rcur
                    if c == cmax:
                        sy.op("pool", [("lk", i2)], [("R", rn)], lambda e: e.tensor_copy(out=R_t[rn], in_=lk_t[i2]))
                    else:
                        sy.op("pool", [("lk", i2), ("R", rcur)], [("R", rn)],
                              lambda e: e.tensor_tensor(out=R_t[rn], in0=R_t[rcur], in1=lk_t[i2], op=ALU.add))
                    rstate["cur"] = rn
                sy.op("act", [ptk], [("att", i2)],
                      lambda e: e.activation(out=att_t[i2], in_=pt, func=AF.Exp))
                if jd >= 0:
                    sy.op("dve", [("att", i2), "consts2"], [("att", i2)],
                          lambda e: e.tensor_tensor(out=att_t[i2], in0=att_t[i2],
                                                    in1=self.m01[:, 384 - 128 * jd:896 - 128 * jd], op=ALU.mult))

            def stage3(n, p):
                nonlocal yi
                i2 = n % NB
                T, c = p["T"], p["c"]
                ob = o_banks[T]
                o_ps, ok = self.psum[ob], ("ps", ob)
                vp, vpk = (vpA, "vpA") if p["hd"] == 0 else (vpB, "vpB")
                sy.op("pe", [vpk, ("att", i2)], [ok],
                      lambda e: e.matmul(o_ps, vp[:, c, :], att_t[i2], start=p["first"], stop=p["last"]))
                if p["last"]:
                    y = self.ybf[yi % 2]
                    yk = ("ybf", yi % 2)
                    yi += 1
                    sy.op("act", [ok], [yk], lambda e: e.activation(out=y, in_=o_ps, func=AF.Copy))
                    self.outproj_acc(wo, ("wo", s), y, yk, T)

            npairs = len(pairs)
            for n in range(npairs + 2):
                if n < npairs:
                    stage1(n, pairs[n])
                if 1 <= n and n - 1 < npairs:
                    stage2(n - 1, pairs[n - 1])
                if 2 <= n:
                    stage3(n - 2, pairs[n - 2])

    def rwkv(self, j):
        sy = self.sy
        self.phase_begin()
        RT = 256
        NRT = S // RT
        CD = 0.6065306597126334
        cv = self.carve
        P1, GA, GB = cv([S], BF16), cv([S], BF16), cv([S], BF16)
        WA2, GA2, GB2 = cv([D], BF16), cv([D], BF16), cv([D], BF16)
        omm, okka = cv([48], F32), cv([8], F32)
        ident, mk4, mkL, cmask = cv([128], F32), cv([512], F32), cv([128], F32), cv([RT], F32)
        gneps, tiny = cv([1], F32), cv([1], F32)
        mark = self.arena_off
        onesw = cv([512], F32)
        LWs, LW = cv([NK, 320], F32), cv([NK, 2, 320], BF16)
        mu0 = VEC_COLS[f"mu{j}_0"]
        mucols = self.vecs[:, mu0:mu0 + 48]
        sy.op("dve", [], ["rc"], lambda e: e.memset(onesw, 1.0))
        sy.op("dve", [], ["rc"], lambda e: e.memset(gneps, GN_EPS))
        sy.op("dve", [], ["rc"], lambda e: e.memset(tiny, 1e-24))
        sy.op("dve", [], ["cmask"], lambda e: e.memset(cmask, 1.0))
        sy.op("dve", [], ["cmask"], lambda e: e.memset(cmask[:, 0:1], 0.0))
        sy.op("dve", [], ["cmask"], lambda e: e.memset(cmask[:, 128:129], 0.0))
        sy.op("dve", ["vecs"], ["omm"], lambda e: e.tensor_scalar(
            out=omm, in0=mucols, scalar1=-1.0, scalar2=1.0, op0=ALU.mult, op1=ALU.add))
        ka0 = VEC_COLS[f"k_a{j}"]
        sy.op("dve", ["vecs"], ["omm"], lambda e: e.tensor_scalar(
            out=okka, in0=self.vecs[:, ka0:ka0 + 8], scalar1=-1.0, scalar2=1.0, op0=ALU.mult, op1=ALU.add))
        sy.op("pool", ["rc"], ["rc2"], lambda e: e.affine_select(
            out=ident, in_=onesw[:, 0:128], pattern=[[-1, 128]], compare_op=ALU.is_equal, fill=0.0,
            base=0, channel_multiplier=1))
        sy.op("pool", ["rc"], ["rc2"], lambda e: e.affine_select(
            out=mkL, in_=onesw[:, 0:128], pattern=[[-1, 128]], compare_op=ALU.is_gt, fill=0.0,
            base=0, channel_multiplier=1))
        sy.op("pool", ["rc"], ["rc2"], lambda e: e.affine_select(
            out=mk4, in_=onesw, pattern=[[0, 2], [1, 2], [1, 128]], compare_op=ALU.is_gt, fill=0.0,
            base=0, channel_multiplier=-1))
        wr = lambda n, jj=j: self.w[n][jj]
        sy.dma("sp", LWs[:, :, 0:64], wr("rwkv_decay_w1").rearrange("(k p) c -> p k c", p=128), [], ["LWs"], "lws")
        sy.dma("sp", LWs[:, :, 64:128], wr("rwkv_iclr_a1").rearrange("(k p) c -> p k c", p=128), [], ["LWs"], "lws")
        sy.dma("sp", LWs[:, :, 128:288], wr("rwkv_gate_g1").rearrange("(k p) c -> p k c", p=128), [], ["LWs"], "lws")
        if j == 1:
            sy.dma("sp", LWs[:, :, 288:320], self.w["rwkv_vres_v1"][0].rearrange("(k p) c -> p k c", p=128), [], ["LWs"], "lws")
        sy.dma("pool", WA2[0:64, :], wr("rwkv_decay_w2"), [], ["W2"], "w2s")
        sy.dma("pool", WA2[64:128, :], wr("rwkv_iclr_a2"), [], ["W2"], "w2s")
        sy.dma("pool", GA2, wr("rwkv_gate_g2")[0:128, :], [], ["W2"], "w2s")
        sy.dma("pool", GB2[0:32, :], wr("rwkv_gate_g2")[128:160, :], [], ["W2"], "w2s")
        if j == 1:
            sy.dma("pool", GB2[32:64, :], self.w["rwkv_vres_v2"][0], [], ["W2"], "w2s")
        blocks = [(0, 64, 1), (64, 128, 4), (128, 288, 5)] + ([(288, 320, 3)] if j == 1 else [])
        for k in range(NK):
            for (c0, c1, m) in blocks:
                sy.op("act", ["LWs", "omm"], ["LW"], lambda e, k=k, c0=c0, c1=c1, m=m: e.activation(
                    out=LW[:, k, 0, c0:c1], in_=LWs[:, k, c0:c1], func=AF.Copy, scale=omm[:, m * 8 + k:m * 8 + k + 1]))
                sy.op("dve", ["LWs", "vecs"], ["LW"], lambda e, k=k, c0=c0, c1=c1, m=m: e.tensor_scalar(
                    out=LW[:, k, 1, c0:c1], in0=LWs[:, k, c0:c1], scalar1=self.vecs[:, mu0 + m * 8 + k:mu0 + m * 8 + k + 1],
                    scalar2=None, op0=ALU.mult))
        NL3 = 64 if j == 1 else 32
        for t in range(NT):
            ts = slice(t * TT, (t + 1) * TT)
            for (c0, M, which) in ((0, 128, 0), (128, 128, 1), (256, NL3, 2)):
                pst, psk = self.ps()
                for k in range(NK):
                    for sh in range(2):
                        rd = [("xn", k, t), "LW"] + ([("xn", k, t - 1)] if (sh and t > 0) else [])
                        sy.op("pe", rd, [psk], lambda e, k=k, sh=sh, c0=c0, M=M, pst=pst: e.matmul(
                            pst[0:M, :], LW[:, k, sh, c0:c0 + M], self.xn[:, k, 2 - sh + t * TT:2 - sh + (t + 1) * TT],
                            start=(k == 0 and sh == 0), stop=(k == NK - 1 and sh == 1)))
                if which == 0:
                    sy.op("act", [psk], [("P1", t)], lambda e, pst=pst: e.activation(out=P1[0:64, ts], in_=pst[0:64, :], func=AF.Tanh))
                    sy.op("act", [psk], [("P1", t)], lambda e, pst=pst: e.activation(out=P1[64:128, ts], in_=pst[64:128, :], func=AF.Copy))
                elif which == 1:
                    sy.op("act", [psk], [("GA", t)], lambda e, pst=pst: e.activation(out=GA[:, ts], in_=pst, func=AF.Sigmoid))
                else:
                    sy.op("act", [psk], [("GB", t)], lambda e, pst=pst: e.activation(out=GB[0:32, ts], in_=pst[0:32, :], func=AF.Sigmoid))
                    if j == 1:
                        sy.op("act", [psk], [("GB", t)], lambda e, pst=pst: e.activation(out=GB[32:64, ts], in_=pst[32:64, :], func=AF.Copy))
        sy.barrier()
        self.arena_off = mark
        Wst = cv([NK, 3, 128], F32)
        Wfs = [cv([NK, 3, 2, 128], BF16) for i in range(2)]
        wo = [cv([D], BF16) for i in range(2)]
        f1 = lambda: cv([RT], F32)
        r_sb, k_sb, sg, csp, pinv, asig, kk, tA, tB, tC, YC, vf = [f1() for _ in range(12)]
        Yb = [f1() for _ in range(2)]
        BN3 = [f1() for _ in range(3)]
        Bset = [(f1(), f1(), cv([2, 2, 128], BF16), cv([RT], BF16), cv([RT], BF16), None, cv([RT], BF16)) for _ in range(2)]
        yout = [cv([RT], BF16) for i in range(2)]
        BH, KH = cv([128], BF16), cv([128], BF16)
        TMp = cv([4, 256], BF16)
        AM32 = [cv([128], F32) for i in range(2)]
        AMb = [cv([3, 128], BF16) for i in range(2)]
        Np = [[cv([128], F32) for i in range(2)] for h in range(2)]
        NpT = [[cv([128], F32) for i in range(2)] for h in range(2)]
        Wt = [[cv([2, 64], F32) for i in range(2)] for h in range(2)]
        Wfin = cv([2, 256], BF16)
        TMb = cv([4, 128], BF16)
        WB = cv([2, 128], BF16)
        M1, NCt = cv([128], F32), cv([128], F32)
        MBD, G = cv([128], BF16), cv([128], BF16)
        ST = [cv([128], BF16) for i in range(2)]
        identb = cv([128], BF16)
        sy.op("dve", ["rc2"], ["rc2"], lambda e: e.tensor_copy(out=identb, in_=ident))
        sy.op("dve", [], ["TMp"], lambda e: e.memset(TMp, 0.0))
        sy.op("dve", [], ["Wfin"], lambda e: e.memset(Wfin, 0.0))
        both = lambda ap2: ap2.rearrange("p (a b) -> p a b", a=4)[:, 0:4:3, :]
        wnames = ("rwkv_w_r", "rwkv_w_k", "rwkv_w_v")
        muidx = (0, 2, 3)

        def load_stage(dc):
            for jj in range(3):
                src = self.w[wnames[jj]][j].rearrange("(k p) c -> p k c", p=128)[:, :, dc * 128:(dc + 1) * 128]
                sy.dma("sp", Wst[:, :, jj, :], src, [], ["Wst"], "wst")
            sy.dma("pool", wo[dc % 2], self.w["rwkv_w_o"][j, dc * 128:(dc + 1) * 128, :], [], [("wo", dc % 2)], f"rwo{dc % 2}")

        def fold_gen(dcn):
            Wfn = Wfs[dcn % 2]
            wfk = ("Wf", dcn % 2)
            for k in range(NK):
                for jj in range(3):
                    m = muidx[jj]
                    sy.op("act", ["Wst", "omm"], [wfk], lambda e, k=k, jj=jj, m=m: e.activation(
                        out=Wfn[:, k, jj, 0, :], in_=Wst[:, k, jj, :], func=AF.Copy, scale=omm[:, m * 8 + k:m * 8 + k + 1]))
                    sy.op("dve", ["Wst", "vecs"], [wfk], lambda e, k=k, jj=jj, m=m: e.tensor_scalar(
                        out=Wfn[:, k, jj, 1, :], in0=Wst[:, k, jj, :],
                        scalar1=self.vecs[:, mu0 + m * 8 + k:mu0 + m * 8 + k + 1], scalar2=None, op0=ALU.mult))
                yield

        load_stage(0)
        for _ in fold_gen(0):
            pass
        stt = {"sti": 0, "yi": 0}
        for dc in range(NK):
            dcs = slice(dc * 128, (dc + 1) * 128)
            Wf = Wfs[dc % 2]
            wfk_cur = ("Wf", dc % 2)
            wod, wok = wo[dc % 2], ("wo", dc % 2)
            stt["sti"] = 0
            sy.op("dve", [], [("ST", 0)], lambda e: e.memset(ST[0], 0.0))
            def make_ctx(rt, b):
                tok0 = rt * RT
                tk = slice(tok0, tok0 + RT)
                t5 = tok0 // TT
                xr = lambda k: [("xn", k, t5)] + ([("xn", k, t5 - 1)] if (tok0 % TT == 0 and t5 > 0) else [])

                def proj(jj):
                    pst, psk = self.ps()
                    for k in range(NK):
                        for sh in range(2):
                            sy.op("pe", xr(k) + [wfk_cur], [psk], lambda e, k=k, sh=sh, pst=pst: e.matmul(
                                pst[:, 0:RT], Wf[:, k, jj, sh, :], self.xn[:, k, 2 - sh + tok0:2 - sh + tok0 + RT],
                                start=(k == 0 and sh == 0), stop=(k == NK - 1 and sh == 1)))
                    return pst[:, 0:RT], psk

                def small(lhsT, rhs, reads, M=128, pst=None, psk=None, start=True, stop=True, n=RT, c0=0):
                    if pst is None:
                        pst, psk = self.ps()
                    sy.op("pe", reads, [psk], lambda e: e.matmul(pst[0:M, c0:c0 + n], lhsT, rhs, start=start, stop=stop))
                    return pst, psk

                V = lambda nm, dc=dc: self.vcol(nm, dc)
                v_sb, cs, AR, BT, KT, BN, vbf = Bset[b]
                BN = BN3[rt % 3]
                Y = Yb[rt % 2]
                yk_ = ("Y", rt % 2)
                bnk = ("BN", rt % 3)
                kq = lambda nm: (nm, b)
                return dict(locals())

            def prologue(rt, b):
                c_ = make_ctx(rt, b)
                tok0, tk, t5, xr, proj, small, V = (c_[n] for n in ('tok0', 'tk', 't5', 'xr', 'proj', 'small', 'V'))
                v_sb, cs, AR, BT, KT, BN, vbf, kq, Y, yk_, bnk = (c_[n] for n in ('v_sb', 'cs', 'AR', 'BT', 'KT', 'BN', 'vbf', 'kq', 'Y', 'yk_', 'bnk'))
                pr, prk = proj(0)
                sy.op("act", [prk], ["r_sb"], lambda e: e.activation(out=r_sb, in_=pr, func=AF.Copy))
                pk, pkk = proj(1)
                sy.op("act", [pkk], ["k_sb"], lambda e: e.activation(out=k_sb, in_=pk, func=AF.Copy))
                pv, pvk = proj(2)
                sy.op("act", [pvk], [kq("v_sb")], lambda e: e.activation(out=v_sb, in_=pv, func=AF.Copy))
                yield
                plw, plwk = small(WA2[0:64, dcs], P1[0:64, tk], ["W2", ("P1", t5)])
                sy.op("act", [plwk, "vecs"], ["sg"], lambda e: e.activation(
                    out=sg, in_=plw[:, 0:RT], func=AF.Sigmoid, bias=V(f"w0{j}"), scale=1.0))
                pa, pak = small(WA2[64:128, dcs], P1[64:128, tk], ["W2", ("P1", t5)])
                sy.op("act", [pak, "vecs"], ["asig"], lambda e: e.activation(
                    out=asig, in_=pa[:, 0:RT], func=AF.Sigmoid, bias=V(f"a0{j}"), scale=1.0))
                if j == 1:
                    pg_, pgk_ = small(GB2[32:64, dcs], GB[32:64, tk], ["W2", ("GB", t5)])
                    sy.op("act", [pgk_, "vecs"], ["tB"], lambda e: e.activation(
                        out=tB, in_=pg_[:, 0:RT], func=AF.Sigmoid, bias=V("v0"), scale=1.0))
                    sy.dma("sp", vf, self.vfirst[dcs, tk], [("vfd", dc, rt)], ["vf"], "vfl")
                    sy.op("dve", ["vf", kq("v_sb")], ["vf"], lambda e: e.tensor_tensor(out=vf, in0=vf, in1=v_sb, op=ALU.subtract))
                    sy.op("dve", ["vf", "tB"], ["vf"], lambda e: e.tensor_tensor(out=vf, in0=vf, in1=tB, op=ALU.mult))
                    sy.op("dve", ["vf", kq("v_sb")], [kq("v_sb")], lambda e: e.tensor_tensor(out=v_sb, in0=v_sb, in1=vf, op=ALU.add))
                else:
                    sy.dma("sp", self.vfirst[dcs, tk], v_sb, [kq("v_sb")], [("vfd", dc, rt)], f"vfs{b}")
                sy.op("act", [kq("v_sb")], [kq("vbf")], lambda e: e.activation(out=vbf, in_=v_sb, func=AF.Copy))
                sy.op("dve", ["sg", "cmask"], [kq("cs")], lambda e: e.tensor_tensor_scan(
                    out=cs, data0=cmask, data1=sg, initial=0.0, op0=ALU.mult, op1=ALU.add))
                sy.op("dve", [kq("cs"), "sg"], ["csp"], lambda e: e.tensor_tensor(out=csp, in0=cs, in1=sg, op=ALU.subtract))
                sy.op("act", [kq("cs")], ["pinv"], lambda e: e.activation(out=pinv, in_=cs, func=AF.Exp, scale=CD))
                sy.op("act", [kq("cs")], [kq("cs")], lambda e: e.activation(out=cs, in_=cs, func=AF.Exp, scale=-CD))
                yield
                sy.op("act", ["csp"], ["csp"], lambda e: e.activation(out=csp, in_=csp, func=AF.Exp, scale=-CD))
                sy.op("dve", ["k_sb", "vecs"], ["kk"], lambda e: e.tensor_scalar(
                    out=kk, in0=k_sb, scalar1=V(f"k_k{j}"), scalar2=None, op0=ALU.mult))
                sy.op("act", ["kk"], ["tA"], lambda e: e.activation(out=tA, in_=kk, func=AF.Square))
                pss, pssk = small(self.bones, tA, ["tA", "consts"])
                sy.op("act", [pssk, "rc"], ["tA"], lambda e: e.activation(out=tA, in_=pss[:, 0:RT], func=AF.Ln, bias=tiny, scale=1.0))
                yield
                sy.op("act", ["tA"], ["tA"], lambda e: e.activation(out=tA, in_=tA, func=AF.Exp, scale=-0.5))
                sy.op("dve", ["kk", "tA"], ["kk"], lambda e: e.tensor_tensor(out=kk, in0=kk, in1=tA, op=ALU.mult))
                sy.op("dve", ["asig", "vecs", "omm"], ["tB"], lambda e: e.tensor_scalar(
                    out=tB, in0=asig, scalar1=V(f"k_a{j}"), scalar2=okka[:, dc:dc + 1], op0=ALU.mult, op1=ALU.add))
                sy.op("dve", ["k_sb", "tB"], ["k_sb"], lambda e: e.tensor_tensor(out=k_sb, in0=k_sb, in1=tB, op=ALU.mult))
                yield
                c3 = lambda ap: ap.rearrange("p (a b) -> p a b", a=2)
                sy.op("dve", ["kk", "csp"], [kq("AR0")], lambda e: e.scalar_tensor_tensor(
                    out=AR[:, :, 0, :], in0=c3(kk), scalar=-1.0, in1=c3(csp), op0=ALU.mult, op1=ALU.mult))
                sy.op("dve", ["r_sb", kq("cs")], [kq("AR1")], lambda e: e.tensor_tensor(
                    out=AR[:, :, 1, :], in0=c3(r_sb), in1=c3(cs), op=ALU.mult))
                sy.op("dve", ["kk", "asig"], ["tB"], lambda e: e.tensor_tensor(out=tB, in0=kk, in1=asig, op=ALU.mult))
                sy.op("dve", ["tB", "pinv"], [kq("BT")], lambda e: e.tensor_tensor(out=BT, in0=tB, in1=pinv, op=ALU.mult))
                sy.op("dve", ["k_sb", "pinv"], [kq("KT")], lambda e: e.tensor_tensor(out=KT, in0=k_sb, in1=pinv, op=ALU.mult))
                yield
                sy.op("dve", ["r_sb", "k_sb", "vecs"], ["tA"], lambda e: e.scalar_tensor_tensor(
                    out=tA, in0=r_sb, scalar=V(f"r_k{j}"), in1=k_sb, op0=ALU.mult, op1=ALU.mult))
                pbn, pbnk = small(self.bones, tA, ["tA", "consts"])
                sy.op("dve", [pbnk, kq("v_sb")], [bnk], lambda e: e.tensor_tensor(out=BN, in0=pbn[:, 0:RT], in1=v_sb, op=ALU.mult))
                yield

            def scanepi(rt, b):
                c_ = make_ctx(rt, b)
                tok0, tk, t5, xr, proj, small, V = (c_[n] for n in ('tok0', 'tk', 't5', 'xr', 'proj', 'small', 'V'))
                v_sb, cs, AR, BT, KT, BN, vbf, kq, Y, yk_, bnk = (c_[n] for n in ('v_sb', 'cs', 'AR', 'BT', 'KT', 'BN', 'vbf', 'kq', 'Y', 'yk_', 'bnk'))
                for ci in range(2):
                    cc = slice(ci * 128, (ci + 1) * 128)
                    pcol = cs[:, ci * 128 + 127:ci * 128 + 128]
                    sy.op("dve", [kq("BT"), kq("cs")], ["BH"], lambda e: e.tensor_scalar(out=BH, in0=BT[:, cc], scalar1=pcol, scalar2=None, op0=ALU.mult))
                    sy.op("dve", [kq("KT"), kq("cs")], ["KH"], lambda e: e.tensor_scalar(out=KH, in0=KT[:, cc], scalar1=pcol, scalar2=None, op0=ALU.mult))
                    ptm, ptmk = self.ps()
                    ptmb = ptm.bitcast(BF16)
                    for q, (src, skey) in enumerate(((AR[:, ci, 0, :], kq("AR0")), (BH, "BH"), (KH, "KH"), (vbf[:, cc], kq("vbf")))):
                        sy.op("pe", [skey, "rc2"], [ptmk], lambda e, q=q, src=src: e.transpose(
                            out=ptmb[:, q * 128:(q + 1) * 128], in_=src, identity=identb))
                    ptm3 = ptmb[:, 0:512].rearrange("p (a b) -> p a b", a=4)
                    sy.op("act", [ptmk], ["TMp"], lambda e: e.activation(out=TMp[:, :, 0:64], in_=ptm3[:, :, 0:64], func=AF.Copy))
                    sy.op("act", [ptmk], ["TMp"], lambda e: e.activation(out=TMp[:, :, 192:256], in_=ptm3[:, :, 64:128], func=AF.Copy))
                    sy.op("act", [ptmk], ["TMb"], lambda e: e.activation(out=TMb, in_=ptm3, func=AF.Copy))
                    hcs = (slice(0, 64), slice(192, 256))
                    for hd in range(2):
                        yield
                        hp = slice(hd * 64, hd * 64 + 64)
                        arh = AR[hp, ci, :, :]
                        pam, pamk = self.ps()
                        sy.op("pe", [kq("BT"), kq("AR0"), kq("AR1")], [pamk], lambda e, pam=pam, arh=arh, hp=hp: e.matmul(
                            pam[:, 0:256], BT[hp, cc], arh, start=True, stop=True))
                        sy.op("pe", [kq("KT"), kq("AR0"), kq("AR1")], [pamk], lambda e, pam=pam, arh=arh, hp=hp: e.matmul(
                            pam[:, 256:512], KT[hp, cc], arh, start=True, stop=True))
                        sy.op("dve", [pamk, "rc2"], [("AM", hd)], lambda e, pam=pam, hd=hd: e.tensor_tensor(
                            out=AM32[hd], in0=pam[:, 0:128], in1=mk4[:, 0:128], op=ALU.mult))
                        sy.op("dve", [pamk, "rc2"], [("AM", hd)], lambda e, pam=pam, hd=hd: e.tensor_tensor(
                            out=AMb[hd], in0=pam[:, 128:512].rearrange("p (a b) -> p a b", a=3),
                            in1=mk4[:, 128:512].rearrange("p (a b) -> p a b", a=3), op=ALU.mult))
                        pnt, pntk = self.ps()
                        sy.op("pe", [kq("BT"), kq("AR0")], [pntk], lambda e, pnt=pnt, hp=hp: e.matmul(
                            pnt[:, 0:128], AR[hp, ci, 0, :], BT[hp, cc], start=True, stop=True))
                        sy.op("dve", [pntk, "rc2"], [("NpT", hd, 0)], lambda e, pnt=pnt, hd=hd: e.tensor_tensor(
                            out=NpT[hd][0], in0=pnt[:, 0:128], in1=mkL, op=ALU.mult))
                        sy.op("pool", ["TMp"], [("Wt", hd, 0)], lambda e, hd=hd: e.tensor_copy(out=Wt[hd][0][:, 0, :], in_=TMp[:, 0, hcs[hd]]))
                        pxv, pxvk = self.ps()
                        sy.op("pe", [("AM", hd), "TMp"], [pxvk], lambda e, pxv=pxv, hd=hd: e.matmul(
                            pxv[:, 0:64], AMb[hd][:, 1, :], TMp[:, 3, hcs[hd]], start=True, stop=True))
                        sy.op("act", [pxvk], [("Wt", hd, 0)], lambda e, pxv=pxv, hd=hd: e.activation(
                            out=Wt[hd][0][:, 1, :], in_=pxv[:, 0:64], func=AF.Copy))
                    yield
                    for lvl in range(7):
                        yield
                        cur, nxt = lvl % 2, (lvl + 1) % 2
                        for hd in range(2):
                            npc = AM32[hd] if lvl == 0 else Np[hd][cur]
                            npk = ("AM", hd) if lvl == 0 else ("Np", hd, cur)
                            wcur = Wt[hd][cur]
                            pw, pwk = self.ps()
                            sy.op("pe", [npk, ("Wt", hd, cur)], [pwk], lambda e, pw=pw, npc=npc, wcur=wcur: e.matmul(
                                pw[:, 0:128], npc, wcur.rearrange("p a b -> p (a b)"), start=True, stop=True))
                            pw3 = pw[:, 0:128].rearrange("p (a b) -> p a b", a=2)
                            if lvl < 6:
                                sy.op("dve", [pwk, ("Wt", hd, cur)], [("Wt", hd, nxt)], lambda e, pw3=pw3, wcur=wcur, hd=hd, nxt=nxt: e.tensor_tensor(
                                    out=Wt[hd][nxt], in0=pw3, in1=wcur, op=ALU.add))
                                pn, pnk = self.ps()
                                sy.op("pe", [npk, ("NpT", hd, cur)], [pnk], lambda e, pn=pn, npc=npc, hd=hd, cur=cur: e.matmul(
                                    pn[:, 0:128], NpT[hd][cur], npc, start=True, stop=True))
                                sy.op("act", [pnk], [("Np", hd, nxt)], lambda e, pn=pn, hd=hd, nxt=nxt: e.activation(
                                    out=Np[hd][nxt], in_=pn[:, 0:128], func=AF.Copy))
                                pn2, pn2k = self.ps()
                                sy.op("pe", [npk, ("NpT", hd, cur)], [pn2k], lambda e, pn2=pn2, npc=npc, hd=hd, cur=cur: e.matmul(
                                    pn2[:, 0:128], npc, NpT[hd][cur], start=True, stop=True))
                                sy.op("act", [pn2k], [("NpT", hd, nxt)], lambda e, pn2=pn2, hd=hd, nxt=nxt: e.activation(
                                    out=NpT[hd][nxt], in_=pn2[:, 0:128], func=AF.Copy))
                            else:
                                sy.op("dve", [pwk, ("Wt", hd, cur)], ["Wfin"], lambda e, pw3=pw3, wcur=wcur, hd=hd: e.tensor_tensor(
                                    out=Wfin[:, :, hcs[hd]], in0=pw3, in1=wcur, op=ALU.add))
                                sy.op("dve", [pwk, ("Wt", hd, cur)], ["WB"], lambda e, pw3=pw3, wcur=wcur, hd=hd: e.tensor_tensor(
                                    out=WB[:, :, hd * 64:(hd + 1) * 64], in0=pw3, in1=wcur, op=ALU.add))
                    yield
                    Ah_b, Uh_b = WB[:, 0, :], WB[:, 1, :]
                    Bh_b, Kh_b, VT_b = TMb[:, 1, :], TMb[:, 2, :], TMb[:, 3, :]
                    stc, stn = ST[stt['sti'] % 2], ST[(stt['sti'] + 1) % 2]
                    stck, stnk = ("ST", stt['sti'] % 2), ("ST", (stt['sti'] + 1) % 2)
                    stt['sti'] += 1
                    pm, pmk = self.ps()
                    sy.op("pe", ["WB", "TMb"], [pmk], lambda e, pm=pm: e.matmul(pm[:, 0:128], Ah_b, Bh_b, start=True, stop=True))
                    sy.op("dve", [pmk, "consts"], ["M1"], lambda e, pm=pm: e.tensor_tensor(out=M1, in0=pm[:, 0:128], in1=self.bones, op=ALU.mult))
                    sy.op("dve", ["M1", "rc2", kq("cs")], ["MBD"], lambda e: e.scalar_tensor_tensor(
                        out=MBD, in0=ident, scalar=pcol, in1=M1, op0=ALU.mult, op1=ALU.add))
                    pn_, pnk_ = self.ps()
                    sy.op("pe", ["WB", "TMb"], [pnk_], lambda e, pn_=pn_: e.matmul(pn_[:, 0:128], Bh_b, Uh_b, start=True, stop=False))
                    sy.op("pe", ["TMb"], [pnk_], lambda e, pn_=pn_: e.matmul(pn_[:, 0:128], Kh_b, VT_b, start=False, stop=True))
                    sy.op("dve", [pnk_, "consts"], ["NCt"], lambda e, pn_=pn_: e.tensor_tensor(out=NCt, in0=pn_[:, 0:128], in1=self.bones, op=ALU.mult))
                    yield
                    pg, pgk = self.ps()
                    sy.op("pe", ["Wfin", ("AM", 0)], [pgk], lambda e, pg=pg: e.matmul(pg[:, 0:128], Wfin[:, 0, 0:128], AMb[0][:, 0, :], start=True, stop=False))
                    sy.op("pe", ["Wfin", ("AM", 1)], [pgk], lambda e, pg=pg: e.matmul(pg[:, 0:128], Wfin[:, 0, 128:256], AMb[1][:, 0, :], start=False, stop=True))
                    sy.op("dve", [pgk, kq("AR1")], ["G"], lambda e, pg=pg: e.tensor_tensor(out=G, in0=pg[:, 0:128], in1=AR[:, ci, 1, :], op=ALU.add))
                    yield
                    py, pyk = self.ps()
                    sy.op("pe", [stck, "G"], [pyk], lambda e, py=py, stc=stc: e.matmul(py[:, 0:128], stc, G, start=True, stop=False))
                    sy.op("pe", ["Wfin", ("AM", 0)], [pyk], lambda e, py=py: e.matmul(py[:, 0:128], Wfin[:, 1, 0:128], AMb[0][:, 0, :], start=False, stop=False))
                    sy.op("pe", ["Wfin", ("AM", 1)], [pyk], lambda e, py=py: e.matmul(py[:, 0:128], Wfin[:, 1, 128:256], AMb[1][:, 0, :], start=False, stop=False))
                    sy.op("pe", ["TMp", ("AM", 0)], [pyk], lambda e, py=py: e.matmul(py[:, 0:128], TMp[:, 3, 0:128], AMb[0][:, 2, :], start=False, stop=False))
                    sy.op("pe", ["TMp", ("AM", 1)], [pyk], lambda e, py=py: e.matmul(py[:, 0:128], TMp[:, 3, 128:256], AMb[1][:, 2, :], start=False, stop=True))
                    sy.op("act", [pyk], [yk_], lambda e, py=py: e.activation(out=Y[:, cc], in_=py[:, 0:128], func=AF.Copy))
                    yield
                    pst_, pstk_ = self.ps()
                    sy.op("pe", ["MBD", stck], [pstk_], lambda e, pst_=pst_, stc=stc: e.matmul(pst_[:, 0:128], MBD, stc, start=True, stop=True))
                    sy.op("dve", [pstk_, "NCt"], [stnk], lambda e, pst_=pst_, stn=stn: e.tensor_tensor(out=stn, in0=pst_[:, 0:128], in1=NCt, op=ALU.add))
                yield

            def epilogue(rt, b):
                c_ = make_ctx(rt, b)
                tok0, tk, t5, xr, proj, small, V = (c_[n] for n in ('tok0', 'tk', 't5', 'xr', 'proj', 'small', 'V'))
                v_sb, cs, AR, BT, KT, BN, vbf, kq, Y, yk_, bnk = (c_[n] for n in ('v_sb', 'cs', 'AR', 'BT', 'KT', 'BN', 'vbf', 'kq', 'Y', 'yk_', 'bnk'))
                pmn, pmnk = small(self.bones, Y, [yk_, "consts"])
                sy.op("dve", [pmnk, yk_], ["YC"], lambda e: e.scalar_tensor_tensor(
                    out=YC, in0=pmn[:, 0:RT], scalar=-1.0 / 64, in1=Y, op0=ALU.mult, op1=ALU.add))
                sy.op("act", ["YC"], ["tC"], lambda e: e.activation(out=tC, in_=YC, func=AF.Square))
                pvr, pvrk = small(self.bones, tC, ["tC", "consts"])
                sy.op("act", [pvrk, "rc"], ["tC"], lambda e: e.activation(out=tC, in_=pvr[:, 0:RT], func=AF.Ln, bias=gneps, scale=1.0 / 64))
                sy.op("act", ["tC"], ["tC"], lambda e: e.activation(out=tC, in_=tC, func=AF.Exp, scale=-0.5))
                sy.op("dve", ["YC", "tC"], ["YC"], lambda e: e.tensor_tensor(out=YC, in0=YC, in1=tC, op=ALU.mult))
                sy.op("dve", ["YC", "vecs"], ["YC"], lambda e: e.tensor_scalar(
                    out=YC, in0=YC, scalar1=V(f"lnx_w{j}"), scalar2=V(f"lnx_b{j}"), op0=ALU.mult, op1=ALU.add))
                sy.op("dve", ["YC", bnk], ["YC"], lambda e: e.tensor_tensor(out=YC, in0=YC, in1=BN, op=ALU.add))
                yield
                pgt, pgtk = small(GA2[:, dcs], GA[:, tk], ["W2", ("GA", t5)], stop=False)
                small(GB2[0:32, dcs], GB[0:32, tk], ["W2", ("GB", t5)], pst=pgt, psk=pgtk, start=False, stop=True)
                yo = yout[stt['yi'] % 2]
                yok = ("yout", stt['yi'] % 2)
                stt['yi'] += 1
                sy.op("dve", ["YC", pgtk], [yok], lambda e, yo=yo: e.tensor_tensor(out=yo, in0=YC, in1=pgt[:, 0:RT], op=ALU.mult))
                for m in range(NK):
                    po, pok = self.ps()
                    sy.op("pe", [wok, yok], [pok], lambda e, m=m, po=po, yo=yo: e.matmul(
                        po[:, 0:RT], wod[:, m * 128:(m + 1) * 128], yo, start=True, stop=True))
                    sy.op("dve", [pok, ("h", m, t5)], [("h", m, t5)], lambda e, m=m, po=po: e.tensor_tensor(
                        out=self.h[:, m, tk], in0=po[:, 0:RT], in1=self.h[:, m, tk], op=ALU.add))
                yield

            for _ in prologue(0, 0):
                pass
            for rt in range(NRT + 1):
                gens = []
                if rt < NRT:
                    gens.append(scanepi(rt, rt % 2))
                if rt >= 1:
                    gens.append(epilogue(rt - 1, (rt - 1) % 2))
                if rt + 1 < NRT:
                    gens.append(prologue(rt + 1, (rt + 1) % 2))
                if dc + 1 < NK:
                    if rt == 1:
                        load_stage(dc + 1)
                    if rt == 4:
                        gens.append(fold_gen(dc + 1))
                while gens:
                    for g_ in list(gens):
                        try:
                            next(g_)
                        except StopIteration:
                            gens.remove(g_)


ALL_LAYERS = []
for _l in range(DEPTH):
    ALL_LAYERS += [("mix", _l), ("mlp", _l)]

WEIGHT_NAMES = ["mlp_up", "mlp_down", "rwkv_w_r", "rwkv_w_k", "rwkv_w_v", "rwkv_w_o",
                "rwkv_decay_w1", "rwkv_decay_w2", "rwkv_iclr_a1", "rwkv_iclr_a2",
                "rwkv_gate_g1", "rwkv_gate_g2", "rwkv_vres_v1", "rwkv_vres_v2",
                "conv_w_in", "conv_w_out", "sb_w_qkv", "sb_w_o"]


def run(inputs, layers, n_cores=8, trace=False):
    inp = {k: np.asarray(v) for k, v in inputs.items()}
    prog = Prog(layers)
    nc = prog.build()
    vecs = pack_vecs(inp)
    wts = {n: np.ascontiguousarray(inp[n], dtype=np.float32) for n in WEIGHT_NAMES}
    in_maps = []
    for b in range(n_cores):
        m = {"xT": np.ascontiguousarray(inp["x"][b].T), "vecs": vecs}
        m.update(wts)
        in_maps.append(m)
    res = run_bass_kernel_spmd(nc, in_maps, core_ids=list(range(n_cores)), trace=trace)
    out = np.stack([np.ascontiguousarray(r["outT"].T) for r in res.results], axis=0)
    return out, res, prog


def kernel(**inputs):
    out, _, _ = run(inputs, ALL_LAYERS)
    return out.astype(np.float32)
```

```python
import numpy as np
import concourse.bass as bass
import concourse.mybir as mybir
from concourse.bass_utils import run_bass_kernel_spmd

F32 = mybir.dt.float32
F32R = mybir.dt.float32r
BF16 = mybir.dt.bfloat16
AF = mybir.ActivationFunctionType
ALU = mybir.AluOpType

D = 1024
S = 2048
NK = 8
TT = 512
NT = S // TT
DFF = 4096
DEPTH = 4
RMS_EPS = 1e-6
GN_EPS = 64e-5


class Sy:
    def __init__(self, nc):
        self.nc = nc
        self.eng = {"pe": nc.tensor, "dve": nc.vector, "act": nc.scalar,
                    "pool": nc.gpsimd, "sp": nc.sync}
        self.sem = {e: nc.alloc_semaphore("s_" + e) for e in self.eng}
        self.cnt = {e: 0 for e in self.eng}
        self.pend = {e: False for e in self.eng}
        self.waited = {e: {} for e in self.eng}
        self.last_w = {}
        self.readers = {}
        self.dsem = {}
        self.dcnt = {}
        self.n_wait = 0
        self.n_inst = 0

    def _wait(self, e, deps):
        need = {}
        for (sk, v) in deps:
            if need.get(sk, 0) < v:
                need[sk] = v
        for sk, v in need.items():
            if sk == e and e == "pe":
                continue
            if self.waited[e].get(sk, 0) >= v:
                continue
            sem = self.sem[sk] if sk in self.sem else self.dsem[sk]
            self.eng[e].wait_ge(sem, v)
            self.waited[e][sk] = v
            self.n_wait += 1

    def _deps(self, reads, writes):
        deps = []
        for k in reads:
            if k in self.last_w:
                deps.append(self.last_w[k])
        for k in writes:
            if k in self.last_w:
                deps.append(self.last_w[k])
            deps.extend(self.readers.get(k, {}).items())
        return deps

    def _record(self, tok, reads, writes):
        for k in reads:
            r = self.readers.setdefault(k, {})
            if r.get(tok[0], 0) < tok[1]:
                r[tok[0]] = tok[1]
        for k in writes:
            self.last_w[k] = tok
            self.readers[k] = {}

    def op(self, e, reads, writes, emit, inc=True):
        psr = [k for k in reads if isinstance(k, tuple) and k[0] == "ps" and k not in writes]
        if psr:
            writes = list(writes) + psr
        self._wait(e, self._deps(reads, writes))
        inst = emit(self.eng[e])
        self.n_inst += 1
        inc = True
        if inc:
            self.cnt[e] += 1
            inst.then_inc(self.sem[e], 1)
            self.pend[e] = False
            tok = (e, self.cnt[e])
        else:
            self.pend[e] = True
            tok = (e, self.cnt[e] + 1)
        self._record(tok, reads, writes)
        return inst

    def dma(self, e, out, in_, reads, writes, sk, **kw):
        if sk not in self.dsem:
            self.dsem[sk] = self.nc.alloc_semaphore("d_" + sk)
            self.dcnt[sk] = 0
        self._wait(e, self._deps(reads, writes))
        inst = self.eng[e].dma_start(out=out, in_=in_, **kw)
        self.dcnt[sk] += 16
        inst.then_inc(self.dsem[sk], 16)
        self.n_inst += 1
        self._record((sk, self.dcnt[sk]), reads, writes)
        return inst

    def barrier(self):
        for e in self.eng:
            deps = [(e2, self.cnt[e2]) for e2 in self.eng if e2 != e and self.cnt[e2] > 0]
            deps += [(sk, self.dcnt[sk]) for sk in self.dsem if self.dcnt[sk] > 0]
            self._wait(e, deps)

    def wait_all(self, e, keys):
        deps = []
        for k in keys:
            if k in self.last_w:
                deps.append(self.last_w[k])
            deps.extend(self.readers.get(k, {}).items())
        self._wait(e, deps)


VEC_COLS = {}


def _vec_layout():
    cols = {}
    off = 0

    def add(name, n=NK):
        nonlocal off
        cols[name] = off
        off += n
    for l in range(DEPTH):
        add(f"mix_norm{l}")
        add(f"mlp_norm{l}")
    for j in range(2):
        for m in range(6):
            add(f"mu{j}_{m}")
        for nm in ("w0", "a0", "k_k", "k_a", "r_k", "lnx_w", "lnx_b"):
            add(f"{nm}{j}")
    add("v0")
    for c in range(3):
        add(f"conv_w{c}")
    add("q_gain", 1)
    add("k_gain", 1)
    return cols, off


VEC_COLS, NVEC = _vec_layout()


def pack_vecs(inp):
    tab = np.zeros((128, NVEC), np.float32)

    def put(name, v):
        v = np.asarray(v, np.float32).reshape(-1)
        c = VEC_COLS[name]
        if v.size == D:
            tab[:, c:c + NK] = v.reshape(NK, 128).T
        else:
            tab[:, c] = np.concatenate([v, v])
    for l in range(DEPTH):
        put(f"mix_norm{l}", inp["mix_norm"][l])
        put(f"mlp_norm{l}", inp["mlp_norm"][l])
    for j in range(2):
        for m in range(6):
            put(f"mu{j}_{m}", inp["rwkv_mu"][j, m])
        put(f"w0{j}", inp["rwkv_decay_w0"][j])
        put(f"a0{j}", inp["rwkv_iclr_a0"][j])
        put(f"k_k{j}", inp["rwkv_k_k"][j])
        put(f"k_a{j}", inp["rwkv_k_a"][j])
        put(f"r_k{j}", inp["rwkv_r_k"][j])
        put(f"lnx_w{j}", inp["rwkv_lnx_w"][j])
        put(f"lnx_b{j}", inp["rwkv_lnx_b"][j])
    put("v0", inp["rwkv_vres_v0"][0])
    for c in range(3):
        put(f"conv_w{c}", inp["conv_w"][0, c])
    put("q_gain", inp["sb_q_norm"][0])
    put("k_gain", inp["sb_k_norm"][0])
    return tab


class Prog:
    def __init__(self, layers, n_layers_mlp=None):
        self.layers = layers
        nc = bass.Bass("TRN2", target_bir_lowering=False)
        self.nc = nc
        self.sy = Sy(nc)
        self._n = 0
        dt = nc.dram_tensor
        self.xT = dt("xT", [D, S], F32, kind="ExternalInput").ap()
        self.vecs_d = dt("vecs", [128, NVEC], F32, kind="ExternalInput").ap()
        self.outT = dt("outT", [D, S], F32, kind="ExternalOutput").ap()
        self.w = {}
        for name, shape in (
            ("mlp_up", [DEPTH, D, DFF]), ("mlp_down", [DEPTH, DFF, D]),
            ("rwkv_w_r", [2, D, D]), ("rwkv_w_k", [2, D, D]), ("rwkv_w_v", [2, D, D]),
            ("rwkv_w_o", [2, D, D]),
            ("rwkv_decay_w1", [2, D, 64]), ("rwkv_decay_w2", [2, 64, D]),
            ("rwkv_iclr_a1", [2, D, 64]), ("rwkv_iclr_a2", [2, 64, D]),
            ("rwkv_gate_g1", [2, D, 160]), ("rwkv_gate_g2", [2, 160, D]),
            ("rwkv_vres_v1", [1, D, 32]), ("rwkv_vres_v2", [1, 32, D]),
            ("conv_w_in", [1, D, 3 * D]), ("conv_w_out", [1, D, D]),
            ("sb_w_qkv", [1, D, 3 * D]), ("sb_w_o", [1, D, D]),
        ):
            self.w[name] = dt(name, shape, F32, kind="ExternalInput").ap()
        self.vfirst = dt("vfirst_scratch", [D, S], F32, kind="Internal").ap()
        self.psum = [nc.alloc_psum_tensor(f"ps{i}", [128, 512], F32).ap() for i in range(8)]
        self.ps_i = 0

    def sb(self, name, shape, dtype):
        return self.nc.alloc_sbuf_tensor(name, shape, dtype).ap()

    def ps(self):
        i = self.ps_i
        self.ps_i = (i + 1) % 6
        return self.psum[i], ("ps", i)

    def phase_begin(self):
        self.sy.barrier()
        self.arena_off = 0

    def carve(self, shape, dtype):
        n = 1
        for d in shape:
            n *= d
        nb = n * (4 if dtype in (F32, F32R) else 2)
        nb = (nb + 31) // 32 * 32
        off = self.arena_off
        assert off + nb <= self.ARENA * 2, (off, nb, self.ARENA * 2)
        self.arena_off = off + nb
        self._n += 1
        return self.nc.alloc_sbuf_tensor_at(f"cv{self._n}", [128] + list(shape), dtype,
                                            offset=self.arena_base + off).ap()

    def vcol(self, name, k=0):
        c = VEC_COLS[name] + k
        return self.vecs[:, c:c + 1]

    def build(self):
        nc, sy = self.nc, self.sy
        self.h = self.sb("h", [128, NK, S], F32)
        self.xn = self.sb("xn", [128, NK, S + 2], BF16)
        self.vecs = self.sb("vecs_sb", [128, NVEC], F32)
        self.ones_bf = self.sb("ones_bf", [128, 128], BF16)
        self.eps_t = self.sb("eps_t", [128, 1], F32)
        self.ARENA = 53 * 1024 + 512
        self.arena = self.sb("arena", [128, self.ARENA], BF16)
        self.arena_base = self.nc.sbuf_base - self.ARENA * 2
        self.arena_off = 0
        self.make_consts()
        sy.dma("sp", self.vecs, self.vecs_d, [], ["vecs"], "misc")
        for k in range(NK):
            sy.dma("sp", self.h[:, k, :], self.xT[k * 128:(k + 1) * 128, :], [], [("h", k, t) for t in range(NT)], f"xin{k}")
        sy.op("dve", [], ["ones"], lambda e: e.memset(self.ones_bf, 1.0))
        sy.op("dve", [], ["eps"], lambda e: e.memset(self.eps_t, RMS_EPS))
        sy.op("dve", [], [("xnpad",)], lambda e: e.memset(self.xn[:, :, 0:2], 0.0))
        for (kind, l) in self.layers:
            if kind == "mlp":
                self.rmsnorm(f"mlp_norm{l}")
                self.mlp(l)
            elif kind == "mix":
                self.rmsnorm(f"mix_norm{l}")
                if l % 3 == 1:
                    self.conv(l // 3)
                elif l % 3 == 2:
                    self.sbatt(l // 3)
                else:
                    self.rwkv(l // 3)
        for k in range(NK):
            sy.dma("sp", self.outT[k * 128:(k + 1) * 128, :], self.h[:, k, :],
                   [("h", k, t) for t in range(NT)], [("out", k)], "out")
        sy.wait_all("sp", [("out", k) for k in range(NK)])
        return nc

    def make_consts(self):
        sy = self.sy
        self.one_col = self.sb("one_col", [128, 1], F32)
        self.bones = self.sb("bones", [128, 128], F32)
        self.tri = self.sb("tri", [128, 128], F32R)
        self.onesr = self.sb("onesr", [128, 128], F32R)
        self.onesw = self.carve([128], F32)
        sy.op("dve", [], ["consts"], lambda e: e.memset(self.one_col, 1.0))
        sy.op("dve", [], ["consts"], lambda e: e.memset(self.bones, 0.0))
        sy.op("dve", [], ["consts"], lambda e: e.memset(self.bones[0:64, 0:64], 1.0))
        sy.op("dve", [], ["consts"], lambda e: e.memset(self.bones[64:128, 64:128], 1.0))
        sy.op("dve", [], ["consts"], lambda e: e.memset(self.onesw, 1.0))
        sy.op("pool", ["consts"], ["consts2"], lambda e: e.affine_select(
            out=self.tri, in_=self.onesw[:, 0:128], pattern=[[-1, 128]], compare_op=ALU.is_ge, fill=0.0,
            base=0, channel_multiplier=1))
        sy.op("pool", ["consts"], ["consts2"], lambda e: e.tensor_copy(out=self.onesr, in_=self.onesw[:, 0:128]))

    def rmsnorm(self, gname):
        sy = self.sy
        self.phase_begin()
        self.sq = [self.carve([TT], BF16) for i in range(2)]
        self.rstd = [self.carve([TT], F32) for i in range(2)]
        for t in range(NT):
            ts = slice(t * TT, (t + 1) * TT)
            pst, psk = self.ps()
            for k in range(NK):
                sq = self.sq[k % 2]
                sqk = ("sq", k % 2)
                sy.op("act", [("h", k, t)], [sqk],
                      lambda e, sq=sq, k=k: e.activation(out=sq, in_=self.h[:, k, ts], func=AF.Square))
                sy.op("pe", [sqk, "ones"], [psk],
                      lambda e, sq=sq, k=k: e.matmul(pst, self.ones_bf, sq, start=(k == 0), stop=(k == NK - 1)),
                      inc=(k == NK - 1))
            rs = self.rstd[t % 2]
            rsk = ("rstd", t % 2)
            sy.op("act", [psk, "eps"], [rsk],
                  lambda e: e.activation(out=rs, in_=pst, func=AF.Ln, bias=self.eps_t, scale=1.0 / D))
            sy.op("act", [rsk], [rsk], lambda e: e.activation(out=rs, in_=rs, func=AF.Exp, scale=-0.5))
            for k in range(NK):
                sy.op("dve", [("h", k, t), rsk, "vecs"], [("xn", k, t)],
                      lambda e, k=k: e.scalar_tensor_tensor(
                          out=self.xn[:, k, 2 + t * TT:2 + (t + 1) * TT], in0=self.h[:, k, ts],
                          scalar=self.vcol(gname, k), in1=rs, op0=ALU.mult, op1=ALU.mult))

    def alloc_mlp(self):
        self.phase_begin()
        self.GF = 512
        self.wup = [self.carve([NK, self.GF], BF16) for i in range(2)]
        self.wdn = [self.carve([self.GF // 128, D], BF16) for i in range(2)]
        self.hT = self.carve([self.GF // 128, S], BF16)
        self.relu_t = [self.carve([TT], F32) for i in range(2)]
        self.mlp_gi = 0

    def mlp_load(self, l, g):
        sy = self.sy
        GF = self.GF
        s = self.mlp_gi % 2
        self.mlp_gi += 1
        src_up = self.w["mlp_up"][l, :, g * GF:(g + 1) * GF].rearrange("(k p) f -> p k f", p=128)
        sy.dma("pool", self.wup[s], src_up, [], [("wup", s)], f"wup{s}")
        src_dn = self.w["mlp_down"][l, g * GF:(g + 1) * GF, :].rearrange("(c p) d -> p c d", p=128)
        sy.dma("pool", self.wdn[s], src_dn, [], [("wdn", s)], f"wdn{s}")
        return s

    def mlp(self, l):
        sy = self.sy
        self.alloc_mlp()
        GF = self.GF
        NG = DFF // GF
        NC = GF // 128
        slots = [None] * NG
        slots[0] = self.mlp_load(l, 0)
        ri = 0
        for g in range(NG):
            if g + 1 < NG:
                slots[g + 1] = self.mlp_load(l, g + 1)
            s = slots[g]
            for t in range(NT):
                for c in range(NC):
                    pst, psk = self.ps()
                    for k in range(NK):
                        sy.op("pe", [("wup", s), ("xn", k, t)], [psk],
                              lambda e, k=k, c=c, t=t: e.matmul(
                                  pst, self.wup[s][:, k, c * 128:(c + 1) * 128],
                                  self.xn[:, k, 2 + t * TT:2 + (t + 1) * TT],
                                  start=(k == 0), stop=(k == NK - 1)),
                              inc=(k == NK - 1))
                    rt = self.relu_t[ri % 2]
                    rk = ("relu", ri % 2)
                    ri += 1
                    sy.op("act", [psk], [rk], lambda e, rt=rt: e.activation(out=rt, in_=pst, func=AF.Relu))
                    sy.op("dve", [rk, psk], [("hT", c, t)],
                          lambda e, rt=rt, c=c, t=t: e.tensor_tensor(
                              out=self.hT[:, c, t * TT:(t + 1) * TT], in0=rt, in1=pst, op=ALU.mult))
            for t in range(NT):
                for m in range(NK):
                    pst, psk = self.ps()
                    for c in range(NC):
                        sy.op("pe", [("wdn", s), ("hT", c, t)], [psk],
                              lambda e, c=c, m=m, t=t: e.matmul(
                                  pst, self.wdn[s][:, c, m * 128:(m + 1) * 128],
                                  self.hT[:, c, t * TT:(t + 1) * TT],
                                  start=(c == 0), stop=(c == NC - 1)),
                              inc=(c == NC - 1))
                    sy.op("dve", [psk, ("h", m, t)], [("h", m, t)],
                          lambda e, m=m, t=t: e.tensor_tensor(
                              out=self.h[:, m, t * TT:(t + 1) * TT], in0=pst,
                              in1=self.h[:, m, t * TT:(t + 1) * TT], op=ALU.add))


    def proj_fm(self, wt, wkey, cols, t, shift=0, pst=None, psk=None, first=True, last=True):
        sy = self.sy
        if pst is None:
            pst, psk = self.ps()
        for k in range(NK):
            sy.op("pe", [wkey, ("xn", k, t)] + ([("xn", k, t - 1)] if (shift and t > 0) else []), [psk],
                  lambda e, k=k: e.matmul(pst, wt[:, k, cols],
                                          self.xn[:, k, 2 - shift + t * TT:2 - shift + (t + 1) * TT],
                                          start=(first and k == 0), stop=(last and k == NK - 1)))
        return pst, psk

    def outproj_acc(self, wo, wokey, y, ykey, t):
        sy = self.sy
        for m in range(NK):
            pst, psk = self.ps()
            sy.op("pe", [wokey, ykey], [psk],
                  lambda e, m=m: e.matmul(pst, wo[:, m * 128:(m + 1) * 128], y, start=True, stop=True))
            sy.op("dve", [psk, ("h", m, t)], [("h", m, t)],
                  lambda e, m=m: e.tensor_tensor(out=self.h[:, m, t * TT:(t + 1) * TT], in0=pst,
                                                 in1=self.h[:, m, t * TT:(t + 1) * TT], op=ALU.add))

    def alloc_mix(self):
        self.phase_begin()
        self.w3 = [self.carve([NK, 3, 128], BF16) for i in range(2)]
        self.wo = [self.carve([D], BF16) for i in range(2)]
        self.ybf = [self.carve([TT], BF16) for i in range(2)]
        self.mix_i = 0

    def load_w3(self, wname, j, dc, wo_name):
        sy = self.sy
        s = self.mix_i % 2
        self.mix_i += 1
        src = self.w[wname][j].rearrange("(k p) f -> p k f", p=128)
        for jj in range(3):
            sy.dma("pool", self.w3[s][:, :, jj, :], src[:, :, jj * D + dc * 128:jj * D + (dc + 1) * 128],
                   [], [("w3", s)], f"w3_{s}")
        sy.dma("pool", self.wo[s], self.w[wo_name][j, dc * 128:(dc + 1) * 128, :], [], [("wo", s)], f"wo_{s}")
        return s

    def conv(self, j):
        sy = self.sy
        self.alloc_mix()
        csb = [self.carve([TT], F32) for i in range(2)]
        bsb = [self.carve([TT], F32) for i in range(2)]
        acc = self.carve([TT], F32)
        zbufs = [self.carve([2 + S], F32) for i in range(2)]
        slots = [None] * NK
        slots[0] = self.load_w3("conv_w_in", j, 0, "conv_w_out")
        slots[1] = self.load_w3("conv_w_in", j, 1, "conv_w_out")
        items = [(dc, t) for dc in range(NK) for t in range(NT)]
        st = {"yi": 0}

        def stage1(n):
            dc, t = items[n]
            i2 = n % 2
            if t == 0:
                sy.op("dve", [], [("z", dc % 2, -1)], lambda e: e.memset(zbufs[dc % 2][:, 0:2], 0.0))
            s = slots[dc]
            w3 = self.w3[s]
            zb = zbufs[dc % 2]
            zs = slice(2 + t * TT, 2 + (t + 1) * TT)
            pb, pbk = self.proj_fm(w3[:, :, 0, :], ("w3", s), slice(0, 128), t)
            sy.op("act", [pbk], [("bsb", i2)], lambda e: e.activation(out=bsb[i2], in_=pb, func=AF.Copy))
            pc, pck = self.proj_fm(w3[:, :, 1, :], ("w3", s), slice(0, 128), t)
            sy.op("act", [pck], [("csb", i2)], lambda e: e.activation(out=csb[i2], in_=pc, func=AF.Copy))
            pu, puk = self.proj_fm(w3[:, :, 2, :], ("w3", s), slice(0, 128), t)
            sy.op("dve", [("csb", i2), puk], [("z", dc % 2, t)],
                  lambda e: e.tensor_tensor(out=zb[:, zs], in0=csb[i2], in1=pu, op=ALU.mult))

        def stage2(n):
            dc, t = items[n]
            i2 = n % 2
            s = slots[dc]
            wo = self.wo[s]
            zb = zbufs[dc % 2]
            zs = slice(2 + t * TT, 2 + (t + 1) * TT)
            zk = [("z", dc % 2, t), ("z", dc % 2, t - 1)]
            sy.op("dve", zk + ["vecs"], ["acc"],
                  lambda e: e.tensor_scalar(out=acc, in0=zb[:, t * TT:(t + 1) * TT],
                                            scalar1=self.vcol("conv_w0", dc), scalar2=None, op0=ALU.mult))
            sy.op("dve", zk + ["acc", "vecs"], ["acc"],
                  lambda e: e.scalar_tensor_tensor(out=acc, in0=zb[:, 1 + t * TT:1 + (t + 1) * TT],
                                                   scalar=self.vcol("conv_w1", dc), in1=acc,
                                                   op0=ALU.mult, op1=ALU.add))
            sy.op("dve", zk + ["acc", "vecs"], ["acc"],
                  lambda e: e.scalar_tensor_tensor(out=acc, in0=zb[:, zs],
                                                   scalar=self.vcol("conv_w2", dc), in1=acc,
                                                   op0=ALU.mult, op1=ALU.add))
            y = self.ybf[st["yi"] % 2]
            yk = ("ybf", st["yi"] % 2)
            st["yi"] += 1
            sy.op("dve", [("bsb", i2), "acc"], [yk], lambda e: e.tensor_tensor(out=y, in0=bsb[i2], in1=acc, op=ALU.mult))
            self.outproj_acc(wo, ("wo", s), y, yk, t)

        for n in range(len(items) + 1):
            if n < len(items):
                stage1(n)
            if n >= 1:
                stage2(n - 1)
                dcp, tp = items[n - 1]
                if tp == NT - 1 and dcp + 2 < NK:
                    slots[dcp + 2] = self.load_w3("conv_w_in", j, dcp + 2, "conv_w_out")

    def sbatt(self, j):
        sy = self.sy
        self.alloc_mix()
        qn = self.carve([S], F32R)
        kn = self.carve([S], F32R)
        vpA = self.carve([16, 128], BF16)
        vpB = self.carve([16, 128], BF16)
        raw_t = [self.carve([TT], F32) for i in range(2)]
        sq_t = [self.carve([TT], F32) for i in range(2)]
        rs_t = [self.carve([TT], F32) for i in range(2)]
        e_t = [self.carve([TT], F32) for i in range(3)]
        sp_t = [self.carve([TT], F32) for i in range(3)]
        lk_t = [self.carve([TT], F32R) for i in range(3)]
        u_t = [self.carve([TT], F32) for i in range(3)]
        arg_t = [self.carve([TT], F32) for i in range(3)]
        att_t = [self.carve([TT], BF16) for i in range(3)]
        R_t = [self.carve([TT], F32R) for i in range(2)]
        qgs = self.carve([1], F32)
        self.m01 = self.carve([896], BF16)
        self.mneg = self.carve([896], F32)
        onesw = self.carve([896], BF16)
        sy.op("dve", [], ["sbc"], lambda e: e.memset(onesw, 1.0))
        sy.op("pool", ["sbc"], ["consts2"], lambda e: e.affine_select(
            out=self.m01, in_=onesw, pattern=[[1, 896]], compare_op=ALU.is_gt, fill=0.0,
            base=-384, channel_multiplier=-1))
        sy.op("pool", ["consts2"], ["consts2"], lambda e: e.tensor_scalar(
            out=self.mneg, in0=self.m01, scalar1=-1.0, scalar2=None, op0=ALU.mult))
        sy.op("dve", ["vecs"], ["qgs"], lambda e: e.tensor_scalar(
            out=qgs, in0=self.vcol("q_gain"), scalar1=0.125, scalar2=None, op0=ALU.mult))
        sy.op("dve", [], ["vpA"], lambda e: e.memset(vpA, 0.0))
        sy.op("dve", [], ["vpB"], lambda e: e.memset(vpB, 0.0))
        slots = [None] * NK
        slots[0] = self.load_w3("sb_w_qkv", j, 0, "sb_w_o")
        ni = 0
        pi = 0
        oi = 0
        yi = 0
        for dc in range(NK):
            if dc + 1 < NK:
                slots[dc + 1] = self.load_w3("sb_w_qkv", j, dc + 1, "sb_w_o")
            s = slots[dc]
            w3, wo = self.w3[s], self.wo[s]
            for t in range(NT):
                ts = slice(t * TT, (t + 1) * TT)
                for (jj, dst, dkey, gcol) in ((0, qn, "qn", qgs), (1, kn, "kn", self.vcol("k_gain"))):
                    pp, ppk = self.proj_fm(w3[:, :, jj, :], ("w3", s), slice(0, 128), t)
                    raw, sq, rs = raw_t[ni % 2], sq_t[ni % 2], rs_t[ni % 2]
                    rk, sk_, rsk = ("raw", ni % 2), ("sqq", ni % 2), ("rsq", ni % 2)
                    ni += 1
                    sy.op("act", [ppk], [rk], lambda e, raw=raw, pp=pp: e.activation(out=raw, in_=pp, func=AF.Copy))
                    sy.op("act", [ppk], [sk_], lambda e, sq=sq, pp=pp: e.activation(out=sq, in_=pp, func=AF.Square))
                    p2, p2k = self.ps()
                    sy.op("pe", [sk_, "consts"], [p2k],
                          lambda e, sq=sq, p2=p2: e.matmul(p2, self.bones, sq, start=True, stop=True))
                    sy.op("act", [p2k, "eps"], [rsk],
                          lambda e, rs=rs, p2=p2: e.activation(out=rs, in_=p2, func=AF.Ln, bias=self.eps_t, scale=1.0 / 64))
                    sy.op("act", [rsk], [rsk], lambda e, rs=rs: e.activation(out=rs, in_=rs, func=AF.Exp, scale=-0.5))
                    sy.op("dve", [rk, rsk, "vecs", "qgs"], [(dkey, t)],
                          lambda e, raw=raw, rs=rs, dst=dst, gcol=gcol: e.scalar_tensor_tensor(
                              out=dst[:, ts], in0=raw, scalar=gcol, in1=rs, op0=ALU.mult, op1=ALU.mult))
                pv, pvk = self.ps()
                for q4 in range(4):
                    for k in range(NK):
                        sy.op("pe", [("w3", s), ("xn", k, t)], [pvk],
                              lambda e, k=k, q4=q4: e.matmul(
                                  pv[:, q4 * 128:(q4 + 1) * 128],
                                  self.xn[:, k, 2 + t * TT + q4 * 128:2 + t * TT + (q4 + 1) * 128],
                                  w3[:, k, 2, :], start=(k == 0), stop=(k == NK - 1)))
                pv3 = pv.rearrange("p (a b) -> p a b", a=4)
                sy.op("act", [pvk], ["vpA"], lambda e, pv3=pv3: e.activation(
                    out=vpA[:, 4 * t:4 * t + 4, 0:64], in_=pv3[:, :, 0:64], func=AF.Copy))
                sy.op("act", [pvk], ["vpB"], lambda e, pv3=pv3: e.activation(
                    out=vpB[:, 4 * t:4 * t + 4, 64:128], in_=pv3[:, :, 64:128], func=AF.Copy))
            pairs = []
            for T in range(NT):
                for hd in range(2):
                    cmax = 4 * T + 3
                    for c in range(cmax, -1, -1):
                        pairs.append(dict(T=T, hd=hd, c=c, cmax=cmax, first=(hd == 0 and c == cmax),
                                          last=(hd == 1 and c == 0)))
            NB = 3
            o_banks = {}
            for T in range(NT):
                o_banks[T] = 6 + oi % 2
                oi += 1
            rstate = {"cur": 0}

            def stage1(n, p):
                i2 = n % NB
                hp = slice(p["hd"] * 64, p["hd"] * 64 + 64)
                T, c = p["T"], p["c"]
                Ts = slice(T * TT, (T + 1) * TT)
                jd = c - 4 * T
                pz, pzk = self.ps()
                sy.op("pe", [("kn", c // 4), ("qn", T)], [pzk],
                      lambda e: e.matmul(pz, kn[hp, c * 128:(c + 1) * 128], qn[hp, Ts], start=True, stop=True))
                sy.op("act", [pzk], [("e", i2)], lambda e: e.activation(out=e_t[i2], in_=pz, func=AF.Exp))
                sy.op("act", [("e", i2), "consts"], [("sp", i2)],
                      lambda e: e.activation(out=sp_t[i2], in_=e_t[i2], func=AF.Ln, bias=self.one_col, scale=1.0))
                if jd >= 0:
                    sy.op("dve", [("sp", i2), "consts2"], [("lk", i2)],
                          lambda e: e.tensor_tensor(out=lk_t[i2], in0=sp_t[i2],
                                                    in1=self.mneg[:, 384 - 128 * jd:896 - 128 * jd], op=ALU.mult))
                else:
                    sy.op("dve", [("sp", i2)], [("lk", i2)],
                          lambda e: e.tensor_scalar(out=lk_t[i2], in0=sp_t[i2], scalar1=-1.0, scalar2=None, op0=ALU.mult))

            def stage2(n, p):
                i2 = n % NB
                T, c, cmax = p["T"], p["c"], p["cmax"]
                jd = c - 4 * T
                if c == cmax:
                    rstate["cur"] = 0
                rcur = rstate["cur"]
                hp = slice(p["hd"] * 64, p["hd"] * 64 + 64)
                Ts = slice(T * TT, (T + 1) * TT)
                pt, ptk = self.ps()
                sy.op("pe", [("lk", i2), "consts2"], [ptk],
                      lambda e: e.matmul(pt, self.tri, lk_t[i2], start=True, stop=False))
                if c < cmax:
                    sy.op("pe", [("R", rcur), "consts2"], [ptk],
                          lambda e: e.matmul(pt, self.onesr, R_t[rcur], start=False, stop=False))
                sy.op("pe", [("kn", c // 4), ("qn", T)], [ptk],
                      lambda e: e.matmul(pt, kn[hp, c * 128:(c + 1) * 128], qn[hp, Ts], start=False, stop=True))
                if c > 0:
                    rn = 1 - rcur
                    if c == cmax:
                        sy.op("pool", [("lk", i2)], [("R", rn)], lambda e: e.tensor_copy(out=R_t[rn], in_=lk_t[i2]))
                    else:
                        sy.op("pool", [("lk", i2), ("R", rcur)], [("R", rn)],
                              lambda e: e.tensor_tensor(out=R_t[rn], in0=R_t[rcur], in1=lk_t[i2], op=ALU.add))
                    rstate["cur"] = rn
                sy.op("act", [ptk], [("att", i2)],
                      lambda e: e.activation(out=att_t[i2], in_=pt, func=AF.Exp))
                if jd >= 0:
                    sy.op("dve", [("att", i2), "consts2"], [("att", i2)],
                          lambda e: e.tensor_tensor(out=att_t[i2], in0=att_t[i2],
                                                    in1=self.m01[:, 384 - 128 * jd:896 - 128 * jd], op=ALU.mult))

            def stage3(n, p):
                nonlocal yi
                i2 = n % NB
                T, c = p["T"], p["c"]
                ob = o_banks[T]
                o_ps, ok = self.psum[ob], ("ps", ob)
                vp, vpk = (vpA, "vpA") if p["hd"] == 0 else (vpB, "vpB")
                sy.op("pe", [vpk, ("att", i2)], [ok],
                      lambda e: e.matmul(o_ps, vp[:, c, :], att_t[i2], start=p["first"], stop=p["last"]))
                if p["last"]:
                    y = self.ybf[yi % 2]
                    yk = ("ybf", yi % 2)
                    yi += 1
                    sy.op("act", [ok], [yk], lambda e: e.activation(out=y, in_=o_ps, func=AF.Copy))
                    self.outproj_acc(wo, ("wo", s), y, yk, T)

            npairs = len(pairs)
            for n in range(npairs + 2):
                if n < npairs:
                    stage1(n, pairs[n])
                if 1 <= n and n - 1 < npairs:
                    stage2(n - 1, pairs[n - 1])
                if 2 <= n:
                    stage3(n - 2, pairs[n - 2])

    def rwkv(self, j):
        sy = self.sy
        self.phase_begin()
        RT = 256
        NRT = S // RT
        CD = 0.6065306597126334
        cv = self.carve
        P1, GA, GB = cv([S], BF16), cv([S], BF16), cv([S], BF16)
        WA2, GA2, GB2 = cv([D], BF16), cv([D], BF16), cv([D], BF16)
        omm, okka = cv([48], F32), cv([8], F32)
        ident, mk4, mkL, cmask = cv([128], F32), cv([512], F32), cv([128], F32), cv([RT], F32)
        gneps, tiny = cv([1], F32), cv([1], F32)
        mark = self.arena_off
        onesw = cv([512], F32)
        LWs, LW = cv([NK, 320], F32), cv([NK, 2, 320], BF16)
        mu0 = VEC_COLS[f"mu{j}_0"]
        mucols = self.vecs[:, mu0:mu0 + 48]
        sy.op("dve", [], ["rc"], lambda e: e.memset(onesw, 1.0))
        sy.op("dve", [], ["rc"], lambda e: e.memset(gneps, GN_EPS))
        sy.op("dve", [], ["rc"], lambda e: e.memset(tiny, 1e-24))
        sy.op("dve", [], ["cmask"], lambda e: e.memset(cmask, 1.0))
        sy.op("dve", [], ["cmask"], lambda e: e.memset(cmask[:, 0:1], 0.0))
        sy.op("dve", [], ["cmask"], lambda e: e.memset(cmask[:, 128:129], 0.0))
        sy.op("dve", ["vecs"], ["omm"], lambda e: e.tensor_scalar(
            out=omm, in0=mucols, scalar1=-1.0, scalar2=1.0, op0=ALU.mult, op1=ALU.add))
        ka0 = VEC_COLS[f"k_a{j}"]
        sy.op("dve", ["vecs"], ["omm"], lambda e: e.tensor_scalar(
            out=okka, in0=self.vecs[:, ka0:ka0 + 8], scalar1=-1.0, scalar2=1.0, op0=ALU.mult, op1=ALU.add))
        sy.op("pool", ["rc"], ["rc2"], lambda e: e.affine_select(
            out=ident, in_=onesw[:, 0:128], pattern=[[-1, 128]], compare_op=ALU.is_equal, fill=0.0,
            base=0, channel_multiplier=1))
        sy.op("pool", ["rc"], ["rc2"], lambda e: e.affine_select(
            out=mkL, in_=onesw[:, 0:128], pattern=[[-1, 128]], compare_op=ALU.is_gt, fill=0.0,
            base=0, channel_multiplier=1))
        sy.op("pool", ["rc"], ["rc2"], lambda e: e.affine_select(
            out=mk4, in_=onesw, pattern=[[0, 2], [1, 2], [1, 128]], compare_op=ALU.is_gt, fill=0.0,
            base=0, channel_multiplier=-1))
        wr = lambda n, jj=j: self.w[n][jj]
        sy.dma("sp", LWs[:, :, 0:64], wr("rwkv_decay_w1").rearrange("(k p) c -> p k c", p=128), [], ["LWs"], "lws")
        sy.dma("sp", LWs[:, :, 64:128], wr("rwkv_iclr_a1").rearrange("(k p) c -> p k c", p=128), [], ["LWs"], "lws")
        sy.dma("sp", LWs[:, :, 128:288], wr("rwkv_gate_g1").rearrange("(k p) c -> p k c", p=128), [], ["LWs"], "lws")
        if j == 1:
            sy.dma("sp", LWs[:, :, 288:320], self.w["rwkv_vres_v1"][0].rearrange("(k p) c -> p k c", p=128), [], ["LWs"], "lws")
        sy.dma("pool", WA2[0:64, :], wr("rwkv_decay_w2"), [], ["W2"], "w2s")
        sy.dma("pool", WA2[64:128, :], wr("rwkv_iclr_a2"), [], ["W2"], "w2s")
        sy.dma("pool", GA2, wr("rwkv_gate_g2")[0:128, :], [], ["W2"], "w2s")
        sy.dma("pool", GB2[0:32, :], wr("rwkv_gate_g2")[128:160, :], [], ["W2"], "w2s")
        if j == 1:
            sy.dma("pool", GB2[32:64, :], self.w["rwkv_vres_v2"][0], [], ["W2"], "w2s")
        blocks = [(0, 64, 1), (64, 128, 4), (128, 288, 5)] + ([(288, 320, 3)] if j == 1 else [])
        for k in range(NK):
            for (c0, c1, m) in blocks:
                sy.op("act", ["LWs", "omm"], ["LW"], lambda e, k=k, c0=c0, c1=c1, m=m: e.activation(
                    out=LW[:, k, 0, c0:c1], in_=LWs[:, k, c0:c1], func=AF.Copy, scale=omm[:, m * 8 + k:m * 8 + k + 1]))
                sy.op("dve", ["LWs", "vecs"], ["LW"], lambda e, k=k, c0=c0, c1=c1, m=m: e.tensor_scalar(
                    out=LW[:, k, 1, c0:c1], in0=LWs[:, k, c0:c1], scalar1=self.vecs[:, mu0 + m * 8 + k:mu0 + m * 8 + k + 1],
                    scalar2=None, op0=ALU.mult))
        NL3 = 64 if j == 1 else 32
        for t in range(NT):
            ts = slice(t * TT, (t + 1) * TT)
            for (c0, M, which) in ((0, 128, 0), (128, 128, 1), (256, NL3, 2)):
                pst, psk = self.ps()
                for k in range(NK):
                    for sh in range(2):
                        rd = [("xn", k, t), "LW"] + ([("xn", k, t - 1)] if (sh and t > 0) else [])
                        sy.op("pe", rd, [psk], lambda e, k=k, sh=sh, c0=c0, M=M, pst=pst: e.matmul(
                            pst[0:M, :], LW[:, k, sh, c0:c0 + M], self.xn[:, k, 2 - sh + t * TT:2 - sh + (t + 1) * TT],
                            start=(k == 0 and sh == 0), stop=(k == NK - 1 and sh == 1)))
                if which == 0:
                    sy.op("act", [psk], [("P1", t)], lambda e, pst=pst: e.activation(out=P1[0:64, ts], in_=pst[0:64, :], func=AF.Tanh))
                    sy.op("act", [psk], [("P1", t)], lambda e, pst=pst: e.activation(out=P1[64:128, ts], in_=pst[64:128, :], func=AF.Copy))
                elif which == 1:
                    sy.op("act", [psk], [("GA", t)], lambda e, pst=pst: e.activation(out=GA[:, ts], in_=pst, func=AF.Sigmoid))
                else:
                    sy.op("act", [psk], [("GB", t)], lambda e, pst=pst: e.activation(out=GB[0:32, ts], in_=pst[0:32, :], func=AF.Sigmoid))
                    if j == 1:
                        sy.op("act", [psk], [("GB", t)], lambda e, pst=pst: e.activation(out=GB[32:64, ts], in_=pst[32:64, :], func=AF.Copy))
        sy.barrier()
        self.arena_off = mark
        Wst = cv([NK, 3, 128], F32)
        Wfs = [cv([NK, 3, 2, 128], BF16) for i in range(2)]
        wo = [cv([D], BF16) for i in range(2)]
        f1 = lambda: cv([RT], F32)
        r_sb, k_sb, sg, csp, pinv, asig, kk, tA, tB, tC, YC, vf = [f1() for _ in range(12)]
        Yb = [f1() for _ in range(2)]
        BN3 = [f1() for _ in range(3)]
        Bset = [(f1(), f1(), cv([2, 2, 128], BF16), cv([RT], BF16), cv([RT], BF16), None, cv([RT], BF16)) for _ in range(2)]
        yout = [cv([RT], BF16) for i in range(2)]
        BH, KH = cv([128], BF16), cv([128], BF16)
        TMp = cv([4, 256], BF16)
        AM32 = [cv([128], F32) for i in range(2)]
        AMb = [cv([3, 128], BF16) for i in range(2)]
        Np = [[cv([128], F32) for i in range(2)] for h in range(2)]
        NpT = [[cv([128], F32) for i in range(2)] for h in range(2)]
        Wt = [[cv([2, 64], F32) for i in range(2)] for h in range(2)]
        Wfin = cv([2, 256], BF16)
        TMb = cv([4, 128], BF16)
        WB = cv([2, 128], BF16)
        M1, NCt = cv([128], F32), cv([128], F32)
        MBD, G = cv([128], BF16), cv([128], BF16)
        ST = [cv([128], BF16) for i in range(2)]
        identb = cv([128], BF16)
        sy.op("dve", ["rc2"], ["rc2"], lambda e: e.tensor_copy(out=identb, in_=ident))
        sy.op("dve", [], ["TMp"], lambda e: e.memset(TMp, 0.0))
        sy.op("dve", [], ["Wfin"], lambda e: e.memset(Wfin, 0.0))
        both = lambda ap2: ap2.rearrange("p (a b) -> p a b", a=4)[:, 0:4:3, :]
        wnames = ("rwkv_w_r", "rwkv_w_k", "rwkv_w_v")
        muidx = (0, 2, 3)

        def load_stage(dc):
            for jj in range(3):
                src = self.w[wnames[jj]][j].rearrange("(k p) c -> p k c", p=128)[:, :, dc * 128:(dc + 1) * 128]
                sy.dma("sp", Wst[:, :, jj, :], src, [], ["Wst"], "wst")
            sy.dma("pool", wo[dc % 2], self.w["rwkv_w_o"][j, dc * 128:(dc + 1) * 128, :], [], [("wo", dc % 2)], f"rwo{dc % 2}")

        def fold_gen(dcn):
            Wfn = Wfs[dcn % 2]
            wfk = ("Wf", dcn % 2)
            for k in range(NK):
                for jj in range(3):
                    m = muidx[jj]
                    sy.op("act", ["Wst", "omm"], [wfk], lambda e, k=k, jj=jj, m=m: e.activation(
                        out=Wfn[:, k, jj, 0, :], in_=Wst[:, k, jj, :], func=AF.Copy, scale=omm[:, m * 8 + k:m * 8 + k + 1]))
                    sy.op("dve", ["Wst", "vecs"], [wfk], lambda e, k=k, jj=jj, m=m: e.tensor_scalar(
                        out=Wfn[:, k, jj, 1, :], in0=Wst[:, k, jj, :],
                        scalar1=self.vecs[:, mu0 + m * 8 + k:mu0 + m * 8 + k + 1], scalar2=None, op0=ALU.mult))
                yield

        load_stage(0)
        for _ in fold_gen(0):
            pass
        stt = {"sti": 0, "yi": 0}
        def make_dc(dc):
            dcs = slice(dc * 128, (dc + 1) * 128)
            Wf = Wfs[dc % 2]
            wfk_cur = ("Wf", dc % 2)
            wod, wok = wo[dc % 2], ("wo", dc % 2)
            def make_ctx(rt, s_):
                b = s_ % 2
                tok0 = rt * RT
                tk = slice(tok0, tok0 + RT)
                t5 = tok0 // TT
                xr = lambda k: [("xn", k, t5)] + ([("xn", k, t5 - 1)] if (tok0 % TT == 0 and t5 > 0) else [])

                def proj(jj):
                    pst, psk = self.ps()
                    for k in range(NK):
                        for sh in range(2):
                            sy.op("pe", xr(k) + [wfk_cur], [psk], lambda e, k=k, sh=sh, pst=pst: e.matmul(
                                pst[:, 0:RT], Wf[:, k, jj, sh, :], self.xn[:, k, 2 - sh + tok0:2 - sh + tok0 + RT],
                                start=(k == 0 and sh == 0), stop=(k == NK - 1 and sh == 1)))
                    return pst[:, 0:RT], psk

                def small(lhsT, rhs, reads, M=128, pst=None, psk=None, start=True, stop=True, n=RT, c0=0):
                    if pst is None:
                        pst, psk = self.ps()
                    sy.op("pe", reads, [psk], lambda e: e.matmul(pst[0:M, c0:c0 + n], lhsT, rhs, start=start, stop=stop))
                    return pst, psk

                V = lambda nm, dc=dc: self.vcol(nm, dc)
                v_sb, cs, AR, BT, KT, BN, vbf = Bset[b]
                BN = BN3[s_ % 3]
                Y = Yb[s_ % 2]
                yk_ = ("Y", s_ % 2)
                bnk = ("BN", s_ % 3)
                kq = lambda nm: (nm, b)
                return dict(locals())

            def prologue(rt, s_):
                b = s_ % 2
                c_ = make_ctx(rt, s_)
                tok0, tk, t5, xr, proj, small, V = (c_[n] for n in ('tok0', 'tk', 't5', 'xr', 'proj', 'small', 'V'))
                v_sb, cs, AR, BT, KT, BN, vbf, kq, Y, yk_, bnk = (c_[n] for n in ('v_sb', 'cs', 'AR', 'BT', 'KT', 'BN', 'vbf', 'kq', 'Y', 'yk_', 'bnk'))
                pr, prk = proj(0)
                sy.op("act", [prk], ["r_sb"], lambda e: e.activation(out=r_sb, in_=pr, func=AF.Copy))
                pk, pkk = proj(1)
                sy.op("act", [pkk], ["k_sb"], lambda e: e.activation(out=k_sb, in_=pk, func=AF.Copy))
                pv, pvk = proj(2)
                sy.op("act", [pvk], [kq("v_sb")], lambda e: e.activation(out=v_sb, in_=pv, func=AF.Copy))
                yield
                plw, plwk = small(WA2[0:64, dcs], P1[0:64, tk], ["W2", ("P1", t5)])
                sy.op("act", [plwk, "vecs"], ["sg"], lambda e: e.activation(
                    out=sg, in_=plw[:, 0:RT], func=AF.Sigmoid, bias=V(f"w0{j}"), scale=1.0))
                pa, pak = small(WA2[64:128, dcs], P1[64:128, tk], ["W2", ("P1", t5)])
                sy.op("act", [pak, "vecs"], ["asig"], lambda e: e.activation(
                    out=asig, in_=pa[:, 0:RT], func=AF.Sigmoid, bias=V(f"a0{j}"), scale=1.0))
                if j == 1:
                    pg_, pgk_ = small(GB2[32:64, dcs], GB[32:64, tk], ["W2", ("GB", t5)])
                    sy.op("act", [pgk_, "vecs"], ["tB"], lambda e: e.activation(
                        out=tB, in_=pg_[:, 0:RT], func=AF.Sigmoid, bias=V("v0"), scale=1.0))
                    sy.dma("sp", vf, self.vfirst[dcs, tk], [("vfd", dc, rt)], ["vf"], "vfl")
                    sy.op("dve", ["vf", kq("v_sb")], ["vf"], lambda e: e.tensor_tensor(out=vf, in0=vf, in1=v_sb, op=ALU.subtract))
                    sy.op("dve", ["vf", "tB"], ["vf"], lambda e: e.tensor_tensor(out=vf, in0=vf, in1=tB, op=ALU.mult))
                    sy.op("dve", ["vf", kq("v_sb")], [kq("v_sb")], lambda e: e.tensor_tensor(out=v_sb, in0=v_sb, in1=vf, op=ALU.add))
                else:
                    sy.dma("sp", self.vfirst[dcs, tk], v_sb, [kq("v_sb")], [("vfd", dc, rt)], f"vfs{b}")
                sy.op("act", [kq("v_sb")], [kq("vbf")], lambda e: e.activation(out=vbf, in_=v_sb, func=AF.Copy))
                sy.op("dve", ["sg", "cmask"], [kq("cs")], lambda e: e.tensor_tensor_scan(
                    out=cs, data0=cmask, data1=sg, initial=0.0, op0=ALU.mult, op1=ALU.add))
                sy.op("dve", [kq("cs"), "sg"], ["csp"], lambda e: e.tensor_tensor(out=csp, in0=cs, in1=sg, op=ALU.subtract))
                sy.op("act", [kq("cs")], ["pinv"], lambda e: e.activation(out=pinv, in_=cs, func=AF.Exp, scale=CD))
                sy.op("act", [kq("cs")], [kq("cs")], lambda e: e.activation(out=cs, in_=cs, func=AF.Exp, scale=-CD))
                yield
                sy.op("act", ["csp"], ["csp"], lambda e: e.activation(out=csp, in_=csp, func=AF.Exp, scale=-CD))
                sy.op("dve", ["k_sb", "vecs"], ["kk"], lambda e: e.tensor_scalar(
                    out=kk, in0=k_sb, scalar1=V(f"k_k{j}"), scalar2=None, op0=ALU.mult))
                sy.op("act", ["kk"], ["tA"], lambda e: e.activation(out=tA, in_=kk, func=AF.Square))
                pss, pssk = small(self.bones, tA, ["tA", "consts"])
                sy.op("act", [pssk, "rc"], ["tA"], lambda e: e.activation(out=tA, in_=pss[:, 0:RT], func=AF.Ln, bias=tiny, scale=1.0))
                yield
                sy.op("act", ["tA"], ["tA"], lambda e: e.activation(out=tA, in_=tA, func=AF.Exp, scale=-0.5))
                sy.op("dve", ["kk", "tA"], ["kk"], lambda e: e.tensor_tensor(out=kk, in0=kk, in1=tA, op=ALU.mult))
                sy.op("dve", ["asig", "vecs", "omm"], ["tB"], lambda e: e.tensor_scalar(
                    out=tB, in0=asig, scalar1=V(f"k_a{j}"), scalar2=okka[:, dc:dc + 1], op0=ALU.mult, op1=ALU.add))
                sy.op("dve", ["k_sb", "tB"], ["k_sb"], lambda e: e.tensor_tensor(out=k_sb, in0=k_sb, in1=tB, op=ALU.mult))
                yield
                c3 = lambda ap: ap.rearrange("p (a b) -> p a b", a=2)
                sy.op("dve", ["kk", "csp"], [kq("AR0")], lambda e: e.scalar_tensor_tensor(
                    out=AR[:, :, 0, :], in0=c3(kk), scalar=-1.0, in1=c3(csp), op0=ALU.mult, op1=ALU.mult))
                sy.op("dve", ["r_sb", kq("cs")], [kq("AR1")], lambda e: e.tensor_tensor(
                    out=AR[:, :, 1, :], in0=c3(r_sb), in1=c3(cs), op=ALU.mult))
                sy.op("dve", ["kk", "asig"], ["tB"], lambda e: e.tensor_tensor(out=tB, in0=kk, in1=asig, op=ALU.mult))
                sy.op("dve", ["tB", "pinv"], [kq("BT")], lambda e: e.tensor_tensor(out=BT, in0=tB, in1=pinv, op=ALU.mult))
                sy.op("dve", ["k_sb", "pinv"], [kq("KT")], lambda e: e.tensor_tensor(out=KT, in0=k_sb, in1=pinv, op=ALU.mult))
                yield
                sy.op("dve", ["r_sb", "k_sb", "vecs"], ["tA"], lambda e: e.scalar_tensor_tensor(
                    out=tA, in0=r_sb, scalar=V(f"r_k{j}"), in1=k_sb, op0=ALU.mult, op1=ALU.mult))
                pbn, pbnk = small(self.bones, tA, ["tA", "consts"])
                sy.op("dve", [pbnk, kq("v_sb")], [bnk], lambda e: e.tensor_tensor(out=BN, in0=pbn[:, 0:RT], in1=v_sb, op=ALU.mult))
                yield

            def scanepi(rt, s_):
                c_ = make_ctx(rt, s_)
                if rt == 0:
                    stt["sti"] = 0
                    sy.op("dve", [], [("ST", 0)], lambda e: e.memset(ST[0], 0.0))
                tok0, tk, t5, xr, proj, small, V = (c_[n] for n in ('tok0', 'tk', 't5', 'xr', 'proj', 'small', 'V'))
                v_sb, cs, AR, BT, KT, BN, vbf, kq, Y, yk_, bnk = (c_[n] for n in ('v_sb', 'cs', 'AR', 'BT', 'KT', 'BN', 'vbf', 'kq', 'Y', 'yk_', 'bnk'))
                for ci in range(2):
                    cc = slice(ci * 128, (ci + 1) * 128)
                    pcol = cs[:, ci * 128 + 127:ci * 128 + 128]
                    sy.op("dve", [kq("BT"), kq("cs")], ["BH"], lambda e: e.tensor_scalar(out=BH, in0=BT[:, cc], scalar1=pcol, scalar2=None, op0=ALU.mult))
                    sy.op("dve", [kq("KT"), kq("cs")], ["KH"], lambda e: e.tensor_scalar(out=KH, in0=KT[:, cc], scalar1=pcol, scalar2=None, op0=ALU.mult))
                    ptm, ptmk = self.ps()
                    ptmb = ptm.bitcast(BF16)
                    for q, (src, skey) in enumerate(((AR[:, ci, 0, :], kq("AR0")), (BH, "BH"), (KH, "KH"), (vbf[:, cc], kq("vbf")))):
                        sy.op("pe", [skey, "rc2"], [ptmk], lambda e, q=q, src=src: e.transpose(
                            out=ptmb[:, q * 128:(q + 1) * 128], in_=src, identity=identb))
                    ptm3 = ptmb[:, 0:512].rearrange("p (a b) -> p a b", a=4)
                    sy.op("act", [ptmk], ["TMp"], lambda e: e.activation(out=TMp[:, :, 0:64], in_=ptm3[:, :, 0:64], func=AF.Copy))
                    sy.op("act", [ptmk], ["TMp"], lambda e: e.activation(out=TMp[:, :, 192:256], in_=ptm3[:, :, 64:128], func=AF.Copy))
                    sy.op("act", [ptmk], ["TMb"], lambda e: e.activation(out=TMb, in_=ptm3, func=AF.Copy))
                    hcs = (slice(0, 64), slice(192, 256))
                    for hd in range(2):
                        yield
                        hp = slice(hd * 64, hd * 64 + 64)
                        arh = AR[hp, ci, :, :]
                        pam, pamk = self.ps()
                        sy.op("pe", [kq("BT"), kq("AR0"), kq("AR1")], [pamk], lambda e, pam=pam, arh=arh, hp=hp: e.matmul(
                            pam[:, 0:256], BT[hp, cc], arh, start=True, stop=True))
                        sy.op("pe", [kq("KT"), kq("AR0"), kq("AR1")], [pamk], lambda e, pam=pam, arh=arh, hp=hp: e.matmul(
                            pam[:, 256:512], KT[hp, cc], arh, start=True, stop=True))
                        sy.op("dve", [pamk, "rc2"], [("AM", hd)], lambda e, pam=pam, hd=hd: e.tensor_tensor(
                            out=AM32[hd], in0=pam[:, 0:128], in1=mk4[:, 0:128], op=ALU.mult))
                        sy.op("dve", [pamk, "rc2"], [("AM", hd)], lambda e, pam=pam, hd=hd: e.tensor_tensor(
                            out=AMb[hd], in0=pam[:, 128:512].rearrange("p (a b) -> p a b", a=3),
                            in1=mk4[:, 128:512].rearrange("p (a b) -> p a b", a=3), op=ALU.mult))
                        pnt, pntk = self.ps()
                        sy.op("pe", [kq("BT"), kq("AR0")], [pntk], lambda e, pnt=pnt, hp=hp: e.matmul(
                            pnt[:, 0:128], AR[hp, ci, 0, :], BT[hp, cc], start=True, stop=True))
                        sy.op("dve", [pntk, "rc2"], [("NpT", hd, 0)], lambda e, pnt=pnt, hd=hd: e.tensor_tensor(
                            out=NpT[hd][0], in0=pnt[:, 0:128], in1=mkL, op=ALU.mult))
                        sy.op("pool", ["TMp"], [("Wt", hd, 0)], lambda e, hd=hd: e.tensor_copy(out=Wt[hd][0][:, 0, :], in_=TMp[:, 0, hcs[hd]]))
                        pxv, pxvk = self.ps()
                        sy.op("pe", [("AM", hd), "TMp"], [pxvk], lambda e, pxv=pxv, hd=hd: e.matmul(
                            pxv[:, 0:64], AMb[hd][:, 1, :], TMp[:, 3, hcs[hd]], start=True, stop=True))
                        sy.op("act", [pxvk], [("Wt", hd, 0)], lambda e, pxv=pxv, hd=hd: e.activation(
                            out=Wt[hd][0][:, 1, :], in_=pxv[:, 0:64], func=AF.Copy))
                    yield
                    for lvl in range(7):
                        yield
                        cur, nxt = lvl % 2, (lvl + 1) % 2
                        for hd in range(2):
                            npc = AM32[hd] if lvl == 0 else Np[hd][cur]
                            npk = ("AM", hd) if lvl == 0 else ("Np", hd, cur)
                            wcur = Wt[hd][cur]
                            pw, pwk = self.ps()
                            sy.op("pe", [npk, ("Wt", hd, cur)], [pwk], lambda e, pw=pw, npc=npc, wcur=wcur: e.matmul(
                                pw[:, 0:128], npc, wcur.rearrange("p a b -> p (a b)"), start=True, stop=True))
                            pw3 = pw[:, 0:128].rearrange("p (a b) -> p a b", a=2)
                            if lvl < 6:
                                sy.op("dve", [pwk, ("Wt", hd, cur)], [("Wt", hd, nxt)], lambda e, pw3=pw3, wcur=wcur, hd=hd, nxt=nxt: e.tensor_tensor(
                                    out=Wt[hd][nxt], in0=pw3, in1=wcur, op=ALU.add))
                                pn, pnk = self.ps()
                                sy.op("pe", [npk, ("NpT", hd, cur)], [pnk], lambda e, pn=pn, npc=npc, hd=hd, cur=cur: e.matmul(
                                    pn[:, 0:128], NpT[hd][cur], npc, start=True, stop=True))
                                sy.op("act", [pnk], [("Np", hd, nxt)], lambda e, pn=pn, hd=hd, nxt=nxt: e.activation(
                                    out=Np[hd][nxt], in_=pn[:, 0:128], func=AF.Copy))
                                pn2, pn2k = self.ps()
                                sy.op("pe", [npk, ("NpT", hd, cur)], [pn2k], lambda e, pn2=pn2, npc=npc, hd=hd, cur=cur: e.matmul(
                                    pn2[:, 0:128], npc, NpT[hd][cur], start=True, stop=True))
                                sy.op("act", [pn2k], [("NpT", hd, nxt)], lambda e, pn2=pn2, hd=hd, nxt=nxt: e.activation(
                                    out=NpT[hd][nxt], in_=pn2[:, 0:128], func=AF.Copy))
                            else:
                                sy.op("dve", [pwk, ("Wt", hd, cur)], ["Wfin"], lambda e, pw3=pw3, wcur=wcur, hd=hd: e.tensor_tensor(
                                    out=Wfin[:, :, hcs[hd]], in0=pw3, in1=wcur, op=ALU.add))
                                sy.op("dve", [pwk, ("Wt", hd, cur)], ["WB"], lambda e, pw3=pw3, wcur=wcur, hd=hd: e.tensor_tensor(
                                    out=WB[:, :, hd * 64:(hd + 1) * 64], in0=pw3, in1=wcur, op=ALU.add))
                    yield
                    Ah_b, Uh_b = WB[:, 0, :], WB[:, 1, :]
                    Bh_b, Kh_b, VT_b = TMb[:, 1, :], TMb[:, 2, :], TMb[:, 3, :]
                    stc, stn = ST[stt['sti'] % 2], ST[(stt['sti'] + 1) % 2]
                    stck, stnk = ("ST", stt['sti'] % 2), ("ST", (stt['sti'] + 1) % 2)
                    stt['sti'] += 1
                    pm, pmk = self.ps()
                    sy.op("pe", ["WB", "TMb"], [pmk], lambda e, pm=pm: e.matmul(pm[:, 0:128], Ah_b, Bh_b, start=True, stop=True))
                    sy.op("dve", [pmk, "consts"], ["M1"], lambda e, pm=pm: e.tensor_tensor(out=M1, in0=pm[:, 0:128], in1=self.bones, op=ALU.mult))
                    sy.op("dve", ["M1", "rc2", kq("cs")], ["MBD"], lambda e: e.scalar_tensor_tensor(
                        out=MBD, in0=ident, scalar=pcol, in1=M1, op0=ALU.mult, op1=ALU.add))
                    pn_, pnk_ = self.ps()
                    sy.op("pe", ["WB", "TMb"], [pnk_], lambda e, pn_=pn_: e.matmul(pn_[:, 0:128], Bh_b, Uh_b, start=True, stop=False))
                    sy.op("pe", ["TMb"], [pnk_], lambda e, pn_=pn_: e.matmul(pn_[:, 0:128], Kh_b, VT_b, start=False, stop=True))
                    sy.op("dve", [pnk_, "consts"], ["NCt"], lambda e, pn_=pn_: e.tensor_tensor(out=NCt, in0=pn_[:, 0:128], in1=self.bones, op=ALU.mult))
                    yield
                    pg, pgk = self.ps()
                    sy.op("pe", ["Wfin", ("AM", 0)], [pgk], lambda e, pg=pg: e.matmul(pg[:, 0:128], Wfin[:, 0, 0:128], AMb[0][:, 0, :], start=True, stop=False))
                    sy.op("pe", ["Wfin", ("AM", 1)], [pgk], lambda e, pg=pg: e.matmul(pg[:, 0:128], Wfin[:, 0, 128:256], AMb[1][:, 0, :], start=False, stop=True))
                    sy.op("dve", [pgk, kq("AR1")], ["G"], lambda e, pg=pg: e.tensor_tensor(out=G, in0=pg[:, 0:128], in1=AR[:, ci, 1, :], op=ALU.add))
                    yield
                    py, pyk = self.ps()
                    sy.op("pe", [stck, "G"], [pyk], lambda e, py=py, stc=stc: e.matmul(py[:, 0:128], stc, G, start=True, stop=False))
                    sy.op("pe", ["Wfin", ("AM", 0)], [pyk], lambda e, py=py: e.matmul(py[:, 0:128], Wfin[:, 1, 0:128], AMb[0][:, 0, :], start=False, stop=False))
                    sy.op("pe", ["Wfin", ("AM", 1)], [pyk], lambda e, py=py: e.matmul(py[:, 0:128], Wfin[:, 1, 128:256], AMb[1][:, 0, :], start=False, stop=False))
                    sy.op("pe", ["TMp", ("AM", 0)], [pyk], lambda e, py=py: e.matmul(py[:, 0:128], TMp[:, 3, 0:128], AMb[0][:, 2, :], start=False, stop=False))
                    sy.op("pe", ["TMp", ("AM", 1)], [pyk], lambda e, py=py: e.matmul(py[:, 0:128], TMp[:, 3, 128:256], AMb[1][:, 2, :], start=False, stop=True))
                    sy.op("act", [pyk], [yk_], lambda e, py=py: e.activation(out=Y[:, cc], in_=py[:, 0:128], func=AF.Copy))
                    yield
                    pst_, pstk_ = self.ps()
                    sy.op("pe", ["MBD", stck], [pstk_], lambda e, pst_=pst_, stc=stc: e.matmul(pst_[:, 0:128], MBD, stc, start=True, stop=True))
                    sy.op("dve", [pstk_, "NCt"], [stnk], lambda e, pst_=pst_, stn=stn: e.tensor_tensor(out=stn, in0=pst_[:, 0:128], in1=NCt, op=ALU.add))
                yield

            def epilogue(rt, s_):
                c_ = make_ctx(rt, s_)
                tok0, tk, t5, xr, proj, small, V = (c_[n] for n in ('tok0', 'tk', 't5', 'xr', 'proj', 'small', 'V'))
                v_sb, cs, AR, BT, KT, BN, vbf, kq, Y, yk_, bnk = (c_[n] for n in ('v_sb', 'cs', 'AR', 'BT', 'KT', 'BN', 'vbf', 'kq', 'Y', 'yk_', 'bnk'))
                pmn, pmnk = small(self.bones, Y, [yk_, "consts"])
                sy.op("dve", [pmnk, yk_], ["YC"], lambda e: e.scalar_tensor_tensor(
                    out=YC, in0=pmn[:, 0:RT], scalar=-1.0 / 64, in1=Y, op0=ALU.mult, op1=ALU.add))
                sy.op("act", ["YC"], ["tC"], lambda e: e.activation(out=tC, in_=YC, func=AF.Square))
                pvr, pvrk = small(self.bones, tC, ["tC", "consts"])
                sy.op("act", [pvrk, "rc"], ["tC"], lambda e: e.activation(out=tC, in_=pvr[:, 0:RT], func=AF.Ln, bias=gneps, scale=1.0 / 64))
                sy.op("act", ["tC"], ["tC"], lambda e: e.activation(out=tC, in_=tC, func=AF.Exp, scale=-0.5))
                sy.op("dve", ["YC", "tC"], ["YC"], lambda e: e.tensor_tensor(out=YC, in0=YC, in1=tC, op=ALU.mult))
                sy.op("dve", ["YC", "vecs"], ["YC"], lambda e: e.tensor_scalar(
                    out=YC, in0=YC, scalar1=V(f"lnx_w{j}"), scalar2=V(f"lnx_b{j}"), op0=ALU.mult, op1=ALU.add))
                sy.op("dve", ["YC", bnk], ["YC"], lambda e: e.tensor_tensor(out=YC, in0=YC, in1=BN, op=ALU.add))
                yield
                pgt, pgtk = small(GA2[:, dcs], GA[:, tk], ["W2", ("GA", t5)], stop=False)
                small(GB2[0:32, dcs], GB[0:32, tk], ["W2", ("GB", t5)], pst=pgt, psk=pgtk, start=False, stop=True)
                yo = yout[stt['yi'] % 2]
                yok = ("yout", stt['yi'] % 2)
                stt['yi'] += 1
                sy.op("dve", ["YC", pgtk], [yok], lambda e, yo=yo: e.tensor_tensor(out=yo, in0=YC, in1=pgt[:, 0:RT], op=ALU.mult))
                for m in range(NK):
                    po, pok = self.ps()
                    sy.op("pe", [wok, yok], [pok], lambda e, m=m, po=po, yo=yo: e.matmul(
                        po[:, 0:RT], wod[:, m * 128:(m + 1) * 128], yo, start=True, stop=True))
                    sy.op("dve", [pok, ("h", m, t5)], [("h", m, t5)], lambda e, m=m, po=po: e.tensor_tensor(
                        out=self.h[:, m, tk], in0=po[:, 0:RT], in1=self.h[:, m, tk], op=ALU.add))
                yield

            return prologue, scanepi, epilogue

        dcg = {}

        def DCG(dc):
            if dc not in dcg:
                dcg[dc] = make_dc(dc)
            return dcg[dc]

        NS = NK * NRT
        for _ in DCG(0)[0](0, 0):
            pass
        for s_ in range(NS + 1):
            gens = []
            if s_ < NS:
                dc, rt = divmod(s_, NRT)
                gens.append(DCG(dc)[1](rt, s_))
            if s_ >= 1:
                dcp, rtp = divmod(s_ - 1, NRT)
                gens.append(DCG(dcp)[2](rtp, s_ - 1))
            if s_ + 1 < NS:
                dcn, rtn = divmod(s_ + 1, NRT)
                gens.append(DCG(dcn)[0](rtn, s_ + 1))
            if s_ < NS and dc + 1 < NK:
                if rt == 1:
                    load_stage(dc + 1)
                if rt == 4:
                    gens.append(fold_gen(dc + 1))
            while gens:
                for g_ in list(gens):
                    try:
                        next(g_)
                    except StopIteration:
                        gens.remove(g_)


ALL_LAYERS = []
for _l in range(DEPTH):
    ALL_LAYERS += [("mix", _l), ("mlp", _l)]

WEIGHT_NAMES = ["mlp_up", "mlp_down", "rwkv_w_r", "rwkv_w_k", "rwkv_w_v", "rwkv_w_o",
                "rwkv_decay_w1", "rwkv_decay_w2", "rwkv_iclr_a1", "rwkv_iclr_a2",
                "rwkv_gate_g1", "rwkv_gate_g2", "rwkv_vres_v1", "rwkv_vres_v2",
                "conv_w_in", "conv_w_out", "sb_w_qkv", "sb_w_o"]


def run(inputs, layers, n_cores=8, trace=False):
    inp = {k: np.asarray(v) for k, v in inputs.items()}
    prog = Prog(layers)
    nc = prog.build()
    vecs = pack_vecs(inp)
    wts = {n: np.ascontiguousarray(inp[n], dtype=np.float32) for n in WEIGHT_NAMES}
    in_maps = []
    for b in range(n_cores):
        m = {"xT": np.ascontiguousarray(inp["x"][b].T), "vecs": vecs}
        m.update(wts)
        in_maps.append(m)
    res = run_bass_kernel_spmd(nc, in_maps, core_ids=list(range(n_cores)), trace=trace)
    out = np.stack([np.ascontiguousarray(r["outT"].T) for r in res.results], axis=0)
    return out, res, prog


def kernel(**inputs):
    out, _, _ = run(inputs, ALL_LAYERS)
    return out.astype(np.float32)
```

```python
import numpy as np
import concourse.bass as bass
import concourse.mybir as mybir
from concourse.bass_utils import run_bass_kernel_spmd

F32 = mybir.dt.float32
F32R = mybir.dt.float32r
BF16 = mybir.dt.bfloat16
AF = mybir.ActivationFunctionType
ALU = mybir.AluOpType

D = 1024
S = 2048
NK = 8
TT = 512
NT = S // TT
DFF = 4096
DEPTH = 4
RMS_EPS = 1e-6
GN_EPS = 64e-5


class Sy:
    def __init__(self, nc):
        self.nc = nc
        self.eng = {"pe": nc.tensor, "dve": nc.vector, "act": nc.scalar,
                    "pool": nc.gpsimd, "sp": nc.sync}
        self.sem = {e: nc.alloc_semaphore("s_" + e) for e in self.eng}
        self.cnt = {e: 0 for e in self.eng}
        self.pend = {e: False for e in self.eng}
        self.waited = {e: {} for e in self.eng}
        self.last_w = {}
        self.readers = {}
        self.dsem = {}
        self.dcnt = {}
        self.n_wait = 0
        self.n_inst = 0

    def _wait(self, e, deps):
        need = {}
        for (sk, v) in deps:
            if need.get(sk, 0) < v:
                need[sk] = v
        for sk, v in need.items():
            if sk == e and e == "pe":
                continue
            if self.waited[e].get(sk, 0) >= v:
                continue
            sem = self.sem[sk] if sk in self.sem else self.dsem[sk]
            self.eng[e].wait_ge(sem, v)
            self.waited[e][sk] = v
            self.n_wait += 1

    def _deps(self, reads, writes):
        deps = []
        for k in reads:
            if k in self.last_w:
                deps.append(self.last_w[k])
        for k in writes:
            if k in self.last_w:
                deps.append(self.last_w[k])
            deps.extend(self.readers.get(k, {}).items())
        return deps

    def _record(self, tok, reads, writes):
        for k in reads:
            r = self.readers.setdefault(k, {})
            if r.get(tok[0], 0) < tok[1]:
                r[tok[0]] = tok[1]
        for k in writes:
            self.last_w[k] = tok
            self.readers[k] = {}

    def op(self, e, reads, writes, emit, inc=True):
        psr = [k for k in reads if isinstance(k, tuple) and k[0] == "ps" and k not in writes]
        if psr:
            writes = list(writes) + psr
        self._wait(e, self._deps(reads, writes))
        inst = emit(self.eng[e])
        self.n_inst += 1
        inc = True
        if inc:
            self.cnt[e] += 1
            inst.then_inc(self.sem[e], 1)
            self.pend[e] = False
            tok = (e, self.cnt[e])
        else:
            self.pend[e] = True
            tok = (e, self.cnt[e] + 1)
        self._record(tok, reads, writes)
        return inst

    def dma(self, e, out, in_, reads, writes, sk, **kw):
        if sk not in self.dsem:
            self.dsem[sk] = self.nc.alloc_semaphore("d_" + sk)
            self.dcnt[sk] = 0
        self._wait(e, self._deps(reads, writes))
        inst = self.eng[e].dma_start(out=out, in_=in_, **kw)
        self.dcnt[sk] += 16
        inst.then_inc(self.dsem[sk], 16)
        self.n_inst += 1
        self._record((sk, self.dcnt[sk]), reads, writes)
        return inst

    def barrier(self):
        for e in self.eng:
            deps = [(e2, self.cnt[e2]) for e2 in self.eng if e2 != e and self.cnt[e2] > 0]
            deps += [(sk, self.dcnt[sk]) for sk in self.dsem if self.dcnt[sk] > 0]
            self._wait(e, deps)

    def wait_all(self, e, keys):
        deps = []
        for k in keys:
            if k in self.last_w:
                deps.append(self.last_w[k])
            deps.extend(self.readers.get(k, {}).items())
        self._wait(e, deps)


VEC_COLS = {}


def _vec_layout():
    cols = {}
    off = 0

    def add(name, n=NK):
        nonlocal off
        cols[name] = off
        off += n
    for l in range(DEPTH):
        add(f"mix_norm{l}")
        add(f"mlp_norm{l}")
    for j in range(2):
        for m in range(6):
            add(f"mu{j}_{m}")
        for nm in ("w0", "a0", "k_k", "k_a", "r_k", "lnx_w", "lnx_b"):
            add(f"{nm}{j}")
    add("v0")
    for c in range(3):
        add(f"conv_w{c}")
    add("q_gain", 1)
    add("k_gain", 1)
    return cols, off


VEC_COLS, NVEC = _vec_layout()


def pack_vecs(inp):
    tab = np.zeros((128, NVEC), np.float32)

    def put(name, v):
        v = np.asarray(v, np.float32).reshape(-1)
        c = VEC_COLS[name]
        if v.size == D:
            tab[:, c:c + NK] = v.reshape(NK, 128).T
        else:
            tab[:, c] = np.concatenate([v, v])
    for l in range(DEPTH):
        put(f"mix_norm{l}", inp["mix_norm"][l])
        put(f"mlp_norm{l}", inp["mlp_norm"][l])
    for j in range(2):
        for m in range(6):
            put(f"mu{j}_{m}", inp["rwkv_mu"][j, m])
        put(f"w0{j}", inp["rwkv_decay_w0"][j])
        put(f"a0{j}", inp["rwkv_iclr_a0"][j])
        put(f"k_k{j}", inp["rwkv_k_k"][j])
        put(f"k_a{j}", inp["rwkv_k_a"][j])
        put(f"r_k{j}", inp["rwkv_r_k"][j])
        put(f"lnx_w{j}", inp["rwkv_lnx_w"][j])
        put(f"lnx_b{j}", inp["rwkv_lnx_b"][j])
    put("v0", inp["rwkv_vres_v0"][0])
    for c in range(3):
        put(f"conv_w{c}", inp["conv_w"][0, c])
    put("q_gain", inp["sb_q_norm"][0])
    put("k_gain", inp["sb_k_norm"][0])
    return tab


class Prog:
    def __init__(self, layers, n_layers_mlp=None):
        self.layers = layers
        nc = bass.Bass("TRN2", target_bir_lowering=False)
        self.nc = nc
        self.sy = Sy(nc)
        self._n = 0
        dt = nc.dram_tensor
        self.xT = dt("xT", [D, S], F32, kind="ExternalInput").ap()
        self.vecs_d = dt("vecs", [128, NVEC], F32, kind="ExternalInput").ap()
        self.outT = dt("outT", [D, S], F32, kind="ExternalOutput").ap()
        self.w = {}
        for name, shape in (
            ("mlp_up", [DEPTH, D, DFF]), ("mlp_down", [DEPTH, DFF, D]),
            ("rwkv_w_r", [2, D, D]), ("rwkv_w_k", [2, D, D]), ("rwkv_w_v", [2, D, D]),
            ("rwkv_w_o", [2, D, D]),
            ("rwkv_decay_w1", [2, D, 64]), ("rwkv_decay_w2", [2, 64, D]),
            ("rwkv_iclr_a1", [2, D, 64]), ("rwkv_iclr_a2", [2, 64, D]),
            ("rwkv_gate_g1", [2, D, 160]), ("rwkv_gate_g2", [2, 160, D]),
            ("rwkv_vres_v1", [1, D, 32]), ("rwkv_vres_v2", [1, 32, D]),
            ("conv_w_in", [1, D, 3 * D]), ("conv_w_out", [1, D, D]),
            ("sb_w_qkv", [1, D, 3 * D]), ("sb_w_o", [1, D, D]),
        ):
            self.w[name] = dt(name, shape, F32, kind="ExternalInput").ap()
        self.vfirst = dt("vfirst_scratch", [D, S], F32, kind="Internal").ap()
        self.psum = [nc.alloc_psum_tensor(f"ps{i}", [128, 512], F32).ap() for i in range(8)]
        self.ps_i = 0
        self.ps_n = 6

    def sb(self, name, shape, dtype):
        return self.nc.alloc_sbuf_tensor(name, shape, dtype).ap()

    def ps(self):
        i = self.ps_i
        self.ps_i = (i + 1) % self.ps_n
        return self.psum[i], ("ps", i)

    def phase_begin(self, ps_n=8):
        self.sy.barrier()
        self.arena_off = 0
        self.ps_n = ps_n
        self.ps_i = 0

    def carve(self, shape, dtype):
        n = 1
        for d in shape:
            n *= d
        nb = n * (4 if dtype in (F32, F32R) else 2)
        nb = (nb + 31) // 32 * 32
        off = self.arena_off
        assert off + nb <= self.ARENA * 2, (off, nb, self.ARENA * 2)
        self.arena_off = off + nb
        self._n += 1
        return self.nc.alloc_sbuf_tensor_at(f"cv{self._n}", [128] + list(shape), dtype,
                                            offset=self.arena_base + off).ap()

    def vcol(self, name, k=0):
        c = VEC_COLS[name] + k
        return self.vecs[:, c:c + 1]

    def build(self):
        nc, sy = self.nc, self.sy
        self.h = self.sb("h", [128, NK, S], F32)
        self.xn = self.sb("xn", [128, NK, S + 2], BF16)
        self.vecs = self.sb("vecs_sb", [128, NVEC], F32)
        self.ones_bf = self.sb("ones_bf", [128, 128], BF16)
        self.eps_t = self.sb("eps_t", [128, 1], F32)
        self.ARENA = 53 * 1024 + 512
        self.arena = self.sb("arena", [128, self.ARENA], BF16)
        self.arena_base = self.nc.sbuf_base - self.ARENA * 2
        self.arena_off = 0
        self.make_consts()
        sy.dma("sp", self.vecs, self.vecs_d, [], ["vecs"], "misc")
        for k in range(NK):
            sy.dma("sp", self.h[:, k, :], self.xT[k * 128:(k + 1) * 128, :], [], [("h", k, t) for t in range(NT)], f"xin{k}")
        sy.op("dve", [], ["ones"], lambda e: e.memset(self.ones_bf, 1.0))
        sy.op("dve", [], ["eps"], lambda e: e.memset(self.eps_t, RMS_EPS))
        sy.op("dve", [], [("xnpad",)], lambda e: e.memset(self.xn[:, :, 0:2], 0.0))
        for (kind, l) in self.layers:
            if kind == "mlp":
                self.rmsnorm(f"mlp_norm{l}")
                self.mlp(l)
            elif kind == "mix":
                self.rmsnorm(f"mix_norm{l}")
                if l % 3 == 1:
                    self.conv(l // 3)
                elif l % 3 == 2:
                    self.sbatt(l // 3)
                else:
                    self.rwkv(l // 3)
        for k in range(NK):
            sy.dma("sp", self.outT[k * 128:(k + 1) * 128, :], self.h[:, k, :],
                   [("h", k, t) for t in range(NT)], [("out", k)], "out")
        sy.wait_all("sp", [("out", k) for k in range(NK)])
        return nc

    def make_consts(self):
        sy = self.sy
        self.one_col = self.sb("one_col", [128, 1], F32)
        self.bones = self.sb("bones", [128, 128], F32)
        self.tri = self.sb("tri", [128, 128], F32R)
        self.onesr = self.sb("onesr", [128, 128], F32R)
        self.onesw = self.carve([128], F32)
        sy.op("dve", [], ["consts"], lambda e: e.memset(self.one_col, 1.0))
        sy.op("dve", [], ["consts"], lambda e: e.memset(self.bones, 0.0))
        sy.op("dve", [], ["consts"], lambda e: e.memset(self.bones[0:64, 0:64], 1.0))
        sy.op("dve", [], ["consts"], lambda e: e.memset(self.bones[64:128, 64:128], 1.0))
        sy.op("dve", [], ["consts"], lambda e: e.memset(self.onesw, 1.0))
        sy.op("pool", ["consts"], ["consts2"], lambda e: e.affine_select(
            out=self.tri, in_=self.onesw[:, 0:128], pattern=[[-1, 128]], compare_op=ALU.is_ge, fill=0.0,
            base=0, channel_multiplier=1))
        sy.op("pool", ["consts"], ["consts2"], lambda e: e.tensor_copy(out=self.onesr, in_=self.onesw[:, 0:128]))

    def rmsnorm(self, gname):
        sy = self.sy
        self.phase_begin()
        self.sq = [self.carve([TT], BF16) for i in range(2)]
        self.rstd = [self.carve([TT], F32) for i in range(2)]
        for t in range(NT):
            ts = slice(t * TT, (t + 1) * TT)
            pst, psk = self.ps()
            for k in range(NK):
                sq = self.sq[k % 2]
                sqk = ("sq", k % 2)
                sy.op("act", [("h", k, t)], [sqk],
                      lambda e, sq=sq, k=k: e.activation(out=sq, in_=self.h[:, k, ts], func=AF.Square))
                sy.op("pe", [sqk, "ones"], [psk],
                      lambda e, sq=sq, k=k: e.matmul(pst, self.ones_bf, sq, start=(k == 0), stop=(k == NK - 1)),
                      inc=(k == NK - 1))
            rs = self.rstd[t % 2]
            rsk = ("rstd", t % 2)
            sy.op("act", [psk, "eps"], [rsk],
                  lambda e: e.activation(out=rs, in_=pst, func=AF.Ln, bias=self.eps_t, scale=1.0 / D))
            sy.op("act", [rsk], [rsk], lambda e: e.activation(out=rs, in_=rs, func=AF.Exp, scale=-0.5))
            for k in range(NK):
                sy.op("dve", [("h", k, t), rsk, "vecs"], [("xn", k, t)],
                      lambda e, k=k: e.scalar_tensor_tensor(
                          out=self.xn[:, k, 2 + t * TT:2 + (t + 1) * TT], in0=self.h[:, k, ts],
                          scalar=self.vcol(gname, k), in1=rs, op0=ALU.mult, op1=ALU.mult))

    def alloc_mlp(self):
        self.phase_begin()
        self.GF = 512
        self.wup = [self.carve([NK, self.GF], BF16) for i in range(2)]
        self.wdn = [self.carve([self.GF // 128, D], BF16) for i in range(2)]
        self.hT = self.carve([self.GF // 128, S], BF16)
        self.relu_t = [self.carve([TT], F32) for i in range(2)]
        self.mlp_gi = 0

    def mlp_load(self, l, g):
        sy = self.sy
        GF = self.GF
        s = self.mlp_gi % 2
        self.mlp_gi += 1
        src_up = self.w["mlp_up"][l, :, g * GF:(g + 1) * GF].rearrange("(k p) f -> p k f", p=128)
        sy.dma("pool", self.wup[s], src_up, [], [("wup", s)], f"wup{s}")
        src_dn = self.w["mlp_down"][l, g * GF:(g + 1) * GF, :].rearrange("(c p) d -> p c d", p=128)
        sy.dma("pool", self.wdn[s], src_dn, [], [("wdn", s)], f"wdn{s}")
        return s

    def mlp(self, l):
        sy = self.sy
        self.alloc_mlp()
        GF = self.GF
        NG = DFF // GF
        NC = GF // 128
        slots = [None] * NG
        slots[0] = self.mlp_load(l, 0)
        ri = 0
        for g in range(NG):
            if g + 1 < NG:
                slots[g + 1] = self.mlp_load(l, g + 1)
            s = slots[g]
            for t in range(NT):
                for c in range(NC):
                    pst, psk = self.ps()
                    for k in range(NK):
                        sy.op("pe", [("wup", s), ("xn", k, t)], [psk],
                              lambda e, k=k, c=c, t=t: e.matmul(
                                  pst, self.wup[s][:, k, c * 128:(c + 1) * 128],
                                  self.xn[:, k, 2 + t * TT:2 + (t + 1) * TT],
                                  start=(k == 0), stop=(k == NK - 1)),
                              inc=(k == NK - 1))
                    rt = self.relu_t[ri % 2]
                    rk = ("relu", ri % 2)
                    ri += 1
                    sy.op("act", [psk], [rk], lambda e, rt=rt: e.activation(out=rt, in_=pst, func=AF.Relu))
                    sy.op("dve", [rk, psk], [("hT", c, t)],
                          lambda e, rt=rt, c=c, t=t: e.tensor_tensor(
                              out=self.hT[:, c, t * TT:(t + 1) * TT], in0=rt, in1=pst, op=ALU.mult))
            for t in range(NT):
                for m in range(NK):
                    pst, psk = self.ps()
                    for c in range(NC):
                        sy.op("pe", [("wdn", s), ("hT", c, t)], [psk],
                              lambda e, c=c, m=m, t=t: e.matmul(
                                  pst, self.wdn[s][:, c, m * 128:(m + 1) * 128],
                                  self.hT[:, c, t * TT:(t + 1) * TT],
                                  start=(c == 0), stop=(c == NC - 1)),
                              inc=(c == NC - 1))
                    sy.op("dve", [psk, ("h", m, t)], [("h", m, t)],
                          lambda e, m=m, t=t: e.tensor_tensor(
                              out=self.h[:, m, t * TT:(t + 1) * TT], in0=pst,
                              in1=self.h[:, m, t * TT:(t + 1) * TT], op=ALU.add))


    def proj_fm(self, wt, wkey, cols, t, shift=0, pst=None, psk=None, first=True, last=True):
        sy = self.sy
        if pst is None:
            pst, psk = self.ps()
        for k in range(NK):
            sy.op("pe", [wkey, ("xn", k, t)] + ([("xn", k, t - 1)] if (shift and t > 0) else []), [psk],
                  lambda e, k=k: e.matmul(pst, wt[:, k, cols],
                                          self.xn[:, k, 2 - shift + t * TT:2 - shift + (t + 1) * TT],
                                          start=(first and k == 0), stop=(last and k == NK - 1)))
        return pst, psk

    def outproj_acc(self, wo, wokey, y, ykey, t):
        sy = self.sy
        for m in range(NK):
            pst, psk = self.ps()
            sy.op("pe", [wokey, ykey], [psk],
                  lambda e, m=m: e.matmul(pst, wo[:, m * 128:(m + 1) * 128], y, start=True, stop=True))
            sy.op("dve", [psk, ("h", m, t)], [("h", m, t)],
                  lambda e, m=m: e.tensor_tensor(out=self.h[:, m, t * TT:(t + 1) * TT], in0=pst,
                                                 in1=self.h[:, m, t * TT:(t + 1) * TT], op=ALU.add))

    def alloc_mix(self):
        self.phase_begin()
        self.w3 = [self.carve([NK, 3, 128], BF16) for i in range(2)]
        self.wo = [self.carve([D], BF16) for i in range(2)]
        self.ybf = [self.carve([TT], BF16) for i in range(2)]
        self.mix_i = 0

    def load_w3(self, wname, j, dc, wo_name):
        sy = self.sy
        s = self.mix_i % 2
        self.mix_i += 1
        src = self.w[wname][j].rearrange("(k p) f -> p k f", p=128)
        for jj in range(3):
            sy.dma("pool", self.w3[s][:, :, jj, :], src[:, :, jj * D + dc * 128:jj * D + (dc + 1) * 128],
                   [], [("w3", s)], f"w3_{s}")
        sy.dma("pool", self.wo[s], self.w[wo_name][j, dc * 128:(dc + 1) * 128, :], [], [("wo", s)], f"wo_{s}")
        return s

    def conv(self, j):
        sy = self.sy
        self.alloc_mix()
        csb = [self.carve([TT], F32) for i in range(2)]
        bsb = [self.carve([TT], F32) for i in range(2)]
        acc = self.carve([TT], F32)
        zbufs = [self.carve([2 + S], F32) for i in range(2)]
        slots = [None] * NK
        slots[0] = self.load_w3("conv_w_in", j, 0, "conv_w_out")
        slots[1] = self.load_w3("conv_w_in", j, 1, "conv_w_out")
        items = [(dc, t) for dc in range(NK) for t in range(NT)]
        st = {"yi": 0}

        def stage1(n):
            dc, t = items[n]
            i2 = n % 2
            if t == 0:
                sy.op("dve", [], [("z", dc % 2, -1)], lambda e: e.memset(zbufs[dc % 2][:, 0:2], 0.0))
            s = slots[dc]
            w3 = self.w3[s]
            zb = zbufs[dc % 2]
            zs = slice(2 + t * TT, 2 + (t + 1) * TT)
            pb, pbk = self.proj_fm(w3[:, :, 0, :], ("w3", s), slice(0, 128), t)
            sy.op("act", [pbk], [("bsb", i2)], lambda e: e.activation(out=bsb[i2], in_=pb, func=AF.Copy))
            pc, pck = self.proj_fm(w3[:, :, 1, :], ("w3", s), slice(0, 128), t)
            sy.op("act", [pck], [("csb", i2)], lambda e: e.activation(out=csb[i2], in_=pc, func=AF.Copy))
            pu, puk = self.proj_fm(w3[:, :, 2, :], ("w3", s), slice(0, 128), t)
            sy.op("dve", [("csb", i2), puk], [("z", dc % 2, t)],
                  lambda e: e.tensor_tensor(out=zb[:, zs], in0=csb[i2], in1=pu, op=ALU.mult))

        def stage2(n):
            dc, t = items[n]
            i2 = n % 2
            s = slots[dc]
            wo = self.wo[s]
            zb = zbufs[dc % 2]
            zs = slice(2 + t * TT, 2 + (t + 1) * TT)
            zk = [("z", dc % 2, t), ("z", dc % 2, t - 1)]
            sy.op("dve", zk + ["vecs"], ["acc"],
                  lambda e: e.tensor_scalar(out=acc, in0=zb[:, t * TT:(t + 1) * TT],
                                            scalar1=self.vcol("conv_w0", dc), scalar2=None, op0=ALU.mult))
            sy.op("dve", zk + ["acc", "vecs"], ["acc"],
                  lambda e: e.scalar_tensor_tensor(out=acc, in0=zb[:, 1 + t * TT:1 + (t + 1) * TT],
                                                   scalar=self.vcol("conv_w1", dc), in1=acc,
                                                   op0=ALU.mult, op1=ALU.add))
            sy.op("dve", zk + ["acc", "vecs"], ["acc"],
                  lambda e: e.scalar_tensor_tensor(out=acc, in0=zb[:, zs],
                                                   scalar=self.vcol("conv_w2", dc), in1=acc,
                                                   op0=ALU.mult, op1=ALU.add))
            y = self.ybf[st["yi"] % 2]
            yk = ("ybf", st["yi"] % 2)
            st["yi"] += 1
            sy.op("dve", [("bsb", i2), "acc"], [yk], lambda e: e.tensor_tensor(out=y, in0=bsb[i2], in1=acc, op=ALU.mult))
            self.outproj_acc(wo, ("wo", s), y, yk, t)

        for n in range(len(items) + 1):
            if n < len(items):
                stage1(n)
            if n >= 1:
                stage2(n - 1)
                dcp, tp = items[n - 1]
                if tp == NT - 1 and dcp + 2 < NK:
                    slots[dcp + 2] = self.load_w3("conv_w_in", j, dcp + 2, "conv_w_out")

    def sbatt(self, j):
        sy = self.sy
        self.alloc_mix()
        self.ps_n = 6
        qn = self.carve([S], F32R)
        kn = self.carve([S], F32R)
        vpA = self.carve([16, 128], BF16)
        vpB = self.carve([16, 128], BF16)
        raw_t = [self.carve([TT], F32) for i in range(2)]
        sq_t = [self.carve([TT], F32) for i in range(2)]
        rs_t = [self.carve([TT], F32) for i in range(2)]
        e_t = [self.carve([TT], F32) for i in range(3)]
        sp_t = [self.carve([TT], F32) for i in range(3)]
        lk_t = [self.carve([TT], F32R) for i in range(3)]
        u_t = [self.carve([TT], F32) for i in range(3)]
        arg_t = [self.carve([TT], F32) for i in range(3)]
        att_t = [self.carve([TT], BF16) for i in range(3)]
        R_t = [self.carve([TT], F32R) for i in range(2)]
        qgs = self.carve([1], F32)
        self.m01 = self.carve([896], BF16)
        self.mneg = self.carve([896], F32)
        onesw = self.carve([896], BF16)
        sy.op("dve", [], ["sbc"], lambda e: e.memset(onesw, 1.0))
        sy.op("pool", ["sbc"], ["consts2"], lambda e: e.affine_select(
            out=self.m01, in_=onesw, pattern=[[1, 896]], compare_op=ALU.is_gt, fill=0.0,
            base=-384, channel_multiplier=-1))
        sy.op("pool", ["consts2"], ["consts2"], lambda e: e.tensor_scalar(
            out=self.mneg, in0=self.m01, scalar1=-1.0, scalar2=None, op0=ALU.mult))
        sy.op("dve", ["vecs"], ["qgs"], lambda e: e.tensor_scalar(
            out=qgs, in0=self.vcol("q_gain"), scalar1=0.125, scalar2=None, op0=ALU.mult))
        sy.op("dve", [], ["vpA"], lambda e: e.memset(vpA, 0.0))
        sy.op("dve", [], ["vpB"], lambda e: e.memset(vpB, 0.0))
        slots = [None] * NK
        slots[0] = self.load_w3("sb_w_qkv", j, 0, "sb_w_o")
        ni = 0
        pi = 0
        oi = 0
        yi = 0
        for dc in range(NK):
            if dc + 1 < NK:
                slots[dc + 1] = self.load_w3("sb_w_qkv", j, dc + 1, "sb_w_o")
            s = slots[dc]
            w3, wo = self.w3[s], self.wo[s]
            for t in range(NT):
                ts = slice(t * TT, (t + 1) * TT)
                for (jj, dst, dkey, gcol) in ((0, qn, "qn", qgs), (1, kn, "kn", self.vcol("k_gain"))):
                    pp, ppk = self.proj_fm(w3[:, :, jj, :], ("w3", s), slice(0, 128), t)
                    raw, sq, rs = raw_t[ni % 2], sq_t[ni % 2], rs_t[ni % 2]
                    rk, sk_, rsk = ("raw", ni % 2), ("sqq", ni % 2), ("rsq", ni % 2)
                    ni += 1
                    sy.op("act", [ppk], [rk], lambda e, raw=raw, pp=pp: e.activation(out=raw, in_=pp, func=AF.Copy))
                    sy.op("act", [ppk], [sk_], lambda e, sq=sq, pp=pp: e.activation(out=sq, in_=pp, func=AF.Square))
                    p2, p2k = self.ps()
                    sy.op("pe", [sk_, "consts"], [p2k],
                          lambda e, sq=sq, p2=p2: e.matmul(p2, self.bones, sq, start=True, stop=True))
                    sy.op("act", [p2k, "eps"], [rsk],
                          lambda e, rs=rs, p2=p2: e.activation(out=rs, in_=p2, func=AF.Ln, bias=self.eps_t, scale=1.0 / 64))
                    sy.op("act", [rsk], [rsk], lambda e, rs=rs: e.activation(out=rs, in_=rs, func=AF.Exp, scale=-0.5))
                    sy.op("dve", [rk, rsk, "vecs", "qgs"], [(dkey, t)],
                          lambda e, raw=raw, rs=rs, dst=dst, gcol=gcol: e.scalar_tensor_tensor(
                              out=dst[:, ts], in0=raw, scalar=gcol, in1=rs, op0=ALU.mult, op1=ALU.mult))
                pv, pvk = self.ps()
                for q4 in range(4):
                    for k in range(NK):
                        sy.op("pe", [("w3", s), ("xn", k, t)], [pvk],
                              lambda e, k=k, q4=q4: e.matmul(
                                  pv[:, q4 * 128:(q4 + 1) * 128],
                                  self.xn[:, k, 2 + t * TT + q4 * 128:2 + t * TT + (q4 + 1) * 128],
                                  w3[:, k, 2, :], start=(k == 0), stop=(k == NK - 1)))
                pv3 = pv.rearrange("p (a b) -> p a b", a=4)
                sy.op("act", [pvk], ["vpA"], lambda e, pv3=pv3: e.activation(
                    out=vpA[:, 4 * t:4 * t + 4, 0:64], in_=pv3[:, :, 0:64], func=AF.Copy))
                sy.op("act", [pvk], ["vpB"], lambda e, pv3=pv3: e.activation(
                    out=vpB[:, 4 * t:4 * t + 4, 64:128], in_=pv3[:, :, 64:128], func=AF.Copy))
            pairs = []
            for T in range(NT):
                for hd in range(2):
                    cmax = 4 * T + 3
                    for c in range(cmax, -1, -1):
                        pairs.append(dict(T=T, hd=hd, c=c, cmax=cmax, first=(hd == 0 and c == cmax),
                                          last=(hd == 1 and c == 0)))
            NB = 3
            o_banks = {}
            for T in range(NT):
                o_banks[T] = 6 + oi % 2
                oi += 1
            rstate = {"cur": 0}

            def stage1(n, p):
                i2 = n % NB
                hp = slice(p["hd"] * 64, p["hd"] * 64 + 64)
                T, c = p["T"], p["c"]
                Ts = slice(T * TT, (T + 1) * TT)
                jd = c - 4 * T
                pz, pzk = self.ps()
                sy.op("pe", [("kn", c // 4), ("qn", T)], [pzk],
                      lambda e: e.matmul(pz, kn[hp, c * 128:(c + 1) * 128], qn[hp, Ts], start=True, stop=True))
                sy.op("act", [pzk], [("e", i2)], lambda e: e.activation(out=e_t[i2], in_=pz, func=AF.Exp))
                sy.op("act", [("e", i2), "consts"], [("sp", i2)],
                      lambda e: e.activation(out=sp_t[i2], in_=e_t[i2], func=AF.Ln, bias=self.one_col, scale=1.0))
                if jd >= 0:
                    sy.op("dve", [("sp", i2), "consts2"], [("lk", i2)],
                          lambda e: e.tensor_tensor(out=lk_t[i2], in0=sp_t[i2],
                                                    in1=self.mneg[:, 384 - 128 * jd:896 - 128 * jd], op=ALU.mult))
                else:
                    sy.op("dve", [("sp", i2)], [("lk", i2)],
                          lambda e: e.tensor_scalar(out=lk_t[i2], in0=sp_t[i2], scalar1=-1.0, scalar2=None, op0=ALU.mult))

            def stage2(n, p):
                i2 = n % NB
                T, c, cmax = p["T"], p["c"], p["cmax"]
                jd = c - 4 * T
                if c == cmax:
                    rstate["cur"] = 0
                rcur = rstate["cur"]
                hp = slice(p["hd"] * 64, p["hd"] * 64 + 64)
                Ts = slice(T * TT, (T + 1) * TT)
                pt, ptk = self.ps()
                sy.op("pe", [("lk", i2), "consts2"], [ptk],
                      lambda e: e.matmul(pt, self.tri, lk_t[i2], start=True, stop=False))
                if c < cmax:
                    sy.op("pe", [("R", rcur), "consts2"], [ptk],
                          lambda e: e.matmul(pt, self.onesr, R_t[rcur], start=False, stop=False))
                sy.op("pe", [("kn", c // 4), ("qn", T)], [ptk],
                      lambda e: e.matmul(pt, kn[hp, c * 128:(c + 1) * 128], qn[hp, Ts], start=False, stop=True))
                if c > 0:
                    rn = 1 - rcur
                    if c == cmax:
                        sy.op("pool", [("lk", i2)], [("R", rn)], lambda e: e.tensor_copy(out=R_t[rn], in_=lk_t[i2]))
                    else:
                        sy.op("pool", [("lk", i2), ("R", rcur)], [("R", rn)],
                              lambda e: e.tensor_tensor(out=R_t[rn], in0=R_t[rcur], in1=lk_t[i2], op=ALU.add))
                    rstate["cur"] = rn
                sy.op("act", [ptk], [("att", i2)],
                      lambda e: e.activation(out=att_t[i2], in_=pt, func=AF.Exp))
                if jd >= 0:
                    sy.op("dve", [("att", i2), "consts2"], [("att", i2)],
                          lambda e: e.tensor_tensor(out=att_t[i2], in0=att_t[i2],
                                                    in1=self.m01[:, 384 - 128 * jd:896 - 128 * jd], op=ALU.mult))

            def stage3(n, p):
                nonlocal yi
                i2 = n % NB
                T, c = p["T"], p["c"]
                ob = o_banks[T]
                o_ps, ok = self.psum[ob], ("ps", ob)
                vp, vpk = (vpA, "vpA") if p["hd"] == 0 else (vpB, "vpB")
                sy.op("pe", [vpk, ("att", i2)], [ok],
                      lambda e: e.matmul(o_ps, vp[:, c, :], att_t[i2], start=p["first"], stop=p["last"]))
                if p["last"]:
                    y = self.ybf[yi % 2]
                    yk = ("ybf", yi % 2)
                    yi += 1
                    sy.op("act", [ok], [yk], lambda e: e.activation(out=y, in_=o_ps, func=AF.Copy))
                    self.outproj_acc(wo, ("wo", s), y, yk, T)

            npairs = len(pairs)
            for n in range(npairs + 2):
                if n < npairs:
                    stage1(n, pairs[n])
                if 1 <= n and n - 1 < npairs:
                    stage2(n - 1, pairs[n - 1])
                if 2 <= n:
                    stage3(n - 2, pairs[n - 2])

    def rwkv(self, j):
        sy = self.sy
        self.phase_begin()
        RT = 256
        NRT = S // RT
        CD = 0.6065306597126334
        cv = self.carve
        P1, GA, GB = cv([S], BF16), cv([S], BF16), cv([S], BF16)
        WA2, GA2, GB2 = cv([D], BF16), cv([D], BF16), cv([D], BF16)
        omm, okka = cv([48], F32), cv([8], F32)
        ident, mk4, mkL, cmask = cv([128], F32), cv([512], F32), cv([128], F32), cv([RT], F32)
        gneps, tiny = cv([1], F32), cv([1], F32)
        mark = self.arena_off
        onesw = cv([512], F32)
        LWs, LW = cv([NK, 320], F32), cv([NK, 2, 320], BF16)
        mu0 = VEC_COLS[f"mu{j}_0"]
        mucols = self.vecs[:, mu0:mu0 + 48]
        sy.op("dve", [], ["rc"], lambda e: e.memset(onesw, 1.0))
        sy.op("dve", [], ["rc"], lambda e: e.memset(gneps, GN_EPS))
        sy.op("dve", [], ["rc"], lambda e: e.memset(tiny, 1e-24))
        sy.op("dve", [], ["cmask"], lambda e: e.memset(cmask, 1.0))
        sy.op("dve", [], ["cmask"], lambda e: e.memset(cmask[:, 0:1], 0.0))
        sy.op("dve", [], ["cmask"], lambda e: e.memset(cmask[:, 128:129], 0.0))
        sy.op("dve", ["vecs"], ["omm"], lambda e: e.tensor_scalar(
            out=omm, in0=mucols, scalar1=-1.0, scalar2=1.0, op0=ALU.mult, op1=ALU.add))
        ka0 = VEC_COLS[f"k_a{j}"]
        sy.op("dve", ["vecs"], ["omm"], lambda e: e.tensor_scalar(
            out=okka, in0=self.vecs[:, ka0:ka0 + 8], scalar1=-1.0, scalar2=1.0, op0=ALU.mult, op1=ALU.add))
        sy.op("pool", ["rc"], ["rc2"], lambda e: e.affine_select(
            out=ident, in_=onesw[:, 0:128], pattern=[[-1, 128]], compare_op=ALU.is_equal, fill=0.0,
            base=0, channel_multiplier=1))
        sy.op("pool", ["rc"], ["rc2"], lambda e: e.affine_select(
            out=mkL, in_=onesw[:, 0:128], pattern=[[-1, 128]], compare_op=ALU.is_gt, fill=0.0,
            base=0, channel_multiplier=1))
        sy.op("pool", ["rc"], ["rc2"], lambda e: e.affine_select(
            out=mk4, in_=onesw, pattern=[[0, 2], [1, 2], [1, 128]], compare_op=ALU.is_gt, fill=0.0,
            base=0, channel_multiplier=-1))
        wr = lambda n, jj=j: self.w[n][jj]
        sy.dma("sp", LWs[:, :, 0:64], wr("rwkv_decay_w1").rearrange("(k p) c -> p k c", p=128), [], ["LWs"], "lws")
        sy.dma("sp", LWs[:, :, 64:128], wr("rwkv_iclr_a1").rearrange("(k p) c -> p k c", p=128), [], ["LWs"], "lws")
        sy.dma("sp", LWs[:, :, 128:288], wr("rwkv_gate_g1").rearrange("(k p) c -> p k c", p=128), [], ["LWs"], "lws")
        if j == 1:
            sy.dma("sp", LWs[:, :, 288:320], self.w["rwkv_vres_v1"][0].rearrange("(k p) c -> p k c", p=128), [], ["LWs"], "lws")
        sy.dma("pool", WA2[0:64, :], wr("rwkv_decay_w2"), [], ["W2"], "w2s")
        sy.dma("pool", WA2[64:128, :], wr("rwkv_iclr_a2"), [], ["W2"], "w2s")
        sy.dma("pool", GA2, wr("rwkv_gate_g2")[0:128, :], [], ["W2"], "w2s")
        sy.dma("pool", GB2[0:32, :], wr("rwkv_gate_g2")[128:160, :], [], ["W2"], "w2s")
        if j == 1:
            sy.dma("pool", GB2[32:64, :], self.w["rwkv_vres_v2"][0], [], ["W2"], "w2s")
        blocks = [(0, 64, 1), (64, 128, 4), (128, 288, 5)] + ([(288, 320, 3)] if j == 1 else [])
        for k in range(NK):
            for (c0, c1, m) in blocks:
                sy.op("act", ["LWs", "omm"], ["LW"], lambda e, k=k, c0=c0, c1=c1, m=m: e.activation(
                    out=LW[:, k, 0, c0:c1], in_=LWs[:, k, c0:c1], func=AF.Copy, scale=omm[:, m * 8 + k:m * 8 + k + 1]))
                sy.op("dve", ["LWs", "vecs"], ["LW"], lambda e, k=k, c0=c0, c1=c1, m=m: e.tensor_scalar(
                    out=LW[:, k, 1, c0:c1], in0=LWs[:, k, c0:c1], scalar1=self.vecs[:, mu0 + m * 8 + k:mu0 + m * 8 + k + 1],
                    scalar2=None, op0=ALU.mult))
        NL3 = 64 if j == 1 else 32
        for t in range(NT):
            ts = slice(t * TT, (t + 1) * TT)
            for (c0, M, which) in ((0, 128, 0), (128, 128, 1), (256, NL3, 2)):
                pst, psk = self.ps()
                for k in range(NK):
                    for sh in range(2):
                        rd = [("xn", k, t), "LW"] + ([("xn", k, t - 1)] if (sh and t > 0) else [])
                        sy.op("pe", rd, [psk], lambda e, k=k, sh=sh, c0=c0, M=M, pst=pst: e.matmul(
                            pst[0:M, :], LW[:, k, sh, c0:c0 + M], self.xn[:, k, 2 - sh + t * TT:2 - sh + (t + 1) * TT],
                            start=(k == 0 and sh == 0), stop=(k == NK - 1 and sh == 1)))
                if which == 0:
                    sy.op("act", [psk], [("P1", t)], lambda e, pst=pst: e.activation(out=P1[0:64, ts], in_=pst[0:64, :], func=AF.Tanh))
                    sy.op("act", [psk], [("P1", t)], lambda e, pst=pst: e.activation(out=P1[64:128, ts], in_=pst[64:128, :], func=AF.Copy))
                elif which == 1:
                    sy.op("act", [psk], [("GA", t)], lambda e, pst=pst: e.activation(out=GA[:, ts], in_=pst, func=AF.Sigmoid))
                else:
                    sy.op("act", [psk], [("GB", t)], lambda e, pst=pst: e.activation(out=GB[0:32, ts], in_=pst[0:32, :], func=AF.Sigmoid))
                    if j == 1:
                        sy.op("act", [psk], [("GB", t)], lambda e, pst=pst: e.activation(out=GB[32:64, ts], in_=pst[32:64, :], func=AF.Copy))
        sy.barrier()
        self.arena_off = mark
        Wst = cv([NK, 3, 128], F32)
        Wfs = [cv([NK, 3, 2, 128], BF16) for i in range(2)]
        wo = [cv([D], BF16) for i in range(2)]
        f1 = lambda: cv([RT], F32)
        r_sb, k_sb, sg, csp, pinv, asig, kk, tA, tB, tC, YC, vf = [f1() for _ in range(12)]
        Yb = [f1() for _ in range(2)]
        BN3 = [f1() for _ in range(3)]
        Bset = [(f1(), f1(), cv([2, 2, 128], BF16), cv([RT], BF16), cv([RT], BF16), None, cv([RT], BF16)) for _ in range(2)]
        yout = [cv([RT], BF16) for i in range(2)]
        BH, KH = cv([128], BF16), cv([128], BF16)
        TMp = cv([4, 256], BF16)
        AM32 = [cv([128], F32) for i in range(2)]
        AMb = [cv([3, 128], BF16) for i in range(2)]
        Np = [[cv([128], F32) for i in range(2)] for h in range(2)]
        NpT = [[cv([128], F32) for i in range(2)] for h in range(2)]
        Wt = [[cv([2, 64], F32) for i in range(2)] for h in range(2)]
        Wfin = cv([2, 256], BF16)
        TMb = cv([4, 128], BF16)
        WB = cv([2, 128], BF16)
        M1, NCt = cv([128], F32), cv([128], F32)
        MBD, G = cv([128], BF16), cv([128], BF16)
        ST = [cv([128], BF16) for i in range(2)]
        identb = cv([128], BF16)
        sy.op("dve", ["rc2"], ["rc2"], lambda e: e.tensor_copy(out=identb, in_=ident))
        sy.op("dve", [], ["TMp"], lambda e: e.memset(TMp, 0.0))
        sy.op("dve", [], ["Wfin"], lambda e: e.memset(Wfin, 0.0))
        both = lambda ap2: ap2.rearrange("p (a b) -> p a b", a=4)[:, 0:4:3, :]
        wnames = ("rwkv_w_r", "rwkv_w_k", "rwkv_w_v")
        muidx = (0, 2, 3)

        def load_stage(dc):
            for jj in range(3):
                src = self.w[wnames[jj]][j].rearrange("(k p) c -> p k c", p=128)[:, :, dc * 128:(dc + 1) * 128]
                sy.dma("sp", Wst[:, :, jj, :], src, [], ["Wst"], "wst")
            sy.dma("pool", wo[dc % 2], self.w["rwkv_w_o"][j, dc * 128:(dc + 1) * 128, :], [], [("wo", dc % 2)], f"rwo{dc % 2}")

        def fold_gen(dcn):
            Wfn = Wfs[dcn % 2]
            wfk = ("Wf", dcn % 2)
            for k in range(NK):
                for jj in range(3):
                    m = muidx[jj]
                    sy.op("act", ["Wst", "omm"], [wfk], lambda e, k=k, jj=jj, m=m: e.activation(
                        out=Wfn[:, k, jj, 0, :], in_=Wst[:, k, jj, :], func=AF.Copy, scale=omm[:, m * 8 + k:m * 8 + k + 1]))
                    sy.op("dve", ["Wst", "vecs"], [wfk], lambda e, k=k, jj=jj, m=m: e.tensor_scalar(
                        out=Wfn[:, k, jj, 1, :], in0=Wst[:, k, jj, :],
                        scalar1=self.vecs[:, mu0 + m * 8 + k:mu0 + m * 8 + k + 1], scalar2=None, op0=ALU.mult))
                yield

        load_stage(0)
        for _ in fold_gen(0):
            pass
        stt = {"sti": 0, "yi": 0}
        def make_dc(dc):
            dcs = slice(dc * 128, (dc + 1) * 128)
            Wf = Wfs[dc % 2]
            wfk_cur = ("Wf", dc % 2)
            wod, wok = wo[dc % 2], ("wo", dc % 2)
            def make_ctx(rt, s_):
                b = s_ % 2
                tok0 = rt * RT
                tk = slice(tok0, tok0 + RT)
                t5 = tok0 // TT
                xr = lambda k: [("xn", k, t5)] + ([("xn", k, t5 - 1)] if (tok0 % TT == 0 and t5 > 0) else [])

                def proj(jj):
                    pst, psk = self.ps()
                    for k in range(NK):
                        for sh in range(2):
                            sy.op("pe", xr(k) + [wfk_cur], [psk], lambda e, k=k, sh=sh, pst=pst: e.matmul(
                                pst[:, 0:RT], Wf[:, k, jj, sh, :], self.xn[:, k, 2 - sh + tok0:2 - sh + tok0 + RT],
                                start=(k == 0 and sh == 0), stop=(k == NK - 1 and sh == 1)))
                    return pst[:, 0:RT], psk

                def small(lhsT, rhs, reads, M=128, pst=None, psk=None, start=True, stop=True, n=RT, c0=0):
                    if pst is None:
                        pst, psk = self.ps()
                    sy.op("pe", reads, [psk], lambda e: e.matmul(pst[0:M, c0:c0 + n], lhsT, rhs, start=start, stop=stop))
                    return pst, psk

                V = lambda nm, dc=dc: self.vcol(nm, dc)
                v_sb, cs, AR, BT, KT, BN, vbf = Bset[b]
                BN = BN3[s_ % 3]
                Y = Yb[s_ % 2]
                yk_ = ("Y", s_ % 2)
                bnk = ("BN", s_ % 3)
                kq = lambda nm: (nm, b)
                return dict(locals())

            def prologue(rt, s_):
                b = s_ % 2
                c_ = make_ctx(rt, s_)
                tok0, tk, t5, xr, proj, small, V = (c_[n] for n in ('tok0', 'tk', 't5', 'xr', 'proj', 'small', 'V'))
                v_sb, cs, AR, BT, KT, BN, vbf, kq, Y, yk_, bnk = (c_[n] for n in ('v_sb', 'cs', 'AR', 'BT', 'KT', 'BN', 'vbf', 'kq', 'Y', 'yk_', 'bnk'))
                pr, prk = proj(0)
                sy.op("act", [prk], ["r_sb"], lambda e: e.activation(out=r_sb, in_=pr, func=AF.Copy))
                pk, pkk = proj(1)
                sy.op("act", [pkk], ["k_sb"], lambda e: e.activation(out=k_sb, in_=pk, func=AF.Copy))
                pv, pvk = proj(2)
                sy.op("act", [pvk], [kq("v_sb")], lambda e: e.activation(out=v_sb, in_=pv, func=AF.Copy))
                yield
                plw, plwk = small(WA2[0:64, dcs], P1[0:64, tk], ["W2", ("P1", t5)])
                sy.op("act", [plwk, "vecs"], ["sg"], lambda e: e.activation(
                    out=sg, in_=plw[:, 0:RT], func=AF.Sigmoid, bias=V(f"w0{j}"), scale=1.0))
                pa, pak = small(WA2[64:128, dcs], P1[64:128, tk], ["W2", ("P1", t5)])
                sy.op("act", [pak, "vecs"], ["asig"], lambda e: e.activation(
                    out=asig, in_=pa[:, 0:RT], func=AF.Sigmoid, bias=V(f"a0{j}"), scale=1.0))
                if j == 1:
                    pg_, pgk_ = small(GB2[32:64, dcs], GB[32:64, tk], ["W2", ("GB", t5)])
                    sy.op("act", [pgk_, "vecs"], ["tB"], lambda e: e.activation(
                        out=tB, in_=pg_[:, 0:RT], func=AF.Sigmoid, bias=V("v0"), scale=1.0))
                    sy.dma("sp", vf, self.vfirst[dcs, tk], [("vfd", dc, rt)], ["vf"], "vfl")
                    sy.op("dve", ["vf", kq("v_sb")], ["vf"], lambda e: e.tensor_tensor(out=vf, in0=vf, in1=v_sb, op=ALU.subtract))
                    sy.op("dve", ["vf", "tB"], ["vf"], lambda e: e.tensor_tensor(out=vf, in0=vf, in1=tB, op=ALU.mult))
                    sy.op("dve", ["vf", kq("v_sb")], [kq("v_sb")], lambda e: e.tensor_tensor(out=v_sb, in0=v_sb, in1=vf, op=ALU.add))
                else:
                    sy.dma("sp", self.vfirst[dcs, tk], v_sb, [kq("v_sb")], [("vfd", dc, rt)], f"vfs{b}")
                sy.op("act", [kq("v_sb")], [kq("vbf")], lambda e: e.activation(out=vbf, in_=v_sb, func=AF.Copy))
                sy.op("dve", ["sg", "cmask"], [kq("cs")], lambda e: e.tensor_tensor_scan(
                    out=cs, data0=cmask, data1=sg, initial=0.0, op0=ALU.mult, op1=ALU.add))
                sy.op("dve", [kq("cs"), "sg"], ["csp"], lambda e: e.tensor_tensor(out=csp, in0=cs, in1=sg, op=ALU.subtract))
                sy.op("act", [kq("cs")], ["pinv"], lambda e: e.activation(out=pinv, in_=cs, func=AF.Exp, scale=CD))
                sy.op("act", [kq("cs")], [kq("cs")], lambda e: e.activation(out=cs, in_=cs, func=AF.Exp, scale=-CD))
                yield
                sy.op("act", ["csp"], ["csp"], lambda e: e.activation(out=csp, in_=csp, func=AF.Exp, scale=-CD))
                sy.op("dve", ["k_sb", "vecs"], ["kk"], lambda e: e.tensor_scalar(
                    out=kk, in0=k_sb, scalar1=V(f"k_k{j}"), scalar2=None, op0=ALU.mult))
                sy.op("act", ["kk"], ["tA"], lambda e: e.activation(out=tA, in_=kk, func=AF.Square))
                pss, pssk = small(self.bones, tA, ["tA", "consts"])
                sy.op("act", [pssk, "rc"], ["tA"], lambda e: e.activation(out=tA, in_=pss[:, 0:RT], func=AF.Ln, bias=tiny, scale=1.0))
                yield
                sy.op("act", ["tA"], ["tA"], lambda e: e.activation(out=tA, in_=tA, func=AF.Exp, scale=-0.5))
                sy.op("dve", ["kk", "tA"], ["kk"], lambda e: e.tensor_tensor(out=kk, in0=kk, in1=tA, op=ALU.mult))
                sy.op("dve", ["asig", "vecs", "omm"], ["tB"], lambda e: e.tensor_scalar(
                    out=tB, in0=asig, scalar1=V(f"k_a{j}"), scalar2=okka[:, dc:dc + 1], op0=ALU.mult, op1=ALU.add))
                sy.op("dve", ["k_sb", "tB"], ["k_sb"], lambda e: e.tensor_tensor(out=k_sb, in0=k_sb, in1=tB, op=ALU.mult))
                yield
                c3 = lambda ap: ap.rearrange("p (a b) -> p a b", a=2)
                sy.op("dve", ["kk", "csp"], [kq("AR0")], lambda e: e.scalar_tensor_tensor(
                    out=AR[:, :, 0, :], in0=c3(kk), scalar=-1.0, in1=c3(csp), op0=ALU.mult, op1=ALU.mult))
                sy.op("dve", ["r_sb", kq("cs")], [kq("AR1")], lambda e: e.tensor_tensor(
                    out=AR[:, :, 1, :], in0=c3(r_sb), in1=c3(cs), op=ALU.mult))
                sy.op("dve", ["kk", "asig"], ["tB"], lambda e: e.tensor_tensor(out=tB, in0=kk, in1=asig, op=ALU.mult))
                sy.op("dve", ["tB", "pinv"], [kq("BT")], lambda e: e.tensor_tensor(out=BT, in0=tB, in1=pinv, op=ALU.mult))
                sy.op("dve", ["k_sb", "pinv"], [kq("KT")], lambda e: e.tensor_tensor(out=KT, in0=k_sb, in1=pinv, op=ALU.mult))
                yield
                sy.op("dve", ["r_sb", "k_sb", "vecs"], ["tA"], lambda e: e.scalar_tensor_tensor(
                    out=tA, in0=r_sb, scalar=V(f"r_k{j}"), in1=k_sb, op0=ALU.mult, op1=ALU.mult))
                pbn, pbnk = small(self.bones, tA, ["tA", "consts"])
                sy.op("dve", [pbnk, kq("v_sb")], [bnk], lambda e: e.tensor_tensor(out=BN, in0=pbn[:, 0:RT], in1=v_sb, op=ALU.mult))
                yield

            def scanepi(rt, s_):
                c_ = make_ctx(rt, s_)
                if rt == 0:
                    stt["sti"] = 0
                    sy.op("dve", [], [("ST", 0)], lambda e: e.memset(ST[0], 0.0))
                tok0, tk, t5, xr, proj, small, V = (c_[n] for n in ('tok0', 'tk', 't5', 'xr', 'proj', 'small', 'V'))
                v_sb, cs, AR, BT, KT, BN, vbf, kq, Y, yk_, bnk = (c_[n] for n in ('v_sb', 'cs', 'AR', 'BT', 'KT', 'BN', 'vbf', 'kq', 'Y', 'yk_', 'bnk'))
                for ci in range(2):
                    cc = slice(ci * 128, (ci + 1) * 128)
                    pcol = cs[:, ci * 128 + 127:ci * 128 + 128]
                    sy.op("dve", [kq("BT"), kq("cs")], ["BH"], lambda e: e.tensor_scalar(out=BH, in0=BT[:, cc], scalar1=pcol, scalar2=None, op0=ALU.mult))
                    sy.op("dve", [kq("KT"), kq("cs")], ["KH"], lambda e: e.tensor_scalar(out=KH, in0=KT[:, cc], scalar1=pcol, scalar2=None, op0=ALU.mult))
                    ptm, ptmk = self.ps()
                    ptmb = ptm.bitcast(BF16)
                    for q, (src, skey) in enumerate(((AR[:, ci, 0, :], kq("AR0")), (BH, "BH"), (KH, "KH"), (vbf[:, cc], kq("vbf")))):
                        sy.op("pe", [skey, "rc2"], [ptmk], lambda e, q=q, src=src: e.transpose(
                            out=ptmb[:, q * 128:(q + 1) * 128], in_=src, identity=identb))
                    ptm3 = ptmb[:, 0:512].rearrange("p (a b) -> p a b", a=4)
                    sy.op("act", [ptmk], ["TMp"], lambda e: e.activation(out=TMp[:, :, 0:64], in_=ptm3[:, :, 0:64], func=AF.Copy))
                    sy.op("act", [ptmk], ["TMp"], lambda e: e.activation(out=TMp[:, :, 192:256], in_=ptm3[:, :, 64:128], func=AF.Copy))
                    sy.op("act", [ptmk], ["TMb"], lambda e: e.activation(out=TMb, in_=ptm3, func=AF.Copy))
                    hcs = (slice(0, 64), slice(192, 256))
                    for hd in range(2):
                        yield
                        hp = slice(hd * 64, hd * 64 + 64)
                        arh = AR[hp, ci, :, :]
                        pam, pamk = self.ps()
                        sy.op("pe", [kq("BT"), kq("AR0"), kq("AR1")], [pamk], lambda e, pam=pam, arh=arh, hp=hp: e.matmul(
                            pam[:, 0:256], BT[hp, cc], arh, start=True, stop=True))
                        sy.op("pe", [kq("KT"), kq("AR0"), kq("AR1")], [pamk], lambda e, pam=pam, arh=arh, hp=hp: e.matmul(
                            pam[:, 256:512], KT[hp, cc], arh, start=True, stop=True))
                        sy.op("dve", [pamk, "rc2"], [("AM", hd)], lambda e, pam=pam, hd=hd: e.tensor_tensor(
                            out=AM32[hd], in0=pam[:, 0:128], in1=mk4[:, 0:128], op=ALU.mult))
                        sy.op("dve", [pamk, "rc2"], [("AM", hd)], lambda e, pam=pam, hd=hd: e.tensor_tensor(
                            out=AMb[hd], in0=pam[:, 128:512].rearrange("p (a b) -> p a b", a=3),
                            in1=mk4[:, 128:512].rearrange("p (a b) -> p a b", a=3), op=ALU.mult))
                        pnt, pntk = self.ps()
                        sy.op("pe", [kq("BT"), kq("AR0")], [pntk], lambda e, pnt=pnt, hp=hp: e.matmul(
                            pnt[:, 0:128], AR[hp, ci, 0, :], BT[hp, cc], start=True, stop=True))
                        sy.op("dve", [pntk, "rc2"], [("NpT", hd, 0)], lambda e, pnt=pnt, hd=hd: e.tensor_tensor(
                            out=NpT[hd][0], in0=pnt[:, 0:128], in1=mkL, op=ALU.mult))
                        sy.op("pool", ["TMp"], [("Wt", hd, 0)], lambda e, hd=hd: e.tensor_copy(out=Wt[hd][0][:, 0, :], in_=TMp[:, 0, hcs[hd]]))
                        pxv, pxvk = self.ps()
                        sy.op("pe", [("AM", hd), "TMp"], [pxvk], lambda e, pxv=pxv, hd=hd: e.matmul(
                            pxv[:, 0:64], AMb[hd][:, 1, :], TMp[:, 3, hcs[hd]], start=True, stop=True))
                        sy.op("act", [pxvk], [("Wt", hd, 0)], lambda e, pxv=pxv, hd=hd: e.activation(
                            out=Wt[hd][0][:, 1, :], in_=pxv[:, 0:64], func=AF.Copy))
                    yield
                    for lvl in range(7):
                        yield
                        cur, nxt = lvl % 2, (lvl + 1) % 2
                        for hd in range(2):
                            npc = AM32[hd] if lvl == 0 else Np[hd][cur]
                            npk = ("AM", hd) if lvl == 0 else ("Np", hd, cur)
                            wcur = Wt[hd][cur]
                            pw, pwk = self.ps()
                            sy.op("pe", [npk, ("Wt", hd, cur)], [pwk], lambda e, pw=pw, npc=npc, wcur=wcur: e.matmul(
                                pw[:, 0:128], npc, wcur.rearrange("p a b -> p (a b)"), start=True, stop=True))
                            pw3 = pw[:, 0:128].rearrange("p (a b) -> p a b", a=2)
                            if lvl < 6:
                                sy.op("dve", [pwk, ("Wt", hd, cur)], [("Wt", hd, nxt)], lambda e, pw3=pw3, wcur=wcur, hd=hd, nxt=nxt: e.tensor_tensor(
                                    out=Wt[hd][nxt], in0=pw3, in1=wcur, op=ALU.add))
                                pn, pnk = self.ps()
                                sy.op("pe", [npk, ("NpT", hd, cur)], [pnk], lambda e, pn=pn, npc=npc, hd=hd, cur=cur: e.matmul(
                                    pn[:, 0:128], NpT[hd][cur], npc, start=True, stop=True))
                                sy.op("act", [pnk], [("Np", hd, nxt)], lambda e, pn=pn, hd=hd, nxt=nxt: e.activation(
                                    out=Np[hd][nxt], in_=pn[:, 0:128], func=AF.Copy))
                                pn2, pn2k = self.ps()
                                sy.op("pe", [npk, ("NpT", hd, cur)], [pn2k], lambda e, pn2=pn2, npc=npc, hd=hd, cur=cur: e.matmul(
                                    pn2[:, 0:128], npc, NpT[hd][cur], start=True, stop=True))
                                sy.op("act", [pn2k], [("NpT", hd, nxt)], lambda e, pn2=pn2, hd=hd, nxt=nxt: e.activation(
                                    out=NpT[hd][nxt], in_=pn2[:, 0:128], func=AF.Copy))
                            else:
                                sy.op("dve", [pwk, ("Wt", hd, cur)], ["Wfin"], lambda e, pw3=pw3, wcur=wcur, hd=hd: e.tensor_tensor(
                                    out=Wfin[:, :, hcs[hd]], in0=pw3, in1=wcur, op=ALU.add))
                                sy.op("dve", [pwk, ("Wt", hd, cur)], ["WB"], lambda e, pw3=pw3, wcur=wcur, hd=hd: e.tensor_tensor(
                                    out=WB[:, :, hd * 64:(hd + 1) * 64], in0=pw3, in1=wcur, op=ALU.add))
                    yield
                    Ah_b, Uh_b = WB[:, 0, :], WB[:, 1, :]
                    Bh_b, Kh_b, VT_b = TMb[:, 1, :], TMb[:, 2, :], TMb[:, 3, :]
                    stc, stn = ST[stt['sti'] % 2], ST[(stt['sti'] + 1) % 2]
                    stck, stnk = ("ST", stt['sti'] % 2), ("ST", (stt['sti'] + 1) % 2)
                    stt['sti'] += 1
                    pm, pmk = self.ps()
                    sy.op("pe", ["WB", "TMb"], [pmk], lambda e, pm=pm: e.matmul(pm[:, 0:128], Ah_b, Bh_b, start=True, stop=True))
                    sy.op("dve", [pmk, "consts"], ["M1"], lambda e, pm=pm: e.tensor_tensor(out=M1, in0=pm[:, 0:128], in1=self.bones, op=ALU.mult))
                    sy.op("dve", ["M1", "rc2", kq("cs")], ["MBD"], lambda e: e.scalar_tensor_tensor(
                        out=MBD, in0=ident, scalar=pcol, in1=M1, op0=ALU.mult, op1=ALU.add))
                    pn_, pnk_ = self.ps()
                    sy.op("pe", ["WB", "TMb"], [pnk_], lambda e, pn_=pn_: e.matmul(pn_[:, 0:128], Bh_b, Uh_b, start=True, stop=False))
                    sy.op("pe", ["TMb"], [pnk_], lambda e, pn_=pn_: e.matmul(pn_[:, 0:128], Kh_b, VT_b, start=False, stop=True))
                    sy.op("dve", [pnk_, "consts"], ["NCt"], lambda e, pn_=pn_: e.tensor_tensor(out=NCt, in0=pn_[:, 0:128], in1=self.bones, op=ALU.mult))
                    yield
                    pg, pgk = self.ps()
                    sy.op("pe", ["Wfin", ("AM", 0)], [pgk], lambda e, pg=pg: e.matmul(pg[:, 0:128], Wfin[:, 0, 0:128], AMb[0][:, 0, :], start=True, stop=False))
                    sy.op("pe", ["Wfin", ("AM", 1)], [pgk], lambda e, pg=pg: e.matmul(pg[:, 0:128], Wfin[:, 0, 128:256], AMb[1][:, 0, :], start=False, stop=True))
                    sy.op("dve", [pgk, kq("AR1")], ["G"], lambda e, pg=pg: e.tensor_tensor(out=G, in0=pg[:, 0:128], in1=AR[:, ci, 1, :], op=ALU.add))
                    yield
                    py, pyk = self.ps()
                    sy.op("pe", [stck, "G"], [pyk], lambda e, py=py, stc=stc: e.matmul(py[:, 0:128], stc, G, start=True, stop=False))
                    sy.op("pe", ["Wfin", ("AM", 0)], [pyk], lambda e, py=py: e.matmul(py[:, 0:128], Wfin[:, 1, 0:128], AMb[0][:, 0, :], start=False, stop=False))
                    sy.op("pe", ["Wfin", ("AM", 1)], [pyk], lambda e, py=py: e.matmul(py[:, 0:128], Wfin[:, 1, 128:256], AMb[1][:, 0, :], start=False, stop=False))
                    sy.op("pe", ["TMp", ("AM", 0)], [pyk], lambda e, py=py: e.matmul(py[:, 0:128], TMp[:, 3, 0:128], AMb[0][:, 2, :], start=False, stop=False))
                    sy.op("pe", ["TMp", ("AM", 1)], [pyk], lambda e, py=py: e.matmul(py[:, 0:128], TMp[:, 3, 128:256], AMb[1][:, 2, :], start=False, stop=True))
                    sy.op("act", [pyk], [yk_], lambda e, py=py: e.activation(out=Y[:, cc], in_=py[:, 0:128], func=AF.Copy))
                    yield
                    pst_, pstk_ = self.ps()
                    sy.op("pe", ["MBD", stck], [pstk_], lambda e, pst_=pst_, stc=stc: e.matmul(pst_[:, 0:128], MBD, stc, start=True, stop=True))
                    sy.op("dve", [pstk_, "NCt"], [stnk], lambda e, pst_=pst_, stn=stn: e.tensor_tensor(out=stn, in0=pst_[:, 0:128], in1=NCt, op=ALU.add))
                yield

            def epilogue(rt, s_):
                c_ = make_ctx(rt, s_)
                tok0, tk, t5, xr, proj, small, V = (c_[n] for n in ('tok0', 'tk', 't5', 'xr', 'proj', 'small', 'V'))
                v_sb, cs, AR, BT, KT, BN, vbf, kq, Y, yk_, bnk = (c_[n] for n in ('v_sb', 'cs', 'AR', 'BT', 'KT', 'BN', 'vbf', 'kq', 'Y', 'yk_', 'bnk'))
                pmn, pmnk = small(self.bones, Y, [yk_, "consts"])
                sy.op("dve", [pmnk, yk_], ["YC"], lambda e: e.scalar_tensor_tensor(
                    out=YC, in0=pmn[:, 0:RT], scalar=-1.0 / 64, in1=Y, op0=ALU.mult, op1=ALU.add))
                sy.op("act", ["YC"], ["tC"], lambda e: e.activation(out=tC, in_=YC, func=AF.Square))
                pvr, pvrk = small(self.bones, tC, ["tC", "consts"])
                sy.op("act", [pvrk, "rc"], ["tC"], lambda e: e.activation(out=tC, in_=pvr[:, 0:RT], func=AF.Ln, bias=gneps, scale=1.0 / 64))
                sy.op("act", ["tC"], ["tC"], lambda e: e.activation(out=tC, in_=tC, func=AF.Exp, scale=-0.5))
                sy.op("dve", ["YC", "tC"], ["YC"], lambda e: e.tensor_tensor(out=YC, in0=YC, in1=tC, op=ALU.mult))
                sy.op("dve", ["YC", "vecs"], ["YC"], lambda e: e.tensor_scalar(
                    out=YC, in0=YC, scalar1=V(f"lnx_w{j}"), scalar2=V(f"lnx_b{j}"), op0=ALU.mult, op1=ALU.add))
                sy.op("dve", ["YC", bnk], ["YC"], lambda e: e.tensor_tensor(out=YC, in0=YC, in1=BN, op=ALU.add))
                yield
                pgt, pgtk = small(GA2[:, dcs], GA[:, tk], ["W2", ("GA", t5)], stop=False)
                small(GB2[0:32, dcs], GB[0:32, tk], ["W2", ("GB", t5)], pst=pgt, psk=pgtk, start=False, stop=True)
                yo = yout[stt['yi'] % 2]
                yok = ("yout", stt['yi'] % 2)
                stt['yi'] += 1
                sy.op("dve", ["YC", pgtk], [yok], lambda e, yo=yo: e.tensor_tensor(out=yo, in0=YC, in1=pgt[:, 0:RT], op=ALU.mult))
                for m in range(NK):
                    po, pok = self.ps()
                    sy.op("pe", [wok, yok], [pok], lambda e, m=m, po=po, yo=yo: e.matmul(
                        po[:, 0:RT], wod[:, m * 128:(m + 1) * 128], yo, start=True, stop=True))
                    sy.op("dve", [pok, ("h", m, t5)], [("h", m, t5)], lambda e, m=m, po=po: e.tensor_tensor(
                        out=self.h[:, m, tk], in0=po[:, 0:RT], in1=self.h[:, m, tk], op=ALU.add))
                yield

            return prologue, scanepi, epilogue

        dcg = {}

        def DCG(dc):
            if dc not in dcg:
                dcg[dc] = make_dc(dc)
            return dcg[dc]

        NS = NK * NRT
        for _ in DCG(0)[0](0, 0):
            pass
        for s_ in range(NS + 1):
            gens = []
            if s_ < NS:
                dc, rt = divmod(s_, NRT)
                gens.append(DCG(dc)[1](rt, s_))
            if s_ >= 1:
                dcp, rtp = divmod(s_ - 1, NRT)
                gens.append(DCG(dcp)[2](rtp, s_ - 1))
            if s_ + 1 < NS:
                dcn, rtn = divmod(s_ + 1, NRT)
                gens.append(DCG(dcn)[0](rtn, s_ + 1))
            if s_ < NS and dc + 1 < NK:
                if rt == 1:
                    load_stage(dc + 1)
                if rt == 4:
                    gens.append(fold_gen(dc + 1))
            while gens:
                for g_ in list(gens):
                    try:
                        next(g_)
                    except StopIteration:
                        gens.remove(g_)


ALL_LAYERS = []
for _l in range(DEPTH):
    ALL_LAYERS += [("mix", _l), ("mlp", _l)]

WEIGHT_NAMES = ["mlp_up", "mlp_down", "rwkv_w_r", "rwkv_w_k", "rwkv_w_v", "rwkv_w_o",
                "rwkv_decay_w1", "rwkv_decay_w2", "rwkv_iclr_a1", "rwkv_iclr_a2",
                "rwkv_gate_g1", "rwkv_gate_g2", "rwkv_vres_v1", "rwkv_vres_v2",
                "conv_w_in", "conv_w_out", "sb_w_qkv", "sb_w_o"]


def run(inputs, layers, n_cores=8, trace=False):
    inp = {k: np.asarray(v) for k, v in inputs.items()}
    prog = Prog(layers)
    nc = prog.build()
    vecs = pack_vecs(inp)
    wts = {n: np.ascontiguousarray(inp[n], dtype=np.float32) for n in WEIGHT_NAMES}
    in_maps = []
    for b in range(n_cores):
        m = {"xT": np.ascontiguousarray(inp["x"][b].T), "vecs": vecs}
        m.update(wts)
        in_maps.append(m)
    res = run_bass_kernel_spmd(nc, in_maps, core_ids=list(range(n_cores)), trace=trace)
    out = np.stack([np.ascontiguousarray(r["outT"].T) for r in res.results], axis=0)
    return out, res, prog


def kernel(**inputs):
    out, _, _ = run(inputs, ALL_LAYERS)
    return out.astype(np.float32)
```

```python
import numpy as np
import concourse.bass as bass
import concourse.mybir as mybir
from concourse.bass_utils import run_bass_kernel_spmd

F32 = mybir.dt.float32
F32R = mybir.dt.float32r
BF16 = mybir.dt.bfloat16
AF = mybir.ActivationFunctionType
ALU = mybir.AluOpType

D = 1024
S = 2048
NK = 8
TT = 512
NT = S // TT
DFF = 4096
DEPTH = 4
RMS_EPS = 1e-6
GN_EPS = 64e-5


class Sy:
    def __init__(self, nc):
        self.nc = nc
        self.eng = {"pe": nc.tensor, "dve": nc.vector, "act": nc.scalar,
                    "pool": nc.gpsimd, "sp": nc.sync}
        self.sem = {e: nc.alloc_semaphore("s_" + e) for e in self.eng}
        self.cnt = {e: 0 for e in self.eng}
        self.pend = {e: False for e in self.eng}
        self.waited = {e: {} for e in self.eng}
        self.last_w = {}
        self.readers = {}
        self.dsem = {}
        self.dcnt = {}
        self.n_wait = 0
        self.n_inst = 0

    def _wait(self, e, deps):
        need = {}
        for (sk, v) in deps:
            if need.get(sk, 0) < v:
                need[sk] = v
        for sk, v in need.items():
            if sk == e and e == "pe":
                continue
            if self.waited[e].get(sk, 0) >= v:
                continue
            sem = self.sem[sk] if sk in self.sem else self.dsem[sk]
            self.eng[e].wait_ge(sem, v)
            self.waited[e][sk] = v
            self.n_wait += 1

    def _deps(self, reads, writes):
        deps = []
        for k in reads:
            if k in self.last_w:
                deps.append(self.last_w[k])
        for k in writes:
            if k in self.last_w:
                deps.append(self.last_w[k])
            deps.extend(self.readers.get(k, {}).items())
        return deps

    def _record(self, tok, reads, writes):
        for k in reads:
            r = self.readers.setdefault(k, {})
            if r.get(tok[0], 0) < tok[1]:
                r[tok[0]] = tok[1]
        for k in writes:
            self.last_w[k] = tok
            self.readers[k] = {}

    def op(self, e, reads, writes, emit, inc=True):
        psr = [k for k in reads if isinstance(k, tuple) and k[0] == "ps" and k not in writes]
        if psr:
            writes = list(writes) + psr
        self._wait(e, self._deps(reads, writes))
        inst = emit(self.eng[e])
        self.n_inst += 1
        inc = True
        if inc:
            self.cnt[e] += 1
            inst.then_inc(self.sem[e], 1)
            self.pend[e] = False
            tok = (e, self.cnt[e])
        else:
            self.pend[e] = True
            tok = (e, self.cnt[e] + 1)
        self._record(tok, reads, writes)
        return inst

    def dma(self, e, out, in_, reads, writes, sk, **kw):
        if sk not in self.dsem:
            self.dsem[sk] = self.nc.alloc_semaphore("d_" + sk)
            self.dcnt[sk] = 0
        self._wait(e, self._deps(reads, writes))
        inst = self.eng[e].dma_start(out=out, in_=in_, **kw)
        self.dcnt[sk] += 16
        inst.then_inc(self.dsem[sk], 16)
        self.n_inst += 1
        self._record((sk, self.dcnt[sk]), reads, writes)
        return inst

    def barrier(self):
        for e in self.eng:
            deps = [(e2, self.cnt[e2]) for e2 in self.eng if e2 != e and self.cnt[e2] > 0]
            deps += [(sk, self.dcnt[sk]) for sk in self.dsem if self.dcnt[sk] > 0]
            self._wait(e, deps)

    def wait_all(self, e, keys):
        deps = []
        for k in keys:
            if k in self.last_w:
                deps.append(self.last_w[k])
            deps.extend(self.readers.get(k, {}).items())
        self._wait(e, deps)


VEC_COLS = {}


def _vec_layout():
    cols = {}
    off = 0

    def add(name, n=NK):
        nonlocal off
        cols[name] = off
        off += n
    for l in range(DEPTH):
        add(f"mix_norm{l}")
        add(f"mlp_norm{l}")
    for j in range(2):
        for m in range(6):
            add(f"mu{j}_{m}")
        for nm in ("w0", "a0", "k_k", "k_a", "r_k", "lnx_w", "lnx_b"):
            add(f"{nm}{j}")
    add("v0")
    for c in range(3):
        add(f"conv_w{c}")
    add("q_gain", 1)
    add("k_gain", 1)
    return cols, off


VEC_COLS, NVEC = _vec_layout()


def pack_vecs(inp):
    tab = np.zeros((128, NVEC), np.float32)

    def put(name, v):
        v = np.asarray(v, np.float32).reshape(-1)
        c = VEC_COLS[name]
        if v.size == D:
            tab[:, c:c + NK] = v.reshape(NK, 128).T
        else:
            tab[:, c] = np.concatenate([v, v])
    for l in range(DEPTH):
        put(f"mix_norm{l}", inp["mix_norm"][l])
        put(f"mlp_norm{l}", inp["mlp_norm"][l])
    for j in range(2):
        for m in range(6):
            put(f"mu{j}_{m}", inp["rwkv_mu"][j, m])
        put(f"w0{j}", inp["rwkv_decay_w0"][j])
        put(f"a0{j}", inp["rwkv_iclr_a0"][j])
        put(f"k_k{j}", inp["rwkv_k_k"][j])
        put(f"k_a{j}", inp["rwkv_k_a"][j])
        put(f"r_k{j}", inp["rwkv_r_k"][j])
        put(f"lnx_w{j}", inp["rwkv_lnx_w"][j])
        put(f"lnx_b{j}", inp["rwkv_lnx_b"][j])
    put("v0", inp["rwkv_vres_v0"][0])
    for c in range(3):
        put(f"conv_w{c}", inp["conv_w"][0, c])
    put("q_gain", inp["sb_q_norm"][0])
    put("k_gain", inp["sb_k_norm"][0])
    return tab


class Prog:
    def __init__(self, layers, n_layers_mlp=None):
        self.layers = layers
        nc = bass.Bass("TRN2", target_bir_lowering=False)
        self.nc = nc
        self.sy = Sy(nc)
        self._n = 0
        dt = nc.dram_tensor
        self.xT = dt("xT", [D, S], F32, kind="ExternalInput").ap()
        self.vecs_d = dt("vecs", [128, NVEC], F32, kind="ExternalInput").ap()
        self.outT = dt("outT", [D, S], F32, kind="ExternalOutput").ap()
        self.w = {}
        for name, shape in (
            ("mlp_up", [DEPTH, D, DFF]), ("mlp_down", [DEPTH, DFF, D]),
            ("rwkv_w_r", [2, D, D]), ("rwkv_w_k", [2, D, D]), ("rwkv_w_v", [2, D, D]),
            ("rwkv_w_o", [2, D, D]),
            ("rwkv_decay_w1", [2, D, 64]), ("rwkv_decay_w2", [2, 64, D]),
            ("rwkv_iclr_a1", [2, D, 64]), ("rwkv_iclr_a2", [2, 64, D]),
            ("rwkv_gate_g1", [2, D, 160]), ("rwkv_gate_g2", [2, 160, D]),
            ("rwkv_vres_v1", [1, D, 32]), ("rwkv_vres_v2", [1, 32, D]),
            ("conv_w_in", [1, D, 3 * D]), ("conv_w_out", [1, D, D]),
            ("sb_w_qkv", [1, D, 3 * D]), ("sb_w_o", [1, D, D]),
        ):
            self.w[name] = dt(name, shape, F32, kind="ExternalInput").ap()
        self.vfirst = dt("vfirst_scratch", [D, S], F32, kind="Internal").ap()
        self.psum = [nc.alloc_psum_tensor(f"ps{i}", [128, 512], F32).ap() for i in range(8)]
        self.ps_i = 0

    def sb(self, name, shape, dtype):
        return self.nc.alloc_sbuf_tensor(name, shape, dtype).ap()

    def ps(self):
        i = self.ps_i
        self.ps_i = (i + 1) % 6
        return self.psum[i], ("ps", i)

    def phase_begin(self):
        self.sy.barrier()
        self.arena_off = 0

    def carve(self, shape, dtype):
        n = 1
        for d in shape:
            n *= d
        nb = n * (4 if dtype in (F32, F32R) else 2)
        nb = (nb + 31) // 32 * 32
        off = self.arena_off
        assert off + nb <= self.ARENA * 2, (off, nb, self.ARENA * 2)
        self.arena_off = off + nb
        self._n += 1
        return self.nc.alloc_sbuf_tensor_at(f"cv{self._n}", [128] + list(shape), dtype,
                                            offset=self.arena_base + off).ap()

    def vcol(self, name, k=0):
        c = VEC_COLS[name] + k
        return self.vecs[:, c:c + 1]

    def build(self):
        nc, sy = self.nc, self.sy
        self.h = self.sb("h", [128, NK, S], F32)
        self.xn = self.sb("xn", [128, NK, S + 2], BF16)
        self.vecs = self.sb("vecs_sb", [128, NVEC], F32)
        self.ones_bf = self.sb("ones_bf", [128, 128], BF16)
        self.eps_t = self.sb("eps_t", [128, 1], F32)
        self.ARENA = 53 * 1024 + 512
        self.arena = self.sb("arena", [128, self.ARENA], BF16)
        self.arena_base = self.nc.sbuf_base - self.ARENA * 2
        self.arena_off = 0
        self.make_consts()
        sy.dma("sp", self.vecs, self.vecs_d, [], ["vecs"], "misc")
        for k in range(NK):
            sy.dma("sp", self.h[:, k, :], self.xT[k * 128:(k + 1) * 128, :], [], [("h", k, t) for t in range(NT)], f"xin{k}")
        sy.op("dve", [], ["ones"], lambda e: e.memset(self.ones_bf, 1.0))
        sy.op("dve", [], ["eps"], lambda e: e.memset(self.eps_t, RMS_EPS))
        sy.op("dve", [], [("xnpad",)], lambda e: e.memset(self.xn[:, :, 0:2], 0.0))
        for (kind, l) in self.layers:
            if kind == "mlp":
                self.rmsnorm(f"mlp_norm{l}")
                self.mlp(l)
            elif kind == "mix":
                self.rmsnorm(f"mix_norm{l}")
                if l % 3 == 1:
                    self.conv(l // 3)
                elif l % 3 == 2:
                    self.sbatt(l // 3)
                else:
                    self.rwkv(l // 3)
        for k in range(NK):
            sy.dma("sp", self.outT[k * 128:(k + 1) * 128, :], self.h[:, k, :],
                   [("h", k, t) for t in range(NT)], [("out", k)], "out")
        sy.wait_all("sp", [("out", k) for k in range(NK)])
        return nc

    def make_consts(self):
        sy = self.sy
        self.one_col = self.sb("one_col", [128, 1], F32)
        self.bones = self.sb("bones", [128, 128], F32)
        self.tri = self.sb("tri", [128, 128], F32R)
        self.onesr = self.sb("onesr", [128, 128], F32R)
        self.onesw = self.carve([128], F32)
        sy.op("dve", [], ["consts"], lambda e: e.memset(self.one_col, 1.0))
        sy.op("dve", [], ["consts"], lambda e: e.memset(self.bones, 0.0))
        sy.op("dve", [], ["consts"], lambda e: e.memset(self.bones[0:64, 0:64], 1.0))
        sy.op("dve", [], ["consts"], lambda e: e.memset(self.bones[64:128, 64:128], 1.0))
        sy.op("dve", [], ["consts"], lambda e: e.memset(self.onesw, 1.0))
        sy.op("pool", ["consts"], ["consts2"], lambda e: e.affine_select(
            out=self.tri, in_=self.onesw[:, 0:128], pattern=[[-1, 128]], compare_op=ALU.is_ge, fill=0.0,
            base=0, channel_multiplier=1))
        sy.op("pool", ["consts"], ["consts2"], lambda e: e.tensor_copy(out=self.onesr, in_=self.onesw[:, 0:128]))

    def rmsnorm(self, gname):
        sy = self.sy
        self.phase_begin()
        self.sq = [self.carve([TT], BF16) for i in range(2)]
        self.rstd = [self.carve([TT], F32) for i in range(2)]
        for t in range(NT):
            ts = slice(t * TT, (t + 1) * TT)
            pst, psk = self.ps()
            for k in range(NK):
                sq = self.sq[k % 2]
                sqk = ("sq", k % 2)
                sy.op("act", [("h", k, t)], [sqk],
                      lambda e, sq=sq, k=k: e.activation(out=sq, in_=self.h[:, k, ts], func=AF.Square))
                sy.op("pe", [sqk, "ones"], [psk],
                      lambda e, sq=sq, k=k: e.matmul(pst, self.ones_bf, sq, start=(k == 0), stop=(k == NK - 1)),
                      inc=(k == NK - 1))
            rs = self.rstd[t % 2]
            rsk = ("rstd", t % 2)
            sy.op("act", [psk, "eps"], [rsk],
                  lambda e: e.activation(out=rs, in_=pst, func=AF.Ln, bias=self.eps_t, scale=1.0 / D))
            sy.op("act", [rsk], [rsk], lambda e: e.activation(out=rs, in_=rs, func=AF.Exp, scale=-0.5))
            for k in range(NK):
                sy.op("dve", [("h", k, t), rsk, "vecs"], [("xn", k, t)],
                      lambda e, k=k: e.scalar_tensor_tensor(
                          out=self.xn[:, k, 2 + t * TT:2 + (t + 1) * TT], in0=self.h[:, k, ts],
                          scalar=self.vcol(gname, k), in1=rs, op0=ALU.mult, op1=ALU.mult))

    def alloc_mlp(self):
        self.phase_begin()
        self.GF = 512
        self.wup = [self.carve([NK, self.GF], BF16) for i in range(2)]
        self.wdn = [self.carve([self.GF // 128, D], BF16) for i in range(2)]
        self.hT = self.carve([self.GF // 128, S], BF16)
        self.relu_t = [self.carve([TT], F32) for i in range(2)]
        self.mlp_gi = 0

    def mlp_load(self, l, g):
        sy = self.sy
        GF = self.GF
        s = self.mlp_gi % 2
        self.mlp_gi += 1
        src_up = self.w["mlp_up"][l, :, g * GF:(g + 1) * GF].rearrange("(k p) f -> p k f", p=128)
        sy.dma("pool", self.wup[s], src_up, [], [("wup", s)], f"wup{s}")
        src_dn = self.w["mlp_down"][l, g * GF:(g + 1) * GF, :].rearrange("(c p) d -> p c d", p=128)
        sy.dma("pool", self.wdn[s], src_dn, [], [("wdn", s)], f"wdn{s}")
        return s

    def mlp(self, l):
        sy = self.sy
        self.alloc_mlp()
        GF = self.GF
        NG = DFF // GF
        NC = GF // 128
        slots = [None] * NG
        slots[0] = self.mlp_load(l, 0)
        ri = 0
        for g in range(NG):
            if g + 1 < NG:
                slots[g + 1] = self.mlp_load(l, g + 1)
            s = slots[g]
            for t in range(NT):
                for c in range(NC):
                    pst, psk = self.ps()
                    for k in range(NK):
                        sy.op("pe", [("wup", s), ("xn", k, t)], [psk],
                              lambda e, k=k, c=c, t=t: e.matmul(
                                  pst, self.wup[s][:, k, c * 128:(c + 1) * 128],
                                  self.xn[:, k, 2 + t * TT:2 + (t + 1) * TT],
                                  start=(k == 0), stop=(k == NK - 1)),
                              inc=(k == NK - 1))
                    rt = self.relu_t[ri % 2]
                    rk = ("relu", ri % 2)
                    ri += 1
                    sy.op("act", [psk], [rk], lambda e, rt=rt: e.activation(out=rt, in_=pst, func=AF.Relu))
                    sy.op("dve", [rk, psk], [("hT", c, t)],
                          lambda e, rt=rt, c=c, t=t: e.tensor_tensor(
                              out=self.hT[:, c, t * TT:(t + 1) * TT], in0=rt, in1=pst, op=ALU.mult))
            for t in range(NT):
                for m in range(NK):
                    pst, psk = self.ps()
                    for c in range(NC):
                        sy.op("pe", [("wdn", s), ("hT", c, t)], [psk],
                              lambda e, c=c, m=m, t=t: e.matmul(
                                  pst, self.wdn[s][:, c, m * 128:(m + 1) * 128],
                                  self.hT[:, c, t * TT:(t + 1) * TT],
                                  start=(c == 0), stop=(c == NC - 1)),
                              inc=(c == NC - 1))
                    sy.op("dve", [psk, ("h", m, t)], [("h", m, t)],
                          lambda e, m=m, t=t: e.tensor_tensor(
                              out=self.h[:, m, t * TT:(t + 1) * TT], in0=pst,
                              in1=self.h[:, m, t * TT:(t + 1) * TT], op=ALU.add))


    def proj_fm(self, wt, wkey, cols, t, shift=0, pst=None, psk=None, first=True, last=True):
        sy = self.sy
        if pst is None:
            pst, psk = self.ps()
        for k in range(NK):
            sy.op("pe", [wkey, ("xn", k, t)] + ([("xn", k, t - 1)] if (shift and t > 0) else []), [psk],
                  lambda e, k=k: e.matmul(pst, wt[:, k, cols],
                                          self.xn[:, k, 2 - shift + t * TT:2 - shift + (t + 1) * TT],
                                          start=(first and k == 0), stop=(last and k == NK - 1)))
        return pst, psk

    def outproj_acc(self, wo, wokey, y, ykey, t):
        sy = self.sy
        for m in range(NK):
            pst, psk = self.ps()
            sy.op("pe", [wokey, ykey], [psk],
                  lambda e, m=m: e.matmul(pst, wo[:, m * 128:(m + 1) * 128], y, start=True, stop=True))
            sy.op("dve", [psk, ("h", m, t)], [("h", m, t)],
                  lambda e, m=m: e.tensor_tensor(out=self.h[:, m, t * TT:(t + 1) * TT], in0=pst,
                                                 in1=self.h[:, m, t * TT:(t + 1) * TT], op=ALU.add))

    def alloc_mix(self):
        self.phase_begin()
        self.w3 = [self.carve([NK, 3, 128], BF16) for i in range(2)]
        self.wo = [self.carve([D], BF16) for i in range(2)]
        self.ybf = [self.carve([TT], BF16) for i in range(2)]
        self.mix_i = 0

    def load_w3(self, wname, j, dc, wo_name):
        sy = self.sy
        s = self.mix_i % 2
        self.mix_i += 1
        src = self.w[wname][j].rearrange("(k p) f -> p k f", p=128)
        for jj in range(3):
            sy.dma("pool", self.w3[s][:, :, jj, :], src[:, :, jj * D + dc * 128:jj * D + (dc + 1) * 128],
                   [], [("w3", s)], f"w3_{s}")
        sy.dma("pool", self.wo[s], self.w[wo_name][j, dc * 128:(dc + 1) * 128, :], [], [("wo", s)], f"wo_{s}")
        return s

    def conv(self, j):
        sy = self.sy
        self.alloc_mix()
        csb = [self.carve([TT], F32) for i in range(2)]
        bsb = [self.carve([TT], F32) for i in range(2)]
        acc = self.carve([TT], F32)
        zbufs = [self.carve([2 + S], F32) for i in range(2)]
        slots = [None] * NK
        slots[0] = self.load_w3("conv_w_in", j, 0, "conv_w_out")
        slots[1] = self.load_w3("conv_w_in", j, 1, "conv_w_out")
        items = [(dc, t) for dc in range(NK) for t in range(NT)]
        st = {"yi": 0}

        def stage1(n):
            dc, t = items[n]
            i2 = n % 2
            if t == 0:
                sy.op("dve", [], [("z", dc % 2, -1)], lambda e: e.memset(zbufs[dc % 2][:, 0:2], 0.0))
            s = slots[dc]
            w3 = self.w3[s]
            zb = zbufs[dc % 2]
            zs = slice(2 + t * TT, 2 + (t + 1) * TT)
            pb, pbk = self.proj_fm(w3[:, :, 0, :], ("w3", s), slice(0, 128), t)
            sy.op("act", [pbk], [("bsb", i2)], lambda e: e.activation(out=bsb[i2], in_=pb, func=AF.Copy))
            pc, pck = self.proj_fm(w3[:, :, 1, :], ("w3", s), slice(0, 128), t)
            sy.op("act", [pck], [("csb", i2)], lambda e: e.activation(out=csb[i2], in_=pc, func=AF.Copy))
            pu, puk = self.proj_fm(w3[:, :, 2, :], ("w3", s), slice(0, 128), t)
            sy.op("dve", [("csb", i2), puk], [("z", dc % 2, t)],
                  lambda e: e.tensor_tensor(out=zb[:, zs], in0=csb[i2], in1=pu, op=ALU.mult))

        def stage2(n):
            dc, t = items[n]
            i2 = n % 2
            s = slots[dc]
            wo = self.wo[s]
            zb = zbufs[dc % 2]
            zs = slice(2 + t * TT, 2 + (t + 1) * TT)
            zk = [("z", dc % 2, t), ("z", dc % 2, t - 1)]
            sy.op("dve", zk + ["vecs"], ["acc"],
                  lambda e: e.tensor_scalar(out=acc, in0=zb[:, t * TT:(t + 1) * TT],
                                            scalar1=self.vcol("conv_w0", dc), scalar2=None, op0=ALU.mult))
            sy.op("dve", zk + ["acc", "vecs"], ["acc"],
                  lambda e: e.scalar_tensor_tensor(out=acc, in0=zb[:, 1 + t * TT:1 + (t + 1) * TT],
                                                   scalar=self.vcol("conv_w1", dc), in1=acc,
                                                   op0=ALU.mult, op1=ALU.add))
            sy.op("dve", zk + ["acc", "vecs"], ["acc"],
                  lambda e: e.scalar_tensor_tensor(out=acc, in0=zb[:, zs],
                                                   scalar=self.vcol("conv_w2", dc), in1=acc,
                                                   op0=ALU.mult, op1=ALU.add))
            y = self.ybf[st["yi"] % 2]
            yk = ("ybf", st["yi"] % 2)
            st["yi"] += 1
            sy.op("dve", [("bsb", i2), "acc"], [yk], lambda e: e.tensor_tensor(out=y, in0=bsb[i2], in1=acc, op=ALU.mult))
            self.outproj_acc(wo, ("wo", s), y, yk, t)

        for n in range(len(items) + 1):
            if n < len(items):
                stage1(n)
            if n >= 1:
                stage2(n - 1)
                dcp, tp = items[n - 1]
                if tp == NT - 1 and dcp + 2 < NK:
                    slots[dcp + 2] = self.load_w3("conv_w_in", j, dcp + 2, "conv_w_out")

    def sbatt(self, j):
        sy = self.sy
        self.alloc_mix()
        qn = self.carve([S], F32R)
        kn = self.carve([S], F32R)
        vpA = self.carve([16, 128], BF16)
        vpB = self.carve([16, 128], BF16)
        raw_t = [self.carve([TT], F32) for i in range(2)]
        sq_t = [self.carve([TT], F32) for i in range(2)]
        rs_t = [self.carve([TT], F32) for i in range(2)]
        e_t = [self.carve([TT], F32) for i in range(3)]
        sp_t = [self.carve([TT], F32) for i in range(3)]
        lk_t = [self.carve([TT], F32R) for i in range(3)]
        u_t = [self.carve([TT], F32) for i in range(3)]
        arg_t = [self.carve([TT], F32) for i in range(3)]
        att_t = [self.carve([TT], BF16) for i in range(3)]
        R_t = [self.carve([TT], F32R) for i in range(2)]
        qgs = self.carve([1], F32)
        self.m01 = self.carve([896], BF16)
        self.mneg = self.carve([896], F32)
        onesw = self.carve([896], BF16)
        sy.op("dve", [], ["sbc"], lambda e: e.memset(onesw, 1.0))
        sy.op("pool", ["sbc"], ["consts2"], lambda e: e.affine_select(
            out=self.m01, in_=onesw, pattern=[[1, 896]], compare_op=ALU.is_gt, fill=0.0,
            base=-384, channel_multiplier=-1))
        sy.op("pool", ["consts2"], ["consts2"], lambda e: e.tensor_scalar(
            out=self.mneg, in0=self.m01, scalar1=-1.0, scalar2=None, op0=ALU.mult))
        sy.op("dve", ["vecs"], ["qgs"], lambda e: e.tensor_scalar(
            out=qgs, in0=self.vcol("q_gain"), scalar1=0.125, scalar2=None, op0=ALU.mult))
        sy.op("dve", [], ["vpA"], lambda e: e.memset(vpA, 0.0))
        sy.op("dve", [], ["vpB"], lambda e: e.memset(vpB, 0.0))
        slots = [None] * NK
        slots[0] = self.load_w3("sb_w_qkv", j, 0, "sb_w_o")
        ni = 0
        pi = 0
        oi = 0
        yi = 0
        for dc in range(NK):
            if dc + 1 < NK:
                slots[dc + 1] = self.load_w3("sb_w_qkv", j, dc + 1, "sb_w_o")
            s = slots[dc]
            w3, wo = self.w3[s], self.wo[s]
            for t in range(NT):
                ts = slice(t * TT, (t + 1) * TT)
                for (jj, dst, dkey, gcol) in ((0, qn, "qn", qgs), (1, kn, "kn", self.vcol("k_gain"))):
                    pp, ppk = self.proj_fm(w3[:, :, jj, :], ("w3", s), slice(0, 128), t)
                    raw, sq, rs = raw_t[ni % 2], sq_t[ni % 2], rs_t[ni % 2]
                    rk, sk_, rsk = ("raw", ni % 2), ("sqq", ni % 2), ("rsq", ni % 2)
                    ni += 1
                    sy.op("act", [ppk], [rk], lambda e, raw=raw, pp=pp: e.activation(out=raw, in_=pp, func=AF.Copy))
                    sy.op("act", [ppk], [sk_], lambda e, sq=sq, pp=pp: e.activation(out=sq, in_=pp, func=AF.Square))
                    p2, p2k = self.ps()
                    sy.op("pe", [sk_, "consts"], [p2k],
                          lambda e, sq=sq, p2=p2: e.matmul(p2, self.bones, sq, start=True, stop=True))
                    sy.op("act", [p2k, "eps"], [rsk],
                          lambda e, rs=rs, p2=p2: e.activation(out=rs, in_=p2, func=AF.Ln, bias=self.eps_t, scale=1.0 / 64))
                    sy.op("act", [rsk], [rsk], lambda e, rs=rs: e.activation(out=rs, in_=rs, func=AF.Exp, scale=-0.5))
                    sy.op("dve", [rk, rsk, "vecs", "qgs"], [(dkey, t)],
                          lambda e, raw=raw, rs=rs, dst=dst, gcol=gcol: e.scalar_tensor_tensor(
                              out=dst[:, ts], in0=raw, scalar=gcol, in1=rs, op0=ALU.mult, op1=ALU.mult))
                pv, pvk = self.ps()
                for q4 in range(4):
                    for k in range(NK):
                        sy.op("pe", [("w3", s), ("xn", k, t)], [pvk],
                              lambda e, k=k, q4=q4: e.matmul(
                                  pv[:, q4 * 128:(q4 + 1) * 128],
                                  self.xn[:, k, 2 + t * TT + q4 * 128:2 + t * TT + (q4 + 1) * 128],
                                  w3[:, k, 2, :], start=(k == 0), stop=(k == NK - 1)))
                pv3 = pv.rearrange("p (a b) -> p a b", a=4)
                sy.op("act", [pvk], ["vpA"], lambda e, pv3=pv3: e.activation(
                    out=vpA[:, 4 * t:4 * t + 4, 0:64], in_=pv3[:, :, 0:64], func=AF.Copy))
                sy.op("act", [pvk], ["vpB"], lambda e, pv3=pv3: e.activation(
                    out=vpB[:, 4 * t:4 * t + 4, 64:128], in_=pv3[:, :, 64:128], func=AF.Copy))
            pairs = []
            for T in range(NT):
                for hd in range(2):
                    cmax = 4 * T + 3
                    for c in range(cmax, -1, -1):
                        pairs.append(dict(T=T, hd=hd, c=c, cmax=cmax, first=(hd == 0 and c == cmax),
                                          last=(hd == 1 and c == 0)))
            NB = 3
            o_banks = {}
            for T in range(NT):
                o_banks[T] = 6 + oi % 2
                oi += 1
            rstate = {"cur": 0}

            def stage1(n, p):
                i2 = n % NB
                hp = slice(p["hd"] * 64, p["hd"] * 64 + 64)
                T, c = p["T"], p["c"]
                Ts = slice(T * TT, (T + 1) * TT)
                jd = c - 4 * T
                pz, pzk = self.ps()
                sy.op("pe", [("kn", c // 4), ("qn", T)], [pzk],
                      lambda e: e.matmul(pz, kn[hp, c * 128:(c + 1) * 128], qn[hp, Ts], start=True, stop=True))
                sy.op("act", [pzk], [("e", i2)], lambda e: e.activation(out=e_t[i2], in_=pz, func=AF.Exp))
                sy.op("act", [("e", i2), "consts"], [("sp", i2)],
                      lambda e: e.activation(out=sp_t[i2], in_=e_t[i2], func=AF.Ln, bias=self.one_col, scale=1.0))
                if jd >= 0:
                    sy.op("dve", [("sp", i2), "consts2"], [("lk", i2)],
                          lambda e: e.tensor_tensor(out=lk_t[i2], in0=sp_t[i2],
                                                    in1=self.mneg[:, 384 - 128 * jd:896 - 128 * jd], op=ALU.mult))
                else:
                    sy.op("dve", [("sp", i2)], [("lk", i2)],
                          lambda e: e.tensor_scalar(out=lk_t[i2], in0=sp_t[i2], scalar1=-1.0, scalar2=None, op0=ALU.mult))

            def stage2(n, p):
                i2 = n % NB
                T, c, cmax = p["T"], p["c"], p["cmax"]
                jd = c - 4 * T
                if c == cmax:
                    rstate["cur"] = 0
                rcur = rstate["cur"]
                hp = slice(p["hd"] * 64, p["hd"] * 64 + 64)
                Ts = slice(T * TT, (T + 1) * TT)
                pt, ptk = self.ps()
                sy.op("pe", [("lk", i2), "consts2"], [ptk],
                      lambda e: e.matmul(pt, self.tri, lk_t[i2], start=True, stop=False))
                if c < cmax:
                    sy.op("pe", [("R", rcur), "consts2"], [ptk],
                          lambda e: e.matmul(pt, self.onesr, R_t[rcur], start=False, stop=False))
                sy.op("pe", [("kn", c // 4), ("qn", T)], [ptk],
                      lambda e: e.matmul(pt, kn[hp, c * 128:(c + 1) * 128], qn[hp, Ts], start=False, stop=True))
                if c > 0:
                    rn = 1 - rcur
                    if c == cmax:
                        sy.op("pool", [("lk", i2)], [("R", rn)], lambda e: e.tensor_copy(out=R_t[rn], in_=lk_t[i2]))
                    else:
                        sy.op("pool", [("lk", i2), ("R", rcur)], [("R", rn)],
                              lambda e: e.tensor_tensor(out=R_t[rn], in0=R_t[rcur], in1=lk_t[i2], op=ALU.add))
                    rstate["cur"] = rn
                sy.op("act", [ptk], [("att", i2)],
                      lambda e: e.activation(out=att_t[i2], in_=pt, func=AF.Exp))
                if jd >= 0:
                    sy.op("dve", [("att", i2), "consts2"], [("att", i2)],
                          lambda e: e.tensor_tensor(out=att_t[i2], in0=att_t[i2],
                                                    in1=self.m01[:, 384 - 128 * jd:896 - 128 * jd], op=ALU.mult))

            def stage3(n, p):
                nonlocal yi
                i2 = n % NB
                T, c = p["T"], p["c"]
                ob = o_banks[T]
                o_ps, ok = self.psum[ob], ("ps", ob)
                vp, vpk = (vpA, "vpA") if p["hd"] == 0 else (vpB, "vpB")
                sy.op("pe", [vpk, ("att", i2)], [ok],
                      lambda e: e.matmul(o_ps, vp[:, c, :], att_t[i2], start=p["first"], stop=p["last"]))
                if p["last"]:
                    y = self.ybf[yi % 2]
                    yk = ("ybf", yi % 2)
                    yi += 1
                    sy.op("act", [ok], [yk], lambda e: e.activation(out=y, in_=o_ps, func=AF.Copy))
                    self.outproj_acc(wo, ("wo", s), y, yk, T)

            npairs = len(pairs)
            for n in range(npairs + 2):
                if n < npairs:
                    stage1(n, pairs[n])
                if 1 <= n and n - 1 < npairs:
                    stage2(n - 1, pairs[n - 1])
                if 2 <= n:
                    stage3(n - 2, pairs[n - 2])

    def rwkv(self, j):
        sy = self.sy
        self.phase_begin()
        RT = 256
        NRT = S // RT
        CD = 0.6065306597126334
        cv = self.carve
        P1, GA, GB = cv([S], BF16), cv([S], BF16), cv([S], BF16)
        WA2, GA2, GB2 = cv([D], BF16), cv([D], BF16), cv([D], BF16)
        omm, okka = cv([48], F32), cv([8], F32)
        ident, mk4, mkL, cmask = cv([128], F32), cv([512], F32), cv([128], F32), cv([RT], F32)
        gneps, tiny = cv([1], F32), cv([1], F32)
        mark = self.arena_off
        onesw = cv([512], F32)
        LWs, LW = cv([NK, 320], F32), cv([NK, 2, 320], BF16)
        mu0 = VEC_COLS[f"mu{j}_0"]
        mucols = self.vecs[:, mu0:mu0 + 48]
        sy.op("dve", [], ["rc"], lambda e: e.memset(onesw, 1.0))
        sy.op("dve", [], ["rc"], lambda e: e.memset(gneps, GN_EPS))
        sy.op("dve", [], ["rc"], lambda e: e.memset(tiny, 1e-24))
        sy.op("dve", [], ["cmask"], lambda e: e.memset(cmask, 1.0))
        sy.op("dve", [], ["cmask"], lambda e: e.memset(cmask[:, 0:1], 0.0))
        sy.op("dve", [], ["cmask"], lambda e: e.memset(cmask[:, 128:129], 0.0))
        sy.op("dve", ["vecs"], ["omm"], lambda e: e.tensor_scalar(
            out=omm, in0=mucols, scalar1=-1.0, scalar2=1.0, op0=ALU.mult, op1=ALU.add))
        ka0 = VEC_COLS[f"k_a{j}"]
        sy.op("dve", ["vecs"], ["omm"], lambda e: e.tensor_scalar(
            out=okka, in0=self.vecs[:, ka0:ka0 + 8], scalar1=-1.0, scalar2=1.0, op0=ALU.mult, op1=ALU.add))
        sy.op("pool", ["rc"], ["rc2"], lambda e: e.affine_select(
            out=ident, in_=onesw[:, 0:128], pattern=[[-1, 128]], compare_op=ALU.is_equal, fill=0.0,
            base=0, channel_multiplier=1))
        sy.op("pool", ["rc"], ["rc2"], lambda e: e.affine_select(
            out=mkL, in_=onesw[:, 0:128], pattern=[[-1, 128]], compare_op=ALU.is_gt, fill=0.0,
            base=0, channel_multiplier=1))
        sy.op("pool", ["rc"], ["rc2"], lambda e: e.affine_select(
            out=mk4, in_=onesw, pattern=[[0, 2], [1, 2], [1, 128]], compare_op=ALU.is_gt, fill=0.0,
            base=0, channel_multiplier=-1))
        wr = lambda n, jj=j: self.w[n][jj]
        sy.dma("sp", LWs[:, :, 0:64], wr("rwkv_decay_w1").rearrange("(k p) c -> p k c", p=128), [], ["LWs"], "lws")
        sy.dma("sp", LWs[:, :, 64:128], wr("rwkv_iclr_a1").rearrange("(k p) c -> p k c", p=128), [], ["LWs"], "lws")
        sy.dma("sp", LWs[:, :, 128:288], wr("rwkv_gate_g1").rearrange("(k p) c -> p k c", p=128), [], ["LWs"], "lws")
        if j == 1:
            sy.dma("sp", LWs[:, :, 288:320], self.w["rwkv_vres_v1"][0].rearrange("(k p) c -> p k c", p=128), [], ["LWs"], "lws")
        sy.dma("pool", WA2[0:64, :], wr("rwkv_decay_w2"), [], ["W2"], "w2s")
        sy.dma("pool", WA2[64:128, :], wr("rwkv_iclr_a2"), [], ["W2"], "w2s")
        sy.dma("pool", GA2, wr("rwkv_gate_g2")[0:128, :], [], ["W2"], "w2s")
        sy.dma("pool", GB2[0:32, :], wr("rwkv_gate_g2")[128:160, :], [], ["W2"], "w2s")
        if j == 1:
            sy.dma("pool", GB2[32:64, :], self.w["rwkv_vres_v2"][0], [], ["W2"], "w2s")
        blocks = [(0, 64, 1), (64, 128, 4), (128, 288, 5)] + ([(288, 320, 3)] if j == 1 else [])
        for k in range(NK):
            for (c0, c1, m) in blocks:
                sy.op("act", ["LWs", "omm"], ["LW"], lambda e, k=k, c0=c0, c1=c1, m=m: e.activation(
                    out=LW[:, k, 0, c0:c1], in_=LWs[:, k, c0:c1], func=AF.Copy, scale=omm[:, m * 8 + k:m * 8 + k + 1]))
                sy.op("dve", ["LWs", "vecs"], ["LW"], lambda e, k=k, c0=c0, c1=c1, m=m: e.tensor_scalar(
                    out=LW[:, k, 1, c0:c1], in0=LWs[:, k, c0:c1], scalar1=self.vecs[:, mu0 + m * 8 + k:mu0 + m * 8 + k + 1],
                    scalar2=None, op0=ALU.mult))
        NL3 = 64 if j == 1 else 32
        for t in range(NT):
            ts = slice(t * TT, (t + 1) * TT)
            for (c0, M, which) in ((0, 128, 0), (128, 128, 1), (256, NL3, 2)):
                pst, psk = self.ps()
                for k in range(NK):
                    for sh in range(2):
                        rd = [("xn", k, t), "LW"] + ([("xn", k, t - 1)] if (sh and t > 0) else [])
                        sy.op("pe", rd, [psk], lambda e, k=k, sh=sh, c0=c0, M=M, pst=pst: e.matmul(
                            pst[0:M, :], LW[:, k, sh, c0:c0 + M], self.xn[:, k, 2 - sh + t * TT:2 - sh + (t + 1) * TT],
                            start=(k == 0 and sh == 0), stop=(k == NK - 1 and sh == 1)))
                if which == 0:
                    sy.op("act", [psk], [("P1", t)], lambda e, pst=pst: e.activation(out=P1[0:64, ts], in_=pst[0:64, :], func=AF.Tanh))
                    sy.op("act", [psk], [("P1", t)], lambda e, pst=pst: e.activation(out=P1[64:128, ts], in_=pst[64:128, :], func=AF.Copy))
                elif which == 1:
                    sy.op("act", [psk], [("GA", t)], lambda e, pst=pst: e.activation(out=GA[:, ts], in_=pst, func=AF.Sigmoid))
                else:
                    sy.op("act", [psk], [("GB", t)], lambda e, pst=pst: e.activation(out=GB[0:32, ts], in_=pst[0:32, :], func=AF.Sigmoid))
                    if j == 1:
                        sy.op("act", [psk], [("GB", t)], lambda e, pst=pst: e.activation(out=GB[32:64, ts], in_=pst[32:64, :], func=AF.Copy))
        sy.barrier()
        self.arena_off = mark
        Wst = cv([NK, 3, 128], F32)
        Wfs = [cv([NK, 3, 2, 128], BF16)]
        wo = [cv([D], BF16) for i in range(2)]
        f1 = lambda: cv([RT], F32)
        r_sb, k_sb, sg, csp, pinv, asig, kk, tA, tB, tC, YC, vf = [f1() for _ in range(12)]
        Yb = [f1() for _ in range(2)]
        BN3 = [f1() for _ in range(3)]
        Bset = [(f1(), f1(), cv([2, 2, 128], BF16), cv([RT], BF16), cv([RT], BF16), None, cv([RT], BF16)) for _ in range(2)]
        yout = [cv([RT], BF16) for i in range(2)]
        def chunk_set():
            return (cv([128], BF16), cv([128], BF16), cv([4, 256], BF16), cv([4, 128], BF16),
                    [cv([128], F32) for i in range(2)], [cv([3, 128], BF16) for i in range(2)],
                    [[cv([128], F32) for i in range(2)] for h in range(2)],
                    [[cv([128], F32) for i in range(2)] for h in range(2)],
                    [[cv([2, 64], F32) for i in range(2)] for h in range(2)],
                    cv([2, 256], BF16), cv([2, 128], BF16))
        CS = [chunk_set() for _ in range(2)]
        M1, NCt = cv([128], F32), cv([128], F32)
        MBD, G = cv([128], BF16), cv([128], BF16)
        ST = [cv([128], BF16) for i in range(2)]
        identb = cv([128], BF16)
        sy.op("dve", ["rc2"], ["rc2"], lambda e: e.tensor_copy(out=identb, in_=ident))
        for ci_ in range(2):
            sy.op("dve", [], [("TMp", ci_)], lambda e, ci_=ci_: e.memset(CS[ci_][2], 0.0))
            sy.op("dve", [], [("Wfin", ci_)], lambda e, ci_=ci_: e.memset(CS[ci_][9], 0.0))
        both = lambda ap2: ap2.rearrange("p (a b) -> p a b", a=4)[:, 0:4:3, :]
        wnames = ("rwkv_w_r", "rwkv_w_k", "rwkv_w_v")
        muidx = (0, 2, 3)

        def load_stage(dc):
            for jj in range(3):
                src = self.w[wnames[jj]][j].rearrange("(k p) c -> p k c", p=128)[:, :, dc * 128:(dc + 1) * 128]
                sy.dma("sp", Wst[:, :, jj, :], src, [], ["Wst"], "wst")
            sy.dma("pool", wo[dc % 2], self.w["rwkv_w_o"][j, dc * 128:(dc + 1) * 128, :], [], [("wo", dc % 2)], f"rwo{dc % 2}")

        def fold_gen(dcn):
            Wfn = Wfs[0]
            wfk = ("Wf", 0)
            for k in range(NK):
                for jj in range(3):
                    m = muidx[jj]
                    sy.op("act", ["Wst", "omm"], [wfk], lambda e, k=k, jj=jj, m=m: e.activation(
                        out=Wfn[:, k, jj, 0, :], in_=Wst[:, k, jj, :], func=AF.Copy, scale=omm[:, m * 8 + k:m * 8 + k + 1]))
                    sy.op("dve", ["Wst", "vecs"], [wfk], lambda e, k=k, jj=jj, m=m: e.tensor_scalar(
                        out=Wfn[:, k, jj, 1, :], in0=Wst[:, k, jj, :],
                        scalar1=self.vecs[:, mu0 + m * 8 + k:mu0 + m * 8 + k + 1], scalar2=None, op0=ALU.mult))
                yield

        load_stage(0)
        for _ in fold_gen(0):
            pass
        stt = {"sti": 0, "yi": 0}
        def make_dc(dc):
            dcs = slice(dc * 128, (dc + 1) * 128)
            Wf = Wfs[0]
            wfk_cur = ("Wf", 0)
            wod, wok = wo[dc % 2], ("wo", dc % 2)
            def make_ctx(rt, s_):
                b = s_ % 2
                tok0 = rt * RT
                tk = slice(tok0, tok0 + RT)
                t5 = tok0 // TT
                xr = lambda k: [("xn", k, t5)] + ([("xn", k, t5 - 1)] if (tok0 % TT == 0 and t5 > 0) else [])

                def proj(jj):
                    pst, psk = self.ps()
                    for k in range(NK):
                        for sh in range(2):
                            sy.op("pe", xr(k) + [wfk_cur], [psk], lambda e, k=k, sh=sh, pst=pst: e.matmul(
                                pst[:, 0:RT], Wf[:, k, jj, sh, :], self.xn[:, k, 2 - sh + tok0:2 - sh + tok0 + RT],
                                start=(k == 0 and sh == 0), stop=(k == NK - 1 and sh == 1)))
                    return pst[:, 0:RT], psk

                def small(lhsT, rhs, reads, M=128, pst=None, psk=None, start=True, stop=True, n=RT, c0=0):
                    if pst is None:
                        pst, psk = self.ps()
                    sy.op("pe", reads, [psk], lambda e: e.matmul(pst[0:M, c0:c0 + n], lhsT, rhs, start=start, stop=stop))
                    return pst, psk

                V = lambda nm, dc=dc: self.vcol(nm, dc)
                v_sb, cs, AR, BT, KT, BN, vbf = Bset[b]
                BN = BN3[s_ % 3]
                Y = Yb[s_ % 2]
                yk_ = ("Y", s_ % 2)
                bnk = ("BN", s_ % 3)
                kq = lambda nm: (nm, b)
                return dict(locals())

            def prologue(rt, s_):
                b = s_ % 2
                c_ = make_ctx(rt, s_)
                tok0, tk, t5, xr, proj, small, V = (c_[n] for n in ('tok0', 'tk', 't5', 'xr', 'proj', 'small', 'V'))
                v_sb, cs, AR, BT, KT, BN, vbf, kq, Y, yk_, bnk = (c_[n] for n in ('v_sb', 'cs', 'AR', 'BT', 'KT', 'BN', 'vbf', 'kq', 'Y', 'yk_', 'bnk'))
                pr, prk = proj(0)
                sy.op("act", [prk], ["r_sb"], lambda e: e.activation(out=r_sb, in_=pr, func=AF.Copy))
                pk, pkk = proj(1)
                sy.op("act", [pkk], ["k_sb"], lambda e: e.activation(out=k_sb, in_=pk, func=AF.Copy))
                pv, pvk = proj(2)
                sy.op("act", [pvk], [kq("v_sb")], lambda e: e.activation(out=v_sb, in_=pv, func=AF.Copy))
                yield
                plw, plwk = small(WA2[0:64, dcs], P1[0:64, tk], ["W2", ("P1", t5)])
                sy.op("act", [plwk, "vecs"], ["sg"], lambda e: e.activation(
                    out=sg, in_=plw[:, 0:RT], func=AF.Sigmoid, bias=V(f"w0{j}"), scale=1.0))
                pa, pak = small(WA2[64:128, dcs], P1[64:128, tk], ["W2", ("P1", t5)])
                sy.op("act", [pak, "vecs"], ["asig"], lambda e: e.activation(
                    out=asig, in_=pa[:, 0:RT], func=AF.Sigmoid, bias=V(f"a0{j}"), scale=1.0))
                if j == 1:
                    pg_, pgk_ = small(GB2[32:64, dcs], GB[32:64, tk], ["W2", ("GB", t5)])
                    sy.op("act", [pgk_, "vecs"], ["tB"], lambda e: e.activation(
                        out=tB, in_=pg_[:, 0:RT], func=AF.Sigmoid, bias=V("v0"), scale=1.0))
                    sy.dma("sp", vf, self.vfirst[dcs, tk], [("vfd", dc, rt)], ["vf"], "vfl")
                    sy.op("dve", ["vf", kq("v_sb")], ["vf"], lambda e: e.tensor_tensor(out=vf, in0=vf, in1=v_sb, op=ALU.subtract))
                    sy.op("dve", ["vf", "tB"], ["vf"], lambda e: e.tensor_tensor(out=vf, in0=vf, in1=tB, op=ALU.mult))
                    sy.op("dve", ["vf", kq("v_sb")], [kq("v_sb")], lambda e: e.tensor_tensor(out=v_sb, in0=v_sb, in1=vf, op=ALU.add))
                else:
                    sy.dma("sp", self.vfirst[dcs, tk], v_sb, [kq("v_sb")], [("vfd", dc, rt)], f"vfs{b}")
                sy.op("act", [kq("v_sb")], [kq("vbf")], lambda e: e.activation(out=vbf, in_=v_sb, func=AF.Copy))
                sy.op("dve", ["sg", "cmask"], [kq("cs")], lambda e: e.tensor_tensor_scan(
                    out=cs, data0=cmask, data1=sg, initial=0.0, op0=ALU.mult, op1=ALU.add))
                sy.op("dve", [kq("cs"), "sg"], ["csp"], lambda e: e.tensor_tensor(out=csp, in0=cs, in1=sg, op=ALU.subtract))
                sy.op("act", [kq("cs")], ["pinv"], lambda e: e.activation(out=pinv, in_=cs, func=AF.Exp, scale=CD))
                sy.op("act", [kq("cs")], [kq("cs")], lambda e: e.activation(out=cs, in_=cs, func=AF.Exp, scale=-CD))
                yield
                sy.op("act", ["csp"], ["csp"], lambda e: e.activation(out=csp, in_=csp, func=AF.Exp, scale=-CD))
                sy.op("dve", ["k_sb", "vecs"], ["kk"], lambda e: e.tensor_scalar(
                    out=kk, in0=k_sb, scalar1=V(f"k_k{j}"), scalar2=None, op0=ALU.mult))
                sy.op("act", ["kk"], ["tA"], lambda e: e.activation(out=tA, in_=kk, func=AF.Square))
                pss, pssk = small(self.bones, tA, ["tA", "consts"])
                sy.op("act", [pssk, "rc"], ["tA"], lambda e: e.activation(out=tA, in_=pss[:, 0:RT], func=AF.Ln, bias=tiny, scale=1.0))
                yield
                sy.op("act", ["tA"], ["tA"], lambda e: e.activation(out=tA, in_=tA, func=AF.Exp, scale=-0.5))
                sy.op("dve", ["kk", "tA"], ["kk"], lambda e: e.tensor_tensor(out=kk, in0=kk, in1=tA, op=ALU.mult))
                sy.op("dve", ["asig", "vecs", "omm"], ["tB"], lambda e: e.tensor_scalar(
                    out=tB, in0=asig, scalar1=V(f"k_a{j}"), scalar2=okka[:, dc:dc + 1], op0=ALU.mult, op1=ALU.add))
                sy.op("dve", ["k_sb", "tB"], ["k_sb"], lambda e: e.tensor_tensor(out=k_sb, in0=k_sb, in1=tB, op=ALU.mult))
                yield
                c3 = lambda ap: ap.rearrange("p (a b) -> p a b", a=2)
                sy.op("dve", ["kk", "csp"], [kq("AR0")], lambda e: e.scalar_tensor_tensor(
                    out=AR[:, :, 0, :], in0=c3(kk), scalar=-1.0, in1=c3(csp), op0=ALU.mult, op1=ALU.mult))
                sy.op("dve", ["r_sb", kq("cs")], [kq("AR1")], lambda e: e.tensor_tensor(
                    out=AR[:, :, 1, :], in0=c3(r_sb), in1=c3(cs), op=ALU.mult))
                sy.op("dve", ["kk", "asig"], ["tB"], lambda e: e.tensor_tensor(out=tB, in0=kk, in1=asig, op=ALU.mult))
                sy.op("dve", ["tB", "pinv"], [kq("BT")], lambda e: e.tensor_tensor(out=BT, in0=tB, in1=pinv, op=ALU.mult))
                sy.op("dve", ["k_sb", "pinv"], [kq("KT")], lambda e: e.tensor_tensor(out=KT, in0=k_sb, in1=pinv, op=ALU.mult))
                yield
                sy.op("dve", ["r_sb", "k_sb", "vecs"], ["tA"], lambda e: e.scalar_tensor_tensor(
                    out=tA, in0=r_sb, scalar=V(f"r_k{j}"), in1=k_sb, op0=ALU.mult, op1=ALU.mult))
                pbn, pbnk = small(self.bones, tA, ["tA", "consts"])
                sy.op("dve", [pbnk, kq("v_sb")], [bnk], lambda e: e.tensor_tensor(out=BN, in0=pbn[:, 0:RT], in1=v_sb, op=ALU.mult))
                yield

            def scanepi(rt, s_):
                c_ = make_ctx(rt, s_)
                if rt == 0:
                    stt["sti"] = 0
                    sy.op("dve", [], [("ST", 0)], lambda e: e.memset(ST[0], 0.0))
                tok0, tk, t5, xr, proj, small, V = (c_[n] for n in ('tok0', 'tk', 't5', 'xr', 'proj', 'small', 'V'))
                v_sb, cs, AR, BT, KT, BN, vbf, kq, Y, yk_, bnk = (c_[n] for n in ('v_sb', 'cs', 'AR', 'BT', 'KT', 'BN', 'vbf', 'kq', 'Y', 'yk_', 'bnk'))
                def pre_gen(ci):
                    BH, KH, TMp, TMb, AM32, AMb, Np, NpT, Wt, Wfin, WB = CS[ci]
                    ck = lambda nm: (nm, ci)
                    cc = slice(ci * 128, (ci + 1) * 128)
                    pcol = cs[:, ci * 128 + 127:ci * 128 + 128]
                    sy.op("dve", [kq("BT"), kq("cs")], [ck("BH")], lambda e: e.tensor_scalar(out=BH, in0=BT[:, cc], scalar1=pcol, scalar2=None, op0=ALU.mult))
                    sy.op("dve", [kq("KT"), kq("cs")], [ck("KH")], lambda e: e.tensor_scalar(out=KH, in0=KT[:, cc], scalar1=pcol, scalar2=None, op0=ALU.mult))
                    ptm, ptmk = self.ps()
                    ptmb = ptm.bitcast(BF16)
                    for q, (src, skey) in enumerate(((AR[:, ci, 0, :], kq("AR0")), (BH, ck("BH")), (KH, ck("KH")), (vbf[:, cc], kq("vbf")))):
                        sy.op("pe", [skey, "rc2"], [ptmk], lambda e, q=q, src=src: e.transpose(
                            out=ptmb[:, q * 128:(q + 1) * 128], in_=src, identity=identb))
                    ptm3 = ptmb[:, 0:512].rearrange("p (a b) -> p a b", a=4)
                    sy.op("act", [ptmk], [ck("TMp")], lambda e: e.activation(out=TMp[:, :, 0:64], in_=ptm3[:, :, 0:64], func=AF.Copy))
                    sy.op("act", [ptmk], [ck("TMp")], lambda e: e.activation(out=TMp[:, :, 192:256], in_=ptm3[:, :, 64:128], func=AF.Copy))
                    sy.op("act", [ptmk], [ck("TMb")], lambda e: e.activation(out=TMb, in_=ptm3, func=AF.Copy))
                    hcs = (slice(0, 64), slice(192, 256))
                    for hd in range(2):
                        yield
                        hp = slice(hd * 64, hd * 64 + 64)
                        arh = AR[hp, ci, :, :]
                        pam, pamk = self.ps()
                        sy.op("pe", [kq("BT"), kq("AR0"), kq("AR1")], [pamk], lambda e, pam=pam, arh=arh, hp=hp: e.matmul(
                            pam[:, 0:256], BT[hp, cc], arh, start=True, stop=True))
                        sy.op("pe", [kq("KT"), kq("AR0"), kq("AR1")], [pamk], lambda e, pam=pam, arh=arh, hp=hp: e.matmul(
                            pam[:, 256:512], KT[hp, cc], arh, start=True, stop=True))
                        sy.op("dve", [pamk, "rc2"], [("AM", ci, hd)], lambda e, pam=pam, hd=hd: e.tensor_tensor(
                            out=AM32[hd], in0=pam[:, 0:128], in1=mk4[:, 0:128], op=ALU.mult))
                        sy.op("dve", [pamk, "rc2"], [("AM", ci, hd)], lambda e, pam=pam, hd=hd: e.tensor_tensor(
                            out=AMb[hd], in0=pam[:, 128:512].rearrange("p (a b) -> p a b", a=3),
                            in1=mk4[:, 128:512].rearrange("p (a b) -> p a b", a=3), op=ALU.mult))
                        pnt, pntk = self.ps()
                        sy.op("pe", [kq("BT"), kq("AR0")], [pntk], lambda e, pnt=pnt, hp=hp: e.matmul(
                            pnt[:, 0:128], AR[hp, ci, 0, :], BT[hp, cc], start=True, stop=True))
                        sy.op("dve", [pntk, "rc2"], [("NpT", ci, hd, 0)], lambda e, pnt=pnt, hd=hd: e.tensor_tensor(
                            out=NpT[hd][0], in0=pnt[:, 0:128], in1=mkL, op=ALU.mult))
                        sy.op("pool", [ck("TMp")], [("Wt", ci, hd, 0)], lambda e, hd=hd: e.tensor_copy(out=Wt[hd][0][:, 0, :], in_=TMp[:, 0, hcs[hd]]))
                        pxv, pxvk = self.ps()
                        sy.op("pe", [("AM", ci, hd), ck("TMp")], [pxvk], lambda e, pxv=pxv, hd=hd: e.matmul(
                            pxv[:, 0:64], AMb[hd][:, 1, :], TMp[:, 3, hcs[hd]], start=True, stop=True))
                        sy.op("act", [pxvk], [("Wt", ci, hd, 0)], lambda e, pxv=pxv, hd=hd: e.activation(
                            out=Wt[hd][0][:, 1, :], in_=pxv[:, 0:64], func=AF.Copy))
                    yield
                    for lvl in range(7):
                        yield
                        cur, nxt = lvl % 2, (lvl + 1) % 2
                        for hd in range(2):
                            npc = AM32[hd] if lvl == 0 else Np[hd][cur]
                            npk = ("AM", ci, hd) if lvl == 0 else ("Np", ci, hd, cur)
                            wcur = Wt[hd][cur]
                            pw, pwk = self.ps()
                            sy.op("pe", [npk, ("Wt", ci, hd, cur)], [pwk], lambda e, pw=pw, npc=npc, wcur=wcur: e.matmul(
                                pw[:, 0:128], npc, wcur.rearrange("p a b -> p (a b)"), start=True, stop=True))
                            pw3 = pw[:, 0:128].rearrange("p (a b) -> p a b", a=2)
                            if lvl < 6:
                                sy.op("dve", [pwk, ("Wt", ci, hd, cur)], [("Wt", ci, hd, nxt)], lambda e, pw3=pw3, wcur=wcur, hd=hd, nxt=nxt: e.tensor_tensor(
                                    out=Wt[hd][nxt], in0=pw3, in1=wcur, op=ALU.add))
                                pn, pnk = self.ps()
                                sy.op("pe", [npk, ("NpT", ci, hd, cur)], [pnk], lambda e, pn=pn, npc=npc, hd=hd, cur=cur: e.matmul(
                                    pn[:, 0:128], NpT[hd][cur], npc, start=True, stop=True))
                                sy.op("act", [pnk], [("Np", ci, hd, nxt)], lambda e, pn=pn, hd=hd, nxt=nxt: e.activation(
                                    out=Np[hd][nxt], in_=pn[:, 0:128], func=AF.Copy))
                                pn2, pn2k = self.ps()
                                sy.op("pe", [npk, ("NpT", ci, hd, cur)], [pn2k], lambda e, pn2=pn2, npc=npc, hd=hd, cur=cur: e.matmul(
                                    pn2[:, 0:128], npc, NpT[hd][cur], start=True, stop=True))
                                sy.op("act", [pn2k], [("NpT", ci, hd, nxt)], lambda e, pn2=pn2, hd=hd, nxt=nxt: e.activation(
                                    out=NpT[hd][nxt], in_=pn2[:, 0:128], func=AF.Copy))
                            else:
                                sy.op("dve", [pwk, ("Wt", ci, hd, cur)], [ck("Wfin")], lambda e, pw3=pw3, wcur=wcur, hd=hd: e.tensor_tensor(
                                    out=Wfin[:, :, hcs[hd]], in0=pw3, in1=wcur, op=ALU.add))
                                sy.op("dve", [pwk, ("Wt", ci, hd, cur)], [ck("WB")], lambda e, pw3=pw3, wcur=wcur, hd=hd: e.tensor_tensor(
                                    out=WB[:, :, hd * 64:(hd + 1) * 64], in0=pw3, in1=wcur, op=ALU.add))
                    yield
                    yield

                def tail_gen(ci):
                    BH, KH, TMp, TMb, AM32, AMb, Np, NpT, Wt, Wfin, WB = CS[ci]
                    ck = lambda nm: (nm, ci)
                    cc = slice(ci * 128, (ci + 1) * 128)
                    pcol = cs[:, ci * 128 + 127:ci * 128 + 128]
                    Ah_b, Uh_b = WB[:, 0, :], WB[:, 1, :]
                    Bh_b, Kh_b, VT_b = TMb[:, 1, :], TMb[:, 2, :], TMb[:, 3, :]
                    stc, stn = ST[stt['sti'] % 2], ST[(stt['sti'] + 1) % 2]
                    stck, stnk = ("ST", stt['sti'] % 2), ("ST", (stt['sti'] + 1) % 2)
                    stt['sti'] += 1
                    pm, pmk = self.ps()
                    sy.op("pe", [ck("WB"), ck("TMb")], [pmk], lambda e, pm=pm: e.matmul(pm[:, 0:128], Ah_b, Bh_b, start=True, stop=True))
                    sy.op("dve", [pmk, "consts"], ["M1"], lambda e, pm=pm: e.tensor_tensor(out=M1, in0=pm[:, 0:128], in1=self.bones, op=ALU.mult))
                    sy.op("dve", ["M1", "rc2", kq("cs")], ["MBD"], lambda e: e.scalar_tensor_tensor(
                        out=MBD, in0=ident, scalar=pcol, in1=M1, op0=ALU.mult, op1=ALU.add))
                    pn_, pnk_ = self.ps()
                    sy.op("pe", [ck("WB"), ck("TMb")], [pnk_], lambda e, pn_=pn_: e.matmul(pn_[:, 0:128], Bh_b, Uh_b, start=True, stop=False))
                    sy.op("pe", [ck("TMb")], [pnk_], lambda e, pn_=pn_: e.matmul(pn_[:, 0:128], Kh_b, VT_b, start=False, stop=True))
                    sy.op("dve", [pnk_, "consts"], ["NCt"], lambda e, pn_=pn_: e.tensor_tensor(out=NCt, in0=pn_[:, 0:128], in1=self.bones, op=ALU.mult))
                    yield
                    pg, pgk = self.ps()
                    sy.op("pe", [ck("Wfin"), ("AM", ci, 0)], [pgk], lambda e, pg=pg: e.matmul(pg[:, 0:128], Wfin[:, 0, 0:128], AMb[0][:, 0, :], start=True, stop=False))
                    sy.op("pe", [ck("Wfin"), ("AM", ci, 1)], [pgk], lambda e, pg=pg: e.matmul(pg[:, 0:128], Wfin[:, 0, 128:256], AMb[1][:, 0, :], start=False, stop=True))
                    sy.op("dve", [pgk, kq("AR1")], ["G"], lambda e, pg=pg: e.tensor_tensor(out=G, in0=pg[:, 0:128], in1=AR[:, ci, 1, :], op=ALU.add))
                    yield
                    py, pyk = self.ps()
                    sy.op("pe", [stck, "G"], [pyk], lambda e, py=py, stc=stc: e.matmul(py[:, 0:128], stc, G, start=True, stop=False))
                    sy.op("pe", [ck("Wfin"), ("AM", ci, 0)], [pyk], lambda e, py=py: e.matmul(py[:, 0:128], Wfin[:, 1, 0:128], AMb[0][:, 0, :], start=False, stop=False))
                    sy.op("pe", [ck("Wfin"), ("AM", ci, 1)], [pyk], lambda e, py=py: e.matmul(py[:, 0:128], Wfin[:, 1, 128:256], AMb[1][:, 0, :], start=False, stop=False))
                    sy.op("pe", [ck("TMp"), ("AM", ci, 0)], [pyk], lambda e, py=py: e.matmul(py[:, 0:128], TMp[:, 3, 0:128], AMb[0][:, 2, :], start=False, stop=False))
                    sy.op("pe", [ck("TMp"), ("AM", ci, 1)], [pyk], lambda e, py=py: e.matmul(py[:, 0:128], TMp[:, 3, 128:256], AMb[1][:, 2, :], start=False, stop=True))
                    sy.op("act", [pyk], [yk_], lambda e, py=py: e.activation(out=Y[:, cc], in_=py[:, 0:128], func=AF.Copy))
                    yield
                    pst_, pstk_ = self.ps()
                    sy.op("pe", ["MBD", stck], [pstk_], lambda e, pst_=pst_, stc=stc: e.matmul(pst_[:, 0:128], MBD, stc, start=True, stop=True))
                    sy.op("dve", [pstk_, "NCt"], [stnk], lambda e, pst_=pst_, stn=stn: e.tensor_tensor(out=stn, in0=pst_[:, 0:128], in1=NCt, op=ALU.add))
                    yield

                pgs = [pre_gen(0), pre_gen(1)]
                while pgs:
                    for g_ in list(pgs):
                        try:
                            next(g_)
                        except StopIteration:
                            pgs.remove(g_)
                    yield
                for ci in range(2):
                    for _ in tail_gen(ci):
                        yield
                yield

            def epilogue(rt, s_):
                c_ = make_ctx(rt, s_)
                tok0, tk, t5, xr, proj, small, V = (c_[n] for n in ('tok0', 'tk', 't5', 'xr', 'proj', 'small', 'V'))
                v_sb, cs, AR, BT, KT, BN, vbf, kq, Y, yk_, bnk = (c_[n] for n in ('v_sb', 'cs', 'AR', 'BT', 'KT', 'BN', 'vbf', 'kq', 'Y', 'yk_', 'bnk'))
                pmn, pmnk = small(self.bones, Y, [yk_, "consts"])
                sy.op("dve", [pmnk, yk_], ["YC"], lambda e: e.scalar_tensor_tensor(
                    out=YC, in0=pmn[:, 0:RT], scalar=-1.0 / 64, in1=Y, op0=ALU.mult, op1=ALU.add))
                sy.op("act", ["YC"], ["tC"], lambda e: e.activation(out=tC, in_=YC, func=AF.Square))
                pvr, pvrk = small(self.bones, tC, ["tC", "consts"])
                sy.op("act", [pvrk, "rc"], ["tC"], lambda e: e.activation(out=tC, in_=pvr[:, 0:RT], func=AF.Ln, bias=gneps, scale=1.0 / 64))
                sy.op("act", ["tC"], ["tC"], lambda e: e.activation(out=tC, in_=tC, func=AF.Exp, scale=-0.5))
                sy.op("dve", ["YC", "tC"], ["YC"], lambda e: e.tensor_tensor(out=YC, in0=YC, in1=tC, op=ALU.mult))
                sy.op("dve", ["YC", "vecs"], ["YC"], lambda e: e.tensor_scalar(
                    out=YC, in0=YC, scalar1=V(f"lnx_w{j}"), scalar2=V(f"lnx_b{j}"), op0=ALU.mult, op1=ALU.add))
                sy.op("dve", ["YC", bnk], ["YC"], lambda e: e.tensor_tensor(out=YC, in0=YC, in1=BN, op=ALU.add))
                yield
                pgt, pgtk = small(GA2[:, dcs], GA[:, tk], ["W2", ("GA", t5)], stop=False)
                small(GB2[0:32, dcs], GB[0:32, tk], ["W2", ("GB", t5)], pst=pgt, psk=pgtk, start=False, stop=True)
                yo = yout[stt['yi'] % 2]
                yok = ("yout", stt['yi'] % 2)
                stt['yi'] += 1
                sy.op("dve", ["YC", pgtk], [yok], lambda e, yo=yo: e.tensor_tensor(out=yo, in0=YC, in1=pgt[:, 0:RT], op=ALU.mult))
                for m in range(NK):
                    po, pok = self.ps()
                    sy.op("pe", [wok, yok], [pok], lambda e, m=m, po=po, yo=yo: e.matmul(
                        po[:, 0:RT], wod[:, m * 128:(m + 1) * 128], yo, start=True, stop=True))
                    sy.op("dve", [pok, ("h", m, t5)], [("h", m, t5)], lambda e, m=m, po=po: e.tensor_tensor(
                        out=self.h[:, m, tk], in0=po[:, 0:RT], in1=self.h[:, m, tk], op=ALU.add))
                yield

            return prologue, scanepi, epilogue

        dcg = {}

        def DCG(dc):
            if dc not in dcg:
                dcg[dc] = make_dc(dc)
            return dcg[dc]

        NS = NK * NRT
        for _ in DCG(0)[0](0, 0):
            pass
        for s_ in range(NS + 1):
            gens = []
            if s_ < NS:
                dc, rt = divmod(s_, NRT)
                gens.append(DCG(dc)[1](rt, s_))
            if s_ >= 1:
                dcp, rtp = divmod(s_ - 1, NRT)
                gens.append(DCG(dcp)[2](rtp, s_ - 1))
            if s_ + 1 < NS:
                dcn, rtn = divmod(s_ + 1, NRT)
                gens.append(DCG(dcn)[0](rtn, s_ + 1))
            if s_ < NS and dc + 1 < NK:
                if rt == 1:
                    load_stage(dc + 1)
                if rt == NRT - 1:
                    for _ in fold_gen(dc + 1):
                        pass
            while gens:
                for g_ in list(gens):
                    try:
                        next(g_)
                    except StopIteration:
                        gens.remove(g_)


ALL_LAYERS = []
for _l in range(DEPTH):
    ALL_LAYERS += [("mix", _l), ("mlp", _l)]

WEIGHT_NAMES = ["mlp_up", "mlp_down", "rwkv_w_r", "rwkv_w_k", "rwkv_w_v", "rwkv_w_o",
                "rwkv_decay_w1", "rwkv_decay_w2", "rwkv_iclr_a1", "rwkv_iclr_a2",
                "rwkv_gate_g1", "rwkv_gate_g2", "rwkv_vres_v1", "rwkv_vres_v2",
                "conv_w_in", "conv_w_out", "sb_w_qkv", "sb_w_o"]


def run(inputs, layers, n_cores=8, trace=False):
    inp = {k: np.asarray(v) for k, v in inputs.items()}
    prog = Prog(layers)
    nc = prog.build()
    vecs = pack_vecs(inp)
    wts = {n: np.ascontiguousarray(inp[n], dtype=np.float32) for n in WEIGHT_NAMES}
    in_maps = []
    for b in range(n_cores):
        m = {"xT": np.ascontiguousarray(inp["x"][b].T), "vecs": vecs}
        m.update(wts)
        in_maps.append(m)
    res = run_bass_kernel_spmd(nc, in_maps, core_ids=list(range(n_cores)), trace=trace)
    out = np.stack([np.ascontiguousarray(r["outT"].T) for r in res.results], axis=0)
    return out, res, prog


def kernel(**inputs):
    out, _, _ = run(inputs, ALL_LAYERS)
    return out.astype(np.float32)
```

```python
import numpy as np
import concourse.bass as bass
import concourse.mybir as mybir
from concourse.bass_utils import run_bass_kernel_spmd

F32 = mybir.dt.float32
F32R = mybir.dt.float32r
BF16 = mybir.dt.bfloat16
AF = mybir.ActivationFunctionType
ALU = mybir.AluOpType

D = 1024
S = 2048
NK = 8
TT = 512
NT = S // TT
DFF = 4096
DEPTH = 4
RMS_EPS = 1e-6
GN_EPS = 64e-5


class Sy:
    def __init__(self, nc):
        self.nc = nc
        self.eng = {"pe": nc.tensor, "dve": nc.vector, "act": nc.scalar,
                    "pool": nc.gpsimd, "sp": nc.sync}
        self.sem = {e: nc.alloc_semaphore("s_" + e) for e in self.eng}
        self.cnt = {e: 0 for e in self.eng}
        self.pend = {e: False for e in self.eng}
        self.waited = {e: {} for e in self.eng}
        self.last_w = {}
        self.readers = {}
        self.dsem = {}
        self.dcnt = {}
        self.n_wait = 0
        self.n_inst = 0

    def _wait(self, e, deps):
        need = {}
        for (sk, v) in deps:
            if need.get(sk, 0) < v:
                need[sk] = v
        for sk, v in need.items():
            if sk == e and e == "pe":
                continue
            if self.waited[e].get(sk, 0) >= v:
                continue
            sem = self.sem[sk] if sk in self.sem else self.dsem[sk]
            self.eng[e].wait_ge(sem, v)
            self.waited[e][sk] = v
            self.n_wait += 1

    def _deps(self, reads, writes):
        deps = []
        for k in reads:
            if k in self.last_w:
                deps.append(self.last_w[k])
        for k in writes:
            if k in self.last_w:
                deps.append(self.last_w[k])
            deps.extend(self.readers.get(k, {}).items())
        return deps

    def _record(self, tok, reads, writes):
        for k in reads:
            r = self.readers.setdefault(k, {})
            if r.get(tok[0], 0) < tok[1]:
                r[tok[0]] = tok[1]
        for k in writes:
            self.last_w[k] = tok
            self.readers[k] = {}

    def op(self, e, reads, writes, emit, inc=True):
        psr = [k for k in reads if isinstance(k, tuple) and k[0] == "ps" and k not in writes]
        if psr:
            writes = list(writes) + psr
        self._wait(e, self._deps(reads, writes))
        inst = emit(self.eng[e])
        self.n_inst += 1
        inc = True
        if inc:
            self.cnt[e] += 1
            inst.then_inc(self.sem[e], 1)
            self.pend[e] = False
            tok = (e, self.cnt[e])
        else:
            self.pend[e] = True
            tok = (e, self.cnt[e] + 1)
        self._record(tok, reads, writes)
        return inst

    def dma(self, e, out, in_, reads, writes, sk, **kw):
        if sk not in self.dsem:
            self.dsem[sk] = self.nc.alloc_semaphore("d_" + sk)
            self.dcnt[sk] = 0
        self._wait(e, self._deps(reads, writes))
        inst = self.eng[e].dma_start(out=out, in_=in_, **kw)
        self.dcnt[sk] += 16
        inst.then_inc(self.dsem[sk], 16)
        self.n_inst += 1
        self._record((sk, self.dcnt[sk]), reads, writes)
        return inst

    def barrier(self):
        for e in self.eng:
            deps = [(e2, self.cnt[e2]) for e2 in self.eng if e2 != e and self.cnt[e2] > 0]
            deps += [(sk, self.dcnt[sk]) for sk in self.dsem if self.dcnt[sk] > 0]
            self._wait(e, deps)

    def wait_all(self, e, keys):
        deps = []
        for k in keys:
            if k in self.last_w:
                deps.append(self.last_w[k])
            deps.extend(self.readers.get(k, {}).items())
        self._wait(e, deps)


VEC_COLS = {}


def _vec_layout():
    cols = {}
    off = 0

    def add(name, n=NK):
        nonlocal off
        cols[name] = off
        off += n
    for l in range(DEPTH):
        add(f"mix_norm{l}")
        add(f"mlp_norm{l}")
    for j in range(2):
        for m in range(6):
            add(f"mu{j}_{m}")
        for nm in ("w0", "a0", "k_k", "k_a", "r_k", "lnx_w", "lnx_b"):
            add(f"{nm}{j}")
    add("v0")
    for c in range(3):
        add(f"conv_w{c}")
    add("q_gain", 1)
    add("k_gain", 1)
    return cols, off


VEC_COLS, NVEC = _vec_layout()


def pack_vecs(inp):
    tab = np.zeros((128, NVEC), np.float32)

    def put(name, v):
        v = np.asarray(v, np.float32).reshape(-1)
        c = VEC_COLS[name]
        if v.size == D:
            tab[:, c:c + NK] = v.reshape(NK, 128).T
        else:
            tab[:, c] = np.concatenate([v, v])
    for l in range(DEPTH):
        put(f"mix_norm{l}", inp["mix_norm"][l])
        put(f"mlp_norm{l}", inp["mlp_norm"][l])
    for j in range(2):
        for m in range(6):
            put(f"mu{j}_{m}", inp["rwkv_mu"][j, m])
        put(f"w0{j}", inp["rwkv_decay_w0"][j])
        put(f"a0{j}", inp["rwkv_iclr_a0"][j])
        put(f"k_k{j}", inp["rwkv_k_k"][j])
        put(f"k_a{j}", inp["rwkv_k_a"][j])
        put(f"r_k{j}", inp["rwkv_r_k"][j])
        put(f"lnx_w{j}", inp["rwkv_lnx_w"][j])
        put(f"lnx_b{j}", inp["rwkv_lnx_b"][j])
    put("v0", inp["rwkv_vres_v0"][0])
    for c in range(3):
        put(f"conv_w{c}", inp["conv_w"][0, c])
    put("q_gain", inp["sb_q_norm"][0])
    put("k_gain", inp["sb_k_norm"][0])
    return tab


class Prog:
    def __init__(self, layers, n_layers_mlp=None):
        self.layers = layers
        nc = bass.Bass("TRN2", target_bir_lowering=False)
        self.nc = nc
        self.sy = Sy(nc)
        self._n = 0
        dt = nc.dram_tensor
        self.xT = dt("xT", [D, S], F32, kind="ExternalInput").ap()
        self.vecs_d = dt("vecs", [128, NVEC], F32, kind="ExternalInput").ap()
        self.outT = dt("outT", [D, S], F32, kind="ExternalOutput").ap()
        self.w = {}
        for name, shape in (
            ("mlp_up", [DEPTH, D, DFF]), ("mlp_down", [DEPTH, DFF, D]),
            ("rwkv_w_r", [2, D, D]), ("rwkv_w_k", [2, D, D]), ("rwkv_w_v", [2, D, D]),
            ("rwkv_w_o", [2, D, D]),
            ("rwkv_decay_w1", [2, D, 64]), ("rwkv_decay_w2", [2, 64, D]),
            ("rwkv_iclr_a1", [2, D, 64]), ("rwkv_iclr_a2", [2, 64, D]),
            ("rwkv_gate_g1", [2, D, 160]), ("rwkv_gate_g2", [2, 160, D]),
            ("rwkv_vres_v1", [1, D, 32]), ("rwkv_vres_v2", [1, 32, D]),
            ("conv_w_in", [1, D, 3 * D]), ("conv_w_out", [1, D, D]),
            ("sb_w_qkv", [1, D, 3 * D]), ("sb_w_o", [1, D, D]),
        ):
            self.w[name] = dt(name, shape, F32, kind="ExternalInput").ap()
        self.vfirst = dt("vfirst_scratch", [D, S], F32, kind="Internal").ap()
        self.psum = [nc.alloc_psum_tensor(f"ps{i}", [128, 512], F32).ap() for i in range(8)]
        self.ps_i = 0

    def sb(self, name, shape, dtype):
        return self.nc.alloc_sbuf_tensor(name, shape, dtype).ap()

    def ps(self):
        i = self.ps_i
        self.ps_i = (i + 1) % 6
        return self.psum[i], ("ps", i)

    def phase_begin(self):
        self.sy.barrier()
        self.arena_off = 0

    def carve(self, shape, dtype):
        n = 1
        for d in shape:
            n *= d
        nb = n * (4 if dtype in (F32, F32R) else 2)
        nb = (nb + 31) // 32 * 32
        off = self.arena_off
        assert off + nb <= self.ARENA * 2, (off, nb, self.ARENA * 2)
        self.arena_off = off + nb
        self._n += 1
        return self.nc.alloc_sbuf_tensor_at(f"cv{self._n}", [128] + list(shape), dtype,
                                            offset=self.arena_base + off).ap()

    def vcol(self, name, k=0):
        c = VEC_COLS[name] + k
        return self.vecs[:, c:c + 1]

    def build(self):
        nc, sy = self.nc, self.sy
        self.h = self.sb("h", [128, NK, S], F32)
        self.xn = self.sb("xn", [128, NK, S + 2], BF16)
        self.vecs = self.sb("vecs_sb", [128, NVEC], F32)
        self.ones_bf = self.sb("ones_bf", [128, 128], BF16)
        self.eps_t = self.sb("eps_t", [128, 1], F32)
        self.ARENA = 53 * 1024 + 512
        self.arena = self.sb("arena", [128, self.ARENA], BF16)
        self.arena_base = self.nc.sbuf_base - self.ARENA * 2
        self.arena_off = 0
        self.make_consts()
        sy.dma("sp", self.vecs, self.vecs_d, [], ["vecs"], "misc")
        for k in range(NK):
            sy.dma("sp", self.h[:, k, :], self.xT[k * 128:(k + 1) * 128, :], [], [("h", k, t) for t in range(NT)], f"xin{k}")
        sy.op("dve", [], ["ones"], lambda e: e.memset(self.ones_bf, 1.0))
        sy.op("dve", [], ["eps"], lambda e: e.memset(self.eps_t, RMS_EPS))
        sy.op("dve", [], [("xnpad",)], lambda e: e.memset(self.xn[:, :, 0:2], 0.0))
        for (kind, l) in self.layers:
            if kind == "mlp":
                self.rmsnorm(f"mlp_norm{l}")
                self.mlp(l)
            elif kind == "mix":
                self.rmsnorm(f"mix_norm{l}")
                if l % 3 == 1:
                    self.conv(l // 3)
                elif l % 3 == 2:
                    self.sbatt(l // 3)
                else:
                    self.rwkv(l // 3)
        for k in range(NK):
            sy.dma("sp", self.outT[k * 128:(k + 1) * 128, :], self.h[:, k, :],
                   [("h", k, t) for t in range(NT)], [("out", k)], "out")
        sy.wait_all("sp", [("out", k) for k in range(NK)])
        return nc

    def make_consts(self):
        sy = self.sy
        self.one_col = self.sb("one_col", [128, 1], F32)
        self.bones = self.sb("bones", [128, 128], F32)
        self.tri = self.sb("tri", [128, 128], F32R)
        self.onesr = self.sb("onesr", [128, 128], F32R)
        self.onesw = self.carve([128], F32)
        sy.op("dve", [], ["consts"], lambda e: e.memset(self.one_col, 1.0))
        sy.op("dve", [], ["consts"], lambda e: e.memset(self.bones, 0.0))
        sy.op("dve", [], ["consts"], lambda e: e.memset(self.bones[0:64, 0:64], 1.0))
        sy.op("dve", [], ["consts"], lambda e: e.memset(self.bones[64:128, 64:128], 1.0))
        sy.op("dve", [], ["consts"], lambda e: e.memset(self.onesw, 1.0))
        sy.op("pool", ["consts"], ["consts2"], lambda e: e.affine_select(
            out=self.tri, in_=self.onesw[:, 0:128], pattern=[[-1, 128]], compare_op=ALU.is_ge, fill=0.0,
            base=0, channel_multiplier=1))
        sy.op("pool", ["consts"], ["consts2"], lambda e: e.tensor_copy(out=self.onesr, in_=self.onesw[:, 0:128]))

    def rmsnorm(self, gname):
        sy = self.sy
        self.phase_begin()
        self.sq = [self.carve([TT], BF16) for i in range(2)]
        self.rstd = [self.carve([TT], F32) for i in range(2)]
        for t in range(NT):
            ts = slice(t * TT, (t + 1) * TT)
            pst, psk = self.ps()
            for k in range(NK):
                sq = self.sq[k % 2]
                sqk = ("sq", k % 2)
                sy.op("act", [("h", k, t)], [sqk],
                      lambda e, sq=sq, k=k: e.activation(out=sq, in_=self.h[:, k, ts], func=AF.Square))
                sy.op("pe", [sqk, "ones"], [psk],
                      lambda e, sq=sq, k=k: e.matmul(pst, self.ones_bf, sq, start=(k == 0), stop=(k == NK - 1)),
                      inc=(k == NK - 1))
            rs = self.rstd[t % 2]
            rsk = ("rstd", t % 2)
            sy.op("act", [psk, "eps"], [rsk],
                  lambda e: e.activation(out=rs, in_=pst, func=AF.Ln, bias=self.eps_t, scale=1.0 / D))
            sy.op("act", [rsk], [rsk], lambda e: e.activation(out=rs, in_=rs, func=AF.Exp, scale=-0.5))
            for k in range(NK):
                sy.op("dve", [("h", k, t), rsk, "vecs"], [("xn", k, t)],
                      lambda e, k=k: e.scalar_tensor_tensor(
                          out=self.xn[:, k, 2 + t * TT:2 + (t + 1) * TT], in0=self.h[:, k, ts],
                          scalar=self.vcol(gname, k), in1=rs, op0=ALU.mult, op1=ALU.mult))

    def alloc_mlp(self):
        self.phase_begin()
        self.GF = 512
        self.wup = [self.carve([NK, self.GF], BF16) for i in range(2)]
        self.wdn = [self.carve([self.GF // 128, D], BF16) for i in range(2)]
        self.hT = self.carve([self.GF // 128, S], BF16)
        self.relu_t = [self.carve([TT], F32) for i in range(2)]
        self.mlp_gi = 0

    def mlp_load(self, l, g):
        sy = self.sy
        GF = self.GF
        s = self.mlp_gi % 2
        self.mlp_gi += 1
        src_up = self.w["mlp_up"][l, :, g * GF:(g + 1) * GF].rearrange("(k p) f -> p k f", p=128)
        sy.dma("pool", self.wup[s], src_up, [], [("wup", s)], f"wup{s}")
        src_dn = self.w["mlp_down"][l, g * GF:(g + 1) * GF, :].rearrange("(c p) d -> p c d", p=128)
        sy.dma("pool", self.wdn[s], src_dn, [], [("wdn", s)], f"wdn{s}")
        return s

    def mlp(self, l):
        sy = self.sy
        self.alloc_mlp()
        GF = self.GF
        NG = DFF // GF
        NC = GF // 128
        slots = [None] * NG
        slots[0] = self.mlp_load(l, 0)
        ri = 0
        for g in range(NG):
            if g + 1 < NG:
                slots[g + 1] = self.mlp_load(l, g + 1)
            s = slots[g]
            for t in range(NT):
                for c in range(NC):
                    pst, psk = self.ps()
                    for k in range(NK):
                        sy.op("pe", [("wup", s), ("xn", k, t)], [psk],
                              lambda e, k=k, c=c, t=t: e.matmul(
                                  pst, self.wup[s][:, k, c * 128:(c + 1) * 128],
                                  self.xn[:, k, 2 + t * TT:2 + (t + 1) * TT],
                                  start=(k == 0), stop=(k == NK - 1)),
                              inc=(k == NK - 1))
                    rt = self.relu_t[ri % 2]
                    rk = ("relu", ri % 2)
                    ri += 1
                    sy.op("act", [psk], [rk], lambda e, rt=rt: e.activation(out=rt, in_=pst, func=AF.Relu))
                    sy.op("dve", [rk, psk], [("hT", c, t)],
                          lambda e, rt=rt, c=c, t=t: e.tensor_tensor(
                              out=self.hT[:, c, t * TT:(t + 1) * TT], in0=rt, in1=pst, op=ALU.mult))
            for t in range(NT):
                for m in range(NK):
                    pst, psk = self.ps()
                    for c in range(NC):
                        sy.op("pe", [("wdn", s), ("hT", c, t)], [psk],
                              lambda e, c=c, m=m, t=t: e.matmul(
                                  pst, self.wdn[s][:, c, m * 128:(m + 1) * 128],
                                  self.hT[:, c, t * TT:(t + 1) * TT],
                                  start=(c == 0), stop=(c == NC - 1)),
                              inc=(c == NC - 1))
                    sy.op("dve", [psk, ("h", m, t)], [("h", m, t)],
                          lambda e, m=m, t=t: e.tensor_tensor(
                              out=self.h[:, m, t * TT:(t + 1) * TT], in0=pst,
                              in1=self.h[:, m, t * TT:(t + 1) * TT], op=ALU.add))


    def proj_fm(self, wt, wkey, cols, t, shift=0, pst=None, psk=None, first=True, last=True):
        sy = self.sy
        if pst is None:
            pst, psk = self.ps()
        for k in range(NK):
            sy.op("pe", [wkey, ("xn", k, t)] + ([("xn", k, t - 1)] if (shift and t > 0) else []), [psk],
                  lambda e, k=k: e.matmul(pst, wt[:, k, cols],
                                          self.xn[:, k, 2 - shift + t * TT:2 - shift + (t + 1) * TT],
                                          start=(first and k == 0), stop=(last and k == NK - 1)))
        return pst, psk

    def outproj_acc(self, wo, wokey, y, ykey, t):
        sy = self.sy
        for m in range(NK):
            pst, psk = self.ps()
            sy.op("pe", [wokey, ykey], [psk],
                  lambda e, m=m: e.matmul(pst, wo[:, m * 128:(m + 1) * 128], y, start=True, stop=True))
            sy.op("dve", [psk, ("h", m, t)], [("h", m, t)],
                  lambda e, m=m: e.tensor_tensor(out=self.h[:, m, t * TT:(t + 1) * TT], in0=pst,
                                                 in1=self.h[:, m, t * TT:(t + 1) * TT], op=ALU.add))

    def alloc_mix(self):
        self.phase_begin()
        self.w3 = [self.carve([NK, 3, 128], BF16) for i in range(2)]
        self.wo = [self.carve([D], BF16) for i in range(2)]
        self.ybf = [self.carve([TT], BF16) for i in range(2)]
        self.mix_i = 0

    def load_w3(self, wname, j, dc, wo_name):
        sy = self.sy
        s = self.mix_i % 2
        self.mix_i += 1
        src = self.w[wname][j].rearrange("(k p) f -> p k f", p=128)
        for jj in range(3):
            sy.dma("pool", self.w3[s][:, :, jj, :], src[:, :, jj * D + dc * 128:jj * D + (dc + 1) * 128],
                   [], [("w3", s)], f"w3_{s}")
        sy.dma("pool", self.wo[s], self.w[wo_name][j, dc * 128:(dc + 1) * 128, :], [], [("wo", s)], f"wo_{s}")
        return s

    def conv(self, j):
        sy = self.sy
        self.alloc_mix()
        csb = [self.carve([TT], F32) for i in range(2)]
        bsb = [self.carve([TT], F32) for i in range(2)]
        acc = self.carve([TT], F32)
        zbufs = [self.carve([2 + S], F32) for i in range(2)]
        slots = [None] * NK
        slots[0] = self.load_w3("conv_w_in", j, 0, "conv_w_out")
        slots[1] = self.load_w3("conv_w_in", j, 1, "conv_w_out")
        items = [(dc, t) for dc in range(NK) for t in range(NT)]
        st = {"yi": 0}

        def stage1(n):
            dc, t = items[n]
            i2 = n % 2
            if t == 0:
                sy.op("dve", [], [("z", dc % 2, -1)], lambda e: e.memset(zbufs[dc % 2][:, 0:2], 0.0))
            s = slots[dc]
            w3 = self.w3[s]
            zb = zbufs[dc % 2]
            zs = slice(2 + t * TT, 2 + (t + 1) * TT)
            pb, pbk = self.proj_fm(w3[:, :, 0, :], ("w3", s), slice(0, 128), t)
            sy.op("act", [pbk], [("bsb", i2)], lambda e: e.activation(out=bsb[i2], in_=pb, func=AF.Copy))
            pc, pck = self.proj_fm(w3[:, :, 1, :], ("w3", s), slice(0, 128), t)
            sy.op("act", [pck], [("csb", i2)], lambda e: e.activation(out=csb[i2], in_=pc, func=AF.Copy))
            pu, puk = self.proj_fm(w3[:, :, 2, :], ("w3", s), slice(0, 128), t)
            sy.op("dve", [("csb", i2), puk], [("z", dc % 2, t)],
                  lambda e: e.tensor_tensor(out=zb[:, zs], in0=csb[i2], in1=pu, op=ALU.mult))

        def stage2(n):
            dc, t = items[n]
            i2 = n % 2
            s = slots[dc]
            wo = self.wo[s]
            zb = zbufs[dc % 2]
            zs = slice(2 + t * TT, 2 + (t + 1) * TT)
            zk = [("z", dc % 2, t), ("z", dc % 2, t - 1)]
            sy.op("dve", zk + ["vecs"], ["acc"],
                  lambda e: e.tensor_scalar(out=acc, in0=zb[:, t * TT:(t + 1) * TT],
                                            scalar1=self.vcol("conv_w0", dc), scalar2=None, op0=ALU.mult))
            sy.op("dve", zk + ["acc", "vecs"], ["acc"],
                  lambda e: e.scalar_tensor_tensor(out=acc, in0=zb[:, 1 + t * TT:1 + (t + 1) * TT],
                                                   scalar=self.vcol("conv_w1", dc), in1=acc,
                                                   op0=ALU.mult, op1=ALU.add))
            sy.op("dve", zk + ["acc", "vecs"], ["acc"],
                  lambda e: e.scalar_tensor_tensor(out=acc, in0=zb[:, zs],
                                                   scalar=self.vcol("conv_w2", dc), in1=acc,
                                                   op0=ALU.mult, op1=ALU.add))
            y = self.ybf[st["yi"] % 2]
            yk = ("ybf", st["yi"] % 2)
            st["yi"] += 1
            sy.op("dve", [("bsb", i2), "acc"], [yk], lambda e: e.tensor_tensor(out=y, in0=bsb[i2], in1=acc, op=ALU.mult))
            self.outproj_acc(wo, ("wo", s), y, yk, t)

        for n in range(len(items) + 1):
            if n < len(items):
                stage1(n)
            if n >= 1:
                stage2(n - 1)
                dcp, tp = items[n - 1]
                if tp == NT - 1 and dcp + 2 < NK:
                    slots[dcp + 2] = self.load_w3("conv_w_in", j, dcp + 2, "conv_w_out")

    def sbatt(self, j):
        sy = self.sy
        self.alloc_mix()
        qn = self.carve([S], F32R)
        kn = self.carve([S], F32R)
        vpA = self.carve([16, 128], BF16)
        vpB = self.carve([16, 128], BF16)
        raw_t = [self.carve([TT], F32) for i in range(2)]
        sq_t = [self.carve([TT], F32) for i in range(2)]
        rs_t = [self.carve([TT], F32) for i in range(2)]
        e_t = [self.carve([TT], F32) for i in range(3)]
        sp_t = [self.carve([TT], F32) for i in range(3)]
        lk_t = [self.carve([TT], F32R) for i in range(3)]
        u_t = [self.carve([TT], F32) for i in range(3)]
        arg_t = [self.carve([TT], F32) for i in range(3)]
        att_t = [self.carve([TT], BF16) for i in range(3)]
        R_t = [self.carve([TT], F32R) for i in range(2)]
        qgs = self.carve([1], F32)
        self.m01 = self.carve([896], BF16)
        self.mneg = self.carve([896], F32)
        onesw = self.carve([896], BF16)
        sy.op("dve", [], ["sbc"], lambda e: e.memset(onesw, 1.0))
        sy.op("pool", ["sbc"], ["consts2"], lambda e: e.affine_select(
            out=self.m01, in_=onesw, pattern=[[1, 896]], compare_op=ALU.is_gt, fill=0.0,
            base=-384, channel_multiplier=-1))
        sy.op("pool", ["consts2"], ["consts2"], lambda e: e.tensor_scalar(
            out=self.mneg, in0=self.m01, scalar1=-1.0, scalar2=None, op0=ALU.mult))
        sy.op("dve", ["vecs"], ["qgs"], lambda e: e.tensor_scalar(
            out=qgs, in0=self.vcol("q_gain"), scalar1=0.125, scalar2=None, op0=ALU.mult))
        sy.op("dve", [], ["vpA"], lambda e: e.memset(vpA, 0.0))
        sy.op("dve", [], ["vpB"], lambda e: e.memset(vpB, 0.0))
        slots = [None] * NK
        slots[0] = self.load_w3("sb_w_qkv", j, 0, "sb_w_o")
        ni = 0
        pi = 0
        oi = 0
        yi = 0
        for dc in range(NK):
            if dc + 1 < NK:
                slots[dc + 1] = self.load_w3("sb_w_qkv", j, dc + 1, "sb_w_o")
            s = slots[dc]
            w3, wo = self.w3[s], self.wo[s]
            for t in range(NT):
                ts = slice(t * TT, (t + 1) * TT)
                pq, pqk = self.proj_fm(w3[:, :, 0, :], ("w3", s), slice(0, 128), t)
                pk_, pkk_ = self.proj_fm(w3[:, :, 1, :], ("w3", s), slice(0, 128), t)
                pv, pvk = self.ps()
                for q4 in range(4):
                    for k in range(NK):
                        sy.op("pe", [("w3", s), ("xn", k, t)], [pvk],
                              lambda e, k=k, q4=q4: e.matmul(
                                  pv[:, q4 * 128:(q4 + 1) * 128],
                                  self.xn[:, k, 2 + t * TT + q4 * 128:2 + t * TT + (q4 + 1) * 128],
                                  w3[:, k, 2, :], start=(k == 0), stop=(k == NK - 1)))
                chains = []
                for (pp, ppk, dst, dkey, gcol) in ((pq, pqk, qn, "qn", qgs), (pk_, pkk_, kn, "kn", self.vcol("k_gain"))):
                    raw, sq, rs = raw_t[ni % 2], sq_t[ni % 2], rs_t[ni % 2]
                    rk, sk_, rsk = ("raw", ni % 2), ("sqq", ni % 2), ("rsq", ni % 2)
                    ni += 1
                    sy.op("act", [ppk], [rk], lambda e, raw=raw, pp=pp: e.activation(out=raw, in_=pp, func=AF.Copy))
                    sy.op("act", [ppk], [sk_], lambda e, sq=sq, pp=pp: e.activation(out=sq, in_=pp, func=AF.Square))
                    chains.append((raw, sq, rs, rk, sk_, rsk, dst, dkey, gcol))
                pv3 = pv.rearrange("p (a b) -> p a b", a=4)
                sy.op("act", [pvk], ["vpA"], lambda e, pv3=pv3: e.activation(
                    out=vpA[:, 4 * t:4 * t + 4, 0:64], in_=pv3[:, :, 0:64], func=AF.Copy))
                sy.op("act", [pvk], ["vpB"], lambda e, pv3=pv3: e.activation(
                    out=vpB[:, 4 * t:4 * t + 4, 64:128], in_=pv3[:, :, 64:128], func=AF.Copy))
                for (raw, sq, rs, rk, sk_, rsk, dst, dkey, gcol) in chains:
                    p2, p2k = self.ps()
                    sy.op("pe", [sk_, "consts"], [p2k],
                          lambda e, sq=sq, p2=p2: e.matmul(p2, self.bones, sq, start=True, stop=True))
                    sy.op("act", [p2k, "eps"], [rsk],
                          lambda e, rs=rs, p2=p2: e.activation(out=rs, in_=p2, func=AF.Ln, bias=self.eps_t, scale=1.0 / 64))
                    sy.op("act", [rsk], [rsk], lambda e, rs=rs: e.activation(out=rs, in_=rs, func=AF.Exp, scale=-0.5))
                    sy.op("dve", [rk, rsk, "vecs", "qgs"], [(dkey, t)],
                          lambda e, raw=raw, rs=rs, dst=dst, gcol=gcol: e.scalar_tensor_tensor(
                              out=dst[:, ts], in0=raw, scalar=gcol, in1=rs, op0=ALU.mult, op1=ALU.mult))
            pairs = []
            for T in range(NT):
                for hd in range(2):
                    cmax = 4 * T + 3
                    for c in range(cmax, -1, -1):
                        pairs.append(dict(T=T, hd=hd, c=c, cmax=cmax, first=(hd == 0 and c == cmax),
                                          last=(hd == 1 and c == 0)))
            NB = 3
            o_banks = {}
            for T in range(NT):
                o_banks[T] = 6 + oi % 2
                oi += 1
            rstate = {"cur": 0}

            def stage1(n, p):
                i2 = n % NB
                hp = slice(p["hd"] * 64, p["hd"] * 64 + 64)
                T, c = p["T"], p["c"]
                Ts = slice(T * TT, (T + 1) * TT)
                jd = c - 4 * T
                pz, pzk = self.ps()
                sy.op("pe", [("kn", c // 4), ("qn", T)], [pzk],
                      lambda e: e.matmul(pz, kn[hp, c * 128:(c + 1) * 128], qn[hp, Ts], start=True, stop=True))
                sy.op("act", [pzk], [("e", i2)], lambda e: e.activation(out=e_t[i2], in_=pz, func=AF.Exp))
                sy.op("act", [("e", i2), "consts"], [("sp", i2)],
                      lambda e: e.activation(out=sp_t[i2], in_=e_t[i2], func=AF.Ln, bias=self.one_col, scale=1.0))
                if jd >= 0:
                    sy.op("dve", [("sp", i2), "consts2"], [("lk", i2)],
                          lambda e: e.tensor_tensor(out=lk_t[i2], in0=sp_t[i2],
                                                    in1=self.mneg[:, 384 - 128 * jd:896 - 128 * jd], op=ALU.mult))
                else:
                    sy.op("dve", [("sp", i2)], [("lk", i2)],
                          lambda e: e.tensor_scalar(out=lk_t[i2], in0=sp_t[i2], scalar1=-1.0, scalar2=None, op0=ALU.mult))

            def stage2(n, p):
                i2 = n % NB
                T, c, cmax = p["T"], p["c"], p["cmax"]
                jd = c - 4 * T
                if c == cmax:
                    rstate["cur"] = 0
                rcur = rstate["cur"]
                hp = slice(p["hd"] * 64, p["hd"] * 64 + 64)
                Ts = slice(T * TT, (T + 1) * TT)
                pt, ptk = self.ps()
                sy.op("pe", [("lk", i2), "consts2"], [ptk],
                      lambda e: e.matmul(pt, self.tri, lk_t[i2], start=True, stop=False))
                if c < cmax:
                    sy.op("pe", [("R", rcur), "consts2"], [ptk],
                          lambda e: e.matmul(pt, self.onesr, R_t[rcur], start=False, stop=False))
                sy.op("pe", [("kn", c // 4), ("qn", T)], [ptk],
                      lambda e: e.matmul(pt, kn[hp, c * 128:(c + 1) * 128], qn[hp, Ts], start=False, stop=True))
                if c > 0:
                    rn = 1 - rcur
                    if c == cmax:
                        sy.op("pool", [("lk", i2)], [("R", rn)], lambda e: e.tensor_copy(out=R_t[rn], in_=lk_t[i2]))
                    else:
                        sy.op("pool", [("lk", i2), ("R", rcur)], [("R", rn)],
                              lambda e: e.tensor_tensor(out=R_t[rn], in0=R_t[rcur], in1=lk_t[i2], op=ALU.add))
                    rstate["cur"] = rn
                sy.op("act", [ptk], [("att", i2)],
                      lambda e: e.activation(out=att_t[i2], in_=pt, func=AF.Exp))
                if jd >= 0:
                    sy.op("dve", [("att", i2), "consts2"], [("att", i2)],
                          lambda e: e.tensor_tensor(out=att_t[i2], in0=att_t[i2],
                                                    in1=self.m01[:, 384 - 128 * jd:896 - 128 * jd], op=ALU.mult))

            def stage3(n, p):
                nonlocal yi
                i2 = n % NB
                T, c = p["T"], p["c"]
                ob = o_banks[T]
                o_ps, ok = self.psum[ob], ("ps", ob)
                vp, vpk = (vpA, "vpA") if p["hd"] == 0 else (vpB, "vpB")
                sy.op("pe", [vpk, ("att", i2)], [ok],
                      lambda e: e.matmul(o_ps, vp[:, c, :], att_t[i2], start=p["first"], stop=p["last"]))
                if p["last"]:
                    y = self.ybf[yi % 2]
                    yk = ("ybf", yi % 2)
                    yi += 1
                    sy.op("act", [ok], [yk], lambda e: e.activation(out=y, in_=o_ps, func=AF.Copy))
                    self.outproj_acc(wo, ("wo", s), y, yk, T)

            npairs = len(pairs)
            for n in range(npairs + 2):
                if n < npairs:
                    stage1(n, pairs[n])
                if 1 <= n and n - 1 < npairs:
                    stage2(n - 1, pairs[n - 1])
                if 2 <= n:
                    stage3(n - 2, pairs[n - 2])

    def rwkv(self, j):
        sy = self.sy
        self.phase_begin()
        RT = 256
        NRT = S // RT
        CD = 0.6065306597126334
        cv = self.carve
        P1, GA, GB = cv([S], BF16), cv([S], BF16), cv([S], BF16)
        WA2, GA2, GB2 = cv([D], BF16), cv([D], BF16), cv([D], BF16)
        omm, okka = cv([48], F32), cv([8], F32)
        ident, mk4, mkL, cmask = cv([128], F32), cv([512], F32), cv([128], F32), cv([RT], F32)
        gneps, tiny = cv([1], F32), cv([1], F32)
        mark = self.arena_off
        onesw = cv([512], F32)
        LWs, LW = cv([NK, 320], F32), cv([NK, 2, 320], BF16)
        mu0 = VEC_COLS[f"mu{j}_0"]
        mucols = self.vecs[:, mu0:mu0 + 48]
        sy.op("dve", [], ["rc"], lambda e: e.memset(onesw, 1.0))
        sy.op("dve", [], ["rc"], lambda e: e.memset(gneps, GN_EPS))
        sy.op("dve", [], ["rc"], lambda e: e.memset(tiny, 1e-24))
        sy.op("dve", [], ["cmask"], lambda e: e.memset(cmask, 1.0))
        sy.op("dve", [], ["cmask"], lambda e: e.memset(cmask[:, 0:1], 0.0))
        sy.op("dve", [], ["cmask"], lambda e: e.memset(cmask[:, 128:129], 0.0))
        sy.op("dve", ["vecs"], ["omm"], lambda e: e.tensor_scalar(
            out=omm, in0=mucols, scalar1=-1.0, scalar2=1.0, op0=ALU.mult, op1=ALU.add))
        ka0 = VEC_COLS[f"k_a{j}"]
        sy.op("dve", ["vecs"], ["omm"], lambda e: e.tensor_scalar(
            out=okka, in0=self.vecs[:, ka0:ka0 + 8], scalar1=-1.0, scalar2=1.0, op0=ALU.mult, op1=ALU.add))
        sy.op("pool", ["rc"], ["rc2"], lambda e: e.affine_select(
            out=ident, in_=onesw[:, 0:128], pattern=[[-1, 128]], compare_op=ALU.is_equal, fill=0.0,
            base=0, channel_multiplier=1))
        sy.op("pool", ["rc"], ["rc2"], lambda e: e.affine_select(
            out=mkL, in_=onesw[:, 0:128], pattern=[[-1, 128]], compare_op=ALU.is_gt, fill=0.0,
            base=0, channel_multiplier=1))
        sy.op("pool", ["rc"], ["rc2"], lambda e: e.affine_select(
            out=mk4, in_=onesw, pattern=[[0, 2], [1, 2], [1, 128]], compare_op=ALU.is_gt, fill=0.0,
            base=0, channel_multiplier=-1))
        wr = lambda n, jj=j: self.w[n][jj]
        sy.dma("sp", LWs[:, :, 0:64], wr("rwkv_decay_w1").rearrange("(k p) c -> p k c", p=128), [], ["LWs"], "lws")
        sy.dma("sp", LWs[:, :, 64:128], wr("rwkv_iclr_a1").rearrange("(k p) c -> p k c", p=128), [], ["LWs"], "lws")
        sy.dma("sp", LWs[:, :, 128:288], wr("rwkv_gate_g1").rearrange("(k p) c -> p k c", p=128), [], ["LWs"], "lws")
        if j == 1:
            sy.dma("sp", LWs[:, :, 288:320], self.w["rwkv_vres_v1"][0].rearrange("(k p) c -> p k c", p=128), [], ["LWs"], "lws")
        sy.dma("pool", WA2[0:64, :], wr("rwkv_decay_w2"), [], ["W2"], "w2s")
        sy.dma("pool", WA2[64:128, :], wr("rwkv_iclr_a2"), [], ["W2"], "w2s")
        sy.dma("pool", GA2, wr("rwkv_gate_g2")[0:128, :], [], ["W2"], "w2s")
        sy.dma("pool", GB2[0:32, :], wr("rwkv_gate_g2")[128:160, :], [], ["W2"], "w2s")
        if j == 1:
            sy.dma("pool", GB2[32:64, :], self.w["rwkv_vres_v2"][0], [], ["W2"], "w2s")
        blocks = [(0, 64, 1), (64, 128, 4), (128, 288, 5)] + ([(288, 320, 3)] if j == 1 else [])
        for k in range(NK):
            for (c0, c1, m) in blocks:
                sy.op("act", ["LWs", "omm"], ["LW"], lambda e, k=k, c0=c0, c1=c1, m=m: e.activation(
                    out=LW[:, k, 0, c0:c1], in_=LWs[:, k, c0:c1], func=AF.Copy, scale=omm[:, m * 8 + k:m * 8 + k + 1]))
                sy.op("dve", ["LWs", "vecs"], ["LW"], lambda e, k=k, c0=c0, c1=c1, m=m: e.tensor_scalar(
                    out=LW[:, k, 1, c0:c1], in0=LWs[:, k, c0:c1], scalar1=self.vecs[:, mu0 + m * 8 + k:mu0 + m * 8 + k + 1],
                    scalar2=None, op0=ALU.mult))
        NL3 = 64 if j == 1 else 32
        for t in range(NT):
            ts = slice(t * TT, (t + 1) * TT)
            for (c0, M, which) in ((0, 128, 0), (128, 128, 1), (256, NL3, 2)):
                pst, psk = self.ps()
                for k in range(NK):
                    for sh in range(2):
                        rd = [("xn", k, t), "LW"] + ([("xn", k, t - 1)] if (sh and t > 0) else [])
                        sy.op("pe", rd, [psk], lambda e, k=k, sh=sh, c0=c0, M=M, pst=pst: e.matmul(
                            pst[0:M, :], LW[:, k, sh, c0:c0 + M], self.xn[:, k, 2 - sh + t * TT:2 - sh + (t + 1) * TT],
                            start=(k == 0 and sh == 0), stop=(k == NK - 1 and sh == 1)))
                if which == 0:
                    sy.op("act", [psk], [("P1", t)], lambda e, pst=pst: e.activation(out=P1[0:64, ts], in_=pst[0:64, :], func=AF.Tanh))
                    sy.op("act", [psk], [("P1", t)], lambda e, pst=pst: e.activation(out=P1[64:128, ts], in_=pst[64:128, :], func=AF.Copy))
                elif which == 1:
                    sy.op("act", [psk], [("GA", t)], lambda e, pst=pst: e.activation(out=GA[:, ts], in_=pst, func=AF.Sigmoid))
                else:
                    sy.op("act", [psk], [("GB", t)], lambda e, pst=pst: e.activation(out=GB[0:32, ts], in_=pst[0:32, :], func=AF.Sigmoid))
                    if j == 1:
                        sy.op("act", [psk], [("GB", t)], lambda e, pst=pst: e.activation(out=GB[32:64, ts], in_=pst[32:64, :], func=AF.Copy))
        sy.barrier()
        self.arena_off = mark
        Wst = cv([NK, 3, 128], F32)
        Wfs = [cv([NK, 3, 2, 128], BF16)]
        wo = [cv([D], BF16) for i in range(2)]
        f1 = lambda: cv([RT], F32)
        r_sb, k_sb, sg, csp, pinv, asig, kk, tA, tB, tC, YC, vf = [f1() for _ in range(12)]
        Yb = [f1() for _ in range(2)]
        BN3 = [f1() for _ in range(3)]
        Bset = [(f1(), f1(), cv([2, 2, 128], BF16), cv([RT], BF16), cv([RT], BF16), None, cv([RT], BF16)) for _ in range(2)]
        yout = [cv([RT], BF16) for i in range(2)]
        def chunk_set():
            return (cv([128], BF16), cv([128], BF16), cv([4, 256], BF16), cv([4, 128], BF16),
                    [cv([128], F32) for i in range(2)], [cv([3, 128], BF16) for i in range(2)],
                    [[cv([128], F32) for i in range(2)] for h in range(2)],
                    [[cv([128], F32) for i in range(2)] for h in range(2)],
                    [[cv([2, 64], F32) for i in range(2)] for h in range(2)],
                    cv([2, 256], BF16), cv([2, 128], BF16))
        CS = [chunk_set() for _ in range(2)]
        M1, NCt = cv([128], F32), cv([128], F32)
        MBD, G = cv([128], BF16), cv([128], BF16)
        ST = [cv([128], BF16) for i in range(2)]
        identb = cv([128], BF16)
        sy.op("dve", ["rc2"], ["rc2"], lambda e: e.tensor_copy(out=identb, in_=ident))
        for ci_ in range(2):
            sy.op("dve", [], [("TMp", ci_)], lambda e, ci_=ci_: e.memset(CS[ci_][2], 0.0))
            sy.op("dve", [], [("Wfin", ci_)], lambda e, ci_=ci_: e.memset(CS[ci_][9], 0.0))
        both = lambda ap2: ap2.rearrange("p (a b) -> p a b", a=4)[:, 0:4:3, :]
        wnames = ("rwkv_w_r", "rwkv_w_k", "rwkv_w_v")
        muidx = (0, 2, 3)

        def load_stage(dc):
            for jj in range(3):
                src = self.w[wnames[jj]][j].rearrange("(k p) c -> p k c", p=128)[:, :, dc * 128:(dc + 1) * 128]
                sy.dma("sp", Wst[:, :, jj, :], src, [], ["Wst"], "wst")
            sy.dma("pool", wo[dc % 2], self.w["rwkv_w_o"][j, dc * 128:(dc + 1) * 128, :], [], [("wo", dc % 2)], f"rwo{dc % 2}")

        def fold_gen(dcn):
            Wfn = Wfs[0]
            wfk = ("Wf", 0)
            for k in range(NK):
                for jj in range(3):
                    m = muidx[jj]
                    sy.op("act", ["Wst", "omm"], [wfk], lambda e, k=k, jj=jj, m=m: e.activation(
                        out=Wfn[:, k, jj, 0, :], in_=Wst[:, k, jj, :], func=AF.Copy, scale=omm[:, m * 8 + k:m * 8 + k + 1]))
                    sy.op("dve", ["Wst", "vecs"], [wfk], lambda e, k=k, jj=jj, m=m: e.tensor_scalar(
                        out=Wfn[:, k, jj, 1, :], in0=Wst[:, k, jj, :],
                        scalar1=self.vecs[:, mu0 + m * 8 + k:mu0 + m * 8 + k + 1], scalar2=None, op0=ALU.mult))
                yield

        load_stage(0)
        for _ in fold_gen(0):
            pass
        stt = {"sti": 0, "yi": 0}
        def make_dc(dc):
            dcs = slice(dc * 128, (dc + 1) * 128)
            Wf = Wfs[0]
            wfk_cur = ("Wf", 0)
            wod, wok = wo[dc % 2], ("wo", dc % 2)
            def make_ctx(rt, s_):
                b = s_ % 2
                tok0 = rt * RT
                tk = slice(tok0, tok0 + RT)
                t5 = tok0 // TT
                xr = lambda k: [("xn", k, t5)] + ([("xn", k, t5 - 1)] if (tok0 % TT == 0 and t5 > 0) else [])

                def proj(jj):
                    pst, psk = self.ps()
                    for k in range(NK):
                        for sh in range(2):
                            sy.op("pe", xr(k) + [wfk_cur], [psk], lambda e, k=k, sh=sh, pst=pst: e.matmul(
                                pst[:, 0:RT], Wf[:, k, jj, sh, :], self.xn[:, k, 2 - sh + tok0:2 - sh + tok0 + RT],
                                start=(k == 0 and sh == 0), stop=(k == NK - 1 and sh == 1)))
                    return pst[:, 0:RT], psk

                def small(lhsT, rhs, reads, M=128, pst=None, psk=None, start=True, stop=True, n=RT, c0=0):
                    if pst is None:
                        pst, psk = self.ps()
                    sy.op("pe", reads, [psk], lambda e: e.matmul(pst[0:M, c0:c0 + n], lhsT, rhs, start=start, stop=stop))
                    return pst, psk

                V = lambda nm, dc=dc: self.vcol(nm, dc)
                v_sb, cs, AR, BT, KT, BN, vbf = Bset[b]
                BN = BN3[s_ % 3]
                Y = Yb[s_ % 2]
                yk_ = ("Y", s_ % 2)
                bnk = ("BN", s_ % 3)
                kq = lambda nm: (nm, b)
                return dict(locals())

            def prologue(rt, s_):
                b = s_ % 2
                c_ = make_ctx(rt, s_)
                tok0, tk, t5, xr, proj, small, V = (c_[n] for n in ('tok0', 'tk', 't5', 'xr', 'proj', 'small', 'V'))
                v_sb, cs, AR, BT, KT, BN, vbf, kq, Y, yk_, bnk = (c_[n] for n in ('v_sb', 'cs', 'AR', 'BT', 'KT', 'BN', 'vbf', 'kq', 'Y', 'yk_', 'bnk'))
                pr, prk = proj(0)
                sy.op("act", [prk], ["r_sb"], lambda e: e.activation(out=r_sb, in_=pr, func=AF.Copy))
                pk, pkk = proj(1)
                sy.op("act", [pkk], ["k_sb"], lambda e: e.activation(out=k_sb, in_=pk, func=AF.Copy))
                pv, pvk = proj(2)
                sy.op("act", [pvk], [kq("v_sb")], lambda e: e.activation(out=v_sb, in_=pv, func=AF.Copy))
                yield
                plw, plwk = small(WA2[0:64, dcs], P1[0:64, tk], ["W2", ("P1", t5)])
                sy.op("act", [plwk, "vecs"], ["sg"], lambda e: e.activation(
                    out=sg, in_=plw[:, 0:RT], func=AF.Sigmoid, bias=V(f"w0{j}"), scale=1.0))
                pa, pak = small(WA2[64:128, dcs], P1[64:128, tk], ["W2", ("P1", t5)])
                sy.op("act", [pak, "vecs"], ["asig"], lambda e: e.activation(
                    out=asig, in_=pa[:, 0:RT], func=AF.Sigmoid, bias=V(f"a0{j}"), scale=1.0))
                if j == 1:
                    pg_, pgk_ = small(GB2[32:64, dcs], GB[32:64, tk], ["W2", ("GB", t5)])
                    sy.op("act", [pgk_, "vecs"], ["tB"], lambda e: e.activation(
                        out=tB, in_=pg_[:, 0:RT], func=AF.Sigmoid, bias=V("v0"), scale=1.0))
                    sy.dma("sp", vf, self.vfirst[dcs, tk], [("vfd", dc, rt)], ["vf"], "vfl")
                    sy.op("dve", ["vf", kq("v_sb")], ["vf"], lambda e: e.tensor_tensor(out=vf, in0=vf, in1=v_sb, op=ALU.subtract))
                    sy.op("dve", ["vf", "tB"], ["vf"], lambda e: e.tensor_tensor(out=vf, in0=vf, in1=tB, op=ALU.mult))
                    sy.op("dve", ["vf", kq("v_sb")], [kq("v_sb")], lambda e: e.tensor_tensor(out=v_sb, in0=v_sb, in1=vf, op=ALU.add))
                else:
                    sy.dma("sp", self.vfirst[dcs, tk], v_sb, [kq("v_sb")], [("vfd", dc, rt)], f"vfs{b}")
                sy.op("act", [kq("v_sb")], [kq("vbf")], lambda e: e.activation(out=vbf, in_=v_sb, func=AF.Copy))
                sy.op("dve", ["sg", "cmask"], [kq("cs")], lambda e: e.tensor_tensor_scan(
                    out=cs, data0=cmask, data1=sg, initial=0.0, op0=ALU.mult, op1=ALU.add))
                sy.op("dve", [kq("cs"), "sg"], ["csp"], lambda e: e.tensor_tensor(out=csp, in0=cs, in1=sg, op=ALU.subtract))
                sy.op("act", [kq("cs")], ["pinv"], lambda e: e.activation(out=pinv, in_=cs, func=AF.Exp, scale=CD))
                sy.op("act", [kq("cs")], [kq("cs")], lambda e: e.activation(out=cs, in_=cs, func=AF.Exp, scale=-CD))
                yield
                sy.op("act", ["csp"], ["csp"], lambda e: e.activation(out=csp, in_=csp, func=AF.Exp, scale=-CD))
                sy.op("dve", ["k_sb", "vecs"], ["kk"], lambda e: e.tensor_scalar(
                    out=kk, in0=k_sb, scalar1=V(f"k_k{j}"), scalar2=None, op0=ALU.mult))
                sy.op("act", ["kk"], ["tA"], lambda e: e.activation(out=tA, in_=kk, func=AF.Square))
                pss, pssk = small(self.bones, tA, ["tA", "consts"])
                sy.op("act", [pssk, "rc"], ["tA"], lambda e: e.activation(out=tA, in_=pss[:, 0:RT], func=AF.Ln, bias=tiny, scale=1.0))
                yield
                sy.op("act", ["tA"], ["tA"], lambda e: e.activation(out=tA, in_=tA, func=AF.Exp, scale=-0.5))
                sy.op("dve", ["kk", "tA"], ["kk"], lambda e: e.tensor_tensor(out=kk, in0=kk, in1=tA, op=ALU.mult))
                sy.op("dve", ["asig", "vecs", "omm"], ["tB"], lambda e: e.tensor_scalar(
                    out=tB, in0=asig, scalar1=V(f"k_a{j}"), scalar2=okka[:, dc:dc + 1], op0=ALU.mult, op1=ALU.add))
                sy.op("dve", ["k_sb", "tB"], ["k_sb"], lambda e: e.tensor_tensor(out=k_sb, in0=k_sb, in1=tB, op=ALU.mult))
                yield
                c3 = lambda ap: ap.rearrange("p (a b) -> p a b", a=2)
                sy.op("dve", ["kk", "csp"], [kq("AR0")], lambda e: e.scalar_tensor_tensor(
                    out=AR[:, :, 0, :], in0=c3(kk), scalar=-1.0, in1=c3(csp), op0=ALU.mult, op1=ALU.mult))
                sy.op("dve", ["r_sb", kq("cs")], [kq("AR1")], lambda e: e.tensor_tensor(
                    out=AR[:, :, 1, :], in0=c3(r_sb), in1=c3(cs), op=ALU.mult))
                sy.op("dve", ["kk", "asig"], ["tB"], lambda e: e.tensor_tensor(out=tB, in0=kk, in1=asig, op=ALU.mult))
                sy.op("dve", ["tB", "pinv"], [kq("BT")], lambda e: e.tensor_tensor(out=BT, in0=tB, in1=pinv, op=ALU.mult))
                sy.op("dve", ["k_sb", "pinv"], [kq("KT")], lambda e: e.tensor_tensor(out=KT, in0=k_sb, in1=pinv, op=ALU.mult))
                yield
                sy.op("dve", ["r_sb", "k_sb", "vecs"], ["tA"], lambda e: e.scalar_tensor_tensor(
                    out=tA, in0=r_sb, scalar=V(f"r_k{j}"), in1=k_sb, op0=ALU.mult, op1=ALU.mult))
                pbn, pbnk = small(self.bones, tA, ["tA", "consts"])
                sy.op("dve", [pbnk, kq("v_sb")], [bnk], lambda e: e.tensor_tensor(out=BN, in0=pbn[:, 0:RT], in1=v_sb, op=ALU.mult))
                yield

            def scanepi(rt, s_):
                c_ = make_ctx(rt, s_)
                if rt == 0:
                    stt["sti"] = 0
                    sy.op("dve", [], [("ST", 0)], lambda e: e.memset(ST[0], 0.0))
                tok0, tk, t5, xr, proj, small, V = (c_[n] for n in ('tok0', 'tk', 't5', 'xr', 'proj', 'small', 'V'))
                v_sb, cs, AR, BT, KT, BN, vbf, kq, Y, yk_, bnk = (c_[n] for n in ('v_sb', 'cs', 'AR', 'BT', 'KT', 'BN', 'vbf', 'kq', 'Y', 'yk_', 'bnk'))
                def pre_gen(ci):
                    BH, KH, TMp, TMb, AM32, AMb, Np, NpT, Wt, Wfin, WB = CS[ci]
                    ck = lambda nm: (nm, ci)
                    cc = slice(ci * 128, (ci + 1) * 128)
                    pcol = cs[:, ci * 128 + 127:ci * 128 + 128]
                    sy.op("dve", [kq("BT"), kq("cs")], [ck("BH")], lambda e: e.tensor_scalar(out=BH, in0=BT[:, cc], scalar1=pcol, scalar2=None, op0=ALU.mult))
                    sy.op("dve", [kq("KT"), kq("cs")], [ck("KH")], lambda e: e.tensor_scalar(out=KH, in0=KT[:, cc], scalar1=pcol, scalar2=None, op0=ALU.mult))
                    ptm, ptmk = self.ps()
                    ptmb = ptm.bitcast(BF16)
                    for q, (src, skey) in enumerate(((AR[:, ci, 0, :], kq("AR0")), (BH, ck("BH")), (KH, ck("KH")), (vbf[:, cc], kq("vbf")))):
                        sy.op("pe", [skey, "rc2"], [ptmk], lambda e, q=q, src=src: e.transpose(
                            out=ptmb[:, q * 128:(q + 1) * 128], in_=src, identity=identb))
                    ptm3 = ptmb[:, 0:512].rearrange("p (a b) -> p a b", a=4)
                    sy.op("act", [ptmk], [ck("TMp")], lambda e: e.activation(out=TMp[:, :, 0:64], in_=ptm3[:, :, 0:64], func=AF.Copy))
                    sy.op("act", [ptmk], [ck("TMp")], lambda e: e.activation(out=TMp[:, :, 192:256], in_=ptm3[:, :, 64:128], func=AF.Copy))
                    sy.op("act", [ptmk], [ck("TMb")], lambda e: e.activation(out=TMb, in_=ptm3, func=AF.Copy))
                    hcs = (slice(0, 64), slice(192, 256))
                    for hd in range(2):
                        yield
                        hp = slice(hd * 64, hd * 64 + 64)
                        arh = AR[hp, ci, :, :]
                        pam, pamk = self.ps()
                        sy.op("pe", [kq("BT"), kq("AR0"), kq("AR1")], [pamk], lambda e, pam=pam, arh=arh, hp=hp: e.matmul(
                            pam[:, 0:256], BT[hp, cc], arh, start=True, stop=True))
                        sy.op("pe", [kq("KT"), kq("AR0"), kq("AR1")], [pamk], lambda e, pam=pam, arh=arh, hp=hp: e.matmul(
                            pam[:, 256:512], KT[hp, cc], arh, start=True, stop=True))
                        sy.op("dve", [pamk, "rc2"], [("AM", ci, hd)], lambda e, pam=pam, hd=hd: e.tensor_tensor(
                            out=AM32[hd], in0=pam[:, 0:128], in1=mk4[:, 0:128], op=ALU.mult))
                        sy.op("dve", [pamk, "rc2"], [("AM", ci, hd)], lambda e, pam=pam, hd=hd: e.tensor_tensor(
                            out=AMb[hd], in0=pam[:, 128:512].rearrange("p (a b) -> p a b", a=3),
                            in1=mk4[:, 128:512].rearrange("p (a b) -> p a b", a=3), op=ALU.mult))
                        pnt, pntk = self.ps()
                        sy.op("pe", [kq("BT"), kq("AR0")], [pntk], lambda e, pnt=pnt, hp=hp: e.matmul(
                            pnt[:, 0:128], AR[hp, ci, 0, :], BT[hp, cc], start=True, stop=True))
                        sy.op("dve", [pntk, "rc2"], [("NpT", ci, hd, 0)], lambda e, pnt=pnt, hd=hd: e.tensor_tensor(
                            out=NpT[hd][0], in0=pnt[:, 0:128], in1=mkL, op=ALU.mult))
                        sy.op("pool", [ck("TMp")], [("Wt", ci, hd, 0)], lambda e, hd=hd: e.tensor_copy(out=Wt[hd][0][:, 0, :], in_=TMp[:, 0, hcs[hd]]))
                        pxv, pxvk = self.ps()
                        sy.op("pe", [("AM", ci, hd), ck("TMp")], [pxvk], lambda e, pxv=pxv, hd=hd: e.matmul(
                            pxv[:, 0:64], AMb[hd][:, 1, :], TMp[:, 3, hcs[hd]], start=True, stop=True))
                        sy.op("act", [pxvk], [("Wt", ci, hd, 0)], lambda e, pxv=pxv, hd=hd: e.activation(
                            out=Wt[hd][0][:, 1, :], in_=pxv[:, 0:64], func=AF.Copy))
                    yield
                    for lvl in range(7):
                        yield
                        cur, nxt = lvl % 2, (lvl + 1) % 2
                        for hd in range(2):
                            npc = AM32[hd] if lvl == 0 else Np[hd][cur]
                            npk = ("AM", ci, hd) if lvl == 0 else ("Np", ci, hd, cur)
                            wcur = Wt[hd][cur]
                            pw, pwk = self.ps()
                            sy.op("pe", [npk, ("Wt", ci, hd, cur)], [pwk], lambda e, pw=pw, npc=npc, wcur=wcur: e.matmul(
                                pw[:, 0:128], npc, wcur.rearrange("p a b -> p (a b)"), start=True, stop=True))
                            pw3 = pw[:, 0:128].rearrange("p (a b) -> p a b", a=2)
                            if lvl < 6:
                                sy.op("dve", [pwk, ("Wt", ci, hd, cur)], [("Wt", ci, hd, nxt)], lambda e, pw3=pw3, wcur=wcur, hd=hd, nxt=nxt: e.tensor_tensor(
                                    out=Wt[hd][nxt], in0=pw3, in1=wcur, op=ALU.add))
                                pn, pnk = self.ps()
                                sy.op("pe", [npk, ("NpT", ci, hd, cur)], [pnk], lambda e, pn=pn, npc=npc, hd=hd, cur=cur: e.matmul(
                                    pn[:, 0:128], NpT[hd][cur], npc, start=True, stop=True))
                                sy.op("act", [pnk], [("Np", ci, hd, nxt)], lambda e, pn=pn, hd=hd, nxt=nxt: e.activation(
                                    out=Np[hd][nxt], in_=pn[:, 0:128], func=AF.Copy))
                                pn2, pn2k = self.ps()
                                sy.op("pe", [npk, ("NpT", ci, hd, cur)], [pn2k], lambda e, pn2=pn2, npc=npc, hd=hd, cur=cur: e.matmul(
                                    pn2[:, 0:128], npc, NpT[hd][cur], start=True, stop=True))
                                sy.op("act", [pn2k], [("NpT", ci, hd, nxt)], lambda e, pn2=pn2, hd=hd, nxt=nxt: e.activation(
                                    out=NpT[hd][nxt], in_=pn2[:, 0:128], func=AF.Copy))
                            else:
                                sy.op("dve", [pwk, ("Wt", ci, hd, cur)], [ck("Wfin")], lambda e, pw3=pw3, wcur=wcur, hd=hd: e.tensor_tensor(
                                    out=Wfin[:, :, hcs[hd]], in0=pw3, in1=wcur, op=ALU.add))
                                sy.op("dve", [pwk, ("Wt", ci, hd, cur)], [ck("WB")], lambda e, pw3=pw3, wcur=wcur, hd=hd: e.tensor_tensor(
                                    out=WB[:, :, hd * 64:(hd + 1) * 64], in0=pw3, in1=wcur, op=ALU.add))
                    yield
                    yield

                def tail_gen(ci):
                    BH, KH, TMp, TMb, AM32, AMb, Np, NpT, Wt, Wfin, WB = CS[ci]
                    ck = lambda nm: (nm, ci)
                    cc = slice(ci * 128, (ci + 1) * 128)
                    pcol = cs[:, ci * 128 + 127:ci * 128 + 128]
                    Ah_b, Uh_b = WB[:, 0, :], WB[:, 1, :]
                    Bh_b, Kh_b, VT_b = TMb[:, 1, :], TMb[:, 2, :], TMb[:, 3, :]
                    stc, stn = ST[stt['sti'] % 2], ST[(stt['sti'] + 1) % 2]
                    stck, stnk = ("ST", stt['sti'] % 2), ("ST", (stt['sti'] + 1) % 2)
                    stt['sti'] += 1
                    pm, pmk = self.ps()
                    sy.op("pe", [ck("WB"), ck("TMb")], [pmk], lambda e, pm=pm: e.matmul(pm[:, 0:128], Ah_b, Bh_b, start=True, stop=True))
                    sy.op("dve", [pmk, "consts"], ["M1"], lambda e, pm=pm: e.tensor_tensor(out=M1, in0=pm[:, 0:128], in1=self.bones, op=ALU.mult))
                    sy.op("dve", ["M1", "rc2", kq("cs")], ["MBD"], lambda e: e.scalar_tensor_tensor(
                        out=MBD, in0=ident, scalar=pcol, in1=M1, op0=ALU.mult, op1=ALU.add))
                    pn_, pnk_ = self.ps()
                    sy.op("pe", [ck("WB"), ck("TMb")], [pnk_], lambda e, pn_=pn_: e.matmul(pn_[:, 0:128], Bh_b, Uh_b, start=True, stop=False))
                    sy.op("pe", [ck("TMb")], [pnk_], lambda e, pn_=pn_: e.matmul(pn_[:, 0:128], Kh_b, VT_b, start=False, stop=True))
                    sy.op("dve", [pnk_, "consts"], ["NCt"], lambda e, pn_=pn_: e.tensor_tensor(out=NCt, in0=pn_[:, 0:128], in1=self.bones, op=ALU.mult))
                    yield
                    pg, pgk = self.ps()
                    sy.op("pe", [ck("Wfin"), ("AM", ci, 0)], [pgk], lambda e, pg=pg: e.matmul(pg[:, 0:128], Wfin[:, 0, 0:128], AMb[0][:, 0, :], start=True, stop=False))
                    sy.op("pe", [ck("Wfin"), ("AM", ci, 1)], [pgk], lambda e, pg=pg: e.matmul(pg[:, 0:128], Wfin[:, 0, 128:256], AMb[1][:, 0, :], start=False, stop=True))
                    sy.op("dve", [pgk, kq("AR1")], ["G"], lambda e, pg=pg: e.tensor_tensor(out=G, in0=pg[:, 0:128], in1=AR[:, ci, 1, :], op=ALU.add))
                    yield
                    py, pyk = self.ps()
                    sy.op("pe", [stck, "G"], [pyk], lambda e, py=py, stc=stc: e.matmul(py[:, 0:128], stc, G, start=True, stop=False))
                    sy.op("pe", [ck("Wfin"), ("AM", ci, 0)], [pyk], lambda e, py=py: e.matmul(py[:, 0:128], Wfin[:, 1, 0:128], AMb[0][:, 0, :], start=False, stop=False))
                    sy.op("pe", [ck("Wfin"), ("AM", ci, 1)], [pyk], lambda e, py=py: e.matmul(py[:, 0:128], Wfin[:, 1, 128:256], AMb[1][:, 0, :], start=False, stop=False))
                    sy.op("pe", [ck("TMp"), ("AM", ci, 0)], [pyk], lambda e, py=py: e.matmul(py[:, 0:128], TMp[:, 3, 0:128], AMb[0][:, 2, :], start=False, stop=False))
                    sy.op("pe", [ck("TMp"), ("AM", ci, 1)], [pyk], lambda e, py=py: e.matmul(py[:, 0:128], TMp[:, 3, 128:256], AMb[1][:, 2, :], start=False, stop=True))
                    sy.op("act", [pyk], [yk_], lambda e, py=py: e.activation(out=Y[:, cc], in_=py[:, 0:128], func=AF.Copy))
                    yield
                    pst_, pstk_ = self.ps()
                    sy.op("pe", ["MBD", stck], [pstk_], lambda e, pst_=pst_, stc=stc: e.matmul(pst_[:, 0:128], MBD, stc, start=True, stop=True))
                    sy.op("dve", [pstk_, "NCt"], [stnk], lambda e, pst_=pst_, stn=stn: e.tensor_tensor(out=stn, in0=pst_[:, 0:128], in1=NCt, op=ALU.add))
                    yield

                pgs = [pre_gen(0), pre_gen(1)]
                while pgs:
                    for g_ in list(pgs):
                        try:
                            next(g_)
                        except StopIteration:
                            pgs.remove(g_)
                    yield
                for ci in range(2):
                    for _ in tail_gen(ci):
                        yield
                yield

            def epilogue(rt, s_):
                c_ = make_ctx(rt, s_)
                tok0, tk, t5, xr, proj, small, V = (c_[n] for n in ('tok0', 'tk', 't5', 'xr', 'proj', 'small', 'V'))
                v_sb, cs, AR, BT, KT, BN, vbf, kq, Y, yk_, bnk = (c_[n] for n in ('v_sb', 'cs', 'AR', 'BT', 'KT', 'BN', 'vbf', 'kq', 'Y', 'yk_', 'bnk'))
                pmn, pmnk = small(self.bones, Y, [yk_, "consts"])
                sy.op("dve", [pmnk, yk_], ["YC"], lambda e: e.scalar_tensor_tensor(
                    out=YC, in0=pmn[:, 0:RT], scalar=-1.0 / 64, in1=Y, op0=ALU.mult, op1=ALU.add))
                sy.op("act", ["YC"], ["tC"], lambda e: e.activation(out=tC, in_=YC, func=AF.Square))
                pvr, pvrk = small(self.bones, tC, ["tC", "consts"])
                sy.op("act", [pvrk, "rc"], ["tC"], lambda e: e.activation(out=tC, in_=pvr[:, 0:RT], func=AF.Ln, bias=gneps, scale=1.0 / 64))
                sy.op("act", ["tC"], ["tC"], lambda e: e.activation(out=tC, in_=tC, func=AF.Exp, scale=-0.5))
                sy.op("dve", ["YC", "tC"], ["YC"], lambda e: e.tensor_tensor(out=YC, in0=YC, in1=tC, op=ALU.mult))
                sy.op("dve", ["YC", "vecs"], ["YC"], lambda e: e.tensor_scalar(
                    out=YC, in0=YC, scalar1=V(f"lnx_w{j}"), scalar2=V(f"lnx_b{j}"), op0=ALU.mult, op1=ALU.add))
                sy.op("dve", ["YC", bnk], ["YC"], lambda e: e.tensor_tensor(out=YC, in0=YC, in1=BN, op=ALU.add))
                yield
                pgt, pgtk = small(GA2[:, dcs], GA[:, tk], ["W2", ("GA", t5)], stop=False)
                small(GB2[0:32, dcs], GB[0:32, tk], ["W2", ("GB", t5)], pst=pgt, psk=pgtk, start=False, stop=True)
                yo = yout[stt['yi'] % 2]
                yok = ("yout", stt['yi'] % 2)
                stt['yi'] += 1
                sy.op("dve", ["YC", pgtk], [yok], lambda e, yo=yo: e.tensor_tensor(out=yo, in0=YC, in1=pgt[:, 0:RT], op=ALU.mult))
                for m in range(NK):
                    po, pok = self.ps()
                    sy.op("pe", [wok, yok], [pok], lambda e, m=m, po=po, yo=yo: e.matmul(
                        po[:, 0:RT], wod[:, m * 128:(m + 1) * 128], yo, start=True, stop=True))
                    sy.op("dve", [pok, ("h", m, t5)], [("h", m, t5)], lambda e, m=m, po=po: e.tensor_tensor(
                        out=self.h[:, m, tk], in0=po[:, 0:RT], in1=self.h[:, m, tk], op=ALU.add))
                yield

            return prologue, scanepi, epilogue

        dcg = {}

        def DCG(dc):
            if dc not in dcg:
                dcg[dc] = make_dc(dc)
            return dcg[dc]

        NS = NK * NRT
        for _ in DCG(0)[0](0, 0):
            pass
        for s_ in range(NS + 1):
            gens = []
            if s_ < NS:
                dc, rt = divmod(s_, NRT)
                gens.append(DCG(dc)[1](rt, s_))
            if s_ >= 1:
                dcp, rtp = divmod(s_ - 1, NRT)
                gens.append(DCG(dcp)[2](rtp, s_ - 1))
            if s_ + 1 < NS:
                dcn, rtn = divmod(s_ + 1, NRT)
                gens.append(DCG(dcn)[0](rtn, s_ + 1))
            if s_ < NS and dc + 1 < NK:
                if rt == 1:
                    load_stage(dc + 1)
                if rt == NRT - 1:
                    for _ in fold_gen(dc + 1):
                        pass
            while gens:
                for g_ in list(gens):
                    try:
                        next(g_)
                    except StopIteration:
                        gens.remove(g_)


ALL_LAYERS = []
for _l in range(DEPTH):
    ALL_LAYERS += [("mix", _l), ("mlp", _l)]

WEIGHT_NAMES = ["mlp_up", "mlp_down", "rwkv_w_r", "rwkv_w_k", "rwkv_w_v", "rwkv_w_o",
                "rwkv_decay_w1", "rwkv_decay_w2", "rwkv_iclr_a1", "rwkv_iclr_a2",
                "rwkv_gate_g1", "rwkv_gate_g2", "rwkv_vres_v1", "rwkv_vres_v2",
                "conv_w_in", "conv_w_out", "sb_w_qkv", "sb_w_o"]


def run(inputs, layers, n_cores=8, trace=False):
    inp = {k: np.asarray(v) for k, v in inputs.items()}
    prog = Prog(layers)
    nc = prog.build()
    vecs = pack_vecs(inp)
    wts = {n: np.ascontiguousarray(inp[n], dtype=np.float32) for n in WEIGHT_NAMES}
    in_maps = []
    for b in range(n_cores):
        m = {"xT": np.ascontiguousarray(inp["x"][b].T), "vecs": vecs}
        m.update(wts)
        in_maps.append(m)
    res = run_bass_kernel_spmd(nc, in_maps, core_ids=list(range(n_cores)), trace=trace)
    out = np.stack([np.ascontiguousarray(r["outT"].T) for r in res.results], axis=0)
    return out, res, prog


def kernel(**inputs):
    out, _, _ = run(inputs, ALL_LAYERS)
    return out.astype(np.float32)
```
